# Optimizing a Trainium2 kernel written in Bass

```python
import math
import jax, jax.numpy as jnp
from jax import lax
import numpy as np

D_MODEL = 1024
BATCH = 8
SEQ = 4096
DEPTH = 2

HEAD_DIM = 64
N_HEADS = D_MODEL // HEAD_DIM
BRANCH_WIDTH = N_HEADS * HEAD_DIM
N_MIXERS = 2
DILATED_PAIRS = ((128, 1), (512, 4), (2048, 16))
N_DIL_GROUPS = len(DILATED_PAIRS)
T5_BUCKETS = 32
T5_MAX_DISTANCE = 1024
GRID_W = 64
NA_ROWS = 8
NA_COLS = 16
NA_COL_BLOCK = 16
NA_SLAB = NA_COL_BLOCK + NA_COLS
RMS_EPS = 1e-6
NEG_INF = -1e30
N_A_LAYERS = (DEPTH + N_MIXERS - 1) // N_MIXERS
N_B_LAYERS = DEPTH // N_MIXERS
A_IN_COLS = N_DIL_GROUPS * 3 * BRANCH_WIDTH + BRANCH_WIDTH
B_IN_COLS = 4 * BRANCH_WIDTH

kernel_name = "hybrid_dilated_neighbourhood_encoder"


def rms_norm(x, g):
    xf = x.astype(jnp.float32)
    y = xf * lax.rsqrt(jnp.mean(xf * xf, axis=-1, keepdims=True) + RMS_EPS)
    return (y * g.astype(jnp.float32)).astype(x.dtype)


def t5_bucket(rel):
    half = T5_BUCKETS // 2
    max_exact = half // 2
    ret = jnp.where(rel > 0, half, 0)
    n = jnp.abs(rel)
    nf = jnp.maximum(n, 1).astype(jnp.float32)
    large = max_exact + (jnp.log(nf / max_exact) / math.log(T5_MAX_DISTANCE / max_exact)
                         * (half - max_exact)).astype(jnp.int32)
    large = jnp.minimum(large, half - 1)
    return ret + jnp.where(n < max_exact, n, large)


def _to_sub(t, dilation):
    b, s, h, dh = t.shape
    return t.reshape(b, s // dilation, dilation, h, dh).transpose(0, 2, 3, 1, 4)


def dilated_group_attention(q, k, v, bias_table, dilation, reach):
    b, s, h, dh = q.shape
    length = s // dilation
    nb = -(-length // reach)
    lp = nb * reach
    qs = jnp.pad(_to_sub(q, dilation), ((0, 0), (0, 0), (0, 0), (0, lp - length), (0, 0)))
    kv_pad = ((0, 0), (0, 0), (0, 0), (reach, lp - length + reach), (0, 0))
    ks = jnp.pad(_to_sub(k, dilation), kv_pad)
    vs = jnp.pad(_to_sub(v, dilation), kv_pad)
    rel = np.arange(3 * reach)[None, :] - reach - np.arange(reach)[:, None]
    near = jnp.asarray(np.abs(rel) <= reach)
    bias = bias_table[:, t5_bucket(jnp.asarray(rel * dilation, dtype=jnp.int32))].astype(jnp.float32)
    scale = dh ** -0.5

    def block(i):
        start = i * reach
        qb = lax.dynamic_slice_in_dim(qs, start, reach, axis=3)
        kb = lax.dynamic_slice_in_dim(ks, start, 3 * reach, axis=3)
        vb = lax.dynamic_slice_in_dim(vs, start, 3 * reach, axis=3)
        key_pos = start - reach + jnp.arange(3 * reach)
        valid = near & ((key_pos >= 0) & (key_pos < length))[None, :]
        logits = jnp.einsum('brhqd,brhkd->brhqk', qb, kb).astype(jnp.float32) * scale + bias
        logits = jnp.where(valid, logits, NEG_INF)
        mx = jnp.max(logits, axis=-1, keepdims=True)
        p = jnp.exp(logits - mx)
        denom = jnp.sum(p, axis=-1)
        out = jnp.einsum('brhqk,brhkd->brhqd', p.astype(vb.dtype), vb).astype(jnp.float32) / denom[..., None]
        return out, mx[..., 0] + jnp.log(denom)

    out, lse = lax.map(block, jnp.arange(nb))
    out = out.transpose(1, 2, 3, 0, 4, 5).reshape(b, dilation, h, lp, dh)[:, :, :, :length]
    out = out.transpose(0, 3, 1, 2, 4).reshape(b, s, h, dh)
    lse = lse.transpose(1, 2, 3, 0, 4).reshape(b, dilation, h, lp)[:, :, :, :length]
    lse = lse.transpose(0, 3, 1, 2).reshape(b, s, h)
    return out, lse


def mixer_a(hn, w_in, w_out, q_gain, k_gain, t5_bias):
    b, s, _ = hn.shape
    proj = hn @ w_in
    n_qkv = N_DIL_GROUPS * 3 * BRANCH_WIDTH
    qkv = proj[..., :n_qkv].reshape(b, s, N_DIL_GROUPS, 3, N_HEADS, HEAD_DIM)
    gate = proj[..., n_qkv:]
    outs, lses = [], []
    for g, (window, dilation) in enumerate(DILATED_PAIRS):
        q = rms_norm(qkv[:, :, g, 0], q_gain[g])
        k = rms_norm(qkv[:, :, g, 1], k_gain[g])
        v = qkv[:, :, g, 2]
        reach = (window // 2) // dilation
        o, lse = dilated_group_attention(q, k, v, t5_bias[g * N_HEADS:(g + 1) * N_HEADS],
                                         dilation, reach)
        outs.append(o)
        lses.append(lse)
    alpha = jax.nn.softmax(jnp.stack(lses), axis=0)
    y = jnp.sum(alpha[..., None] * jnp.stack(outs), axis=0)
    y = y.reshape(b, s, BRANCH_WIDTH).astype(hn.dtype) * jax.nn.silu(gate)
    return y @ w_out


def mixer_b(hn, w_in, w_out, q_gain, k_gain, rpb):
    b, s, _ = hn.shape
    rows = s // GRID_W
    wr = min(NA_ROWS, rows)
    proj = hn @ w_in
    q, k, v, gate = jnp.split(proj, 4, axis=-1)
    q = rms_norm(q.reshape(b, s, N_HEADS, HEAD_DIM), q_gain) * (HEAD_DIM ** -0.5)
    k = rms_norm(k.reshape(b, s, N_HEADS, HEAD_DIM), k_gain)
    v = v.reshape(b, s, N_HEADS, HEAD_DIM)

    def to_grid(t):
        return t.reshape(b, rows, GRID_W, N_HEADS, HEAD_DIM).transpose(0, 3, 1, 2, 4)

    qg, kg, vg = to_grid(q), to_grid(k), to_grid(v)
    n_cb = GRID_W // NA_COL_BLOCK
    qcol = np.arange(GRID_W).reshape(n_cb, NA_COL_BLOCK)
    cstart = np.clip(qcol - NA_COLS // 2, 0, GRID_W - NA_COLS)
    slab0 = np.clip(np.arange(n_cb) * NA_COL_BLOCK - NA_COLS // 2, 0, GRID_W - NA_SLAB)
    slab_cols = slab0[:, None] + np.arange(NA_SLAB)
    kc = slab_cols[:, None, :]
    col_valid = jnp.asarray((kc >= cstart[..., None]) & (kc < cstart[..., None] + NA_COLS))
    col_idx = np.clip(kc - qcol[..., None], -(NA_COLS - 1), NA_COLS - 1) + NA_COLS - 1
    col_bias = rpb[:, :, col_idx].astype(jnp.float32)
    slab_cols_j = jnp.asarray(slab_cols)

    def row(r):
        rs = jnp.clip(r - wr // 2, 0, rows - wr)
        qr = lax.dynamic_index_in_dim(qg, r, axis=2, keepdims=False)
        qr = qr.reshape(b, N_HEADS, n_cb, NA_COL_BLOCK, HEAD_DIM)
        kr = lax.dynamic_slice_in_dim(kg, rs, wr, axis=2)[:, :, :, slab_cols_j, :]
        vr = lax.dynamic_slice_in_dim(vg, rs, wr, axis=2)[:, :, :, slab_cols_j, :]
        row_idx = rs + jnp.arange(wr) - r + NA_ROWS - 1
        bias = col_bias[:, row_idx].transpose(0, 2, 3, 1, 4)
        logits = jnp.einsum('bhcqd,bhrcjd->bhcqrj', qr, kr).astype(jnp.float32) + bias
        logits = jnp.where(col_valid[:, :, None, :], logits, NEG_INF)
        p = jax.nn.softmax(logits.reshape(b, N_HEADS, n_cb, NA_COL_BLOCK, wr * NA_SLAB), axis=-1)
        p = p.reshape(b, N_HEADS, n_cb, NA_COL_BLOCK, wr, NA_SLAB).astype(vr.dtype)
        out = jnp.einsum('bhcqrj,bhrcjd->bhcqd', p, vr)
        return out.reshape(b, N_HEADS, GRID_W, HEAD_DIM)

    out = lax.map(row, jnp.arange(rows))
    y = out.transpose(1, 0, 3, 2, 4).reshape(b, s, BRANCH_WIDTH).astype(hn.dtype)
    y = y * jax.nn.silu(gate)
    return y @ w_out


def setup_inputs(seed: int = 0) -> dict:
    key = jax.random.key(seed)
    ks = jax.random.split(key, 13)
    f32 = jnp.float32
    x = jax.random.normal(ks[0], (BATCH, SEQ, D_MODEL), f32)
    norm_gain = 1.0 + 0.02 * jax.random.normal(ks[1], (DEPTH, D_MODEL), f32)
    a_w_in = jax.random.normal(ks[2], (N_A_LAYERS, D_MODEL, A_IN_COLS), f32) * D_MODEL ** -0.5
    a_w_out = jax.random.normal(ks[3], (N_A_LAYERS, BRANCH_WIDTH, D_MODEL), f32) * BRANCH_WIDTH ** -0.5
    a_q_gain = 1.0 + 0.02 * jax.random.normal(ks[4], (N_A_LAYERS, N_DIL_GROUPS, HEAD_DIM), f32)
    a_k_gain = 1.0 + 0.02 * jax.random.normal(ks[5], (N_A_LAYERS, N_DIL_GROUPS, HEAD_DIM), f32)
    t5_bias = 0.1 * jax.random.normal(ks[6], (N_DIL_GROUPS * N_HEADS, T5_BUCKETS), f32)
    b_w_in = jax.random.normal(ks[7], (N_B_LAYERS, D_MODEL, B_IN_COLS), f32) * D_MODEL ** -0.5
    b_w_out = jax.random.normal(ks[8], (N_B_LAYERS, BRANCH_WIDTH, D_MODEL), f32) * BRANCH_WIDTH ** -0.5
    b_q_gain = 1.0 + 0.02 * jax.random.normal(ks[9], (N_B_LAYERS, HEAD_DIM), f32)
    b_k_gain = 1.0 + 0.02 * jax.random.normal(ks[10], (N_B_LAYERS, HEAD_DIM), f32)
    b_rpb = 0.1 * jax.random.normal(ks[11], (N_B_LAYERS, N_HEADS, 2 * NA_ROWS - 1, 2 * NA_COLS - 1), f32)
    return {"x": x, "norm_gain": norm_gain, "a_w_in": a_w_in, "a_w_out": a_w_out,
            "a_q_gain": a_q_gain, "a_k_gain": a_k_gain, "t5_bias": t5_bias,
            "b_w_in": b_w_in, "b_w_out": b_w_out, "b_q_gain": b_q_gain,
            "b_k_gain": b_k_gain, "b_rpb": b_rpb}


def reference(x, norm_gain, a_w_in, a_w_out, a_q_gain, a_k_gain, t5_bias,
              b_w_in, b_w_out, b_q_gain, b_k_gain, b_rpb):
    for i in range(DEPTH):
        hn = rms_norm(x, norm_gain[i])
        j = i // N_MIXERS
        if i % N_MIXERS == 0:
            y = mixer_a(hn, a_w_in[j], a_w_out[j], a_q_gain[j], a_k_gain[j], t5_bias)
        else:
            y = mixer_b(hn, b_w_in[j], b_w_out[j], b_q_gain[j], b_k_gain[j], b_rpb[j])
        x = x + y.astype(x.dtype)
    return x
```

```python
import contextlib
import numpy as np
import concourse.bass as bass
import concourse.mybir as mybir
from concourse.bass_utils import run_bass_kernel_spmd

F32 = mybir.dt.float32
BF16 = mybir.dt.bfloat16
AF = mybir.ActivationFunctionType
ALU = mybir.AluOpType

S = 4096
D = 1024
NCORES = 8
DILS = (1, 4, 16)
EPS = 1e-6
NEG = -30000.0


class Buf:
    __slots__ = ("name", "lw", "rd")

    def __init__(self, name=""):
        self.name = name
        self.lw = None
        self.rd = []


class DmaSlot:
    __slots__ = ("sem", "count", "name")

    def __init__(self, name):
        self.name = name
        self.sem = None
        self.count = 0


class Op:
    __slots__ = ("eng", "fn", "deps", "slot", "signal", "tick", "semkey", "known", "idx")


COMPUTE = ("pe", "act", "dve", "pool")
ENGS = ("pe", "act", "dve", "pool", "sp")


class Sync:
    def __init__(self, nc, es, nslots=40):
        self.nc = nc
        self.esem = {e: es.enter_context(nc.semaphore("sem_" + e)) for e in COMPUTE}
        self.tick = {e: 0 for e in COMPUTE}
        self.slots = []
        for i in range(nslots):
            s = DmaSlot("dq%d" % i)
            s.sem = es.enter_context(nc.semaphore(s.name))
            self.slots.append(s)


class Prog:
    def __init__(self, sync):
        self.sync = sync
        self.nc = sync.nc
        self.ops = []
        self.nslot = 0

    def slot(self, name=""):
        s = self.sync.slots[self.nslot]
        self.nslot += 1
        return s

    def add(self, eng, fn, reads=(), writes=(), slot=None):
        op = Op()
        op.eng = eng
        op.fn = fn
        op.slot = slot
        op.signal = slot is not None
        op.tick = None
        op.idx = len(self.ops)
        deps = {}
        is_dma = slot is not None
        for b in reads:
            w = b.lw
            if w is not None:
                if is_dma or w.slot is not None or w.eng != eng or eng != "pe":
                    deps[w.idx] = w
        for b in writes:
            w = b.lw
            if w is not None and (is_dma or w.slot is not None or w.eng != eng):
                deps[w.idx] = w
            for r in b.rd:
                if is_dma or r.slot is not None or r.eng != eng:
                    deps[r.idx] = r
        for b in reads:
            b.rd.append(op)
        for b in writes:
            b.lw = op
            b.rd = []
        op.deps = list(deps.values())
        for d in op.deps:
            d.signal = True
        self.ops.append(op)
        return op

    def pe(self, fn, reads=(), writes=()):
        return self.add("pe", fn, reads, writes)

    def act(self, fn, reads=(), writes=()):
        return self.add("act", fn, reads, writes)

    def dve(self, fn, reads=(), writes=()):
        return self.add("dve", fn, reads, writes)

    def pool(self, fn, reads=(), writes=()):
        return self.add("pool", fn, reads, writes)

    def dma(self, slot, fn, reads=(), writes=()):
        return self.add("sp", fn, reads, writes, slot=slot)

    def emit(self):
        nc = self.nc
        sy = self.sync
        for op in self.ops:
            if op.slot is not None:
                op.slot.count += 16
                op.tick = op.slot.count
                op.semkey = op.slot
            elif op.signal:
                sy.tick[op.eng] += 1
                op.tick = sy.tick[op.eng]
                op.semkey = op.eng
        base = {}
        for e in COMPUTE:
            base[e] = 0
        clock = {e: {} for e in ENGS}
        start_tick = dict(self._start_tick)
        start_slot = dict(self._start_slot)
        for e in ENGS:
            for k, v in start_tick.items():
                clock[e][k] = v
            for k, v in start_slot.items():
                clock[e][k] = v
        plan = {e: [] for e in ENGS}
        for op in self.ops:
            ck = clock[op.eng]
            need = {}
            for d in op.deps:
                if ck.get(d.semkey, 0) >= d.tick:
                    continue
                if need.get(d.semkey, 0) < d.tick:
                    need[d.semkey] = d.tick
            for d in op.deps:
                for k, v in d.known.items():
                    if ck.get(k, 0) < v:
                        ck[k] = v
            waits = list(need.items())
            for k, v in waits:
                if ck.get(k, 0) < v:
                    ck[k] = v
            if op.tick is not None:
                kn = dict(ck)
                if kn.get(op.semkey, 0) < op.tick:
                    kn[op.semkey] = op.tick
                op.known = kn
            plan[op.eng].append((op, waits))
        final_waits = [(s, s.count) for s in sy.slots[: self.nslot] if s.count > start_slot.get(s, 0)]
        esem = sy.esem

        def semof(k):
            return k.sem if isinstance(k, DmaSlot) else esem[k]

        def run(engname):
            def body(eng):
                for op, waits in plan[engname]:
                    for k, v in waits:
                        eng.wait_ge(semof(k), v)
                    ins = op.fn(eng)
                    if op.slot is not None:
                        ins.then_inc(op.slot.sem, 16)
                    elif op.tick is not None:
                        ins.then_inc(esem[engname], 1)
                if engname == "sp":
                    for s, v in final_waits:
                        eng.wait_ge(s.sem, v)
            return body

        with nc.Block() as block:
            block.tensor(run("pe"))
            block.scalar(run("act"))
            block.vector(run("dve"))
            block.gpsimd(run("pool"))
            block.sync(run("sp"))

    def begin(self):
        sy = self.sync
        self._start_tick = dict(sy.tick)
        self._start_slot = {s: s.count for s in sy.slots}
        return self


_UID = [0]


def uid():
    _UID[0] += 1
    return "_u%d" % _UID[0]


def fview(ap, dims):
    return bass.AP(tensor=ap.tensor, offset=ap.offset, ap=[list(ap.ap[0])] + [list(d) for d in dims])


def FV(t, p0, p1, off, dims):
    return fview(t[p0:p1, off:off + 1], dims)


class Rot:
    def __init__(self, items):
        self.items = items
        self.bufs = [Buf() for _ in items]
        self.i = 0

    def next(self):
        k = self.i % len(self.items)
        self.i += 1
        return self.items[k], self.bufs[k]


DBG = {}


def dump(P, name, t, bufs, dt=None):
    nc = P.nc
    shape = list(t.shape)
    d = nc.dram_tensor("dbg_" + name, shape, dt or t.dtype, kind="ExternalOutput").ap()
    P.dma(P.slot(), lambda e: e.dma_start(out=d[:, :], in_=t[:, :]), reads=bufs)


def phase_norm(sync, es_outer, xsrc, hnT, ident):
    nc = sync.nc
    P = Prog(sync).begin()
    with contextlib.ExitStack() as es:
        sfx = uid()

        def sb(name, shape, dt):
            return es.enter_context(nc.sbuf_tensor(name + sfx, shape, dt))
        xt = Rot([sb("n_xt%d" % i, [128, D], F32) for i in range(3)])
        junk = sb("n_junk", [128, D], BF16)
        hn0 = Rot([sb("n_hn%d" % i, [128, D], BF16) for i in range(2)])
        ss = sb("n_ss", [128, 32], F32)
        ln = sb("n_ln", [128, 32], F32)
        rs = sb("n_rs", [128, 32], F32)
        epsc = sb("n_eps", [128, 1], F32)
        ptr = Rot([es.enter_context(nc.psum_tensor("n_ptr%d" % i + sfx, [128, D], BF16)) for i in range(2)])
        bjunk = Buf()
        beps = Buf()
        bhn = Buf()
        slots = [P.slot() for _ in range(3)]
        P.dve(lambda e: e.memset(epsc[:], EPS), writes=[beps])
        for i in range(32):
            x_t, bx = xt.next()
            sl = slots[i % 3]
            P.dma(sl, lambda e, x_t=x_t, i=i: e.dma_start(out=x_t[:], in_=xsrc[128 * i:128 * (i + 1), :]), writes=[bx])
            bss = Buf()
            P.act(lambda e, x_t=x_t, i=i: e.activation(out=junk[:], in_=x_t[:], func=AF.Square, accum_out=ss[:, i:i + 1]),
                  reads=[bx], writes=[bjunk, bss])
            bln = Buf()
            P.act(lambda e, i=i: e.activation(out=ln[:, i:i + 1], in_=ss[:, i:i + 1], func=AF.Ln, bias=epsc[:], scale=1.0 / D),
                  reads=[bss, beps], writes=[bln])
            brs = Buf()
            P.act(lambda e, i=i: e.activation(out=rs[:, i:i + 1], in_=ln[:, i:i + 1], func=AF.Exp, scale=-0.5),
                  reads=[bln], writes=[brs])
            h_t, bh = hn0.next()
            P.dve(lambda e, h_t=h_t, x_t=x_t, i=i: e.tensor_scalar(out=h_t[:], in0=x_t[:], scalar1=rs[:, i:i + 1], scalar2=None, op0=ALU.mult),
                  reads=[bx, brs], writes=[bh])
            p_t, bp = ptr.next()
            for kc in range(8):
                P.pe(lambda e, p_t=p_t, h_t=h_t, kc=kc: e.transpose(p_t[:, kc * 128:(kc + 1) * 128], h_t[:, kc * 128:(kc + 1) * 128], ident[:]),
                     reads=[bh], writes=[bp])
            dst = lambda i=i: FV(hnT, 0, 128, 128 * i, [[S, 8], [1, 128]])
            src = lambda p_t=p_t: FV(p_t, 0, 128, 0, [[128, 8], [1, 128]])
            if i % 2 == 0:
                P.act(lambda e, dst=dst, src=src: e.activation(out=dst(), in_=src(), func=AF.Copy), reads=[bp], writes=[bhn])
            else:
                P.dve(lambda e, dst=dst, src=src: e.tensor_copy(out=dst(), in_=src()), reads=[bp], writes=[bhn])
        if DBG.get("hnT"):
            dump(P, "hnT", hnT, [bhn])
            dump(P, "rs", rs, [bhn])
        P.emit()


def phase_outproj(sync, xsrc, wo_dram, ysc, out):
    nc = sync.nc
    P = Prog(sync).begin()
    with contextlib.ExitStack() as es:
        sfx = uid()

        def sb(name, shape, dt):
            return es.enter_context(nc.sbuf_tensor(name + sfx, shape, dt))
        wst = Rot([sb("o_wst%d" % i, [128, D], F32) for i in range(2)])
        wo = sb("o_wo", [128, 8 * D], BF16)
        bwo = Buf()
        xt = Rot([sb("o_xt%d" % i, [128, D], F32) for i in range(2)])
        ot = Rot([sb("o_ot%d" % i, [128, D], F32) for i in range(2)])
        yt = Rot([sb("o_yt%d" % i, [128, 8 * 512], BF16) for i in range(2)])
        po = Rot([es.enter_context(nc.psum_tensor("o_po%d" % i + sfx, [128, 512], F32)) for i in range(4)])
        s_w = [P.slot() for _ in range(2)]
        s_x = [P.slot() for _ in range(2)]
        s_y = [P.slot() for _ in range(2)]
        s_o = [P.slot() for _ in range(2)]
        for kc in range(8):
            w_t, bw = wst.next()
            P.dma(s_w[kc % 2], lambda e, w_t=w_t, kc=kc: e.dma_start(out=w_t[:], in_=wo_dram[kc * 128:(kc + 1) * 128, :]), writes=[bw])
            P.pool(lambda e, w_t=w_t, kc=kc: e.tensor_copy(out=wo[:, kc * D:(kc + 1) * D], in_=w_t[:]), reads=[bw], writes=[bwo])
        y_t = by = None
        for i in range(32):
            if i % 4 == 0:
                y_t, by = yt.next()
                tb = i // 4
                P.dma(s_y[tb % 2], lambda e, y_t=y_t, tb=tb: e.dma_start(
                    out=FV(y_t, 0, 128, 0, [[512, 8], [1, 512]]),
                    in_=ysc[:, :, tb * 512:(tb + 1) * 512].rearrange("k p t -> p k t")), writes=[by])
            x_t, bx = xt.next()
            P.dma(s_x[i % 2], lambda e, x_t=x_t, i=i: e.dma_start(out=x_t[:], in_=xsrc[128 * i:128 * (i + 1), :]), writes=[bx])
            o_t, bo = ot.next()
            for nb in range(2):
                p_t, bp = po.next()
                for kc in range(8):
                    P.pe(lambda e, p_t=p_t, y_t=y_t, kc=kc, nb=nb, i=i: e.matmul(
                        p_t[:], lhsT=y_t[:, kc * 512 + (i % 4) * 128: kc * 512 + (i % 4) * 128 + 128],
                        rhs=wo[:, kc * D + nb * 512: kc * D + nb * 512 + 512], start=(kc == 0), stop=(kc == 7)),
                        reads=[by, bwo], writes=[bp])
                P.dve(lambda e, p_t=p_t, x_t=x_t, o_t=o_t, nb=nb: e.tensor_tensor(
                    out=o_t[:, nb * 512:(nb + 1) * 512], in0=p_t[:], in1=x_t[:, nb * 512:(nb + 1) * 512], op=ALU.add),
                    reads=[bp, bx], writes=[bo])
            P.dma(s_o[i % 2], lambda e, o_t=o_t, i=i: e.dma_start(out=out[128 * i:128 * (i + 1), :], in_=o_t[:]), reads=[bo])
        P.emit()


def qk_geometry(d):
    L = S // d
    return L, L // 128


def phase_attn(sync, layer, hnT, dr, ysc):
    nc = sync.nc
    P = Prog(sync).begin()
    isA = layer == "A"
    groups = (0, 1, 2) if isA else (0,)
    w_dram, wg_dram, bias_dram = dr["w"], dr["wg"], dr["bias"]
    with contextlib.ExitStack() as es:
        sfx = uid()

        def sb(name, shape, dt):
            return es.enter_context(nc.sbuf_tensor(name + sfx, shape, dt))
        ngc = sb("a_ngc", [128, 8], F32)
        gq = sb("a_gq", [128, 3], F32)
        gk = sb("a_gk", [128, 3], F32)
        epsc = sb("a_eps", [128, 1], F32)
        blk = sb("a_blk", [128, 128], BF16)
        bconst = Buf()
        s_c = P.slot()
        P.dma(s_c, lambda e: e.dma_start(out=ngc[:], in_=dr["ng"][:, :]), writes=[bconst])
        P.dma(s_c, lambda e: e.dma_start(out=gq[:], in_=dr["gq"][:, :]), writes=[bconst])
        P.dma(s_c, lambda e: e.dma_start(out=gk[:], in_=dr["gk"][:, :]), writes=[bconst])
        P.dve(lambda e: e.memset(epsc[:], EPS), writes=[bconst])
        P.dve(lambda e: e.memset(blk[:], 0.0), writes=[bconst])
        P.dve(lambda e: e.memset(blk[0:64, 0:64], 1.0 / 64), writes=[bconst])
        P.dve(lambda e: e.memset(blk[64:128, 64:128], 1.0 / 64), writes=[bconst])
        P.dve(lambda e: e.tensor_scalar(out=gq[:], in0=gq[:], scalar1=0.125, scalar2=None, op0=ALU.mult),
              reads=[bconst], writes=[bconst])
        qT = sb("a_qT", [128, S], BF16)
        kT = sb("a_kT", [128, S], BF16)
        Vt = sb("a_V", [128, 32 * 192], BF16)
        bq, bk, bV = Buf(), Buf(), Buf()
        P.dve(lambda e: e.memset(FV(Vt, 0, 128, 64, [[192, 32], [1, 64]]), 1.0), writes=[bV])
        sgT = sb("a_sgT", [128, S], BF16)
        bsg = Buf()
        if isA:
            acc = [sb("a_acc%d" % h, [128, S], F32) for h in range(2)]
            bacc = [Buf(), Buf()]
        wst = Rot([sb("a_wst%d" % i, [128, 384], F32) for i in range(4)])
        s_w = [P.slot() for _ in range(4)]
        wb = Rot([sb("a_wb%d" % i, [128, 8 * 384], BF16) for i in range(2)])
        wgb = sb("a_wgb", [128, 8 * 128], BF16)
        bwg = Buf()
        ebw = 2 * 256 if isA else dr["ebw"]
        ebst = Rot([sb("a_ebst%d" % i, [128, 768], F32) for i in range(2)])
        s_eb = [P.slot() for _ in range(2)]
        eb = Rot([sb("a_eb%d" % i, [128, ebw], BF16) for i in range(2)])
        sq = Rot([sb("a_sq%d" % i, [128, 512], BF16) for i in range(2)])
        lnb = Rot([sb("a_ln%d" % i, [128, 512], F32) for i in range(2)])
        rstd = Rot([sb("a_rstd%d" % i, [128, 512], F32) for i in range(2)])
        ew = 512 if isA else 768
        ex = Rot([sb("a_ex%d" % i, [128, ew], BF16) for i in range(3)])
        pT = Rot([sb("a_pT%d" % i, [128, ew], BF16) for i in range(4 if isA else 7)])
        rec = Rot([sb("a_rec%d" % i, [128, 512], F32) for i in range(2)])
        tmp = Rot([sb("a_tmp%d" % i, [128, 512], F32) for i in range(2)])
        yT = Rot([sb("a_yT%d" % i, [128, 1024], BF16) for i in range(2)])
        s_y = [P.slot() for _ in range(2)]
        banks = [es.enter_context(nc.psum_tensor("a_ps%d" % i + sfx, [128, 512], F32)) for i in range(8)]
        bbank = [Buf() for _ in range(8)]

        def bankrot(ids):
            r = Rot([banks[i] for i in ids])
            r.bufs = [bbank[i] for i in ids]
            return r
        pq = bankrot([0, 1])
        pss = bankrot([2, 3])
        pv = bankrot([4, 5])
        pg = bankrot([6, 7])
        if isA:
            ps = bankrot([0, 1, 2])
            po = bankrot([3, 4, 5])
            ph = bankrot([6, 7])
        else:
            ps2 = [(banks[0], banks[1]), (banks[2], banks[3])]
            ps2b = [(bbank[0], bbank[1]), (bbank[2], bbank[3])]
            po = bankrot([4, 5, 6, 7])

        def load_unit(hp, g):
            w_b, bw = wb.next()
            for kc in range(8):
                w_t, bs = wst.next()
                sl = s_w[(wst.i - 1) % 4]
                P.dma(sl, lambda e, w_t=w_t, kc=kc: e.dma_start(out=w_t[:], in_=w_dram[hp, g, kc * 128:(kc + 1) * 128, :]), writes=[bs])
                P.pool(lambda e, w_t=w_t, w_b=w_b, kc=kc: e.tensor_scalar(
                    out=w_b[:, kc * 384:(kc + 1) * 384], in0=w_t[:], scalar1=ngc[:, kc:kc + 1], scalar2=None, op0=ALU.mult),
                    reads=[bs, bconst], writes=[bw])
            return w_b, bw

        def load_gate(hp):
            for kc in range(8):
                w_t, bs = wst.next()
                sl = s_w[(wst.i - 1) % 4]
                P.dma(sl, lambda e, w_t=w_t, kc=kc: e.dma_start(out=w_t[:, 0:128], in_=wg_dram[hp, kc * 128:(kc + 1) * 128, :]), writes=[bs])
                P.pool(lambda e, w_t=w_t, kc=kc: e.tensor_scalar(
                    out=wgb[:, kc * 128:(kc + 1) * 128], in0=w_t[:, 0:128], scalar1=ngc[:, kc:kc + 1], scalar2=None, op0=ALU.mult),
                    reads=[bs, bconst], writes=[bwg])

        def gate_proj():
            for tb in range(8):
                p_t, bp = pg.next()
                for kc in range(8):
                    P.pe(lambda e, p_t=p_t, kc=kc, tb=tb: e.matmul(
                        p_t[:], lhsT=wgb[:, kc * 128:(kc + 1) * 128], rhs=hnT[:, kc * S + tb * 512: kc * S + tb * 512 + 512],
                        start=(kc == 0), stop=(kc == 7)), reads=[bwg], writes=[bp])
                P.act(lambda e, p_t=p_t, tb=tb: e.activation(out=sgT[:, tb * 512:(tb + 1) * 512], in_=p_t[:], func=AF.Silu),
                      reads=[bp], writes=[bsg])

        def qk_proj(w_b, bw, g, d):
            L = S // d
            for which in range(2):
                for tb in range(8):
                    p_t, bp = pq.next()
                    for kc in range(8):
                        P.pe(lambda e, p_t=p_t, kc=kc, tb=tb, which=which: e.matmul(
                            p_t[:], lhsT=w_b[:, kc * 384 + which * 128: kc * 384 + which * 128 + 128],
                            rhs=hnT[:, kc * S + tb * 512: kc * S + tb * 512 + 512], start=(kc == 0), stop=(kc == 7)),
                            reads=[bw], writes=[bp])
                    s_t, bs = sq.next()
                    P.act(lambda e, s_t=s_t, p_t=p_t: e.activation(out=s_t[:], in_=p_t[:], func=AF.Square), reads=[bp], writes=[bs])
                    ss_t, bss = pss.next()
                    P.pe(lambda e, ss_t=ss_t, s_t=s_t: e.matmul(ss_t[:], lhsT=blk[:], rhs=s_t[:], start=True, stop=True),
                         reads=[bs, bconst], writes=[bss])
                    l_t, bl = lnb.next()
                    P.act(lambda e, l_t=l_t, ss_t=ss_t: e.activation(out=l_t[:], in_=ss_t[:], func=AF.Ln, bias=epsc[:], scale=1.0),
                          reads=[bss, bconst], writes=[bl])
                    r_t, br = rstd.next()
                    P.act(lambda e, l_t=l_t, r_t=r_t: e.activation(out=r_t[:], in_=l_t[:], func=AF.Exp, scale=-0.5), reads=[bl], writes=[br])
                    if which == 0:
                        n = 512 // d
                        P.dve(lambda e, p_t=p_t, r_t=r_t, tb=tb, g=g, d=d, L=L, n=n: e.scalar_tensor_tensor(
                            out=FV(qT, 0, 128, tb * n, [[L, d], [1, n]]),
                            in0=FV(p_t, 0, 128, 0, [[1, d], [d, n]]), scalar=gq[:, g:g + 1],
                            in1=FV(r_t, 0, 128, 0, [[1, d], [d, n]]), op0=ALU.mult, op1=ALU.mult),
                            reads=[bp, br, bconst], writes=[bq])
                    else:
                        P.dve(lambda e, p_t=p_t, r_t=r_t, tb=tb, g=g: e.scalar_tensor_tensor(
                            out=kT[:, tb * 512:(tb + 1) * 512], in0=p_t[:], scalar=gk[:, g:g + 1], in1=r_t[:],
                            op0=ALU.mult, op1=ALU.mult), reads=[bp, br, bconst], writes=[bk])

        def v_proj(w_b, bw, d):
            L, nC = qk_geometry(d)
            for c0 in range(0, 32, 4):
                p_t, bp = pv.next()
                for cc in range(4):
                    c = c0 + cc
                    r, i = divmod(c, nC)
                    t0 = r + d * 128 * i
                    for kc in range(8):
                        P.pe(lambda e, p_t=p_t, kc=kc, cc=cc, t0=t0, d=d: e.matmul(
                            p_t[:, cc * 128:(cc + 1) * 128],
                            lhsT=FV(hnT, 0, 128, kc * S + t0, [[d, 128]]),
                            rhs=w_b[:, kc * 384 + 256: kc * 384 + 384], start=(kc == 0), stop=(kc == 7)),
                            reads=[bw], writes=[bp])
                P.act(lambda e, p_t=p_t, c0=c0: e.activation(
                    out=FV(Vt, 0, 128, c0 * 192, [[192, 4], [128, 2], [1, 64]]),
                    in_=FV(p_t, 0, 128, 0, [[128, 4], [64, 2], [1, 64]]), func=AF.Copy), reads=[bp], writes=[bV])

        def load_eb(hp, g):
            st, bs = ebst.next()
            sl = s_eb[(ebst.i - 1) % 2]
            e_t, be = eb.next()
            P.dma(sl, lambda e, st=st: e.dma_start(out=st[:, 0:512], in_=bias_dram[hp, g, :, :]), writes=[bs])
            P.act(lambda e, st=st, e_t=e_t: e.activation(out=e_t[:, 0:512], in_=st[:, 0:512], func=AF.Exp), reads=[bs], writes=[be])
            return e_t, be

        def attention_A(g, d, e_t, be, first):
            L, nC = qk_geometry(d)
            for h in range(2):
                hs = slice(64 * h, 64 * h + 64)
                vof = 0 if h == 0 else 64
                acc_h = acc[h]
                ba = bacc[h]

                def acc_out(src, dst_fn):
                    if first:
                        P.dve(lambda e, src=src, dst_fn=dst_fn: e.tensor_copy(out=dst_fn(), in_=src()), reads=[src.buf], writes=[ba])
                    else:
                        P.dve(lambda e, src=src, dst_fn=dst_fn: e.tensor_tensor(out=dst_fn(), in0=src(), in1=dst_fn(), op=ALU.add),
                              reads=[src.buf], writes=[ba])

                for r in range(d):
                    qb = r * L
                    ptiles = {}

                    def pcol(i, lo):
                        p_t, bp = ptiles[i // 2]
                        return p_t, bp, (i % 2) * 256 + lo
                    h_t, bh = ph.next()
                    o_t = bo = None
                    for m in range(nC // 2):
                        s_t, bs = ps.next()
                        for cc in range(2):
                            i = 2 * m + cc
                            qlo = max(0, 128 * i - 64)
                            qhi = min(L, 128 * i + 192)
                            lo = qlo - (128 * i - 64)
                            n = qhi - qlo
                            P.pe(lambda e, s_t=s_t, cc=cc, lo=lo, n=n, qlo=qlo, i=i, r=r, d=d, hs=hs, qb=qb: e.matmul(
                                s_t[:, cc * 256 + lo: cc * 256 + lo + n],
                                lhsT=FV(kT, hs.start, hs.stop, r + d * 128 * i, [[d, 128]]),
                                rhs=qT[hs, qb + qlo: qb + qlo + n], start=True, stop=True),
                                reads=[bk, bq], writes=[bs])
                        x_t, bx = ex.next()
                        P.act(lambda e, x_t=x_t, s_t=s_t: e.activation(out=x_t[:, 0:512], in_=s_t[:], func=AF.Exp), reads=[bs], writes=[bx])
                        p_t, bp = pT.next()
                        P.dve(lambda e, p_t=p_t, x_t=x_t, h=h: e.tensor_tensor(
                            out=FV(p_t, 0, 128, 0, [[256, 2], [1, 256]]), in0=FV(x_t, 0, 128, 0, [[256, 2], [1, 256]]),
                            in1=FV(e_t, 0, 128, h * 256, [[0, 2], [1, 256]]), op=ALU.mult), reads=[bx, be], writes=[bp])
                        ptiles[m] = (p_t, bp)
                        if m == 0:
                            pa, bpa, c0 = pcol(0, 64)
                            c1 = r * nC
                            P.pe(lambda e, h_t=h_t, pa=pa, c0=c0, vof=vof, c1=c1: e.matmul(
                                h_t[:, 0:64], lhsT=Vt[:, c1 * 192 + vof: c1 * 192 + vof + 128], rhs=pa[:, c0:c0 + 64], start=True, stop=True),
                                reads=[bV, bpa], writes=[bh])
                        for jj in ([2 * m - 1] if m > 0 else []) + ([2 * m] if 2 * m <= nC - 2 else []):
                            bb = jj % 4
                            if bb == 0:
                                o_t, bo = po.next()
                            pa, bpa, ca = pcol(jj, 128)
                            pb_, bpb, cb = pcol(jj + 1, 0)
                            c1 = r * nC + jj
                            P.pe(lambda e, o_t=o_t, pa=pa, ca=ca, bb=bb, vof=vof, c1=c1: e.matmul(
                                o_t[:, bb * 128:(bb + 1) * 128], lhsT=Vt[:, c1 * 192 + vof: c1 * 192 + vof + 128],
                                rhs=pa[:, ca:ca + 128], start=True, stop=False), reads=[bV, bpa], writes=[bo])
                            P.pe(lambda e, o_t=o_t, pb_=pb_, cb=cb, bb=bb, vof=vof, c1=c1: e.matmul(
                                o_t[:, bb * 128:(bb + 1) * 128], lhsT=Vt[:, (c1 + 1) * 192 + vof: (c1 + 1) * 192 + vof + 128],
                                rhs=pb_[:, cb:cb + 128], start=False, stop=True), reads=[bV, bpb], writes=[bo])
                            if bb == 3 or jj == nC - 2:
                                j0 = jj - bb
                                m0 = 64 + 128 * j0
                                nq = 128 * (bb + 1)
                                src = lambda o_t=o_t, nq=nq: o_t[:, 0:nq]
                                src.buf = bo
                                acc_out(src, lambda r=r, d=d, m0=m0, nq=nq, acc_h=acc_h: FV(acc_h, 0, 128, r + d * m0, [[d, nq]]))
                    pa, bpa, c0 = pcol(nC - 1, 128)
                    cl = r * nC + nC - 1
                    P.pe(lambda e, h_t=h_t, pa=pa, c0=c0, vof=vof, cl=cl: e.matmul(
                        h_t[:, 64:128], lhsT=Vt[:, cl * 192 + vof: cl * 192 + vof + 128], rhs=pa[:, c0:c0 + 64], start=True, stop=True),
                        reads=[bV, bpa], writes=[bh])
                    src = lambda h_t=h_t: FV(h_t, 0, 128, 0, [[64, 2], [1, 64]])
                    src.buf = bh
                    acc_out(src, lambda r=r, d=d, L=L, acc_h=acc_h: FV(acc_h, 0, 128, r, [[d * (L - 64), 2], [d, 64]]))

        def normalize_A(hp):
            for q4 in range(4):
                y_t, by = yT.next()
                for hb in range(2):
                    tb = q4 * 2 + hb
                    cs = slice(tb * 512, (tb + 1) * 512)
                    r_t, br = rec.next()
                    t_t, bt = tmp.next()
                    P.dve(lambda e, r_t=r_t, cs=cs: e.reciprocal(out=r_t[0:64, :], in_=acc[0][64:128, cs]), reads=[bacc[0]], writes=[br])
                    P.dve(lambda e, r_t=r_t, cs=cs: e.reciprocal(out=r_t[64:128, :], in_=acc[1][0:64, cs]), reads=[bacc[1]], writes=[br])
                    P.dve(lambda e, r_t=r_t, t_t=t_t, cs=cs: e.tensor_tensor(out=t_t[0:64, :], in0=acc[0][0:64, cs], in1=r_t[0:64, :], op=ALU.mult),
                          reads=[br], writes=[bt])
                    P.dve(lambda e, r_t=r_t, t_t=t_t, cs=cs: e.tensor_tensor(out=t_t[64:128, :], in0=acc[1][64:128, cs], in1=r_t[64:128, :], op=ALU.mult),
                          reads=[br], writes=[bt])
                    P.dve(lambda e, t_t=t_t, y_t=y_t, cs=cs, hb=hb: e.tensor_tensor(out=y_t[:, hb * 512:(hb + 1) * 512], in0=t_t[:], in1=sgT[:, cs], op=ALU.mult),
                          reads=[bt, bsg], writes=[by])
                sl = s_y[(yT.i - 1) % 2]
                P.dma(sl, lambda e, y_t=y_t, q4=q4: e.dma_start(out=ysc[hp, :, q4 * 1024:(q4 + 1) * 1024], in_=y_t[:]), reads=[by])

        def attention_B(hp, h, e_t, be):
            tiles = dr["tiles"]
            hs = slice(64 * h, 64 * h + 64)
            vof = 0 if h == 0 else 64
            num = slice(0, 64) if h == 0 else slice(64, 128)
            den = slice(64, 128) if h == 0 else slice(0, 64)
            contribs = []
            for Q in range(32):
                full, part = [], []
                for R in range(32):
                    qlo, nr = tiles[R][1], tiles[R][2]
                    lo_r = max(qlo, 2 * Q)
                    hi_r = min(qlo + nr - 1, 2 * Q + 1)
                    if lo_r > hi_r:
                        continue
                    (full if hi_r - lo_r == 1 else part).append((R, lo_r, hi_r - lo_r + 1))
                assert full
                contribs.append(full + part)
            ptl = {}
            st = dict(o_t=None, bo=None, y_t=None, by=None)

            def do_block(Q):
                if Q % 4 == 0:
                    st["o_t"], st["bo"] = po.next()
                o_t, bo = st["o_t"], st["bo"]
                cl = contribs[Q]
                for n_, (R, row0, nrow) in enumerate(cl):
                    p_t, bp = ptl[R]
                    c0 = (row0 - tiles[R][1]) * 64
                    oc = (Q % 4) * 128 + (row0 - 2 * Q) * 64
                    nn = nrow * 64
                    P.pe(lambda e, o_t=o_t, p_t=p_t, c0=c0, oc=oc, nn=nn, R=R, first=(n_ == 0), last=(n_ == len(cl) - 1): e.matmul(
                        o_t[:, oc:oc + nn], lhsT=Vt[:, R * 192 + vof: R * 192 + vof + 128],
                        rhs=p_t[:, c0:c0 + nn], start=first, stop=last), reads=[bV, bp], writes=[bo])
                if Q % 4 == 3:
                    tb = Q // 4
                    cs = slice(tb * 512, (tb + 1) * 512)
                    if tb % 2 == 0:
                        st["y_t"], st["by"] = yTB[h].next()
                    y_t, by = st["y_t"], st["by"]
                    r_t, br = rec.next()
                    t_t, bt = tmp.next()
                    P.dve(lambda e, r_t=r_t, o_t=o_t: e.reciprocal(out=r_t[num, :], in_=o_t[den, :]), reads=[bo], writes=[br])
                    P.dve(lambda e, r_t=r_t, o_t=o_t, t_t=t_t: e.tensor_tensor(out=t_t[num, :], in0=o_t[num, :], in1=r_t[num, :], op=ALU.mult),
                          reads=[bo, br], writes=[bt])
                    P.dve(lambda e, t_t=t_t, y_t=y_t, cs=cs, tb=tb: e.tensor_tensor(
                        out=y_t[num, (tb % 2) * 512:(tb % 2) * 512 + 512], in0=t_t[num, :], in1=sgT[num, cs], op=ALU.mult),
                        reads=[bt, bsg], writes=[by])
                    if tb % 2 == 1:
                        q4 = tb // 2
                        sl = s_yB[h][(yTB[h].i - 1) % 2]
                        P.dma(sl, lambda e, y_t=y_t, q4=q4: e.dma_start(
                            out=ysc[hp, num, q4 * 1024:(q4 + 1) * 1024], in_=y_t[num, :]), reads=[by])

            lastR = [max(R for R, _, _ in contribs[Q]) for Q in range(32)]
            for R in range(32):
                toff, qlo, nr = tiles[R]
                n = nr * 64
                k_ = R % 2
                sA, sB = ps2[k_]
                bA, bB = ps2b[k_]
                n1 = min(n, 512)
                P.pe(lambda e, sA=sA, R=R, qlo=qlo, n1=n1: e.matmul(
                    sA[:, 0:n1], lhsT=kT[hs, R * 128:(R + 1) * 128], rhs=qT[hs, qlo * 64: qlo * 64 + n1], start=True, stop=True),
                    reads=[bk, bq], writes=[bA])
                x_t, bx = ex.next()
                P.act(lambda e, x_t=x_t, sA=sA, n1=n1: e.activation(out=x_t[:, 0:n1], in_=sA[:, 0:n1], func=AF.Exp), reads=[bA], writes=[bx])
                if n > 512:
                    n2 = n - 512
                    P.pe(lambda e, sB=sB, R=R, qlo=qlo, n2=n2: e.matmul(
                        sB[:, 0:n2], lhsT=kT[hs, R * 128:(R + 1) * 128], rhs=qT[hs, qlo * 64 + 512: qlo * 64 + 512 + n2], start=True, stop=True),
                        reads=[bk, bq], writes=[bB])
                    P.act(lambda e, x_t=x_t, sB=sB, n2=n2: e.activation(out=x_t[:, 512:512 + n2], in_=sB[:, 0:n2], func=AF.Exp), reads=[bB], writes=[bx])
                p_t, bp = pT.next()
                P.dve(lambda e, p_t=p_t, x_t=x_t, n=n, toff=toff: e.tensor_tensor(
                    out=p_t[:, 0:n], in0=x_t[:, 0:n], in1=e_t[:, toff: toff + n], op=ALU.mult),
                    reads=[bx, be], writes=[bp])
                ptl[R] = (p_t, bp)
                for Q in range(32):
                    if lastR[Q] == R:
                        do_block(Q)

        def load_eb_B(hp, h):
            e_t, be = eb.next()
            ebw1 = dr["ebw"]
            off = 0
            while off < ebw1:
                n = min(768, ebw1 - off)
                st_, bs = ebst.next()
                sl = s_eb[(ebst.i - 1) % 2]
                P.dma(sl, lambda e, st_=st_, off=off, n=n: e.dma_start(out=st_[:, 0:n], in_=bias_dram[2 * hp + h, :, off:off + n]), writes=[bs])
                P.act(lambda e, st_=st_, e_t=e_t, off=off, n=n: e.activation(
                    out=e_t[:, off: off + n], in_=st_[:, 0:n], func=AF.Exp), reads=[bs], writes=[be])
                off += n
            return e_t, be

        if not isA:
            yTB = [Rot([sb("b_yT%d_%d" % (h, i), [128, 1024], BF16) for i in range(2)]) for h in range(2)]
            s_yB = [[P.slot() for _ in range(2)] for h in range(2)]

        dstop = DBG.get("stop")
        dg = DBG.get("g", 0)
        for hp in range(8):
            load_gate(hp)
            gate_proj()
            if isA:
                for g in groups:
                    d = DILS[g]
                    w_b, bw = load_unit(hp, g)
                    e_t, be = load_eb(hp, g)
                    qk_proj(w_b, bw, g, d)
                    v_proj(w_b, bw, d)
                    if dstop == "proj" and g == dg:
                        dump(P, "qT", qT, [bq]); dump(P, "kT", kT, [bk]); dump(P, "Vt", Vt, [bV]); dump(P, "sgT", sgT, [bsg])
                        dump(P, "wb", w_b, [bw]); dump(P, "eb", e_t, [be])
                        break
                    attention_A(g, d, e_t, be, first=(g == 0))
                    if dstop == "attn" and g == dg:
                        dump(P, "acc0", acc[0], [bacc[0]]); dump(P, "acc1", acc[1], [bacc[1]])
                        break
                if dstop in ("proj", "attn"):
                    break
                normalize_A(hp)
                if dstop == "hp0":
                    break
            else:
                w_b, bw = load_unit(hp, 0)
                qk_proj(w_b, bw, 0, 1)
                v_proj(w_b, bw, 1)
                for h in range(2):
                    e_t, be = load_eb_B(hp, h)
                    attention_B(hp, h, e_t, be)
        P.emit()


_T5_LUT = None


def _t5_bucket_np(rel):
    import math
    half, me = 16, 8
    ret = np.where(rel > 0, half, 0)
    n = np.abs(rel)
    nf = np.maximum(n, 1).astype(np.float32)
    large = me + (np.log(nf / np.float32(me)) / np.float32(math.log(1024 / me)) * np.float32(half - me)).astype(np.int32)
    large = np.minimum(large, half - 1)
    return ret + np.where(n < me, n, large)


def _bias_tiles_A(t5_bias):
    a = np.arange(128)[:, None]
    b = np.arange(256)[None, :]
    rel = a - b + 64
    valid = (b - a >= 0) & (b - a <= 128)
    out = np.empty((8, 3, 128, 2, 256), np.float32)
    for g, d in enumerate(DILS):
        idx = _t5_bucket_np(rel * d)
        for h in range(16):
            t = t5_bias[g * 16 + h][idx]
            out[h // 2, g, :, h % 2, :] = np.where(valid, t, np.float32(NEG))
    return out


def _geom_B():
    rows = 64
    r = np.arange(rows)
    rs = np.clip(r - 4, 0, rows - 8)
    c = np.arange(64)
    cs = np.clip(c - 8, 0, 64 - 16)
    tiles = []
    uniq = {}
    maps = []
    off = 0
    for R in range(32):
        krs = np.array([2 * R, 2 * R + 1])
        qrows = [q for q in range(rows) if (rs[q] <= krs[1]) and (rs[q] + 7 >= krs[0])]
        qlo, nr = qrows[0], len(qrows)
        assert qrows == list(range(qlo, qlo + nr))
        kr = np.repeat(krs, 64)[:, None]
        kc = np.tile(c, 2)[:, None]
        qr = np.repeat(np.arange(qlo, qlo + nr), 64)[None, :]
        qc = np.tile(c, nr)[None, :]
        valid = (kr >= rs[qr]) & (kr <= rs[qr] + 7) & (kc >= cs[qc]) & (kc < cs[qc] + 16)
        ridx = np.clip(kr - qr + 7, 0, 14)
        cidx = np.clip(kc - qc, -15, 15) + 15
        key = (nr, valid.tobytes(), ridx.tobytes())
        if key not in uniq:
            uniq[key] = off
            maps.append((off, ridx + 0 * cidx, cidx + 0 * ridx, valid))
            off += nr * 64
        tiles.append((uniq[key], qlo, nr))
    return tiles, maps, off


def _bias_tiles_B(rpb, maps, ebw):
    out = np.empty((16, 128, ebw), np.float32)
    for off, ridx, cidx, valid in maps:
        n = valid.shape[1]
        for h in range(16):
            out[h, :, off:off + n] = np.where(valid, rpb[h][ridx, cidx], np.float32(NEG))
    return out


def _unit_weights(w_in, ngroups):
    wu = np.empty((8, ngroups, D, 384), np.float32)
    for hp in range(8):
        for g in range(ngroups):
            for j in range(3):
                c0 = g * 3072 + j * 1024 + hp * 128
                wu[hp, g, :, j * 128:(j + 1) * 128] = w_in[:, c0:c0 + 128]
    gc = ngroups * 3072
    wg = np.ascontiguousarray(w_in[:, gc:gc + 1024].reshape(D, 8, 128).transpose(1, 0, 2))
    return wu, wg


_GEOM_B = None


def build_nc(layers="AB"):
    global _GEOM_B
    if _GEOM_B is None:
        _GEOM_B = _geom_B()
    tilesB, mapsB, ebwB = _GEOM_B
    nc = bass.Bass("TRN2", target_bir_lowering=False)

    def din(name, shape, dt=F32):
        return nc.dram_tensor(name, list(shape), dt, kind="ExternalInput").ap()
    x = din("x", [S, D])
    ident_d = din("ident_d", [128, 128])
    drA = dict(w=din("wA", [8, 3, D, 384]), wg=din("wgA", [8, D, 128]), bias=din("biasA", [8, 3, 128, 512]),
               ng=din("ngA", [128, 8]), gq=din("gqA", [128, 3]), gk=din("gkA", [128, 3]))
    woA = din("woA", [D, D])
    drB = dict(w=din("wB", [8, 1, D, 384]), wg=din("wgB", [8, D, 128]), bias=din("biasB", [16, 128, ebwB]),
               ng=din("ngB", [128, 8]), gq=din("gqB", [128, 3]), gk=din("gkB", [128, 3]), tiles=tilesB, ebw=ebwB)
    woB = din("woB", [D, D])
    out = nc.dram_tensor("out", [S, D], F32, kind="ExternalOutput").ap()
    ysc = nc.dram_tensor("ysc", [8, 128, S], BF16).ap()
    with contextlib.ExitStack() as es:
        sync = Sync(nc, es)
        hnT = es.enter_context(nc.sbuf_tensor("hnT", [128, 8 * S], BF16))
        ident = es.enter_context(nc.sbuf_tensor("ident", [128, 128], BF16))
        identf = es.enter_context(nc.sbuf_tensor("identf", [128, 128], F32))
        P0 = Prog(sync).begin()
        bi = Buf()
        P0.dma(P0.slot(), lambda e: e.dma_start(out=identf[:], in_=ident_d[:, :]), writes=[bi])
        P0.dve(lambda e: e.tensor_copy(out=ident[:], in_=identf[:]), reads=[bi], writes=[bi])
        P0.emit()
        src = x
        if "A" in layers:
            phase_norm(sync, es, src, hnT, ident)
            if DBG.get("stop") != "norm":
                phase_attn(sync, "A", hnT, drA, ysc)
            if not DBG.get("stop"):
                phase_outproj(sync, src, woA, ysc, out)
            src = out
        if "B" in layers:
            phase_norm(sync, es, src, hnT, ident)
            phase_attn(sync, "B", hnT, drB, ysc)
            phase_outproj(sync, src, woB, ysc, out)
    return nc


def host_inputs(norm_gain, a_w_in, a_w_out, a_q_gain, a_k_gain, t5_bias, b_w_in, b_w_out, b_q_gain, b_k_gain, b_rpb):
    global _GEOM_B
    if _GEOM_B is None:
        _GEOM_B = _geom_B()
    tilesB, mapsB, ebwB = _GEOM_B
    f = lambda a: np.ascontiguousarray(np.asarray(a, dtype=np.float32))
    wA, wgA = _unit_weights(f(a_w_in)[0], 3)
    wB, wgB = _unit_weights(f(b_w_in)[0], 1)
    ng = f(norm_gain)

    def gcol(gn):
        gn = f(gn).reshape(-1, 64)
        o = np.ones((128, 3), np.float32)
        for g in range(gn.shape[0]):
            o[:, g] = np.tile(gn[g], 2)
        return o
    shared = dict(
        ident_d=np.eye(128, dtype=np.float32),
        wA=wA, wgA=wgA, woA=f(a_w_out)[0],
        biasA=np.ascontiguousarray(_bias_tiles_A(f(t5_bias)).reshape(8, 3, 128, 512)),
        ngA=np.ascontiguousarray(ng[0].reshape(8, 128).T), gqA=gcol(a_q_gain[0]), gkA=gcol(a_k_gain[0]),
        wB=wB, wgB=wgB, woB=f(b_w_out)[0],
        biasB=_bias_tiles_B(f(b_rpb)[0], mapsB, ebwB),
        ngB=np.ascontiguousarray(ng[1].reshape(8, 128).T), gqB=gcol(b_q_gain), gkB=gcol(b_k_gain),
    )
    return shared


def kernel(x, norm_gain, a_w_in, a_w_out, a_q_gain, a_k_gain, t5_bias, b_w_in, b_w_out, b_q_gain, b_k_gain, b_rpb):
    x = np.ascontiguousarray(np.asarray(x, dtype=np.float32))
    shared = host_inputs(norm_gain, a_w_in, a_w_out, a_q_gain, a_k_gain, t5_bias, b_w_in, b_w_out, b_q_gain, b_k_gain, b_rpb)
    nc = build_nc("AB")
    in_maps = [dict(shared, x=x[c]) for c in range(NCORES)]
    res = run_bass_kernel_spmd(nc, in_maps, core_ids=list(range(NCORES)))
    return np.stack([np.asarray(r["out"], dtype=np.float32) for r in res.results], axis=0)
```

```python
import contextlib
import numpy as np
import concourse.bass as bass
import concourse.mybir as mybir
from concourse.bass_utils import run_bass_kernel_spmd

F32 = mybir.dt.float32
BF16 = mybir.dt.bfloat16
AF = mybir.ActivationFunctionType
ALU = mybir.AluOpType

S = 4096
D = 1024
NCORES = 8
DILS = (1, 4, 16)
EPS = 1e-6
NEG = -30000.0


class Buf:
    __slots__ = ("name", "lw", "rd")

    def __init__(self, name=""):
        self.name = name
        self.lw = None
        self.rd = []


class DmaSlot:
    __slots__ = ("sem", "count", "name")

    def __init__(self, name):
        self.name = name
        self.sem = None
        self.count = 0


class Op:
    __slots__ = ("eng", "fn", "deps", "slot", "signal", "tick", "semkey", "known", "idx")


COMPUTE = ("pe", "act", "dve", "pool")
ENGS = ("pe", "act", "dve", "pool", "sp")


class Sync:
    def __init__(self, nc, es, nslots=40):
        self.nc = nc
        self.esem = {e: es.enter_context(nc.semaphore("sem_" + e)) for e in COMPUTE}
        self.tick = {e: 0 for e in COMPUTE}
        self.slots = []
        for i in range(nslots):
            s = DmaSlot("dq%d" % i)
            s.sem = es.enter_context(nc.semaphore(s.name))
            self.slots.append(s)


class Prog:
    def __init__(self, sync):
        self.sync = sync
        self.nc = sync.nc
        self.ops = []
        self.nslot = 0

    def slot(self, name=""):
        s = self.sync.slots[self.nslot]
        self.nslot += 1
        return s

    def add(self, eng, fn, reads=(), writes=(), slot=None):
        op = Op()
        op.eng = eng
        op.fn = fn
        op.slot = slot
        op.signal = slot is not None
        op.tick = None
        op.idx = len(self.ops)
        deps = {}
        is_dma = slot is not None
        for b in reads:
            w = b.lw
            if w is not None:
                if is_dma or w.slot is not None or w.eng != eng or eng != "pe":
                    deps[w.idx] = w
        for b in writes:
            w = b.lw
            if w is not None and (is_dma or w.slot is not None or w.eng != eng):
                deps[w.idx] = w
            for r in b.rd:
                if is_dma or r.slot is not None or r.eng != eng:
                    deps[r.idx] = r
        for b in reads:
            b.rd.append(op)
        for b in writes:
            b.lw = op
            b.rd = []
        op.deps = list(deps.values())
        for d in op.deps:
            d.signal = True
        self.ops.append(op)
        return op

    def pe(self, fn, reads=(), writes=()):
        return self.add("pe", fn, reads, writes)

    def act(self, fn, reads=(), writes=()):
        return self.add("act", fn, reads, writes)

    def dve(self, fn, reads=(), writes=()):
        return self.add("dve", fn, reads, writes)

    def pool(self, fn, reads=(), writes=()):
        return self.add("pool", fn, reads, writes)

    def dma(self, slot, fn, reads=(), writes=()):
        return self.add("sp", fn, reads, writes, slot=slot)

    def emit(self):
        nc = self.nc
        sy = self.sync
        for op in self.ops:
            if op.slot is not None:
                op.slot.count += 16
                op.tick = op.slot.count
                op.semkey = op.slot
            elif op.signal:
                sy.tick[op.eng] += 1
                op.tick = sy.tick[op.eng]
                op.semkey = op.eng
        base = {}
        for e in COMPUTE:
            base[e] = 0
        clock = {e: {} for e in ENGS}
        start_tick = dict(self._start_tick)
        start_slot = dict(self._start_slot)
        for e in ENGS:
            for k, v in start_tick.items():
                clock[e][k] = v
            for k, v in start_slot.items():
                clock[e][k] = v
        plan = {e: [] for e in ENGS}
        for op in self.ops:
            ck = clock[op.eng]
            need = {}
            for d in op.deps:
                if ck.get(d.semkey, 0) >= d.tick:
                    continue
                if need.get(d.semkey, 0) < d.tick:
                    need[d.semkey] = d.tick
            for d in op.deps:
                for k, v in d.known.items():
                    if ck.get(k, 0) < v:
                        ck[k] = v
            waits = list(need.items())
            for k, v in waits:
                if ck.get(k, 0) < v:
                    ck[k] = v
            if op.tick is not None:
                kn = dict(ck)
                if kn.get(op.semkey, 0) < op.tick:
                    kn[op.semkey] = op.tick
                op.known = kn
            plan[op.eng].append((op, waits))
        final_waits = [(s, s.count) for s in sy.slots[: self.nslot] if s.count > start_slot.get(s, 0)]
        esem = sy.esem

        def semof(k):
            return k.sem if isinstance(k, DmaSlot) else esem[k]

        def run(engname):
            def body(eng):
                for op, waits in plan[engname]:
                    for k, v in waits:
                        eng.wait_ge(semof(k), v)
                    ins = op.fn(eng)
                    if op.slot is not None:
                        ins.then_inc(op.slot.sem, 16)
                    elif op.tick is not None:
                        ins.then_inc(esem[engname], 1)
                if engname == "sp":
                    for s, v in final_waits:
                        eng.wait_ge(s.sem, v)
            return body

        with nc.Block() as block:
            block.tensor(run("pe"))
            block.scalar(run("act"))
            block.vector(run("dve"))
            block.gpsimd(run("pool"))
            block.sync(run("sp"))

    def begin(self):
        sy = self.sync
        self._start_tick = dict(sy.tick)
        self._start_slot = {s: s.count for s in sy.slots}
        return self


_UID = [0]


def uid():
    _UID[0] += 1
    return "_u%d" % _UID[0]


def fview(ap, dims):
    return bass.AP(tensor=ap.tensor, offset=ap.offset, ap=[list(ap.ap[0])] + [list(d) for d in dims])


def FV(t, p0, p1, off, dims):
    return fview(t[p0:p1, off:off + 1], dims)


class Rot:
    def __init__(self, items):
        self.items = items
        self.bufs = [Buf() for _ in items]
        self.i = 0

    def next(self):
        k = self.i % len(self.items)
        self.i += 1
        return self.items[k], self.bufs[k]


DBG = {}


def dump(P, name, t, bufs, dt=None):
    nc = P.nc
    shape = list(t.shape)
    d = nc.dram_tensor("dbg_" + name, shape, dt or t.dtype, kind="ExternalOutput").ap()
    P.dma(P.slot(), lambda e: e.dma_start(out=d[:, :], in_=t[:, :]), reads=bufs)


def phase_norm(sync, es_outer, xsrc, hnT, ident):
    nc = sync.nc
    P = Prog(sync).begin()
    with contextlib.ExitStack() as es:
        sfx = uid()

        def sb(name, shape, dt):
            return es.enter_context(nc.sbuf_tensor(name + sfx, shape, dt))
        xt = Rot([sb("n_xt%d" % i, [128, D], F32) for i in range(3)])
        junk = sb("n_junk", [128, D], BF16)
        hn0 = Rot([sb("n_hn%d" % i, [128, D], BF16) for i in range(2)])
        ss = sb("n_ss", [128, 32], F32)
        ln = sb("n_ln", [128, 32], F32)
        rs = sb("n_rs", [128, 32], F32)
        epsc = sb("n_eps", [128, 1], F32)
        ptr = Rot([es.enter_context(nc.psum_tensor("n_ptr%d" % i + sfx, [128, D], BF16)) for i in range(2)])
        bjunk = Buf()
        beps = Buf()
        bhn = Buf()
        slots = [P.slot() for _ in range(3)]
        P.dve(lambda e: e.memset(epsc[:], EPS), writes=[beps])
        for i in range(32):
            x_t, bx = xt.next()
            sl = slots[i % 3]
            P.dma(sl, lambda e, x_t=x_t, i=i: e.dma_start(out=x_t[:], in_=xsrc[128 * i:128 * (i + 1), :]), writes=[bx])
            bss = Buf()
            P.act(lambda e, x_t=x_t, i=i: e.activation(out=junk[:], in_=x_t[:], func=AF.Square, accum_out=ss[:, i:i + 1]),
                  reads=[bx], writes=[bjunk, bss])
            bln = Buf()
            P.act(lambda e, i=i: e.activation(out=ln[:, i:i + 1], in_=ss[:, i:i + 1], func=AF.Ln, bias=epsc[:], scale=1.0 / D),
                  reads=[bss, beps], writes=[bln])
            brs = Buf()
            P.act(lambda e, i=i: e.activation(out=rs[:, i:i + 1], in_=ln[:, i:i + 1], func=AF.Exp, scale=-0.5),
                  reads=[bln], writes=[brs])
            h_t, bh = hn0.next()
            P.dve(lambda e, h_t=h_t, x_t=x_t, i=i: e.tensor_scalar(out=h_t[:], in0=x_t[:], scalar1=rs[:, i:i + 1], scalar2=None, op0=ALU.mult),
                  reads=[bx, brs], writes=[bh])
            p_t, bp = ptr.next()
            for kc in range(8):
                P.pe(lambda e, p_t=p_t, h_t=h_t, kc=kc: e.transpose(p_t[:, kc * 128:(kc + 1) * 128], h_t[:, kc * 128:(kc + 1) * 128], ident[:]),
                     reads=[bh], writes=[bp])
            dst = lambda i=i: FV(hnT, 0, 128, 128 * i, [[S, 8], [1, 128]])
            src = lambda p_t=p_t: FV(p_t, 0, 128, 0, [[128, 8], [1, 128]])
            if i % 2 == 0:
                P.act(lambda e, dst=dst, src=src: e.activation(out=dst(), in_=src(), func=AF.Copy), reads=[bp], writes=[bhn])
            else:
                P.dve(lambda e, dst=dst, src=src: e.tensor_copy(out=dst(), in_=src()), reads=[bp], writes=[bhn])
        if DBG.get("hnT"):
            dump(P, "hnT", hnT, [bhn])
            dump(P, "rs", rs, [bhn])
        P.emit()


def phase_outproj(sync, xsrc, wo_dram, ysc, out):
    nc = sync.nc
    P = Prog(sync).begin()
    with contextlib.ExitStack() as es:
        sfx = uid()

        def sb(name, shape, dt):
            return es.enter_context(nc.sbuf_tensor(name + sfx, shape, dt))
        wst = Rot([sb("o_wst%d" % i, [128, D], F32) for i in range(2)])
        wo = sb("o_wo", [128, 8 * D], BF16)
        bwo = Buf()
        xt = Rot([sb("o_xt%d" % i, [128, D], F32) for i in range(2)])
        ot = Rot([sb("o_ot%d" % i, [128, D], F32) for i in range(2)])
        yt = Rot([sb("o_yt%d" % i, [128, 8 * 512], BF16) for i in range(2)])
        po = Rot([es.enter_context(nc.psum_tensor("o_po%d" % i + sfx, [128, 512], F32)) for i in range(4)])
        s_w = [P.slot() for _ in range(2)]
        s_x = [P.slot() for _ in range(2)]
        s_y = [P.slot() for _ in range(2)]
        s_o = [P.slot() for _ in range(2)]
        for kc in range(8):
            w_t, bw = wst.next()
            P.dma(s_w[kc % 2], lambda e, w_t=w_t, kc=kc: e.dma_start(out=w_t[:], in_=wo_dram[kc * 128:(kc + 1) * 128, :]), writes=[bw])
            P.pool(lambda e, w_t=w_t, kc=kc: e.tensor_copy(out=wo[:, kc * D:(kc + 1) * D], in_=w_t[:]), reads=[bw], writes=[bwo])
        y_t = by = None
        for i in range(32):
            if i % 4 == 0:
                y_t, by = yt.next()
                tb = i // 4
                P.dma(s_y[tb % 2], lambda e, y_t=y_t, tb=tb: e.dma_start(
                    out=FV(y_t, 0, 128, 0, [[512, 8], [1, 512]]),
                    in_=ysc[:, :, tb * 512:(tb + 1) * 512].rearrange("k p t -> p k t")), writes=[by])
            x_t, bx = xt.next()
            P.dma(s_x[i % 2], lambda e, x_t=x_t, i=i: e.dma_start(out=x_t[:], in_=xsrc[128 * i:128 * (i + 1), :]), writes=[bx])
            o_t, bo = ot.next()
            for nb in range(2):
                p_t, bp = po.next()
                for kc in range(8):
                    P.pe(lambda e, p_t=p_t, y_t=y_t, kc=kc, nb=nb, i=i: e.matmul(
                        p_t[:], lhsT=y_t[:, kc * 512 + (i % 4) * 128: kc * 512 + (i % 4) * 128 + 128],
                        rhs=wo[:, kc * D + nb * 512: kc * D + nb * 512 + 512], start=(kc == 0), stop=(kc == 7)),
                        reads=[by, bwo], writes=[bp])
                P.dve(lambda e, p_t=p_t, x_t=x_t, o_t=o_t, nb=nb: e.tensor_tensor(
                    out=o_t[:, nb * 512:(nb + 1) * 512], in0=p_t[:], in1=x_t[:, nb * 512:(nb + 1) * 512], op=ALU.add),
                    reads=[bp, bx], writes=[bo])
            P.dma(s_o[i % 2], lambda e, o_t=o_t, i=i: e.dma_start(out=out[128 * i:128 * (i + 1), :], in_=o_t[:]), reads=[bo])
        P.emit()


def qk_geometry(d):
    L = S // d
    return L, L // 128


def merge_steps(a, b):
    out = []
    na, nb = len(a), len(b)
    if na == 0 or nb == 0:
        return list(a) + list(b)
    ia = ib = 0
    while ia < na or ib < nb:
        if ib >= nb or (ia < na and ia * nb <= ib * na):
            out.append(a[ia]); ia += 1
        else:
            out.append(b[ib]); ib += 1
    return out


def phase_attn(sync, layer, hnT, dr, ysc):
    nc = sync.nc
    P = Prog(sync).begin()
    isA = layer == "A"
    w_dram, wg_dram, bias_dram = dr["w"], dr["wg"], dr["bias"]
    units = [(hp, g) for hp in range(8) for g in ((0, 1, 2) if isA else (0,))]
    with contextlib.ExitStack() as es:
        sfx = uid()

        def sb(name, shape, dt):
            return es.enter_context(nc.sbuf_tensor(name + sfx, shape, dt))
        ngc = sb("a_ngc", [128, 8], F32)
        gq = sb("a_gq", [128, 3], F32)
        gk = sb("a_gk", [128, 3], F32)
        epsc = sb("a_eps", [128, 1], F32)
        blk = sb("a_blk", [128, 128], BF16)
        bconst = Buf()
        s_c = P.slot()
        P.dma(s_c, lambda e: e.dma_start(out=ngc[:], in_=dr["ng"][:, :]), writes=[bconst])
        P.dma(s_c, lambda e: e.dma_start(out=gq[:], in_=dr["gq"][:, :]), writes=[bconst])
        P.dma(s_c, lambda e: e.dma_start(out=gk[:], in_=dr["gk"][:, :]), writes=[bconst])
        P.dve(lambda e: e.memset(epsc[:], EPS), writes=[bconst])
        P.dve(lambda e: e.memset(blk[:], 0.0), reads=[bconst], writes=[bconst])
        P.dve(lambda e: e.memset(blk[0:64, 0:64], 1.0 / 64), reads=[bconst], writes=[bconst])
        P.dve(lambda e: e.memset(blk[64:128, 64:128], 1.0 / 64), reads=[bconst], writes=[bconst])
        P.dve(lambda e: e.tensor_scalar(out=gq[:], in0=gq[:], scalar1=0.125, scalar2=None, op0=ALU.mult),
              reads=[bconst], writes=[bconst])
        qTr = Rot([sb("a_qT%d" % i, [128, S], BF16) for i in range(2)])
        kTr = Rot([sb("a_kT%d" % i, [128, S], BF16) for i in range(2)])
        Vt = sb("a_V", [128, 32 * 192], BF16)
        bV = Buf()
        P.dve(lambda e: e.memset(FV(Vt, 0, 128, 64, [[192, 32], [1, 64]]), 1.0), writes=[bV])
        sgT = sb("a_sgT", [128, S], BF16)
        bsg = Buf()
        if isA:
            acc = [sb("a_acc%d" % h, [128, S], F32) for h in range(2)]
            bacc = [Buf(), Buf()]
        wst = Rot([sb("a_wst%d" % i, [128, 384], F32) for i in range(4)])
        s_w = [P.slot() for _ in range(4)]
        wb = Rot([sb("a_wb%d" % i, [128, 8 * 384], BF16) for i in range(2)])
        wgb = sb("a_wgb", [128, 8 * 128], BF16)
        bwg = Buf()
        ebw = 2 * 256 if isA else dr["ebw"]
        ebst = Rot([sb("a_ebst%d" % i, [128, 512 if isA else 768], F32) for i in range(2)])
        s_eb = [P.slot() for _ in range(2)]
        eb = Rot([sb("a_eb%d" % i, [128, ebw], BF16) for i in range(2)])
        sq = Rot([sb("a_sq%d" % i, [128, 512], BF16) for i in range(2)])
        lnb = Rot([sb("a_ln%d" % i, [128, 512], F32) for i in range(2)])
        rstd = Rot([sb("a_rstd%d" % i, [128, 512], F32) for i in range(2)])
        ew = 512 if isA else 768
        ex = Rot([sb("a_ex%d" % i, [128, ew], BF16) for i in range(4 if isA else 3)])
        pT = Rot([sb("a_pT%d" % i, [128, ew], BF16) for i in range(8 if isA else 7)])
        rec = Rot([sb("a_rec%d" % i, [128, 512], F32) for i in range(1)])
        tmp = Rot([sb("a_tmp%d" % i, [128, 512], F32) for i in range(1)])
        if isA:
            yT = Rot([sb("a_yT%d" % i, [128, 1024], BF16) for i in range(2)])
            s_y = [P.slot() for _ in range(2)]
        else:
            yTB = [Rot([sb("b_yT%d_%d" % (h, i), [128, 1024], BF16) for i in range(2)]) for h in range(2)]
            s_yB = [[P.slot() for _ in range(2)] for h in range(2)]
        banks = [es.enter_context(nc.psum_tensor("a_ps%d" % i + sfx, [128, 512], F32)) for i in range(8)]
        bbank = [Buf() for _ in range(8)]

        def bankrot(ids):
            r = Rot([banks[i] for i in ids])
            r.bufs = [bbank[i] for i in ids]
            return r
        pq = bankrot([0, 1])
        pss = bankrot([2])
        if isA:
            ps_h = [bankrot([3]), bankrot([4])]
            po_h = [bankrot([5]), bankrot([6])]
            ph = bankrot([7])
            pv = bankrot([3, 4])
            pg = bankrot([5, 6])
        else:
            psB = (banks[3], banks[4])
            psBb = (bbank[3], bbank[4])
            po = bankrot([5, 6, 7])
            pv = bankrot([3, 4])
            pg = bankrot([5, 6])

        def wload_steps(u):
            hp, g = u
            w_b, bw = wb.next()
            pend = []
            steps = []

            def conv(w_t, bs, kc):
                P.dve(lambda e: e.tensor_scalar(
                    out=w_b[:, kc * 384:(kc + 1) * 384], in0=w_t[:], scalar1=ngc[:, kc:kc + 1], scalar2=None, op0=ALU.mult),
                    reads=[bs, bconst], writes=[bw])
            for kc in range(8):
                def step(kc=kc):
                    w_t, bs = wst.next()
                    sl = s_w[(wst.i - 1) % 4]
                    P.dma(sl, lambda e: e.dma_start(out=w_t[:], in_=w_dram[hp, g, kc * 128:(kc + 1) * 128, :]), writes=[bs])
                    pend.append((w_t, bs, kc))
                    if len(pend) > 2:
                        conv(*pend.pop(0))
                steps.append(step)

            def flush():
                while pend:
                    conv(*pend.pop(0))
            steps.append(flush)
            return (w_b, bw), steps

        def gload_steps(hp):
            pend = []
            steps = []

            def conv(w_t, bs, kc):
                P.dve(lambda e: e.tensor_scalar(
                    out=wgb[:, kc * 128:(kc + 1) * 128], in0=w_t[:, 0:128], scalar1=ngc[:, kc:kc + 1], scalar2=None, op0=ALU.mult),
                    reads=[bs, bconst], writes=[bwg])
            for kc in range(8):
                def step(kc=kc):
                    w_t, bs = wst.next()
                    sl = s_w[(wst.i - 1) % 4]
                    P.dma(sl, lambda e: e.dma_start(out=w_t[:, 0:128], in_=wg_dram[hp, kc * 128:(kc + 1) * 128, :]), writes=[bs])
                    pend.append((w_t, bs, kc))
                    if len(pend) > 2:
                        conv(*pend.pop(0))
                steps.append(step)

            def flush():
                while pend:
                    conv(*pend.pop(0))
            steps.append(flush)
            return steps

        def eb_dma_A(u):
            hp, g = u
            st_, bs = ebst.next()
            sl = s_eb[(ebst.i - 1) % 2]
            P.dma(sl, lambda e: e.dma_start(out=st_[:, 0:512], in_=bias_dram[hp, g, :, :]), writes=[bs])
            return st_, bs

        def eb_conv_A(st_, bs):
            e_t, be = eb.next()
            P.act(lambda e: e.activation(out=e_t[:, 0:512], in_=st_[:, 0:512], func=AF.Exp), reads=[bs], writes=[be])
            return e_t, be

        def load_eb_B(hp, h):
            e_t, be = eb.next()
            ebw1 = dr["ebw"]
            off = 0
            while off < ebw1:
                n = min(768, ebw1 - off)
                st_, bs = ebst.next()
                sl = s_eb[(ebst.i - 1) % 2]
                P.dma(sl, lambda e, st_=st_, off=off, n=n: e.dma_start(out=st_[:, 0:n], in_=bias_dram[2 * hp + h, :, off:off + n]), writes=[bs])
                P.act(lambda e, st_=st_, e_t=e_t, off=off, n=n: e.activation(
                    out=e_t[:, off: off + n], in_=st_[:, 0:n], func=AF.Exp), reads=[bs], writes=[be])
                off += n
            return e_t, be

        def gate_steps():
            steps = []
            for tb in range(8):
                def step(tb=tb):
                    p_t, bp = pg.next()
                    for kc in range(8):
                        P.pe(lambda e, p_t=p_t, kc=kc: e.matmul(
                            p_t[:], lhsT=wgb[:, kc * 128:(kc + 1) * 128], rhs=hnT[:, kc * S + tb * 512: kc * S + tb * 512 + 512],
                            start=(kc == 0), stop=(kc == 7)), reads=[bwg], writes=[bp])
                    P.act(lambda e, p_t=p_t: e.activation(out=sgT[:, tb * 512:(tb + 1) * 512], in_=p_t[:], func=AF.Silu),
                          reads=[bp], writes=[bsg])
                steps.append(step)
            return steps

        def qk_steps(u, w_b, bw, q_t, bq, k_t, bk):
            hp, g = u
            d = DILS[g] if isA else 1
            L = S // d
            pend = []

            def tail(p_t, bp, s_t, bs, which, tb):
                ss_t, bss = pss.next()
                P.pe(lambda e: e.matmul(ss_t[:], lhsT=blk[:], rhs=s_t[:], start=True, stop=True),
                     reads=[bs, bconst], writes=[bss])
                l_t, bl = lnb.next()
                P.act(lambda e: e.activation(out=l_t[:], in_=ss_t[:], func=AF.Ln, bias=epsc[:], scale=1.0),
                      reads=[bss, bconst], writes=[bl])
                r_t, br = rstd.next()
                P.act(lambda e: e.activation(out=r_t[:], in_=l_t[:], func=AF.Exp, scale=-0.5), reads=[bl], writes=[br])
                if which == 0:
                    n = 512 // d
                    P.dve(lambda e: e.scalar_tensor_tensor(
                        out=FV(q_t, 0, 128, tb * n, [[L, d], [1, n]]),
                        in0=FV(p_t, 0, 128, 0, [[1, d], [d, n]]), scalar=gq[:, g:g + 1],
                        in1=FV(r_t, 0, 128, 0, [[1, d], [d, n]]), op0=ALU.mult, op1=ALU.mult),
                        reads=[bp, br, bconst], writes=[bq])
                else:
                    P.dve(lambda e: e.scalar_tensor_tensor(
                        out=k_t[:, tb * 512:(tb + 1) * 512], in0=p_t[:], scalar=gk[:, g:g + 1], in1=r_t[:],
                        op0=ALU.mult, op1=ALU.mult), reads=[bp, br, bconst], writes=[bk])

            steps = []
            for t in range(16):
                def step(t=t):
                    which, tb = divmod(t, 8)
                    p_t, bp = pq.next()
                    for kc in range(8):
                        P.pe(lambda e, kc=kc: e.matmul(
                            p_t[:], lhsT=w_b[:, kc * 384 + which * 128: kc * 384 + which * 128 + 128],
                            rhs=hnT[:, kc * S + tb * 512: kc * S + tb * 512 + 512], start=(kc == 0), stop=(kc == 7)),
                            reads=[bw], writes=[bp])
                    s_t, bs = sq.next()
                    P.act(lambda e: e.activation(out=s_t[:], in_=p_t[:], func=AF.Square), reads=[bp], writes=[bs])
                    if pend:
                        tail(*pend.pop())
                    pend.append((p_t, bp, s_t, bs, which, tb))
                steps.append(step)

            def flush():
                tail(*pend.pop())
            steps.append(flush)
            return steps

        def v_steps(u, w_b, bw):
            hp, g = u
            d = DILS[g] if isA else 1
            L, nC = qk_geometry(d)
            steps = []
            for c0 in range(0, 32, 4):
                def step(c0=c0):
                    p_t, bp = pv.next()
                    for cc in range(4):
                        c = c0 + cc
                        r, i = divmod(c, nC)
                        t0 = r + d * 128 * i
                        for kc in range(8):
                            P.pe(lambda e, kc=kc, cc=cc, t0=t0: e.matmul(
                                p_t[:, cc * 128:(cc + 1) * 128],
                                lhsT=FV(hnT, 0, 128, kc * S + t0, [[d, 128]]),
                                rhs=w_b[:, kc * 384 + 256: kc * 384 + 384], start=(kc == 0), stop=(kc == 7)),
                                reads=[bw], writes=[bp])
                    P.act(lambda e: e.activation(
                        out=FV(Vt, 0, 128, c0 * 192, [[192, 4], [128, 2], [1, 64]]),
                        in_=FV(p_t, 0, 128, 0, [[128, 4], [64, 2], [1, 64]]), func=AF.Copy), reads=[bp], writes=[bV])
                steps.append(step)
            return steps

        def attn_steps_A(u, q_t, bq, k_t, bk, e_t, be):
            hp, g = u
            d = DILS[g]
            first = g == 0
            L, nC = qk_geometry(d)
            pend = []
            steps = []

            def acc_out(h, srcf, sbuf, dstf):
                if first:
                    P.dve(lambda e: e.tensor_copy(out=dstf(), in_=srcf()), reads=[sbuf], writes=[bacc[h]])
                else:
                    P.dve(lambda e: e.tensor_tensor(out=dstf(), in0=srcf(), in1=dstf(), op=ALU.add), reads=[sbuf], writes=[bacc[h]])

            def make_tail(r, m, ptl, state):
                def tail():
                    for h in range(2):
                        vof = 0 if h == 0 else 64
                        acc_h = acc[h]

                        def pcol(i, lo):
                            p_t, bp = ptl[(h, i // 2)]
                            return p_t, bp, (i % 2) * 256 + lo
                        if m == 0:
                            hslot = state["ph"]
                            pa, bpa, c0 = pcol(0, 64)
                            c1 = r * nC
                            P.pe(lambda e, pa=pa, c0=c0, c1=c1, vof=vof, h=h: e.matmul(
                                hslot[0][:, h * 128: h * 128 + 64], lhsT=Vt[:, c1 * 192 + vof: c1 * 192 + vof + 128],
                                rhs=pa[:, c0:c0 + 64], start=True, stop=True), reads=[bV, bpa], writes=[hslot[1]])
                        for jj in ([2 * m - 1] if m > 0 else []) + ([2 * m] if 2 * m <= nC - 2 else []):
                            bb = jj % 4
                            if bb == 0:
                                state["po%d" % h] = po_h[h].next()
                            o_t, bo = state["po%d" % h]
                            pa, bpa, ca = pcol(jj, 128)
                            pb_, bpb, cb = pcol(jj + 1, 0)
                            c1 = r * nC + jj
                            P.pe(lambda e, o_t=o_t, pa=pa, ca=ca, bb=bb, vof=vof, c1=c1: e.matmul(
                                o_t[:, bb * 128:(bb + 1) * 128], lhsT=Vt[:, c1 * 192 + vof: c1 * 192 + vof + 128],
                                rhs=pa[:, ca:ca + 128], start=True, stop=False), reads=[bV, bpa], writes=[bo])
                            P.pe(lambda e, o_t=o_t, pb_=pb_, cb=cb, bb=bb, vof=vof, c1=c1: e.matmul(
                                o_t[:, bb * 128:(bb + 1) * 128], lhsT=Vt[:, (c1 + 1) * 192 + vof: (c1 + 1) * 192 + vof + 128],
                                rhs=pb_[:, cb:cb + 128], start=False, stop=True), reads=[bV, bpb], writes=[bo])
                            if bb == 3 or jj == nC - 2:
                                j0 = jj - bb
                                m0 = 64 + 128 * j0
                                nq = 128 * (bb + 1)
                                acc_out(h, lambda o_t=o_t, nq=nq: o_t[:, 0:nq], bo,
                                        lambda acc_h=acc_h, m0=m0, nq=nq: FV(acc_h, 0, 128, r + d * m0, [[d, nq]]))
                        if m == nC // 2 - 1:
                            hslot = state["ph"]
                            pa, bpa, c0 = pcol(nC - 1, 128)
                            cl = r * nC + nC - 1
                            P.pe(lambda e, pa=pa, c0=c0, cl=cl, vof=vof, h=h: e.matmul(
                                hslot[0][:, h * 128 + 64: h * 128 + 128], lhsT=Vt[:, cl * 192 + vof: cl * 192 + vof + 128],
                                rhs=pa[:, c0:c0 + 64], start=True, stop=True), reads=[bV, bpa], writes=[hslot[1]])
                            acc_out(h, lambda h=h: FV(hslot[0], 0, 128, h * 128, [[64, 2], [1, 64]]), hslot[1],
                                    lambda acc_h=acc_h: FV(acc_h, 0, 128, r, [[d * (L - 64), 2], [d, 64]]))
                return tail

            for r in range(d):
                ptl = {}
                state = {}
                for m in range(nC // 2):
                    def step(r=r, m=m, ptl=ptl, state=state):
                        if m == 0:
                            state["ph"] = ph.next()
                        stl = [ps_h[0].next(), ps_h[1].next()]
                        for cc in range(2):
                            i = 2 * m + cc
                            qlo = max(0, 128 * i - 64)
                            qhi = min(L, 128 * i + 192)
                            lo = qlo - (128 * i - 64)
                            n = qhi - qlo
                            for h in range(2):
                                s_t, bs = stl[h]
                                P.pe(lambda e, s_t=s_t, cc=cc, lo=lo, n=n, qlo=qlo, i=i, h=h: e.matmul(
                                    s_t[:, cc * 256 + lo: cc * 256 + lo + n],
                                    lhsT=FV(k_t, 64 * h, 64 * h + 64, r + d * 128 * i, [[d, 128]]),
                                    rhs=q_t[64 * h:64 * h + 64, r * L + qlo: r * L + qlo + n], start=True, stop=True),
                                    reads=[bk, bq], writes=[bs])
                        for h in range(2):
                            s_t, bs = stl[h]
                            x_t, bx = ex.next()
                            P.act(lambda e, x_t=x_t, s_t=s_t: e.activation(out=x_t[:, 0:512], in_=s_t[:], func=AF.Exp), reads=[bs], writes=[bx])
                            p_t, bp = pT.next()
                            P.dve(lambda e, p_t=p_t, x_t=x_t, h=h: e.tensor_tensor(
                                out=FV(p_t, 0, 128, 0, [[256, 2], [1, 256]]), in0=FV(x_t, 0, 128, 0, [[256, 2], [1, 256]]),
                                in1=FV(e_t, 0, 128, h * 256, [[0, 2], [1, 256]]), op=ALU.mult), reads=[bx, be], writes=[bp])
                            ptl[(h, m)] = (p_t, bp)
                        if pend:
                            pend.pop()()
                        pend.append(make_tail(r, m, ptl, state))
                    steps.append(step)

            def flush():
                pend.pop()()
            steps.append(flush)
            return steps

        def normalize_steps_A(hp):
            steps = []
            st = {}
            for tb in range(8):
                def step(tb=tb):
                    q4, hb = divmod(tb, 2)
                    if hb == 0:
                        st["y"] = yT.next()
                    y_t, by = st["y"]
                    cs = slice(tb * 512, (tb + 1) * 512)
                    r_t, br = rec.next()
                    t_t, bt = tmp.next()
                    P.dve(lambda e: e.reciprocal(out=r_t[0:64, :], in_=acc[0][64:128, cs]), reads=[bacc[0]], writes=[br])
                    P.dve(lambda e: e.reciprocal(out=r_t[64:128, :], in_=acc[1][0:64, cs]), reads=[bacc[1]], writes=[br])
                    P.dve(lambda e: e.tensor_tensor(out=t_t[0:64, :], in0=acc[0][0:64, cs], in1=r_t[0:64, :], op=ALU.mult),
                          reads=[br], writes=[bt])
                    P.dve(lambda e: e.tensor_tensor(out=t_t[64:128, :], in0=acc[1][64:128, cs], in1=r_t[64:128, :], op=ALU.mult),
                          reads=[br], writes=[bt])
                    P.dve(lambda e: e.tensor_tensor(out=y_t[:, hb * 512:(hb + 1) * 512], in0=t_t[:], in1=sgT[:, cs], op=ALU.mult),
                          reads=[bt, bsg], writes=[by])
                    if hb == 1:
                        sl = s_y[(yT.i - 1) % 2]
                        P.dma(sl, lambda e: e.dma_start(out=ysc[hp, :, q4 * 1024:(q4 + 1) * 1024], in_=y_t[:]), reads=[by])
                steps.append(step)
            return steps

        def attn_steps_B(hp, h, q_t, bq, k_t, bk, e_t, be):
            tiles = dr["tiles"]
            hs = slice(64 * h, 64 * h + 64)
            vof = 0 if h == 0 else 64
            num = slice(0, 64) if h == 0 else slice(64, 128)
            den = slice(64, 128) if h == 0 else slice(0, 64)
            contribs = []
            for Q in range(32):
                full, part = [], []
                for R in range(32):
                    qlo, nr = tiles[R][1], tiles[R][2]
                    lo_r = max(qlo, 2 * Q)
                    hi_r = min(qlo + nr - 1, 2 * Q + 1)
                    if lo_r > hi_r:
                        continue
                    (full if hi_r - lo_r == 1 else part).append((R, lo_r, hi_r - lo_r + 1))
                assert full
                contribs.append(full + part)
            lastR = [max(R for R, _, _ in contribs[Q]) for Q in range(32)]
            ptl = {}
            st = dict(o=None, y=None)
            pend = []
            steps = []

            def do_block(Q):
                if Q % 4 == 0:
                    st["o"] = po.next()
                o_t, bo = st["o"]
                cl = contribs[Q]
                for n_, (R, row0, nrow) in enumerate(cl):
                    p_t, bp = ptl[R]
                    c0 = (row0 - tiles[R][1]) * 64
                    oc = (Q % 4) * 128 + (row0 - 2 * Q) * 64
                    nn = nrow * 64
                    P.pe(lambda e, p_t=p_t, c0=c0, oc=oc, nn=nn, R=R, first=(n_ == 0), last=(n_ == len(cl) - 1): e.matmul(
                        o_t[:, oc:oc + nn], lhsT=Vt[:, R * 192 + vof: R * 192 + vof + 128],
                        rhs=p_t[:, c0:c0 + nn], start=first, stop=last), reads=[bV, bp], writes=[bo])
                if Q % 4 == 3:
                    tb = Q // 4
                    cs = slice(tb * 512, (tb + 1) * 512)
                    if tb % 2 == 0:
                        st["y"] = yTB[h].next()
                    y_t, by = st["y"]
                    r_t, br = rec.next()
                    t_t, bt = tmp.next()
                    P.dve(lambda e: e.reciprocal(out=r_t[num, :], in_=o_t[den, :]), reads=[bo], writes=[br])
                    P.dve(lambda e: e.tensor_tensor(out=t_t[num, :], in0=o_t[num, :], in1=r_t[num, :], op=ALU.mult),
                          reads=[bo, br], writes=[bt])
                    P.dve(lambda e: e.tensor_tensor(
                        out=y_t[num, (tb % 2) * 512:(tb % 2) * 512 + 512], in0=t_t[num, :], in1=sgT[num, cs], op=ALU.mult),
                        reads=[bt, bsg], writes=[by])
                    if tb % 2 == 1:
                        q4 = tb // 2
                        sl = s_yB[h][(yTB[h].i - 1) % 2]
                        P.dma(sl, lambda e: e.dma_start(
                            out=ysc[hp, num, q4 * 1024:(q4 + 1) * 1024], in_=y_t[num, :]), reads=[by])

            for R in range(32):
                def step(R=R):
                    toff, qlo, nr = tiles[R]
                    n = nr * 64
                    sA, sB = psB
                    bA, bB = psBb
                    n1 = min(n, 512)
                    P.pe(lambda e: e.matmul(
                        sA[:, 0:n1], lhsT=k_t[hs, R * 128:(R + 1) * 128], rhs=q_t[hs, qlo * 64: qlo * 64 + n1], start=True, stop=True),
                        reads=[bk, bq], writes=[bA])
                    x_t, bx = ex.next()
                    P.act(lambda e: e.activation(out=x_t[:, 0:n1], in_=sA[:, 0:n1], func=AF.Exp), reads=[bA], writes=[bx])
                    if n > 512:
                        n2 = n - 512
                        P.pe(lambda e: e.matmul(
                            sB[:, 0:n2], lhsT=k_t[hs, R * 128:(R + 1) * 128], rhs=q_t[hs, qlo * 64 + 512: qlo * 64 + 512 + n2], start=True, stop=True),
                            reads=[bk, bq], writes=[bB])
                        P.act(lambda e: e.activation(out=x_t[:, 512:512 + n2], in_=sB[:, 0:n2], func=AF.Exp), reads=[bB], writes=[bx])
                    p_t, bp = pT.next()
                    P.dve(lambda e: e.tensor_tensor(
                        out=p_t[:, 0:n], in0=x_t[:, 0:n], in1=e_t[:, toff: toff + n], op=ALU.mult),
                        reads=[bx, be], writes=[bp])
                    ptl[R] = (p_t, bp)
                    if pend:
                        pend.pop()()

                    def tail():
                        for Q in range(32):
                            if lastR[Q] == R:
                                do_block(Q)
                    pend.append(tail)
                steps.append(step)

            def flush():
                pend.pop()()
            steps.append(flush)
            return steps

        def run(steps):
            for st_ in steps:
                st_()

        dstop = DBG.get("stop")
        nU = len(units)
        wts = {}
        qk = {}
        ebd = {}
        wts[0], ws = wload_steps(units[0])
        run(ws)
        run(gload_steps(0))
        if isA:
            ebd[0] = eb_dma_A(units[0])
        if nU > 1:
            wts[1], ws1 = wload_steps(units[1])
        else:
            ws1 = []
        qk[0] = qTr.next() + kTr.next()
        run(merge_steps(qk_steps(units[0], *wts[0], *qk[0]), ws1))
        run(v_steps(units[0], *wts[0]))
        run(gate_steps())
        for ui, u in enumerate(units):
            hp, g = u
            nxt = units[ui + 1] if ui + 1 < nU else None
            wsteps = []
            if ui + 2 < nU:
                wts[ui + 2], wsteps = wload_steps(units[ui + 2])
            q_t, bq, k_t, bk = qk[ui]
            nsteps = []
            if nxt is not None:
                qk[ui + 1] = qTr.next() + kTr.next()
                nsteps = qk_steps(nxt, *wts[ui + 1], *qk[ui + 1])
            if isA:
                if nxt is not None:
                    ebd[ui + 1] = eb_dma_A(nxt)
                e_t, be = eb_conv_A(*ebd[ui])
                last = g == 2
                gsteps = gload_steps(hp + 1) if (g == 1 and hp + 1 < 8) else []
                run(merge_steps(merge_steps(attn_steps_A(u, q_t, bq, k_t, bk, e_t, be), nsteps), wsteps + gsteps))
                if last:
                    vs = v_steps(nxt, *wts[ui + 1]) if nxt is not None else []
                    run(merge_steps(normalize_steps_A(hp), vs))
                    if hp + 1 < 8:
                        run(gate_steps())
                elif nxt is not None:
                    run(v_steps(nxt, *wts[ui + 1]))
            else:
                gsteps = gload_steps(hp + 1) if hp + 1 < 8 else []
                e0 = load_eb_B(hp, 0)
                a0 = attn_steps_B(hp, 0, q_t, bq, k_t, bk, *e0)
                run(merge_steps(merge_steps(a0, nsteps[: len(nsteps) // 2]), wsteps))
                e1 = load_eb_B(hp, 1)
                a1 = attn_steps_B(hp, 1, q_t, bq, k_t, bk, *e1)
                run(merge_steps(merge_steps(a1, nsteps[len(nsteps) // 2:]), gsteps))
                if nxt is not None:
                    run(v_steps(nxt, *wts[ui + 1]))
                if hp + 1 < 8:
                    run(gate_steps())
            if dstop == "hp0" and ((isA and g == 2) or not isA):
                break
        P.emit()


_T5_LUT = None


def _t5_bucket_np(rel):
    import math
    half, me = 16, 8
    ret = np.where(rel > 0, half, 0)
    n = np.abs(rel)
    nf = np.maximum(n, 1).astype(np.float32)
    large = me + (np.log(nf / np.float32(me)) / np.float32(math.log(1024 / me)) * np.float32(half - me)).astype(np.int32)
    large = np.minimum(large, half - 1)
    return ret + np.where(n < me, n, large)


def _bias_tiles_A(t5_bias):
    a = np.arange(128)[:, None]
    b = np.arange(256)[None, :]
    rel = a - b + 64
    valid = (b - a >= 0) & (b - a <= 128)
    out = np.empty((8, 3, 128, 2, 256), np.float32)
    for g, d in enumerate(DILS):
        idx = _t5_bucket_np(rel * d)
        for h in range(16):
            t = t5_bias[g * 16 + h][idx]
            out[h // 2, g, :, h % 2, :] = np.where(valid, t, np.float32(NEG))
    return out


def _geom_B():
    rows = 64
    r = np.arange(rows)
    rs = np.clip(r - 4, 0, rows - 8)
    c = np.arange(64)
    cs = np.clip(c - 8, 0, 64 - 16)
    tiles = []
    uniq = {}
    maps = []
    off = 0
    for R in range(32):
        krs = np.array([2 * R, 2 * R + 1])
        qrows = [q for q in range(rows) if (rs[q] <= krs[1]) and (rs[q] + 7 >= krs[0])]
        qlo, nr = qrows[0], len(qrows)
        assert qrows == list(range(qlo, qlo + nr))
        kr = np.repeat(krs, 64)[:, None]
        kc = np.tile(c, 2)[:, None]
        qr = np.repeat(np.arange(qlo, qlo + nr), 64)[None, :]
        qc = np.tile(c, nr)[None, :]
        valid = (kr >= rs[qr]) & (kr <= rs[qr] + 7) & (kc >= cs[qc]) & (kc < cs[qc] + 16)
        ridx = np.clip(kr - qr + 7, 0, 14)
        cidx = np.clip(kc - qc, -15, 15) + 15
        key = (nr, valid.tobytes(), ridx.tobytes())
        if key not in uniq:
            uniq[key] = off
            maps.append((off, ridx + 0 * cidx, cidx + 0 * ridx, valid))
            off += nr * 64
        tiles.append((uniq[key], qlo, nr))
    return tiles, maps, off


def _bias_tiles_B(rpb, maps, ebw):
    out = np.empty((16, 128, ebw), np.float32)
    for off, ridx, cidx, valid in maps:
        n = valid.shape[1]
        for h in range(16):
            out[h, :, off:off + n] = np.where(valid, rpb[h][ridx, cidx], np.float32(NEG))
    return out


def _unit_weights(w_in, ngroups):
    wu = np.empty((8, ngroups, D, 384), np.float32)
    for hp in range(8):
        for g in range(ngroups):
            for j in range(3):
                c0 = g * 3072 + j * 1024 + hp * 128
                wu[hp, g, :, j * 128:(j + 1) * 128] = w_in[:, c0:c0 + 128]
    gc = ngroups * 3072
    wg = np.ascontiguousarray(w_in[:, gc:gc + 1024].reshape(D, 8, 128).transpose(1, 0, 2))
    return wu, wg


_GEOM_B = None


def build_nc(layers="AB"):
    global _GEOM_B
    if _GEOM_B is None:
        _GEOM_B = _geom_B()
    tilesB, mapsB, ebwB = _GEOM_B
    nc = bass.Bass("TRN2", target_bir_lowering=False)

    def din(name, shape, dt=F32):
        return nc.dram_tensor(name, list(shape), dt, kind="ExternalInput").ap()
    x = din("x", [S, D])
    ident_d = din("ident_d", [128, 128])
    drA = dict(w=din("wA", [8, 3, D, 384]), wg=din("wgA", [8, D, 128]), bias=din("biasA", [8, 3, 128, 512]),
               ng=din("ngA", [128, 8]), gq=din("gqA", [128, 3]), gk=din("gkA", [128, 3]))
    woA = din("woA", [D, D])
    drB = dict(w=din("wB", [8, 1, D, 384]), wg=din("wgB", [8, D, 128]), bias=din("biasB", [16, 128, ebwB]),
               ng=din("ngB", [128, 8]), gq=din("gqB", [128, 3]), gk=din("gkB", [128, 3]), tiles=tilesB, ebw=ebwB)
    woB = din("woB", [D, D])
    out = nc.dram_tensor("out", [S, D], F32, kind="ExternalOutput").ap()
    ysc = nc.dram_tensor("ysc", [8, 128, S], BF16).ap()
    with contextlib.ExitStack() as es:
        sync = Sync(nc, es)
        hnT = es.enter_context(nc.sbuf_tensor("hnT", [128, 8 * S], BF16))
        ident = es.enter_context(nc.sbuf_tensor("ident", [128, 128], BF16))
        identf = es.enter_context(nc.sbuf_tensor("identf", [128, 128], F32))
        P0 = Prog(sync).begin()
        bi = Buf()
        P0.dma(P0.slot(), lambda e: e.dma_start(out=identf[:], in_=ident_d[:, :]), writes=[bi])
        P0.dve(lambda e: e.tensor_copy(out=ident[:], in_=identf[:]), reads=[bi], writes=[bi])
        P0.emit()
        src = x
        if "A" in layers:
            phase_norm(sync, es, src, hnT, ident)
            if DBG.get("stop") != "norm":
                phase_attn(sync, "A", hnT, drA, ysc)
            if not DBG.get("stop"):
                phase_outproj(sync, src, woA, ysc, out)
            src = out
        if "B" in layers:
            phase_norm(sync, es, src, hnT, ident)
            phase_attn(sync, "B", hnT, drB, ysc)
            phase_outproj(sync, src, woB, ysc, out)
    return nc


def host_inputs(norm_gain, a_w_in, a_w_out, a_q_gain, a_k_gain, t5_bias, b_w_in, b_w_out, b_q_gain, b_k_gain, b_rpb):
    global _GEOM_B
    if _GEOM_B is None:
        _GEOM_B = _geom_B()
    tilesB, mapsB, ebwB = _GEOM_B
    f = lambda a: np.ascontiguousarray(np.asarray(a, dtype=np.float32))
    wA, wgA = _unit_weights(f(a_w_in)[0], 3)
    wB, wgB = _unit_weights(f(b_w_in)[0], 1)
    ng = f(norm_gain)

    def gcol(gn):
        gn = f(gn).reshape(-1, 64)
        o = np.ones((128, 3), np.float32)
        for g in range(gn.shape[0]):
            o[:, g] = np.tile(gn[g], 2)
        return o
    shared = dict(
        ident_d=np.eye(128, dtype=np.float32),
        wA=wA, wgA=wgA, woA=f(a_w_out)[0],
        biasA=np.ascontiguousarray(_bias_tiles_A(f(t5_bias)).reshape(8, 3, 128, 512)),
        ngA=np.ascontiguousarray(ng[0].reshape(8, 128).T), gqA=gcol(a_q_gain[0]), gkA=gcol(a_k_gain[0]),
        wB=wB, wgB=wgB, woB=f(b_w_out)[0],
        biasB=_bias_tiles_B(f(b_rpb)[0], mapsB, ebwB),
        ngB=np.ascontiguousarray(ng[1].reshape(8, 128).T), gqB=gcol(b_q_gain), gkB=gcol(b_k_gain),
    )
    return shared


def kernel(x, norm_gain, a_w_in, a_w_out, a_q_gain, a_k_gain, t5_bias, b_w_in, b_w_out, b_q_gain, b_k_gain, b_rpb):
    x = np.ascontiguousarray(np.asarray(x, dtype=np.float32))
    shared = host_inputs(norm_gain, a_w_in, a_w_out, a_q_gain, a_k_gain, t5_bias, b_w_in, b_w_out, b_q_gain, b_k_gain, b_rpb)
    nc = build_nc("AB")
    in_maps = [dict(shared, x=x[c]) for c in range(NCORES)]
    res = run_bass_kernel_spmd(nc, in_maps, core_ids=list(range(NCORES)))
    return np.stack([np.asarray(r["out"], dtype=np.float32) for r in res.results], axis=0)
```

```python
import contextlib
import numpy as np
import concourse.bass as bass
import concourse.mybir as mybir
from concourse.bass_utils import run_bass_kernel_spmd

F32 = mybir.dt.float32
BF16 = mybir.dt.bfloat16
AF = mybir.ActivationFunctionType
ALU = mybir.AluOpType

S = 4096
D = 1024
NCORES = 8
DILS = (1, 4, 16)
EPS = 1e-6
NEG = -30000.0


class Buf:
    __slots__ = ("name", "lw", "rd")

    def __init__(self, name=""):
        self.name = name
        self.lw = None
        self.rd = []


class DmaSlot:
    __slots__ = ("sem", "count", "name")

    def __init__(self, name):
        self.name = name
        self.sem = None
        self.count = 0


class Op:
    __slots__ = ("eng", "fn", "deps", "slot", "signal", "tick", "semkey", "known", "idx")


COMPUTE = ("pe", "act", "dve", "pool")
ENGS = ("pe", "act", "dve", "pool", "sp")


class Sync:
    def __init__(self, nc, es, nslots=40):
        self.nc = nc
        self.esem = {e: es.enter_context(nc.semaphore("sem_" + e)) for e in COMPUTE}
        self.tick = {e: 0 for e in COMPUTE}
        self.slots = []
        for i in range(nslots):
            s = DmaSlot("dq%d" % i)
            s.sem = es.enter_context(nc.semaphore(s.name))
            self.slots.append(s)


class Prog:
    def __init__(self, sync):
        self.sync = sync
        self.nc = sync.nc
        self.ops = []
        self.nslot = 0

    def slot(self, name=""):
        s = self.sync.slots[self.nslot]
        self.nslot += 1
        return s

    def add(self, eng, fn, reads=(), writes=(), slot=None):
        op = Op()
        op.eng = eng
        op.fn = fn
        op.slot = slot
        op.signal = slot is not None
        op.tick = None
        op.idx = len(self.ops)
        deps = {}
        is_dma = slot is not None
        for b in reads:
            w = b.lw
            if w is not None:
                if is_dma or w.slot is not None or w.eng != eng or eng != "pe":
                    deps[w.idx] = w
        for b in writes:
            w = b.lw
            if w is not None and (is_dma or w.slot is not None or w.eng != eng):
                deps[w.idx] = w
            for r in b.rd:
                if is_dma or r.slot is not None or r.eng != eng:
                    deps[r.idx] = r
        for b in reads:
            b.rd.append(op)
        for b in writes:
            b.lw = op
            b.rd = []
        op.deps = list(deps.values())
        for d in op.deps:
            d.signal = True
        self.ops.append(op)
        return op

    def pe(self, fn, reads=(), writes=()):
        return self.add("pe", fn, reads, writes)

    def act(self, fn, reads=(), writes=()):
        return self.add("act", fn, reads, writes)

    def dve(self, fn, reads=(), writes=()):
        return self.add("dve", fn, reads, writes)

    def pool(self, fn, reads=(), writes=()):
        return self.add("pool", fn, reads, writes)

    def dma(self, slot, fn, reads=(), writes=()):
        return self.add("sp", fn, reads, writes, slot=slot)

    def emit(self):
        nc = self.nc
        sy = self.sync
        for op in self.ops:
            if op.slot is not None:
                op.slot.count += 16
                op.tick = op.slot.count
                op.semkey = op.slot
            elif op.signal:
                sy.tick[op.eng] += 1
                op.tick = sy.tick[op.eng]
                op.semkey = op.eng
        base = {}
        for e in COMPUTE:
            base[e] = 0
        clock = {e: {} for e in ENGS}
        start_tick = dict(self._start_tick)
        start_slot = dict(self._start_slot)
        for e in ENGS:
            for k, v in start_tick.items():
                clock[e][k] = v
            for k, v in start_slot.items():
                clock[e][k] = v
        plan = {e: [] for e in ENGS}
        for op in self.ops:
            ck = clock[op.eng]
            need = {}
            for d in op.deps:
                if ck.get(d.semkey, 0) >= d.tick:
                    continue
                if need.get(d.semkey, 0) < d.tick:
                    need[d.semkey] = d.tick
            for d in op.deps:
                for k, v in d.known.items():
                    if ck.get(k, 0) < v:
                        ck[k] = v
            waits = list(need.items())
            for k, v in waits:
                if ck.get(k, 0) < v:
                    ck[k] = v
            if op.tick is not None:
                kn = dict(ck)
                if kn.get(op.semkey, 0) < op.tick:
                    kn[op.semkey] = op.tick
                op.known = kn
            plan[op.eng].append((op, waits))
        final_waits = [(s, s.count) for s in sy.slots[: self.nslot] if s.count > start_slot.get(s, 0)]
        esem = sy.esem

        def semof(k):
            return k.sem if isinstance(k, DmaSlot) else esem[k]

        def run(engname):
            def body(eng):
                for op, waits in plan[engname]:
                    for k, v in waits:
                        eng.wait_ge(semof(k), v)
                    ins = op.fn(eng)
                    if op.slot is not None:
                        ins.then_inc(op.slot.sem, 16)
                    elif op.tick is not None:
                        ins.then_inc(esem[engname], 1)
                if engname == "sp":
                    for s, v in final_waits:
                        eng.wait_ge(s.sem, v)
            return body

        with nc.Block() as block:
            block.tensor(run("pe"))
            block.scalar(run("act"))
            block.vector(run("dve"))
            block.gpsimd(run("pool"))
            block.sync(run("sp"))

    def begin(self):
        sy = self.sync
        self._start_tick = dict(sy.tick)
        self._start_slot = {s: s.count for s in sy.slots}
        return self


_UID = [0]


def uid():
    _UID[0] += 1
    return "_u%d" % _UID[0]


def fview(ap, dims):
    return bass.AP(tensor=ap.tensor, offset=ap.offset, ap=[list(ap.ap[0])] + [list(d) for d in dims])


def FV(t, p0, p1, off, dims):
    return fview(t[p0:p1, off:off + 1], dims)


class Rot:
    def __init__(self, items):
        self.items = items
        self.bufs = [Buf() for _ in items]
        self.i = 0

    def next(self):
        k = self.i % len(self.items)
        self.i += 1
        return self.items[k], self.bufs[k]


DBG = {}


def dump(P, name, t, bufs, dt=None):
    nc = P.nc
    shape = list(t.shape)
    d = nc.dram_tensor("dbg_" + name, shape, dt or t.dtype, kind="ExternalOutput").ap()
    P.dma(P.slot(), lambda e: e.dma_start(out=d[:, :], in_=t[:, :]), reads=bufs)


def phase_norm(sync, es_outer, xsrc, hnT, ident):
    nc = sync.nc
    P = Prog(sync).begin()
    with contextlib.ExitStack() as es:
        sfx = uid()

        def sb(name, shape, dt):
            return es.enter_context(nc.sbuf_tensor(name + sfx, shape, dt))
        xt = Rot([sb("n_xt%d" % i, [128, D], F32) for i in range(3)])
        junk = sb("n_junk", [128, D], BF16)
        hn0 = Rot([sb("n_hn%d" % i, [128, D], BF16) for i in range(2)])
        ss = sb("n_ss", [128, 32], F32)
        ln = sb("n_ln", [128, 32], F32)
        rs = sb("n_rs", [128, 32], F32)
        epsc = sb("n_eps", [128, 1], F32)
        ptr = Rot([es.enter_context(nc.psum_tensor("n_ptr%d" % i + sfx, [128, D], BF16)) for i in range(2)])
        bjunk = Buf()
        beps = Buf()
        bhn = Buf()
        slots = [P.slot() for _ in range(3)]
        P.dve(lambda e: e.memset(epsc[:], EPS), writes=[beps])
        for i in range(32):
            x_t, bx = xt.next()
            sl = slots[i % 3]
            P.dma(sl, lambda e, x_t=x_t, i=i: e.dma_start(out=x_t[:], in_=xsrc[128 * i:128 * (i + 1), :]), writes=[bx])
            bss = Buf()
            P.act(lambda e, x_t=x_t, i=i: e.activation(out=junk[:], in_=x_t[:], func=AF.Square, accum_out=ss[:, i:i + 1]),
                  reads=[bx], writes=[bjunk, bss])
            bln = Buf()
            P.act(lambda e, i=i: e.activation(out=ln[:, i:i + 1], in_=ss[:, i:i + 1], func=AF.Ln, bias=epsc[:], scale=1.0 / D),
                  reads=[bss, beps], writes=[bln])
            brs = Buf()
            P.act(lambda e, i=i: e.activation(out=rs[:, i:i + 1], in_=ln[:, i:i + 1], func=AF.Exp, scale=-0.5),
                  reads=[bln], writes=[brs])
            h_t, bh = hn0.next()
            P.dve(lambda e, h_t=h_t, x_t=x_t, i=i: e.tensor_scalar(out=h_t[:], in0=x_t[:], scalar1=rs[:, i:i + 1], scalar2=None, op0=ALU.mult),
                  reads=[bx, brs], writes=[bh])
            p_t, bp = ptr.next()
            for kc in range(8):
                P.pe(lambda e, p_t=p_t, h_t=h_t, kc=kc: e.transpose(p_t[:, kc * 128:(kc + 1) * 128], h_t[:, kc * 128:(kc + 1) * 128], ident[:]),
                     reads=[bh], writes=[bp])
            dst = lambda i=i: FV(hnT, 0, 128, 128 * i, [[S, 8], [1, 128]])
            src = lambda p_t=p_t: FV(p_t, 0, 128, 0, [[128, 8], [1, 128]])
            if i % 2 == 0:
                P.act(lambda e, dst=dst, src=src: e.activation(out=dst(), in_=src(), func=AF.Copy), reads=[bp], writes=[bhn])
            else:
                P.dve(lambda e, dst=dst, src=src: e.tensor_copy(out=dst(), in_=src()), reads=[bp], writes=[bhn])
        if DBG.get("hnT"):
            dump(P, "hnT", hnT, [bhn])
            dump(P, "rs", rs, [bhn])
        P.emit()


def phase_outproj(sync, xsrc, wo_dram, ysc, out):
    nc = sync.nc
    P = Prog(sync).begin()
    with contextlib.ExitStack() as es:
        sfx = uid()

        def sb(name, shape, dt):
            return es.enter_context(nc.sbuf_tensor(name + sfx, shape, dt))
        wst = Rot([sb("o_wst%d" % i, [128, D], F32) for i in range(2)])
        wo = sb("o_wo", [128, 8 * D], BF16)
        bwo = Buf()
        xt = Rot([sb("o_xt%d" % i, [128, D], F32) for i in range(2)])
        ot = Rot([sb("o_ot%d" % i, [128, D], F32) for i in range(2)])
        yt = Rot([sb("o_yt%d" % i, [128, 8 * 512], BF16) for i in range(2)])
        po = Rot([es.enter_context(nc.psum_tensor("o_po%d" % i + sfx, [128, 512], F32)) for i in range(4)])
        s_w = [P.slot() for _ in range(2)]
        s_x = [P.slot() for _ in range(2)]
        s_y = [P.slot() for _ in range(2)]
        s_o = [P.slot() for _ in range(2)]
        for kc in range(8):
            w_t, bw = wst.next()
            P.dma(s_w[kc % 2], lambda e, w_t=w_t, kc=kc: e.dma_start(out=w_t[:], in_=wo_dram[kc * 128:(kc + 1) * 128, :]), writes=[bw])
            if kc % 2 == 0:
                P.dve(lambda e, w_t=w_t, kc=kc: e.tensor_copy(out=wo[:, kc * D:(kc + 1) * D], in_=w_t[:]), reads=[bw], writes=[bwo])
            else:
                P.act(lambda e, w_t=w_t, kc=kc: e.activation(out=wo[:, kc * D:(kc + 1) * D], in_=w_t[:], func=AF.Copy), reads=[bw], writes=[bwo])
        y_t = by = None
        for i in range(32):
            if i % 4 == 0:
                y_t, by = yt.next()
                tb = i // 4
                P.dma(s_y[tb % 2], lambda e, y_t=y_t, tb=tb: e.dma_start(
                    out=FV(y_t, 0, 128, 0, [[512, 8], [1, 512]]),
                    in_=ysc[:, :, tb * 512:(tb + 1) * 512].rearrange("k p t -> p k t")), writes=[by])
            x_t, bx = xt.next()
            P.dma(s_x[i % 2], lambda e, x_t=x_t, i=i: e.dma_start(out=x_t[:], in_=xsrc[128 * i:128 * (i + 1), :]), writes=[bx])
            o_t, bo = ot.next()
            for nb in range(2):
                p_t, bp = po.next()
                for kc in range(8):
                    P.pe(lambda e, p_t=p_t, y_t=y_t, kc=kc, nb=nb, i=i: e.matmul(
                        p_t[:], lhsT=y_t[:, kc * 512 + (i % 4) * 128: kc * 512 + (i % 4) * 128 + 128],
                        rhs=wo[:, kc * D + nb * 512: kc * D + nb * 512 + 512], start=(kc == 0), stop=(kc == 7)),
                        reads=[by, bwo], writes=[bp])
                P.dve(lambda e, p_t=p_t, x_t=x_t, o_t=o_t, nb=nb: e.tensor_tensor(
                    out=o_t[:, nb * 512:(nb + 1) * 512], in0=p_t[:], in1=x_t[:, nb * 512:(nb + 1) * 512], op=ALU.add),
                    reads=[bp, bx], writes=[bo])
            P.dma(s_o[i % 2], lambda e, o_t=o_t, i=i: e.dma_start(out=out[128 * i:128 * (i + 1), :], in_=o_t[:]), reads=[bo])
        P.emit()


def qk_geometry(d):
    L = S // d
    return L, L // 128


def merge_steps(a, b):
    out = []
    na, nb = len(a), len(b)
    if na == 0 or nb == 0:
        return list(a) + list(b)
    ia = ib = 0
    while ia < na or ib < nb:
        if ib >= nb or (ia < na and ia * nb <= ib * na):
            out.append(a[ia]); ia += 1
        else:
            out.append(b[ib]); ib += 1
    return out


def phase_attn(sync, layer, hnT, dr, ysc):
    nc = sync.nc
    P = Prog(sync).begin()
    isA = layer == "A"
    w_dram, wg_dram, bias_dram = dr["w"], dr["wg"], dr["bias"]
    units = [(hp, g) for hp in range(8) for g in ((0, 1, 2) if isA else (0,))]
    with contextlib.ExitStack() as es:
        sfx = uid()

        def sb(name, shape, dt):
            return es.enter_context(nc.sbuf_tensor(name + sfx, shape, dt))
        ngc = sb("a_ngc", [128, 8], F32)
        gq = sb("a_gq", [128, 3], F32)
        gk = sb("a_gk", [128, 3], F32)
        epsc = sb("a_eps", [128, 1], F32)
        blk = sb("a_blk", [128, 128], BF16)
        bconst = Buf()
        s_c = P.slot()
        P.dma(s_c, lambda e: e.dma_start(out=ngc[:], in_=dr["ng"][:, :]), writes=[bconst])
        P.dma(s_c, lambda e: e.dma_start(out=gq[:], in_=dr["gq"][:, :]), writes=[bconst])
        P.dma(s_c, lambda e: e.dma_start(out=gk[:], in_=dr["gk"][:, :]), writes=[bconst])
        P.dve(lambda e: e.memset(epsc[:], EPS), writes=[bconst])
        P.dve(lambda e: e.memset(blk[:], 0.0), reads=[bconst], writes=[bconst])
        P.dve(lambda e: e.memset(blk[0:64, 0:64], 1.0 / 64), reads=[bconst], writes=[bconst])
        P.dve(lambda e: e.memset(blk[64:128, 64:128], 1.0 / 64), reads=[bconst], writes=[bconst])
        P.dve(lambda e: e.tensor_scalar(out=gq[:], in0=gq[:], scalar1=0.125, scalar2=None, op0=ALU.mult),
              reads=[bconst], writes=[bconst])
        qTr = Rot([sb("a_qT%d" % i, [128, S], BF16) for i in range(2)])
        kTr = Rot([sb("a_kT%d" % i, [128, S], BF16) for i in range(2)])
        Vt = sb("a_V", [128, 32 * 192], BF16)
        bV = Buf()
        P.dve(lambda e: e.memset(FV(Vt, 0, 128, 64, [[192, 32], [1, 64]]), 1.0), writes=[bV])
        sgT = sb("a_sgT", [128, S], BF16)
        bsg = Buf()
        if isA:
            acc = [sb("a_acc%d" % h, [128, S], F32) for h in range(2)]
            bacc = [Buf(), Buf()]
        wst = Rot([sb("a_wst%d" % i, [128, 384], F32) for i in range(4)])
        s_w = [P.slot() for _ in range(4)]
        wb = Rot([sb("a_wb%d" % i, [128, 8 * 384], BF16) for i in range(2)])
        wgb = sb("a_wgb", [128, 8 * 128], BF16)
        bwg = Buf()
        ebw = 2 * 256 if isA else dr["ebw"]
        ebst = Rot([sb("a_ebst%d" % i, [128, 512 if isA else 768], F32) for i in range(2)])
        s_eb = [P.slot() for _ in range(2)]
        eb = Rot([sb("a_eb%d" % i, [128, ebw], BF16) for i in range(2)])
        sq = Rot([sb("a_sq%d" % i, [128, 512], BF16) for i in range(2)])
        lnb = Rot([sb("a_ln%d" % i, [128, 512], F32) for i in range(2)])
        rstd = Rot([sb("a_rstd%d" % i, [128, 512], F32) for i in range(2)])
        ew = 512 if isA else 768
        ex = Rot([sb("a_ex%d" % i, [128, ew], BF16) for i in range(4 if isA else 3)])
        pT = Rot([sb("a_pT%d" % i, [128, ew], BF16) for i in range(8 if isA else 7)])
        rec = Rot([sb("a_rec%d" % i, [128, 512], F32) for i in range(1)])
        tmp = Rot([sb("a_tmp%d" % i, [128, 512], F32) for i in range(1)])
        if isA:
            yT = Rot([sb("a_yT%d" % i, [128, 1024], BF16) for i in range(2)])
            s_y = [P.slot() for _ in range(2)]
        else:
            yTB = [Rot([sb("b_yT%d_%d" % (h, i), [128, 1024], BF16) for i in range(2)]) for h in range(2)]
            s_yB = [[P.slot() for _ in range(2)] for h in range(2)]
        banks = [es.enter_context(nc.psum_tensor("a_ps%d" % i + sfx, [128, 512], F32)) for i in range(8)]
        bbank = [Buf() for _ in range(8)]

        def bankrot(ids):
            r = Rot([banks[i] for i in ids])
            r.bufs = [bbank[i] for i in ids]
            return r
        pq = bankrot([0, 1, 7])
        pss = bankrot([2])
        if isA:
            ps_h = [bankrot([3]), bankrot([4])]
            po_h = [bankrot([5]), bankrot([6])]
            pv = bankrot([3, 4])
            pg = bankrot([5, 6])
        else:
            psB = (banks[3], banks[4])
            psBb = (bbank[3], bbank[4])
            po = bankrot([5, 6])
            pv = bankrot([3, 4])
            pg = bankrot([5, 6])

        def wload_steps(u):
            hp, g = u
            w_b, bw = wb.next()
            pend = []
            steps = []

            def conv(w_t, bs, kc):
                P.dve(lambda e: e.tensor_scalar(
                    out=w_b[:, kc * 384:(kc + 1) * 384], in0=w_t[:], scalar1=ngc[:, kc:kc + 1], scalar2=None, op0=ALU.mult),
                    reads=[bs, bconst], writes=[bw])
            for kc in range(8):
                def step(kc=kc):
                    w_t, bs = wst.next()
                    sl = s_w[(wst.i - 1) % 4]
                    P.dma(sl, lambda e: e.dma_start(out=w_t[:], in_=w_dram[hp, g, kc * 128:(kc + 1) * 128, :]), writes=[bs])
                    pend.append((w_t, bs, kc))
                    if len(pend) > 2:
                        conv(*pend.pop(0))
                steps.append(step)

            def flush():
                while pend:
                    conv(*pend.pop(0))
            steps.append(flush)
            return (w_b, bw), steps

        def gload_steps(hp):
            pend = []
            steps = []

            def conv(w_t, bs, kc):
                P.dve(lambda e: e.tensor_scalar(
                    out=wgb[:, kc * 128:(kc + 1) * 128], in0=w_t[:, 0:128], scalar1=ngc[:, kc:kc + 1], scalar2=None, op0=ALU.mult),
                    reads=[bs, bconst], writes=[bwg])
            for kc in range(8):
                def step(kc=kc):
                    w_t, bs = wst.next()
                    sl = s_w[(wst.i - 1) % 4]
                    P.dma(sl, lambda e: e.dma_start(out=w_t[:, 0:128], in_=wg_dram[hp, kc * 128:(kc + 1) * 128, :]), writes=[bs])
                    pend.append((w_t, bs, kc))
                    if len(pend) > 2:
                        conv(*pend.pop(0))
                steps.append(step)

            def flush():
                while pend:
                    conv(*pend.pop(0))
            steps.append(flush)
            return steps

        def eb_dma_A(u):
            hp, g = u
            st_, bs = ebst.next()
            sl = s_eb[(ebst.i - 1) % 2]
            P.dma(sl, lambda e: e.dma_start(out=st_[:, 0:512], in_=bias_dram[hp, g, :, :]), writes=[bs])
            return st_, bs

        def eb_conv_A(st_, bs):
            e_t, be = eb.next()
            P.act(lambda e: e.activation(out=e_t[:, 0:512], in_=st_[:, 0:512], func=AF.Exp), reads=[bs], writes=[be])
            return e_t, be

        def load_eb_B(hp, h):
            e_t, be = eb.next()
            ebw1 = dr["ebw"]
            off = 0
            while off < ebw1:
                n = min(768, ebw1 - off)
                st_, bs = ebst.next()
                sl = s_eb[(ebst.i - 1) % 2]
                P.dma(sl, lambda e, st_=st_, off=off, n=n: e.dma_start(out=st_[:, 0:n], in_=bias_dram[2 * hp + h, :, off:off + n]), writes=[bs])
                P.act(lambda e, st_=st_, e_t=e_t, off=off, n=n: e.activation(
                    out=e_t[:, off: off + n], in_=st_[:, 0:n], func=AF.Exp), reads=[bs], writes=[be])
                off += n
            return e_t, be

        def gate_steps():
            steps = []
            for tb in range(8):
                def step(tb=tb):
                    p_t, bp = pg.next()
                    for kc in range(8):
                        P.pe(lambda e, p_t=p_t, kc=kc: e.matmul(
                            p_t[:], lhsT=wgb[:, kc * 128:(kc + 1) * 128], rhs=hnT[:, kc * S + tb * 512: kc * S + tb * 512 + 512],
                            start=(kc == 0), stop=(kc == 7)), reads=[bwg], writes=[bp])
                    P.act(lambda e, p_t=p_t: e.activation(out=sgT[:, tb * 512:(tb + 1) * 512], in_=p_t[:], func=AF.Silu),
                          reads=[bp], writes=[bsg])
                steps.append(step)
            return steps

        def qk_steps(u, w_b, bw, q_t, bq, k_t, bk):
            hp, g = u
            d = DILS[g] if isA else 1
            L = S // d
            pend = []

            def tail(p_t, bp, s_t, bs, which, tb):
                ss_t, bss = pss.next()
                P.pe(lambda e: e.matmul(ss_t[:], lhsT=blk[:], rhs=s_t[:], start=True, stop=True),
                     reads=[bs, bconst], writes=[bss])
                l_t, bl = lnb.next()
                P.act(lambda e: e.activation(out=l_t[:], in_=ss_t[:], func=AF.Ln, bias=epsc[:], scale=1.0),
                      reads=[bss, bconst], writes=[bl])
                r_t, br = rstd.next()
                P.act(lambda e: e.activation(out=r_t[:], in_=l_t[:], func=AF.Exp, scale=-0.5), reads=[bl], writes=[br])
                if which == 0:
                    n = 512 // d
                    P.dve(lambda e: e.scalar_tensor_tensor(
                        out=FV(q_t, 0, 128, tb * n, [[L, d], [1, n]]),
                        in0=FV(p_t, 0, 128, 0, [[1, d], [d, n]]), scalar=gq[:, g:g + 1],
                        in1=FV(r_t, 0, 128, 0, [[1, d], [d, n]]), op0=ALU.mult, op1=ALU.mult),
                        reads=[bp, br, bconst], writes=[bq])
                else:
                    P.dve(lambda e: e.scalar_tensor_tensor(
                        out=k_t[:, tb * 512:(tb + 1) * 512], in0=p_t[:], scalar=gk[:, g:g + 1], in1=r_t[:],
                        op0=ALU.mult, op1=ALU.mult), reads=[bp, br, bconst], writes=[bk])

            steps = []
            for t in range(16):
                def step(t=t):
                    which, tb = divmod(t, 8)
                    p_t, bp = pq.next()
                    for kc in range(8):
                        P.pe(lambda e, kc=kc: e.matmul(
                            p_t[:], lhsT=w_b[:, kc * 384 + which * 128: kc * 384 + which * 128 + 128],
                            rhs=hnT[:, kc * S + tb * 512: kc * S + tb * 512 + 512], start=(kc == 0), stop=(kc == 7)),
                            reads=[bw], writes=[bp])
                    s_t, bs = sq.next()
                    P.act(lambda e: e.activation(out=s_t[:], in_=p_t[:], func=AF.Square), reads=[bp], writes=[bs])
                    if pend:
                        tail(*pend.pop())
                    pend.append((p_t, bp, s_t, bs, which, tb))
                steps.append(step)

            def flush():
                tail(*pend.pop())
            steps.append(flush)
            return steps

        def v_steps(u, w_b, bw):
            hp, g = u
            d = DILS[g] if isA else 1
            L, nC = qk_geometry(d)
            steps = []
            for c0 in range(0, 32, 4):
                def step(c0=c0):
                    p_t, bp = pv.next()
                    for cc in range(4):
                        c = c0 + cc
                        r, i = divmod(c, nC)
                        t0 = r + d * 128 * i
                        for kc in range(8):
                            P.pe(lambda e, kc=kc, cc=cc, t0=t0: e.matmul(
                                p_t[:, cc * 128:(cc + 1) * 128],
                                lhsT=FV(hnT, 0, 128, kc * S + t0, [[d, 128]]),
                                rhs=w_b[:, kc * 384 + 256: kc * 384 + 384], start=(kc == 0), stop=(kc == 7)),
                                reads=[bw], writes=[bp])
                    P.act(lambda e: e.activation(
                        out=FV(Vt, 0, 128, c0 * 192, [[192, 4], [128, 2], [1, 64]]),
                        in_=FV(p_t, 0, 128, 0, [[128, 4], [64, 2], [1, 64]]), func=AF.Copy), reads=[bp], writes=[bV])
                steps.append(step)
            return steps

        def attn_steps_A(u, q_t, bq, k_t, bk, e_t, be):
            hp, g = u
            d = DILS[g]
            first = g == 0
            L, nC = qk_geometry(d)
            pend = []
            steps = []

            def acc_out(h, srcf, sbuf, dstf):
                if first:
                    P.dve(lambda e: e.tensor_copy(out=dstf(), in_=srcf()), reads=[sbuf], writes=[bacc[h]])
                else:
                    P.dve(lambda e: e.tensor_tensor(out=dstf(), in0=srcf(), in1=dstf(), op=ALU.add), reads=[sbuf], writes=[bacc[h]])

            def make_tail(r, m, ptl, state):
                def tail():
                    for h in range(2):
                        vof = 0 if h == 0 else 64
                        acc_h = acc[h]
                        key = "po%d" % h

                        def pcol(i, lo):
                            p_t, bp = ptl[(h, i // 2)]
                            return p_t, bp, (i % 2) * 256 + lo

                        def mm(o_t, bo, col, n, chunk, pa, bpa, ca, start, stop, vof=vof):
                            P.pe(lambda e: e.matmul(
                                o_t[:, col:col + n], lhsT=Vt[:, chunk * 192 + vof: chunk * 192 + vof + 128],
                                rhs=pa[:, ca:ca + n], start=start, stop=stop), reads=[bV, bpa], writes=[bo])
                        if d == 16:
                            if r % 2 == 0:
                                state[key] = po_h[h].next()
                            o_t, bo = state[key]
                            base = (r % 2) * 256
                            c1 = r * nC
                            pa, bpa, ca = pcol(0, 64)
                            mm(o_t, bo, base, 64, c1, pa, bpa, ca, True, True)
                            pa, bpa, ca = pcol(0, 128)
                            mm(o_t, bo, base + 64, 128, c1, pa, bpa, ca, True, False)
                            pa, bpa, ca = pcol(1, 0)
                            mm(o_t, bo, base + 64, 128, c1 + 1, pa, bpa, ca, False, True)
                            pa, bpa, ca = pcol(1, 128)
                            mm(o_t, bo, base + 192, 64, c1 + 1, pa, bpa, ca, True, True)
                            if r % 2 == 1:
                                acc_out(h, lambda o_t=o_t: o_t[:, 0:512], bo,
                                        lambda acc_h=acc_h: FV(acc_h, 0, 128, r - 1, [[1, 2], [16, 256]]))
                            continue
                        if m == 0:
                            state[key] = po_h[h].next()
                            o_t, bo = state[key]
                            pa, bpa, ca = pcol(0, 64)
                            mm(o_t, bo, 64, 64, r * nC, pa, bpa, ca, True, True)
                        for jj in ([2 * m - 1] if m > 0 else []) + ([2 * m] if 2 * m <= nC - 2 else []):
                            if jj < 3:
                                slot = jj + 1
                            else:
                                slot = (jj - 3) % 4
                                if slot == 0:
                                    state[key] = po_h[h].next()
                            o_t, bo = state[key]
                            pa, bpa, ca = pcol(jj, 128)
                            pb_, bpb, cb = pcol(jj + 1, 0)
                            c1 = r * nC + jj
                            mm(o_t, bo, slot * 128, 128, c1, pa, bpa, ca, True, False)
                            mm(o_t, bo, slot * 128, 128, c1 + 1, pb_, bpb, cb, False, True)
                            if jj == 2:
                                acc_out(h, lambda o_t=o_t: o_t[:, 64:512], bo,
                                        lambda acc_h=acc_h: FV(acc_h, 0, 128, r, [[d, 448]]))
                            elif jj > 2 and slot == 3:
                                m0 = 64 + 128 * (jj - 3)
                                acc_out(h, lambda o_t=o_t: o_t[:, 0:512], bo,
                                        lambda acc_h=acc_h, m0=m0: FV(acc_h, 0, 128, r + d * m0, [[d, 512]]))
                        if m == nC // 2 - 1:
                            assert (nC - 2 - 3) % 4 == 3
                            o_t, bo = po_h[h].next()
                            pa, bpa, ca = pcol(nC - 1, 128)
                            mm(o_t, bo, 0, 64, r * nC + nC - 1, pa, bpa, ca, True, True)
                            acc_out(h, lambda o_t=o_t: o_t[:, 0:64], bo,
                                    lambda acc_h=acc_h: FV(acc_h, 0, 128, r + d * (L - 64), [[d, 64]]))
                return tail

            state = {}
            for r in range(d):
                ptl = {}
                for m in range(nC // 2):
                    def step(r=r, m=m, ptl=ptl, state=state):
                        stl = [ps_h[0].next(), ps_h[1].next()]
                        for cc in range(2):
                            i = 2 * m + cc
                            qlo = max(0, 128 * i - 64)
                            qhi = min(L, 128 * i + 192)
                            lo = qlo - (128 * i - 64)
                            n = qhi - qlo
                            for h in range(2):
                                s_t, bs = stl[h]
                                P.pe(lambda e, s_t=s_t, cc=cc, lo=lo, n=n, qlo=qlo, i=i, h=h: e.matmul(
                                    s_t[:, cc * 256 + lo: cc * 256 + lo + n],
                                    lhsT=FV(k_t, 64 * h, 64 * h + 64, r + d * 128 * i, [[d, 128]]),
                                    rhs=q_t[64 * h:64 * h + 64, r * L + qlo: r * L + qlo + n], start=True, stop=True),
                                    reads=[bk, bq], writes=[bs])
                        for h in range(2):
                            s_t, bs = stl[h]
                            x_t, bx = ex.next()
                            P.act(lambda e, x_t=x_t, s_t=s_t: e.activation(out=x_t[:, 0:512], in_=s_t[:], func=AF.Exp), reads=[bs], writes=[bx])
                            p_t, bp = pT.next()
                            P.dve(lambda e, p_t=p_t, x_t=x_t, h=h: e.tensor_tensor(
                                out=FV(p_t, 0, 128, 0, [[256, 2], [1, 256]]), in0=FV(x_t, 0, 128, 0, [[256, 2], [1, 256]]),
                                in1=FV(e_t, 0, 128, h * 256, [[0, 2], [1, 256]]), op=ALU.mult), reads=[bx, be], writes=[bp])
                            ptl[(h, m)] = (p_t, bp)
                        if pend:
                            pend.pop()()
                        pend.append(make_tail(r, m, ptl, state))
                    steps.append(step)

            def flush():
                pend.pop()()
            steps.append(flush)
            return steps

        def normalize_steps_A(hp):
            steps = []
            st = {}
            for tb in range(8):
                def step(tb=tb):
                    q4, hb = divmod(tb, 2)
                    if hb == 0:
                        st["y"] = yT.next()
                    y_t, by = st["y"]
                    cs = slice(tb * 512, (tb + 1) * 512)
                    r_t, br = rec.next()
                    t_t, bt = tmp.next()
                    P.dve(lambda e: e.reciprocal(out=r_t[0:64, :], in_=acc[0][64:128, cs]), reads=[bacc[0]], writes=[br])
                    P.dve(lambda e: e.reciprocal(out=r_t[64:128, :], in_=acc[1][0:64, cs]), reads=[bacc[1]], writes=[br])
                    P.dve(lambda e: e.tensor_tensor(out=t_t[0:64, :], in0=acc[0][0:64, cs], in1=r_t[0:64, :], op=ALU.mult),
                          reads=[br], writes=[bt])
                    P.dve(lambda e: e.tensor_tensor(out=t_t[64:128, :], in0=acc[1][64:128, cs], in1=r_t[64:128, :], op=ALU.mult),
                          reads=[br], writes=[bt])
                    P.dve(lambda e: e.tensor_tensor(out=y_t[:, hb * 512:(hb + 1) * 512], in0=t_t[:], in1=sgT[:, cs], op=ALU.mult),
                          reads=[bt, bsg], writes=[by])
                    if hb == 1:
                        sl = s_y[(yT.i - 1) % 2]
                        P.dma(sl, lambda e: e.dma_start(out=ysc[hp, :, q4 * 1024:(q4 + 1) * 1024], in_=y_t[:]), reads=[by])
                steps.append(step)
            return steps

        def attn_steps_B(hp, h, q_t, bq, k_t, bk, e_t, be):
            tiles = dr["tiles"]
            hs = slice(64 * h, 64 * h + 64)
            vof = 0 if h == 0 else 64
            num = slice(0, 64) if h == 0 else slice(64, 128)
            den = slice(64, 128) if h == 0 else slice(0, 64)
            contribs = []
            for Q in range(32):
                full, part = [], []
                for R in range(32):
                    qlo, nr = tiles[R][1], tiles[R][2]
                    lo_r = max(qlo, 2 * Q)
                    hi_r = min(qlo + nr - 1, 2 * Q + 1)
                    if lo_r > hi_r:
                        continue
                    (full if hi_r - lo_r == 1 else part).append((R, lo_r, hi_r - lo_r + 1))
                assert full
                contribs.append(full + part)
            lastR = [max(R for R, _, _ in contribs[Q]) for Q in range(32)]
            ptl = {}
            st = dict(o=None, y=None)
            pend = []
            steps = []

            def do_block(Q):
                if Q % 4 == 0:
                    st["o"] = po.next()
                o_t, bo = st["o"]
                cl = contribs[Q]
                for n_, (R, row0, nrow) in enumerate(cl):
                    p_t, bp = ptl[R]
                    c0 = (row0 - tiles[R][1]) * 64
                    oc = (Q % 4) * 128 + (row0 - 2 * Q) * 64
                    nn = nrow * 64
                    P.pe(lambda e, p_t=p_t, c0=c0, oc=oc, nn=nn, R=R, first=(n_ == 0), last=(n_ == len(cl) - 1): e.matmul(
                        o_t[:, oc:oc + nn], lhsT=Vt[:, R * 192 + vof: R * 192 + vof + 128],
                        rhs=p_t[:, c0:c0 + nn], start=first, stop=last), reads=[bV, bp], writes=[bo])
                if Q % 4 == 3:
                    tb = Q // 4
                    cs = slice(tb * 512, (tb + 1) * 512)
                    if tb % 2 == 0:
                        st["y"] = yTB[h].next()
                    y_t, by = st["y"]
                    r_t, br = rec.next()
                    t_t, bt = tmp.next()
                    P.dve(lambda e: e.reciprocal(out=r_t[num, :], in_=o_t[den, :]), reads=[bo], writes=[br])
                    P.dve(lambda e: e.tensor_tensor(out=t_t[num, :], in0=o_t[num, :], in1=r_t[num, :], op=ALU.mult),
                          reads=[bo, br], writes=[bt])
                    P.dve(lambda e: e.tensor_tensor(
                        out=y_t[num, (tb % 2) * 512:(tb % 2) * 512 + 512], in0=t_t[num, :], in1=sgT[num, cs], op=ALU.mult),
                        reads=[bt, bsg], writes=[by])
                    if tb % 2 == 1:
                        q4 = tb // 2
                        sl = s_yB[h][(yTB[h].i - 1) % 2]
                        P.dma(sl, lambda e: e.dma_start(
                            out=ysc[hp, num, q4 * 1024:(q4 + 1) * 1024], in_=y_t[num, :]), reads=[by])

            for R in range(32):
                def step(R=R):
                    toff, qlo, nr = tiles[R]
                    n = nr * 64
                    sA, sB = psB
                    bA, bB = psBb
                    n1 = min(n, 512)
                    P.pe(lambda e: e.matmul(
                        sA[:, 0:n1], lhsT=k_t[hs, R * 128:(R + 1) * 128], rhs=q_t[hs, qlo * 64: qlo * 64 + n1], start=True, stop=True),
                        reads=[bk, bq], writes=[bA])
                    x_t, bx = ex.next()
                    P.act(lambda e: e.activation(out=x_t[:, 0:n1], in_=sA[:, 0:n1], func=AF.Exp), reads=[bA], writes=[bx])
                    if n > 512:
                        n2 = n - 512
                        P.pe(lambda e: e.matmul(
                            sB[:, 0:n2], lhsT=k_t[hs, R * 128:(R + 1) * 128], rhs=q_t[hs, qlo * 64 + 512: qlo * 64 + 512 + n2], start=True, stop=True),
                            reads=[bk, bq], writes=[bB])
                        P.act(lambda e: e.activation(out=x_t[:, 512:512 + n2], in_=sB[:, 0:n2], func=AF.Exp), reads=[bB], writes=[bx])
                    p_t, bp = pT.next()
                    P.dve(lambda e: e.tensor_tensor(
                        out=p_t[:, 0:n], in0=x_t[:, 0:n], in1=e_t[:, toff: toff + n], op=ALU.mult),
                        reads=[bx, be], writes=[bp])
                    ptl[R] = (p_t, bp)
                    if pend:
                        pend.pop()()

                    def tail():
                        for Q in range(32):
                            if lastR[Q] == R:
                                do_block(Q)
                    pend.append(tail)
                steps.append(step)

            def flush():
                pend.pop()()
            steps.append(flush)
            return steps

        def run(steps):
            for st_ in steps:
                st_()

        dstop = DBG.get("stop")
        nU = len(units)
        wts = {}
        qk = {}
        ebd = {}
        wts[0], ws = wload_steps(units[0])
        run(ws)
        run(gload_steps(0))
        if isA:
            ebd[0] = eb_dma_A(units[0])
        if nU > 1:
            wts[1], ws1 = wload_steps(units[1])
        else:
            ws1 = []
        qk[0] = qTr.next() + kTr.next()
        run(merge_steps(qk_steps(units[0], *wts[0], *qk[0]), ws1))
        run(v_steps(units[0], *wts[0]))
        run(gate_steps())
        for ui, u in enumerate(units):
            hp, g = u
            nxt = units[ui + 1] if ui + 1 < nU else None
            wsteps = []
            if ui + 2 < nU:
                wts[ui + 2], wsteps = wload_steps(units[ui + 2])
            q_t, bq, k_t, bk = qk[ui]
            nsteps = []
            if nxt is not None:
                qk[ui + 1] = qTr.next() + kTr.next()
                nsteps = qk_steps(nxt, *wts[ui + 1], *qk[ui + 1])
            if isA:
                if nxt is not None:
                    ebd[ui + 1] = eb_dma_A(nxt)
                e_t, be = eb_conv_A(*ebd[ui])
                last = g == 2
                gsteps = gload_steps(hp + 1) if (g == 1 and hp + 1 < 8) else []
                run(merge_steps(merge_steps(attn_steps_A(u, q_t, bq, k_t, bk, e_t, be), nsteps), wsteps + gsteps))
                if last:
                    vs = v_steps(nxt, *wts[ui + 1]) if nxt is not None else []
                    run(merge_steps(normalize_steps_A(hp), vs))
                    if hp + 1 < 8:
                        run(gate_steps())
                elif nxt is not None:
                    run(v_steps(nxt, *wts[ui + 1]))
            else:
                gsteps = gload_steps(hp + 1) if hp + 1 < 8 else []
                e0 = load_eb_B(hp, 0)
                a0 = attn_steps_B(hp, 0, q_t, bq, k_t, bk, *e0)
                run(merge_steps(merge_steps(a0, nsteps[: len(nsteps) // 2]), wsteps))
                e1 = load_eb_B(hp, 1)
                a1 = attn_steps_B(hp, 1, q_t, bq, k_t, bk, *e1)
                run(merge_steps(merge_steps(a1, nsteps[len(nsteps) // 2:]), gsteps))
                if nxt is not None:
                    run(v_steps(nxt, *wts[ui + 1]))
                if hp + 1 < 8:
                    run(gate_steps())
            if dstop == "hp0" and ((isA and g == 2) or not isA):
                break
        P.emit()


_T5_LUT = None


def _t5_bucket_np(rel):
    import math
    half, me = 16, 8
    ret = np.where(rel > 0, half, 0)
    n = np.abs(rel)
    nf = np.maximum(n, 1).astype(np.float32)
    large = me + (np.log(nf / np.float32(me)) / np.float32(math.log(1024 / me)) * np.float32(half - me)).astype(np.int32)
    large = np.minimum(large, half - 1)
    return ret + np.where(n < me, n, large)


def _bias_tiles_A(t5_bias):
    a = np.arange(128)[:, None]
    b = np.arange(256)[None, :]
    rel = a - b + 64
    valid = (b - a >= 0) & (b - a <= 128)
    out = np.empty((8, 3, 128, 2, 256), np.float32)
    for g, d in enumerate(DILS):
        idx = _t5_bucket_np(rel * d)
        for h in range(16):
            t = t5_bias[g * 16 + h][idx]
            out[h // 2, g, :, h % 2, :] = np.where(valid, t, np.float32(NEG))
    return out


def _geom_B():
    rows = 64
    r = np.arange(rows)
    rs = np.clip(r - 4, 0, rows - 8)
    c = np.arange(64)
    cs = np.clip(c - 8, 0, 64 - 16)
    tiles = []
    uniq = {}
    maps = []
    off = 0
    for R in range(32):
        krs = np.array([2 * R, 2 * R + 1])
        qrows = [q for q in range(rows) if (rs[q] <= krs[1]) and (rs[q] + 7 >= krs[0])]
        qlo, nr = qrows[0], len(qrows)
        assert qrows == list(range(qlo, qlo + nr))
        kr = np.repeat(krs, 64)[:, None]
        kc = np.tile(c, 2)[:, None]
        qr = np.repeat(np.arange(qlo, qlo + nr), 64)[None, :]
        qc = np.tile(c, nr)[None, :]
        valid = (kr >= rs[qr]) & (kr <= rs[qr] + 7) & (kc >= cs[qc]) & (kc < cs[qc] + 16)
        ridx = np.clip(kr - qr + 7, 0, 14)
        cidx = np.clip(kc - qc, -15, 15) + 15
        key = (nr, valid.tobytes(), ridx.tobytes())
        if key not in uniq:
            uniq[key] = off
            maps.append((off, ridx + 0 * cidx, cidx + 0 * ridx, valid))
            off += nr * 64
        tiles.append((uniq[key], qlo, nr))
    return tiles, maps, off


def _bias_tiles_B(rpb, maps, ebw):
    out = np.empty((16, 128, ebw), np.float32)
    for off, ridx, cidx, valid in maps:
        n = valid.shape[1]
        for h in range(16):
            out[h, :, off:off + n] = np.where(valid, rpb[h][ridx, cidx], np.float32(NEG))
    return out


def _unit_weights(w_in, ngroups):
    wu = np.empty((8, ngroups, D, 384), np.float32)
    for hp in range(8):
        for g in range(ngroups):
            for j in range(3):
                c0 = g * 3072 + j * 1024 + hp * 128
                wu[hp, g, :, j * 128:(j + 1) * 128] = w_in[:, c0:c0 + 128]
    gc = ngroups * 3072
    wg = np.ascontiguousarray(w_in[:, gc:gc + 1024].reshape(D, 8, 128).transpose(1, 0, 2))
    return wu, wg


_GEOM_B = None


def build_nc(layers="AB"):
    global _GEOM_B
    if _GEOM_B is None:
        _GEOM_B = _geom_B()
    tilesB, mapsB, ebwB = _GEOM_B
    nc = bass.Bass("TRN2", target_bir_lowering=False)

    def din(name, shape, dt=F32):
        return nc.dram_tensor(name, list(shape), dt, kind="ExternalInput").ap()
    x = din("x", [S, D])
    ident_d = din("ident_d", [128, 128])
    drA = dict(w=din("wA", [8, 3, D, 384]), wg=din("wgA", [8, D, 128]), bias=din("biasA", [8, 3, 128, 512]),
               ng=din("ngA", [128, 8]), gq=din("gqA", [128, 3]), gk=din("gkA", [128, 3]))
    woA = din("woA", [D, D])
    drB = dict(w=din("wB", [8, 1, D, 384]), wg=din("wgB", [8, D, 128]), bias=din("biasB", [16, 128, ebwB]),
               ng=din("ngB", [128, 8]), gq=din("gqB", [128, 3]), gk=din("gkB", [128, 3]), tiles=tilesB, ebw=ebwB)
    woB = din("woB", [D, D])
    out = nc.dram_tensor("out", [S, D], F32, kind="ExternalOutput").ap()
    ysc = nc.dram_tensor("ysc", [8, 128, S], BF16).ap()
    with contextlib.ExitStack() as es:
        sync = Sync(nc, es)
        hnT = es.enter_context(nc.sbuf_tensor("hnT", [128, 8 * S], BF16))
        ident = es.enter_context(nc.sbuf_tensor("ident", [128, 128], BF16))
        identf = es.enter_context(nc.sbuf_tensor("identf", [128, 128], F32))
        P0 = Prog(sync).begin()
        bi = Buf()
        P0.dma(P0.slot(), lambda e: e.dma_start(out=identf[:], in_=ident_d[:, :]), writes=[bi])
        P0.dve(lambda e: e.tensor_copy(out=ident[:], in_=identf[:]), reads=[bi], writes=[bi])
        P0.emit()
        src = x
        if "A" in layers:
            phase_norm(sync, es, src, hnT, ident)
            if DBG.get("stop") != "norm":
                phase_attn(sync, "A", hnT, drA, ysc)
            if not DBG.get("stop"):
                phase_outproj(sync, src, woA, ysc, out)
            src = out
        if "B" in layers:
            phase_norm(sync, es, src, hnT, ident)
            phase_attn(sync, "B", hnT, drB, ysc)
            phase_outproj(sync, src, woB, ysc, out)
    return nc


def host_inputs(norm_gain, a_w_in, a_w_out, a_q_gain, a_k_gain, t5_bias, b_w_in, b_w_out, b_q_gain, b_k_gain, b_rpb):
    global _GEOM_B
    if _GEOM_B is None:
        _GEOM_B = _geom_B()
    tilesB, mapsB, ebwB = _GEOM_B
    f = lambda a: np.ascontiguousarray(np.asarray(a, dtype=np.float32))
    wA, wgA = _unit_weights(f(a_w_in)[0], 3)
    wB, wgB = _unit_weights(f(b_w_in)[0], 1)
    ng = f(norm_gain)

    def gcol(gn):
        gn = f(gn).reshape(-1, 64)
        o = np.ones((128, 3), np.float32)
        for g in range(gn.shape[0]):
            o[:, g] = np.tile(gn[g], 2)
        return o
    shared = dict(
        ident_d=np.eye(128, dtype=np.float32),
        wA=wA, wgA=wgA, woA=f(a_w_out)[0],
        biasA=np.ascontiguousarray(_bias_tiles_A(f(t5_bias)).reshape(8, 3, 128, 512)),
        ngA=np.ascontiguousarray(ng[0].reshape(8, 128).T), gqA=gcol(a_q_gain[0]), gkA=gcol(a_k_gain[0]),
        wB=wB, wgB=wgB, woB=f(b_w_out)[0],
        biasB=_bias_tiles_B(f(b_rpb)[0], mapsB, ebwB),
        ngB=np.ascontiguousarray(ng[1].reshape(8, 128).T), gqB=gcol(b_q_gain), gkB=gcol(b_k_gain),
    )
    return shared


def kernel(x, norm_gain, a_w_in, a_w_out, a_q_gain, a_k_gain, t5_bias, b_w_in, b_w_out, b_q_gain, b_k_gain, b_rpb):
    x = np.ascontiguousarray(np.asarray(x, dtype=np.float32))
    shared = host_inputs(norm_gain, a_w_in, a_w_out, a_q_gain, a_k_gain, t5_bias, b_w_in, b_w_out, b_q_gain, b_k_gain, b_rpb)
    nc = build_nc("AB")
    in_maps = [dict(shared, x=x[c]) for c in range(NCORES)]
    res = run_bass_kernel_spmd(nc, in_maps, core_ids=list(range(NCORES)))
    return np.stack([np.asarray(r["out"], dtype=np.float32) for r in res.results], axis=0)
```

```python
import contextlib
import numpy as np
import concourse.bass as bass
import concourse.mybir as mybir
from concourse.bass_utils import run_bass_kernel_spmd

F32 = mybir.dt.float32
BF16 = mybir.dt.bfloat16
AF = mybir.ActivationFunctionType
ALU = mybir.AluOpType

S = 4096
D = 1024
NCORES = 8
DILS = (1, 4, 16)
EPS = 1e-6
NEG = -30000.0


class Buf:
    __slots__ = ("name", "lw", "rd")

    def __init__(self, name=""):
        self.name = name
        self.lw = None
        self.rd = []


class DmaSlot:
    __slots__ = ("sem", "count", "name")

    def __init__(self, name):
        self.name = name
        self.sem = None
        self.count = 0


class Op:
    __slots__ = ("eng", "fn", "deps", "slot", "signal", "tick", "semkey", "known", "idx")


COMPUTE = ("pe", "act", "dve", "pool")
ENGS = ("pe", "act", "dve", "pool", "sp")


class Sync:
    def __init__(self, nc, es, nslots=40):
        self.nc = nc
        self.esem = {e: es.enter_context(nc.semaphore("sem_" + e)) for e in COMPUTE}
        self.tick = {e: 0 for e in COMPUTE}
        self.slots = []
        for i in range(nslots):
            s = DmaSlot("dq%d" % i)
            s.sem = es.enter_context(nc.semaphore(s.name))
            self.slots.append(s)


class Prog:
    def __init__(self, sync):
        self.sync = sync
        self.nc = sync.nc
        self.ops = []
        self.nslot = 0

    def slot(self, name=""):
        s = self.sync.slots[self.nslot]
        self.nslot += 1
        return s

    def add(self, eng, fn, reads=(), writes=(), slot=None):
        op = Op()
        op.eng = eng
        op.fn = fn
        op.slot = slot
        op.signal = slot is not None
        op.tick = None
        op.idx = len(self.ops)
        deps = {}
        is_dma = slot is not None
        for b in reads:
            w = b.lw
            if w is not None:
                if is_dma or w.slot is not None or w.eng != eng or eng != "pe":
                    deps[w.idx] = w
        for b in writes:
            w = b.lw
            if w is not None and (is_dma or w.slot is not None or w.eng != eng):
                deps[w.idx] = w
            for r in b.rd:
                if is_dma or r.slot is not None or r.eng != eng:
                    deps[r.idx] = r
        for b in reads:
            b.rd.append(op)
        for b in writes:
            b.lw = op
            b.rd = []
        op.deps = list(deps.values())
        for d in op.deps:
            d.signal = True
        self.ops.append(op)
        return op

    def pe(self, fn, reads=(), writes=()):
        return self.add("pe", fn, reads, writes)

    def act(self, fn, reads=(), writes=()):
        return self.add("act", fn, reads, writes)

    def dve(self, fn, reads=(), writes=()):
        return self.add("dve", fn, reads, writes)

    def pool(self, fn, reads=(), writes=()):
        return self.add("pool", fn, reads, writes)

    def dma(self, slot, fn, reads=(), writes=()):
        return self.add("sp", fn, reads, writes, slot=slot)

    def emit(self):
        nc = self.nc
        sy = self.sync
        for op in self.ops:
            if op.slot is not None:
                op.slot.count += 16
                op.tick = op.slot.count
                op.semkey = op.slot
            elif op.signal:
                sy.tick[op.eng] += 1
                op.tick = sy.tick[op.eng]
                op.semkey = op.eng
        base = {}
        for e in COMPUTE:
            base[e] = 0
        clock = {e: {} for e in ENGS}
        start_tick = dict(self._start_tick)
        start_slot = dict(self._start_slot)
        for e in ENGS:
            for k, v in start_tick.items():
                clock[e][k] = v
            for k, v in start_slot.items():
                clock[e][k] = v
        plan = {e: [] for e in ENGS}
        for op in self.ops:
            ck = clock[op.eng]
            need = {}
            for d in op.deps:
                if ck.get(d.semkey, 0) >= d.tick:
                    continue
                if need.get(d.semkey, 0) < d.tick:
                    need[d.semkey] = d.tick
            for d in op.deps:
                for k, v in d.known.items():
                    if ck.get(k, 0) < v:
                        ck[k] = v
            waits = list(need.items())
            for k, v in waits:
                if ck.get(k, 0) < v:
                    ck[k] = v
            if op.tick is not None:
                kn = dict(ck)
                if kn.get(op.semkey, 0) < op.tick:
                    kn[op.semkey] = op.tick
                op.known = kn
            plan[op.eng].append((op, waits))
        final_waits = [(s, s.count) for s in sy.slots[: self.nslot] if s.count > start_slot.get(s, 0)]
        esem = sy.esem

        def semof(k):
            return k.sem if isinstance(k, DmaSlot) else esem[k]

        def run(engname):
            def body(eng):
                for op, waits in plan[engname]:
                    for k, v in waits:
                        eng.wait_ge(semof(k), v)
                    ins = op.fn(eng)
                    if op.slot is not None:
                        ins.then_inc(op.slot.sem, 16)
                    elif op.tick is not None:
                        ins.then_inc(esem[engname], 1)
                if engname == "sp":
                    for s, v in final_waits:
                        eng.wait_ge(s.sem, v)
            return body

        with nc.Block() as block:
            block.tensor(run("pe"))
            block.scalar(run("act"))
            block.vector(run("dve"))
            block.gpsimd(run("pool"))
            block.sync(run("sp"))

    def begin(self):
        sy = self.sync
        self._start_tick = dict(sy.tick)
        self._start_slot = {s: s.count for s in sy.slots}
        return self


_UID = [0]


def uid():
    _UID[0] += 1
    return "_u%d" % _UID[0]


def fview(ap, dims):
    return bass.AP(tensor=ap.tensor, offset=ap.offset, ap=[list(ap.ap[0])] + [list(d) for d in dims])


def FV(t, p0, p1, off, dims):
    return fview(t[p0:p1, off:off + 1], dims)


class Rot:
    def __init__(self, items):
        self.items = items
        self.bufs = [Buf() for _ in items]
        self.i = 0

    def next(self):
        k = self.i % len(self.items)
        self.i += 1
        return self.items[k], self.bufs[k]


DBG = {}


def dump(P, name, t, bufs, dt=None):
    nc = P.nc
    shape = list(t.shape)
    d = nc.dram_tensor("dbg_" + name, shape, dt or t.dtype, kind="ExternalOutput").ap()
    P.dma(P.slot(), lambda e: e.dma_start(out=d[:, :], in_=t[:, :]), reads=bufs)


def phase_norm(sync, es_outer, xsrc, hnT, ident):
    nc = sync.nc
    P = Prog(sync).begin()
    with contextlib.ExitStack() as es:
        sfx = uid()

        def sb(name, shape, dt):
            return es.enter_context(nc.sbuf_tensor(name + sfx, shape, dt))
        xt = Rot([sb("n_xt%d" % i, [128, D], F32) for i in range(8)])
        junk = sb("n_junk", [128, D], BF16)
        hn0 = Rot([sb("n_hn%d" % i, [128, D], BF16) for i in range(2)])
        ss = sb("n_ss", [128, 32], F32)
        ln = sb("n_ln", [128, 32], F32)
        rs = sb("n_rs", [128, 32], F32)
        epsc = sb("n_eps", [128, 1], F32)
        ptr = Rot([es.enter_context(nc.psum_tensor("n_ptr%d" % i + sfx, [128, D], BF16)) for i in range(2)])
        bjunk = Buf()
        beps = Buf()
        bhn = Buf()
        slots = [P.slot() for _ in range(8)]
        P.dve(lambda e: e.memset(epsc[:], EPS), writes=[beps])
        NB = 4
        for i0 in range(0, 32, NB):
            grp = []
            bssg = Buf()
            for i in range(i0, i0 + NB):
                x_t, bx = xt.next()
                sl = slots[i % len(slots)]
                P.dma(sl, lambda e, x_t=x_t, i=i: e.dma_start(out=x_t[:], in_=xsrc[128 * i:128 * (i + 1), :]), writes=[bx])
                P.act(lambda e, x_t=x_t, i=i: e.activation(out=junk[:], in_=x_t[:], func=AF.Square, accum_out=ss[:, i:i + 1]),
                      reads=[bx], writes=[bjunk, bssg])
                grp.append((i, x_t, bx))
            bln = Buf()
            P.act(lambda e, i0=i0: e.activation(out=ln[:, i0:i0 + NB], in_=ss[:, i0:i0 + NB], func=AF.Ln, bias=epsc[:], scale=1.0 / D),
                  reads=[bssg, beps], writes=[bln])
            brs = Buf()
            P.act(lambda e, i0=i0: e.activation(out=rs[:, i0:i0 + NB], in_=ln[:, i0:i0 + NB], func=AF.Exp, scale=-0.5),
                  reads=[bln], writes=[brs])
            for i, x_t, bx in grp:
                h_t, bh = hn0.next()
                P.dve(lambda e, h_t=h_t, x_t=x_t, i=i: e.tensor_scalar(out=h_t[:], in0=x_t[:], scalar1=rs[:, i:i + 1], scalar2=None, op0=ALU.mult),
                      reads=[bx, brs], writes=[bh])
                p_t, bp = ptr.next()
                for kc in range(8):
                    P.pe(lambda e, p_t=p_t, h_t=h_t, kc=kc: e.transpose(p_t[:, kc * 128:(kc + 1) * 128], h_t[:, kc * 128:(kc + 1) * 128], ident[:]),
                         reads=[bh], writes=[bp])
                dst = lambda i=i: FV(hnT, 0, 128, 128 * i, [[S, 8], [1, 128]])
                src = lambda p_t=p_t: FV(p_t, 0, 128, 0, [[128, 8], [1, 128]])
                if i % 4 == 3:
                    P.act(lambda e, dst=dst, src=src: e.activation(out=dst(), in_=src(), func=AF.Copy), reads=[bp], writes=[bhn])
                else:
                    P.dve(lambda e, dst=dst, src=src: e.tensor_copy(out=dst(), in_=src()), reads=[bp], writes=[bhn])
        if DBG.get("hnT"):
            dump(P, "hnT", hnT, [bhn])
            dump(P, "rs", rs, [bhn])
        P.emit()


def phase_outproj(sync, xsrc, wo_dram, ysc, out):
    nc = sync.nc
    P = Prog(sync).begin()
    with contextlib.ExitStack() as es:
        sfx = uid()

        def sb(name, shape, dt):
            return es.enter_context(nc.sbuf_tensor(name + sfx, shape, dt))
        wst = Rot([sb("o_wst%d" % i, [128, D], F32) for i in range(2)])
        wo = sb("o_wo", [128, 8 * D], BF16)
        bwo = Buf()
        xt = Rot([sb("o_xt%d" % i, [128, D], F32) for i in range(3)])
        ot = Rot([sb("o_ot%d" % i, [128, D], F32) for i in range(2)])
        yt = Rot([sb("o_yt%d" % i, [128, 8 * 512], BF16) for i in range(2)])
        po = Rot([es.enter_context(nc.psum_tensor("o_po%d" % i + sfx, [128, 512], F32)) for i in range(4)])
        s_w = [P.slot() for _ in range(2)]
        s_x = [P.slot() for _ in range(3)]
        s_y = [P.slot() for _ in range(2)]
        s_o = [P.slot() for _ in range(2)]
        for kc in range(8):
            w_t, bw = wst.next()
            P.dma(s_w[kc % 2], lambda e, w_t=w_t, kc=kc: e.dma_start(out=w_t[:], in_=wo_dram[kc * 128:(kc + 1) * 128, :]), writes=[bw])
            if kc % 2 == 0:
                P.dve(lambda e, w_t=w_t, kc=kc: e.tensor_copy(out=wo[:, kc * D:(kc + 1) * D], in_=w_t[:]), reads=[bw], writes=[bwo])
            else:
                P.act(lambda e, w_t=w_t, kc=kc: e.activation(out=wo[:, kc * D:(kc + 1) * D], in_=w_t[:], func=AF.Copy), reads=[bw], writes=[bwo])
        xts = {}
        yts = {}

        def load_x(i):
            x_t, bx = xt.next()
            P.dma(s_x[i % 3], lambda e: e.dma_start(out=x_t[:], in_=xsrc[128 * i:128 * (i + 1), :]), writes=[bx])
            xts[i] = (x_t, bx)

        def load_y(tb):
            y_t, by = yt.next()
            P.dma(s_y[tb % 2], lambda e: e.dma_start(
                out=FV(y_t, 0, 128, 0, [[512, 8], [1, 512]]),
                in_=ysc[:, :, tb * 512:(tb + 1) * 512].rearrange("k p t -> p k t")), writes=[by])
            yts[tb] = (y_t, by)
        load_y(0)
        load_x(0)
        load_x(1)
        for i in range(32):
            if i + 2 < 32:
                load_x(i + 2)
            if i % 4 == 0 and i // 4 + 1 < 8:
                load_y(i // 4 + 1)
            y_t, by = yts[i // 4]
            x_t, bx = xts[i]
            o_t, bo = ot.next()
            for nb in range(2):
                p_t, bp = po.next()
                for kc in range(8):
                    P.pe(lambda e, p_t=p_t, y_t=y_t, kc=kc, nb=nb, i=i: e.matmul(
                        p_t[:], lhsT=y_t[:, kc * 512 + (i % 4) * 128: kc * 512 + (i % 4) * 128 + 128],
                        rhs=wo[:, kc * D + nb * 512: kc * D + nb * 512 + 512], start=(kc == 0), stop=(kc == 7)),
                        reads=[by, bwo], writes=[bp])
                P.dve(lambda e, p_t=p_t, x_t=x_t, o_t=o_t, nb=nb: e.tensor_tensor(
                    out=o_t[:, nb * 512:(nb + 1) * 512], in0=p_t[:], in1=x_t[:, nb * 512:(nb + 1) * 512], op=ALU.add),
                    reads=[bp, bx], writes=[bo])
            P.dma(s_o[i % 2], lambda e, o_t=o_t, i=i: e.dma_start(out=out[128 * i:128 * (i + 1), :], in_=o_t[:]), reads=[bo])
        P.emit()


def qk_geometry(d):
    L = S // d
    return L, L // 128


def merge_steps(a, b):
    out = []
    na, nb = len(a), len(b)
    if na == 0 or nb == 0:
        return list(a) + list(b)
    ia = ib = 0
    while ia < na or ib < nb:
        if ib >= nb or (ia < na and ia * nb <= ib * na):
            out.append(a[ia]); ia += 1
        else:
            out.append(b[ib]); ib += 1
    return out


def phase_attn(sync, layer, hnT, dr, ysc):
    nc = sync.nc
    P = Prog(sync).begin()
    isA = layer == "A"
    w_dram, wg_dram, bias_dram = dr["w"], dr["wg"], dr["bias"]
    units = [(hp, g) for hp in range(8) for g in ((0, 1, 2) if isA else (0,))]
    with contextlib.ExitStack() as es:
        sfx = uid()

        def sb(name, shape, dt):
            return es.enter_context(nc.sbuf_tensor(name + sfx, shape, dt))
        ngc = sb("a_ngc", [128, 8], F32)
        gq = sb("a_gq", [128, 3], F32)
        gk = sb("a_gk", [128, 3], F32)
        epsc = sb("a_eps", [128, 1], F32)
        blk = sb("a_blk", [128, 128], BF16)
        bconst = Buf()
        s_c = P.slot()
        P.dma(s_c, lambda e: e.dma_start(out=ngc[:], in_=dr["ng"][:, :]), writes=[bconst])
        P.dma(s_c, lambda e: e.dma_start(out=gq[:], in_=dr["gq"][:, :]), writes=[bconst])
        P.dma(s_c, lambda e: e.dma_start(out=gk[:], in_=dr["gk"][:, :]), writes=[bconst])
        P.dve(lambda e: e.memset(epsc[:], EPS), writes=[bconst])
        P.dve(lambda e: e.memset(blk[:], 0.0), reads=[bconst], writes=[bconst])
        P.dve(lambda e: e.memset(blk[0:64, 0:64], 1.0 / 64), reads=[bconst], writes=[bconst])
        P.dve(lambda e: e.memset(blk[64:128, 64:128], 1.0 / 64), reads=[bconst], writes=[bconst])
        P.dve(lambda e: e.tensor_scalar(out=gq[:], in0=gq[:], scalar1=0.125, scalar2=None, op0=ALU.mult),
              reads=[bconst], writes=[bconst])
        qTr = Rot([sb("a_qT%d" % i, [128, S], BF16) for i in range(2)])
        kTr = Rot([sb("a_kT%d" % i, [128, S], BF16) for i in range(2)])
        Vt = sb("a_V", [128, 32 * 192], BF16)
        bV = Buf()
        P.dve(lambda e: e.memset(FV(Vt, 0, 128, 64, [[192, 32], [1, 64]]), 1.0), writes=[bV])
        sgT = sb("a_sgT", [128, S], BF16)
        bsg = Buf()
        if isA:
            acc = [sb("a_acc%d" % h, [128, S], F32) for h in range(2)]
            bacc = [Buf(), Buf()]
        wst = Rot([sb("a_wst%d" % i, [128, 384], F32) for i in range(4)])
        s_w = [P.slot() for _ in range(4)]
        wb = Rot([sb("a_wb%d" % i, [128, 8 * 384], BF16) for i in range(2)])
        wgb = sb("a_wgb", [128, 8 * 128], BF16)
        bwg = Buf()
        ebw = 2 * 256 if isA else dr["ebw"]
        ebst = Rot([sb("a_ebst%d" % i, [128, 512 if isA else 768], F32) for i in range(2)])
        s_eb = [P.slot() for _ in range(2)]
        eb = Rot([sb("a_eb%d" % i, [128, ebw], BF16) for i in range(2)])
        sq = Rot([sb("a_sq%d" % i, [128, 512], BF16) for i in range(2)])
        lnb = Rot([sb("a_ln%d" % i, [128, 512], F32) for i in range(2)])
        rstd = Rot([sb("a_rstd%d" % i, [128, 512], F32) for i in range(2)])
        ew = 512 if isA else 768
        ex = Rot([sb("a_ex%d" % i, [128, ew], BF16) for i in range(4 if isA else 3)])
        pT = Rot([sb("a_pT%d" % i, [128, ew], BF16) for i in range(8 if isA else 7)])
        rec = Rot([sb("a_rec%d" % i, [128, 512], F32) for i in range(1 if isA else 2)])
        tmp = Rot([sb("a_tmp%d" % i, [128, 512], F32) for i in range(1)])
        if isA:
            yT = Rot([sb("a_yT%d" % i, [128, 1024], BF16) for i in range(2)])
            s_y = [P.slot() for _ in range(2)]
        else:
            yTB = [Rot([sb("b_yT%d_%d" % (h, i), [128, 1024], BF16) for i in range(2)]) for h in range(2)]
            s_yB = [[P.slot() for _ in range(2)] for h in range(2)]
        print("phase_attn", layer, "sbuf bytes remaining", nc.sbuf_bytes_remaining)
        banks = [es.enter_context(nc.psum_tensor("a_ps%d" % i + sfx, [128, 512], F32)) for i in range(8)]
        bbank = [Buf() for _ in range(8)]

        def bankrot(ids):
            r = Rot([banks[i] for i in ids])
            r.bufs = [bbank[i] for i in ids]
            return r
        pq = bankrot([0, 1, 7])
        pss = bankrot([2])
        if isA:
            ps_h = [bankrot([3]), bankrot([4])]
            po_h = [bankrot([5]), bankrot([6])]
            pv = bankrot([3, 4])
            pg = bankrot([5, 6])
        else:
            pq = bankrot([0, 1])
            psA = bankrot([3, 4])
            bRem = [Buf(), Buf()]
            po = bankrot([6, 7])
            pv = bankrot([3, 4])
            pg = bankrot([6, 7])

        def wload_steps(u):
            hp, g = u
            w_b, bw = wb.next()
            pend = []
            steps = []

            def conv(w_t, bs, kc):
                P.dve(lambda e: e.tensor_scalar(
                    out=w_b[:, kc * 384:(kc + 1) * 384], in0=w_t[:], scalar1=ngc[:, kc:kc + 1], scalar2=None, op0=ALU.mult),
                    reads=[bs, bconst], writes=[bw])
            for kc in range(8):
                def step(kc=kc):
                    w_t, bs = wst.next()
                    sl = s_w[(wst.i - 1) % 4]
                    P.dma(sl, lambda e: e.dma_start(out=w_t[:], in_=w_dram[hp, g, kc * 128:(kc + 1) * 128, :]), writes=[bs])
                    pend.append((w_t, bs, kc))
                    if len(pend) > 2:
                        conv(*pend.pop(0))
                steps.append(step)

            def flush():
                while pend:
                    conv(*pend.pop(0))
            steps.append(flush)
            return (w_b, bw), steps

        def gload_steps(hp):
            pend = []
            steps = []

            def conv(w_t, bs, kc):
                P.dve(lambda e: e.tensor_scalar(
                    out=wgb[:, kc * 128:(kc + 1) * 128], in0=w_t[:, 0:128], scalar1=ngc[:, kc:kc + 1], scalar2=None, op0=ALU.mult),
                    reads=[bs, bconst], writes=[bwg])
            for kc in range(8):
                def step(kc=kc):
                    w_t, bs = wst.next()
                    sl = s_w[(wst.i - 1) % 4]
                    P.dma(sl, lambda e: e.dma_start(out=w_t[:, 0:128], in_=wg_dram[hp, kc * 128:(kc + 1) * 128, :]), writes=[bs])
                    pend.append((w_t, bs, kc))
                    if len(pend) > 2:
                        conv(*pend.pop(0))
                steps.append(step)

            def flush():
                while pend:
                    conv(*pend.pop(0))
            steps.append(flush)
            return steps

        def eb_dma_A(u):
            hp, g = u
            st_, bs = ebst.next()
            sl = s_eb[(ebst.i - 1) % 2]
            P.dma(sl, lambda e: e.dma_start(out=st_[:, 0:512], in_=bias_dram[hp, g, :, :]), writes=[bs])
            return st_, bs

        def eb_conv_A(st_, bs):
            e_t, be = eb.next()
            P.act(lambda e: e.activation(out=e_t[:, 0:512], in_=st_[:, 0:512], func=AF.Exp), reads=[bs], writes=[be])
            return e_t, be

        def load_eb_B(hp, h):
            e_t, be = eb.next()
            ebw1 = dr["ebw"]
            off = 0
            while off < ebw1:
                n = min(768, ebw1 - off)
                st_, bs = ebst.next()
                sl = s_eb[(ebst.i - 1) % 2]
                P.dma(sl, lambda e, st_=st_, off=off, n=n: e.dma_start(out=st_[:, 0:n], in_=bias_dram[2 * hp + h, :, off:off + n]), writes=[bs])
                P.act(lambda e, st_=st_, e_t=e_t, off=off, n=n: e.activation(
                    out=e_t[:, off: off + n], in_=st_[:, 0:n], func=AF.Exp), reads=[bs], writes=[be])
                off += n
            return e_t, be

        def gate_steps():
            steps = []
            for tb in range(8):
                def step(tb=tb):
                    p_t, bp = pg.next()
                    for kc in range(8):
                        P.pe(lambda e, p_t=p_t, kc=kc: e.matmul(
                            p_t[:], lhsT=wgb[:, kc * 128:(kc + 1) * 128], rhs=hnT[:, kc * S + tb * 512: kc * S + tb * 512 + 512],
                            start=(kc == 0), stop=(kc == 7)), reads=[bwg], writes=[bp])
                    P.act(lambda e, p_t=p_t: e.activation(out=sgT[:, tb * 512:(tb + 1) * 512], in_=p_t[:], func=AF.Silu),
                          reads=[bp], writes=[bsg])
                steps.append(step)
            return steps

        def qk_steps(u, w_b, bw, q_t, bq, k_t, bk):
            hp, g = u
            d = DILS[g] if isA else 1
            L = S // d
            pend = []

            def tail(p_t, bp, s_t, bs, which, tb):
                ss_t, bss = pss.next()
                P.pe(lambda e: e.matmul(ss_t[:], lhsT=blk[:], rhs=s_t[:], start=True, stop=True),
                     reads=[bs, bconst], writes=[bss])
                l_t, bl = lnb.next()
                P.act(lambda e: e.activation(out=l_t[:], in_=ss_t[:], func=AF.Ln, bias=epsc[:], scale=1.0),
                      reads=[bss, bconst], writes=[bl])
                r_t, br = rstd.next()
                P.act(lambda e: e.activation(out=r_t[:], in_=l_t[:], func=AF.Exp, scale=-0.5), reads=[bl], writes=[br])
                if which == 0:
                    n = 512 // d
                    P.dve(lambda e: e.scalar_tensor_tensor(
                        out=FV(q_t, 0, 128, tb * n, [[L, d], [1, n]]),
                        in0=FV(p_t, 0, 128, 0, [[1, d], [d, n]]), scalar=gq[:, g:g + 1],
                        in1=FV(r_t, 0, 128, 0, [[1, d], [d, n]]), op0=ALU.mult, op1=ALU.mult),
                        reads=[bp, br, bconst], writes=[bq])
                else:
                    P.dve(lambda e: e.scalar_tensor_tensor(
                        out=k_t[:, tb * 512:(tb + 1) * 512], in0=p_t[:], scalar=gk[:, g:g + 1], in1=r_t[:],
                        op0=ALU.mult, op1=ALU.mult), reads=[bp, br, bconst], writes=[bk])

            steps = []
            for t in range(16):
                def step(t=t):
                    which, tb = divmod(t, 8)
                    p_t, bp = pq.next()
                    for kc in range(8):
                        P.pe(lambda e, kc=kc: e.matmul(
                            p_t[:], lhsT=w_b[:, kc * 384 + which * 128: kc * 384 + which * 128 + 128],
                            rhs=hnT[:, kc * S + tb * 512: kc * S + tb * 512 + 512], start=(kc == 0), stop=(kc == 7)),
                            reads=[bw], writes=[bp])
                    s_t, bs = sq.next()
                    P.act(lambda e: e.activation(out=s_t[:], in_=p_t[:], func=AF.Square), reads=[bp], writes=[bs])
                    if pend:
                        tail(*pend.pop())
                    pend.append((p_t, bp, s_t, bs, which, tb))
                steps.append(step)

            def flush():
                tail(*pend.pop())
            steps.append(flush)
            return steps

        def v_steps(u, w_b, bw):
            hp, g = u
            d = DILS[g] if isA else 1
            L, nC = qk_geometry(d)
            steps = []
            for c0 in range(0, 32, 4):
                def step(c0=c0):
                    p_t, bp = pv.next()
                    for cc in range(4):
                        c = c0 + cc
                        r, i = divmod(c, nC)
                        t0 = r + d * 128 * i
                        for kc in range(8):
                            P.pe(lambda e, kc=kc, cc=cc, t0=t0: e.matmul(
                                p_t[:, cc * 128:(cc + 1) * 128],
                                lhsT=FV(hnT, 0, 128, kc * S + t0, [[d, 128]]),
                                rhs=w_b[:, kc * 384 + 256: kc * 384 + 384], start=(kc == 0), stop=(kc == 7)),
                                reads=[bw], writes=[bp])
                    P.act(lambda e: e.activation(
                        out=FV(Vt, 0, 128, c0 * 192, [[192, 4], [128, 2], [1, 64]]),
                        in_=FV(p_t, 0, 128, 0, [[128, 4], [64, 2], [1, 64]]), func=AF.Copy), reads=[bp], writes=[bV])
                steps.append(step)
            return steps

        def attn_steps_A(u, q_t, bq, k_t, bk, e_t, be):
            hp, g = u
            d = DILS[g]
            first = g == 0
            L, nC = qk_geometry(d)
            pend = []
            steps = []

            def acc_out(h, srcf, sbuf, dstf):
                if first:
                    P.dve(lambda e: e.tensor_copy(out=dstf(), in_=srcf()), reads=[sbuf], writes=[bacc[h]])
                else:
                    P.dve(lambda e: e.tensor_tensor(out=dstf(), in0=srcf(), in1=dstf(), op=ALU.add), reads=[sbuf], writes=[bacc[h]])

            def make_tail(r, m, ptl, state):
                def tail():
                    for h in range(2):
                        vof = 0 if h == 0 else 64
                        acc_h = acc[h]
                        key = "po%d" % h

                        def pcol(i, lo):
                            p_t, bp = ptl[(h, i // 2)]
                            return p_t, bp, (i % 2) * 256 + lo

                        def mm(o_t, bo, col, n, chunk, pa, bpa, ca, start, stop, vof=vof):
                            P.pe(lambda e: e.matmul(
                                o_t[:, col:col + n], lhsT=Vt[:, chunk * 192 + vof: chunk * 192 + vof + 128],
                                rhs=pa[:, ca:ca + n], start=start, stop=stop), reads=[bV, bpa], writes=[bo])
                        if d == 16:
                            if r % 2 == 0:
                                state[key] = po_h[h].next()
                            o_t, bo = state[key]
                            base = (r % 2) * 256
                            c1 = r * nC
                            pa, bpa, ca = pcol(0, 64)
                            mm(o_t, bo, base, 64, c1, pa, bpa, ca, True, True)
                            pa, bpa, ca = pcol(0, 128)
                            mm(o_t, bo, base + 64, 128, c1, pa, bpa, ca, True, False)
                            pa, bpa, ca = pcol(1, 0)
                            mm(o_t, bo, base + 64, 128, c1 + 1, pa, bpa, ca, False, True)
                            pa, bpa, ca = pcol(1, 128)
                            mm(o_t, bo, base + 192, 64, c1 + 1, pa, bpa, ca, True, True)
                            if r % 2 == 1:
                                acc_out(h, lambda o_t=o_t: o_t[:, 0:512], bo,
                                        lambda acc_h=acc_h: FV(acc_h, 0, 128, r - 1, [[1, 2], [16, 256]]))
                            continue
                        if m == 0:
                            state[key] = po_h[h].next()
                            o_t, bo = state[key]
                            pa, bpa, ca = pcol(0, 64)
                            mm(o_t, bo, 64, 64, r * nC, pa, bpa, ca, True, True)
                        for jj in ([2 * m - 1] if m > 0 else []) + ([2 * m] if 2 * m <= nC - 2 else []):
                            if jj < 3:
                                slot = jj + 1
                            else:
                                slot = (jj - 3) % 4
                                if slot == 0:
                                    state[key] = po_h[h].next()
                            o_t, bo = state[key]
                            pa, bpa, ca = pcol(jj, 128)
                            pb_, bpb, cb = pcol(jj + 1, 0)
                            c1 = r * nC + jj
                            mm(o_t, bo, slot * 128, 128, c1, pa, bpa, ca, True, False)
                            mm(o_t, bo, slot * 128, 128, c1 + 1, pb_, bpb, cb, False, True)
                            if jj == 2:
                                acc_out(h, lambda o_t=o_t: o_t[:, 64:512], bo,
                                        lambda acc_h=acc_h: FV(acc_h, 0, 128, r, [[d, 448]]))
                            elif jj > 2 and slot == 3:
                                m0 = 64 + 128 * (jj - 3)
                                acc_out(h, lambda o_t=o_t: o_t[:, 0:512], bo,
                                        lambda acc_h=acc_h, m0=m0: FV(acc_h, 0, 128, r + d * m0, [[d, 512]]))
                        if m == nC // 2 - 1:
                            assert (nC - 2 - 3) % 4 == 3
                            o_t, bo = po_h[h].next()
                            pa, bpa, ca = pcol(nC - 1, 128)
                            mm(o_t, bo, 0, 64, r * nC + nC - 1, pa, bpa, ca, True, True)
                            acc_out(h, lambda o_t=o_t: o_t[:, 0:64], bo,
                                    lambda acc_h=acc_h: FV(acc_h, 0, 128, r + d * (L - 64), [[d, 64]]))
                return tail

            state = {}
            for r in range(d):
                ptl = {}
                for m in range(nC // 2):
                    def step(r=r, m=m, ptl=ptl, state=state):
                        stl = [ps_h[0].next(), ps_h[1].next()]
                        for cc in range(2):
                            i = 2 * m + cc
                            qlo = max(0, 128 * i - 64)
                            qhi = min(L, 128 * i + 192)
                            lo = qlo - (128 * i - 64)
                            n = qhi - qlo
                            for h in range(2):
                                s_t, bs = stl[h]
                                P.pe(lambda e, s_t=s_t, cc=cc, lo=lo, n=n, qlo=qlo, i=i, h=h: e.matmul(
                                    s_t[:, cc * 256 + lo: cc * 256 + lo + n],
                                    lhsT=FV(k_t, 64 * h, 64 * h + 64, r + d * 128 * i, [[d, 128]]),
                                    rhs=q_t[64 * h:64 * h + 64, r * L + qlo: r * L + qlo + n], start=True, stop=True),
                                    reads=[bk, bq], writes=[bs])
                        for h in range(2):
                            s_t, bs = stl[h]
                            x_t, bx = ex.next()
                            P.act(lambda e, x_t=x_t, s_t=s_t: e.activation(out=x_t[:, 0:512], in_=s_t[:], func=AF.Exp), reads=[bs], writes=[bx])
                            p_t, bp = pT.next()
                            P.dve(lambda e, p_t=p_t, x_t=x_t, h=h: e.tensor_tensor(
                                out=FV(p_t, 0, 128, 0, [[256, 2], [1, 256]]), in0=FV(x_t, 0, 128, 0, [[256, 2], [1, 256]]),
                                in1=FV(e_t, 0, 128, h * 256, [[0, 2], [1, 256]]), op=ALU.mult), reads=[bx, be], writes=[bp])
                            ptl[(h, m)] = (p_t, bp)
                        if pend:
                            pend.pop()()
                        pend.append(make_tail(r, m, ptl, state))
                    steps.append(step)

            def flush():
                pend.pop()()
            steps.append(flush)
            return steps

        def normalize_steps_A(hp):
            steps = []
            st = {}
            for tb in range(8):
                def step(tb=tb):
                    q4, hb = divmod(tb, 2)
                    if hb == 0:
                        st["y"] = yT.next()
                    y_t, by = st["y"]
                    cs = slice(tb * 512, (tb + 1) * 512)
                    r_t, br = rec.next()
                    t_t, bt = tmp.next()
                    P.act(lambda e: e.activation(out=r_t[0:64, :], in_=acc[0][64:128, cs], func=AF.Ln), reads=[bacc[0]], writes=[br])
                    P.act(lambda e: e.activation(out=r_t[64:128, :], in_=acc[1][0:64, cs], func=AF.Ln), reads=[bacc[1]], writes=[br])
                    P.act(lambda e: e.activation(out=r_t[:], in_=r_t[:], func=AF.Exp, scale=-1.0), reads=[br], writes=[br])
                    P.dve(lambda e: e.tensor_tensor(out=t_t[0:64, :], in0=acc[0][0:64, cs], in1=r_t[0:64, :], op=ALU.mult),
                          reads=[br], writes=[bt])
                    P.dve(lambda e: e.tensor_tensor(out=t_t[64:128, :], in0=acc[1][64:128, cs], in1=r_t[64:128, :], op=ALU.mult),
                          reads=[br], writes=[bt])
                    P.dve(lambda e: e.tensor_tensor(out=y_t[:, hb * 512:(hb + 1) * 512], in0=t_t[:], in1=sgT[:, cs], op=ALU.mult),
                          reads=[bt, bsg], writes=[by])
                    if hb == 1:
                        sl = s_y[(yT.i - 1) % 2]
                        P.dma(sl, lambda e: e.dma_start(out=ysc[hp, :, q4 * 1024:(q4 + 1) * 1024], in_=y_t[:]), reads=[by])
                steps.append(step)
            return steps

        def attn_steps_B(hp, h, q_t, bq, k_t, bk, e_t, be):
            tiles = dr["tiles"]
            hs = slice(64 * h, 64 * h + 64)
            vof = 0 if h == 0 else 64
            num = slice(0, 64) if h == 0 else slice(64, 128)
            den = slice(64, 128) if h == 0 else slice(0, 64)
            contribs = []
            for Q in range(32):
                full, part = [], []
                for R in range(32):
                    qlo, nr = tiles[R][1], tiles[R][2]
                    lo_r = max(qlo, 2 * Q)
                    hi_r = min(qlo + nr - 1, 2 * Q + 1)
                    if lo_r > hi_r:
                        continue
                    (full if hi_r - lo_r == 1 else part).append((R, lo_r, hi_r - lo_r + 1))
                assert full
                contribs.append(full + part)
            lastR = [max(R for R, _, _ in contribs[Q]) for Q in range(32)]
            ptl = {}
            st = dict(o=None, y=None)
            pend = []
            steps = []

            def do_block(Q):
                if Q % 4 == 0:
                    st["o"] = po.next()
                o_t, bo = st["o"]
                cl = contribs[Q]
                for n_, (R, row0, nrow) in enumerate(cl):
                    p_t, bp = ptl[R]
                    c0 = (row0 - tiles[R][1]) * 64
                    oc = (Q % 4) * 128 + (row0 - 2 * Q) * 64
                    nn = nrow * 64
                    P.pe(lambda e, p_t=p_t, c0=c0, oc=oc, nn=nn, R=R, first=(n_ == 0), last=(n_ == len(cl) - 1): e.matmul(
                        o_t[:, oc:oc + nn], lhsT=Vt[:, R * 192 + vof: R * 192 + vof + 128],
                        rhs=p_t[:, c0:c0 + nn], start=first, stop=last), reads=[bV, bp], writes=[bo])
                if Q % 4 == 3:
                    tb = Q // 4
                    cs = slice(tb * 512, (tb + 1) * 512)
                    if tb % 2 == 0:
                        st["y"] = yTB[h].next()
                    y_t, by = st["y"]
                    r_t, br = rec.next()
                    t_t, bt = tmp.next()
                    P.act(lambda e: e.activation(out=r_t[num, :], in_=o_t[den, :], func=AF.Ln), reads=[bo], writes=[br])
                    P.act(lambda e: e.activation(out=r_t[num, :], in_=r_t[num, :], func=AF.Exp, scale=-1.0), reads=[br], writes=[br])
                    P.dve(lambda e: e.tensor_tensor(out=t_t[num, :], in0=o_t[num, :], in1=r_t[num, :], op=ALU.mult),
                          reads=[bo, br], writes=[bt])
                    P.dve(lambda e: e.tensor_tensor(
                        out=y_t[num, (tb % 2) * 512:(tb % 2) * 512 + 512], in0=t_t[num, :], in1=sgT[num, cs], op=ALU.mult),
                        reads=[bt, bsg], writes=[by])
                    if tb % 2 == 1:
                        q4 = tb // 2
                        sl = s_yB[h][(yTB[h].i - 1) % 2]
                        P.dma(sl, lambda e: e.dma_start(
                            out=ysc[hp, num, q4 * 1024:(q4 + 1) * 1024], in_=y_t[num, :]), reads=[by])

            for R in range(32):
                def step(R=R):
                    toff, qlo, nr = tiles[R]
                    n = nr * 64
                    sA, bA = psA.next()
                    sB = banks[5]
                    bB = bbank[5]
                    n1 = min(n, 512)
                    P.pe(lambda e: e.matmul(
                        sA[:, 0:n1], lhsT=k_t[hs, R * 128:(R + 1) * 128], rhs=q_t[hs, qlo * 64: qlo * 64 + n1], start=True, stop=True),
                        reads=[bk, bq], writes=[bA])
                    x_t, bx = ex.next()
                    P.act(lambda e: e.activation(out=x_t[:, 0:n1], in_=sA[:, 0:n1], func=AF.Exp), reads=[bA], writes=[bx])
                    if n > 512:
                        n2 = n - 512
                        P.pe(lambda e: e.matmul(
                            sB[:, 0:n2], lhsT=k_t[hs, R * 128:(R + 1) * 128], rhs=q_t[hs, qlo * 64 + 512: qlo * 64 + 512 + n2], start=True, stop=True),
                            reads=[bk, bq], writes=[bB])
                        P.act(lambda e: e.activation(out=x_t[:, 512:512 + n2], in_=sB[:, 0:n2], func=AF.Exp), reads=[bB], writes=[bx])
                    p_t, bp = pT.next()
                    P.dve(lambda e: e.tensor_tensor(
                        out=p_t[:, 0:n], in0=x_t[:, 0:n], in1=e_t[:, toff: toff + n], op=ALU.mult),
                        reads=[bx, be], writes=[bp])
                    ptl[R] = (p_t, bp)
                    if pend:
                        pend.pop()()

                    def tail():
                        for Q in range(32):
                            if lastR[Q] == R:
                                do_block(Q)
                    pend.append(tail)
                steps.append(step)

            def flush():
                pend.pop()()
            steps.append(flush)
            return steps

        def run(steps):
            for st_ in steps:
                st_()

        dstop = DBG.get("stop")
        nU = len(units)
        wts = {}
        qk = {}
        ebd = {}
        wts[0], ws = wload_steps(units[0])
        run(ws)
        run(gload_steps(0))
        if isA:
            ebd[0] = eb_dma_A(units[0])
        if nU > 1:
            wts[1], ws1 = wload_steps(units[1])
        else:
            ws1 = []
        qk[0] = qTr.next() + kTr.next()
        run(merge_steps(qk_steps(units[0], *wts[0], *qk[0]), ws1))
        run(v_steps(units[0], *wts[0]))
        run(gate_steps())
        for ui, u in enumerate(units):
            hp, g = u
            nxt = units[ui + 1] if ui + 1 < nU else None
            wsteps = []
            if ui + 2 < nU:
                wts[ui + 2], wsteps = wload_steps(units[ui + 2])
            q_t, bq, k_t, bk = qk[ui]
            nsteps = []
            if nxt is not None:
                qk[ui + 1] = qTr.next() + kTr.next()
                nsteps = qk_steps(nxt, *wts[ui + 1], *qk[ui + 1])
            if isA:
                if nxt is not None:
                    ebd[ui + 1] = eb_dma_A(nxt)
                e_t, be = eb_conv_A(*ebd[ui])
                last = g == 2
                gsteps = gload_steps(hp + 1) if (g == 1 and hp + 1 < 8) else []
                run(merge_steps(merge_steps(attn_steps_A(u, q_t, bq, k_t, bk, e_t, be), nsteps), wsteps + gsteps))
                if last:
                    vs = v_steps(nxt, *wts[ui + 1]) if nxt is not None else []
                    run(merge_steps(normalize_steps_A(hp), vs))
                    if hp + 1 < 8:
                        run(gate_steps())
                elif nxt is not None:
                    run(v_steps(nxt, *wts[ui + 1]))
            else:
                gsteps = gload_steps(hp + 1) if hp + 1 < 8 else []
                e0 = load_eb_B(hp, 0)
                a0 = attn_steps_B(hp, 0, q_t, bq, k_t, bk, *e0)
                run(merge_steps(merge_steps(a0, nsteps[: len(nsteps) // 2]), wsteps))
                e1 = load_eb_B(hp, 1)
                a1 = attn_steps_B(hp, 1, q_t, bq, k_t, bk, *e1)
                run(merge_steps(merge_steps(a1, nsteps[len(nsteps) // 2:]), gsteps))
                if nxt is not None:
                    run(v_steps(nxt, *wts[ui + 1]))
                if hp + 1 < 8:
                    run(gate_steps())
            if dstop == "hp0" and ((isA and g == 2) or not isA):
                break
        P.emit()


_T5_LUT = None


def _t5_bucket_np(rel):
    import math
    half, me = 16, 8
    ret = np.where(rel > 0, half, 0)
    n = np.abs(rel)
    nf = np.maximum(n, 1).astype(np.float32)
    large = me + (np.log(nf / np.float32(me)) / np.float32(math.log(1024 / me)) * np.float32(half - me)).astype(np.int32)
    large = np.minimum(large, half - 1)
    return ret + np.where(n < me, n, large)


def _bias_tiles_A(t5_bias):
    a = np.arange(128)[:, None]
    b = np.arange(256)[None, :]
    rel = a - b + 64
    valid = (b - a >= 0) & (b - a <= 128)
    out = np.empty((8, 3, 128, 2, 256), np.float32)
    for g, d in enumerate(DILS):
        idx = _t5_bucket_np(rel * d)
        for h in range(16):
            t = t5_bias[g * 16 + h][idx]
            out[h // 2, g, :, h % 2, :] = np.where(valid, t, np.float32(NEG))
    return out


def _geom_B():
    rows = 64
    r = np.arange(rows)
    rs = np.clip(r - 4, 0, rows - 8)
    c = np.arange(64)
    cs = np.clip(c - 8, 0, 64 - 16)
    tiles = []
    uniq = {}
    maps = []
    off = 0
    for R in range(32):
        krs = np.array([2 * R, 2 * R + 1])
        qrows = [q for q in range(rows) if (rs[q] <= krs[1]) and (rs[q] + 7 >= krs[0])]
        qlo, nr = qrows[0], len(qrows)
        assert qrows == list(range(qlo, qlo + nr))
        kr = np.repeat(krs, 64)[:, None]
        kc = np.tile(c, 2)[:, None]
        qr = np.repeat(np.arange(qlo, qlo + nr), 64)[None, :]
        qc = np.tile(c, nr)[None, :]
        valid = (kr >= rs[qr]) & (kr <= rs[qr] + 7) & (kc >= cs[qc]) & (kc < cs[qc] + 16)
        ridx = np.clip(kr - qr + 7, 0, 14)
        cidx = np.clip(kc - qc, -15, 15) + 15
        key = (nr, valid.tobytes(), ridx.tobytes())
        if key not in uniq:
            uniq[key] = off
            maps.append((off, ridx + 0 * cidx, cidx + 0 * ridx, valid))
            off += nr * 64
        tiles.append((uniq[key], qlo, nr))
    return tiles, maps, off


def _bias_tiles_B(rpb, maps, ebw):
    out = np.empty((16, 128, ebw), np.float32)
    for off, ridx, cidx, valid in maps:
        n = valid.shape[1]
        for h in range(16):
            out[h, :, off:off + n] = np.where(valid, rpb[h][ridx, cidx], np.float32(NEG))
    return out


def _unit_weights(w_in, ngroups):
    wu = np.empty((8, ngroups, D, 384), np.float32)
    for hp in range(8):
        for g in range(ngroups):
            for j in range(3):
                c0 = g * 3072 + j * 1024 + hp * 128
                wu[hp, g, :, j * 128:(j + 1) * 128] = w_in[:, c0:c0 + 128]
    gc = ngroups * 3072
    wg = np.ascontiguousarray(w_in[:, gc:gc + 1024].reshape(D, 8, 128).transpose(1, 0, 2))
    return wu, wg


_GEOM_B = None


def build_nc(layers="AB"):
    global _GEOM_B
    if _GEOM_B is None:
        _GEOM_B = _geom_B()
    tilesB, mapsB, ebwB = _GEOM_B
    nc = bass.Bass("TRN2", target_bir_lowering=False)

    def din(name, shape, dt=F32):
        return nc.dram_tensor(name, list(shape), dt, kind="ExternalInput").ap()
    x = din("x", [S, D])
    ident_d = din("ident_d", [128, 128])
    drA = dict(w=din("wA", [8, 3, D, 384]), wg=din("wgA", [8, D, 128]), bias=din("biasA", [8, 3, 128, 512]),
               ng=din("ngA", [128, 8]), gq=din("gqA", [128, 3]), gk=din("gkA", [128, 3]))
    woA = din("woA", [D, D])
    drB = dict(w=din("wB", [8, 1, D, 384]), wg=din("wgB", [8, D, 128]), bias=din("biasB", [16, 128, ebwB]),
               ng=din("ngB", [128, 8]), gq=din("gqB", [128, 3]), gk=din("gkB", [128, 3]), tiles=tilesB, ebw=ebwB)
    woB = din("woB", [D, D])
    out = nc.dram_tensor("out", [S, D], F32, kind="ExternalOutput").ap()
    ysc = nc.dram_tensor("ysc", [8, 128, S], BF16).ap()
    with contextlib.ExitStack() as es:
        sync = Sync(nc, es)
        hnT = es.enter_context(nc.sbuf_tensor("hnT", [128, 8 * S], BF16))
        ident = es.enter_context(nc.sbuf_tensor("ident", [128, 128], BF16))
        identf = es.enter_context(nc.sbuf_tensor("identf", [128, 128], F32))
        P0 = Prog(sync).begin()
        bi = Buf()
        P0.dma(P0.slot(), lambda e: e.dma_start(out=identf[:], in_=ident_d[:, :]), writes=[bi])
        P0.dve(lambda e: e.tensor_copy(out=ident[:], in_=identf[:]), reads=[bi], writes=[bi])
        P0.emit()
        src = x
        if "A" in layers:
            phase_norm(sync, es, src, hnT, ident)
            if DBG.get("stop") != "norm":
                phase_attn(sync, "A", hnT, drA, ysc)
            if not DBG.get("stop"):
                phase_outproj(sync, src, woA, ysc, out)
            src = out
        if "B" in layers:
            phase_norm(sync, es, src, hnT, ident)
            phase_attn(sync, "B", hnT, drB, ysc)
            phase_outproj(sync, src, woB, ysc, out)
    return nc


def host_inputs(norm_gain, a_w_in, a_w_out, a_q_gain, a_k_gain, t5_bias, b_w_in, b_w_out, b_q_gain, b_k_gain, b_rpb):
    global _GEOM_B
    if _GEOM_B is None:
        _GEOM_B = _geom_B()
    tilesB, mapsB, ebwB = _GEOM_B
    f = lambda a: np.ascontiguousarray(np.asarray(a, dtype=np.float32))
    wA, wgA = _unit_weights(f(a_w_in)[0], 3)
    wB, wgB = _unit_weights(f(b_w_in)[0], 1)
    ng = f(norm_gain)

    def gcol(gn):
        gn = f(gn).reshape(-1, 64)
        o = np.ones((128, 3), np.float32)
        for g in range(gn.shape[0]):
            o[:, g] = np.tile(gn[g], 2)
        return o
    shared = dict(
        ident_d=np.eye(128, dtype=np.float32),
        wA=wA, wgA=wgA, woA=f(a_w_out)[0],
        biasA=np.ascontiguousarray(_bias_tiles_A(f(t5_bias)).reshape(8, 3, 128, 512)),
        ngA=np.ascontiguousarray(ng[0].reshape(8, 128).T), gqA=gcol(a_q_gain[0]), gkA=gcol(a_k_gain[0]),
        wB=wB, wgB=wgB, woB=f(b_w_out)[0],
        biasB=_bias_tiles_B(f(b_rpb)[0], mapsB, ebwB),
        ngB=np.ascontiguousarray(ng[1].reshape(8, 128).T), gqB=gcol(b_q_gain), gkB=gcol(b_k_gain),
    )
    return shared


def kernel(x, norm_gain, a_w_in, a_w_out, a_q_gain, a_k_gain, t5_bias, b_w_in, b_w_out, b_q_gain, b_k_gain, b_rpb):
    x = np.ascontiguousarray(np.asarray(x, dtype=np.float32))
    shared = host_inputs(norm_gain, a_w_in, a_w_out, a_q_gain, a_k_gain, t5_bias, b_w_in, b_w_out, b_q_gain, b_k_gain, b_rpb)
    nc = build_nc("AB")
    in_maps = [dict(shared, x=x[c]) for c in range(NCORES)]
    res = run_bass_kernel_spmd(nc, in_maps, core_ids=list(range(NCORES)))
    return np.stack([np.asarray(r["out"], dtype=np.float32) for r in res.results], axis=0)
```

```python
import contextlib
import numpy as np
import concourse.bass as bass
import concourse.mybir as mybir
from concourse.bass_utils import run_bass_kernel_spmd

F32 = mybir.dt.float32
BF16 = mybir.dt.bfloat16
AF = mybir.ActivationFunctionType
ALU = mybir.AluOpType

S = 4096
D = 1024
NCORES = 8
DILS = (1, 4, 16)
EPS = 1e-6
NEG = -30000.0


class Buf:
    __slots__ = ("name", "lw", "rd")

    def __init__(self, name=""):
        self.name = name
        self.lw = None
        self.rd = []


class DmaSlot:
    __slots__ = ("sem", "count", "name")

    def __init__(self, name):
        self.name = name
        self.sem = None
        self.count = 0


class Op:
    __slots__ = ("eng", "fn", "deps", "slot", "signal", "tick", "semkey", "known", "idx")


COMPUTE = ("pe", "act", "dve", "pool")
ENGS = ("pe", "act", "dve", "pool", "sp")


class Sync:
    def __init__(self, nc, es, nslots=40):
        self.nc = nc
        self.esem = {e: es.enter_context(nc.semaphore("sem_" + e)) for e in COMPUTE}
        self.tick = {e: 0 for e in COMPUTE}
        self.slots = []
        for i in range(nslots):
            s = DmaSlot("dq%d" % i)
            s.sem = es.enter_context(nc.semaphore(s.name))
            self.slots.append(s)


class Prog:
    def __init__(self, sync):
        self.sync = sync
        self.nc = sync.nc
        self.ops = []
        self.nslot = 0

    def slot(self, name=""):
        s = self.sync.slots[self.nslot]
        self.nslot += 1
        return s

    def add(self, eng, fn, reads=(), writes=(), slot=None):
        op = Op()
        op.eng = eng
        op.fn = fn
        op.slot = slot
        op.signal = slot is not None
        op.tick = None
        op.idx = len(self.ops)
        deps = {}
        is_dma = slot is not None
        for b in reads:
            w = b.lw
            if w is not None:
                if is_dma or w.slot is not None or w.eng != eng or eng != "pe":
                    deps[w.idx] = w
        for b in writes:
            w = b.lw
            if w is not None and (is_dma or w.slot is not None or w.eng != eng):
                deps[w.idx] = w
            for r in b.rd:
                if is_dma or r.slot is not None or r.eng != eng:
                    deps[r.idx] = r
        for b in reads:
            b.rd.append(op)
        for b in writes:
            b.lw = op
            b.rd = []
        op.deps = list(deps.values())
        for d in op.deps:
            d.signal = True
        self.ops.append(op)
        return op

    def pe(self, fn, reads=(), writes=()):
        return self.add("pe", fn, reads, writes)

    def act(self, fn, reads=(), writes=()):
        return self.add("act", fn, reads, writes)

    def dve(self, fn, reads=(), writes=()):
        return self.add("dve", fn, reads, writes)

    def pool(self, fn, reads=(), writes=()):
        return self.add("pool", fn, reads, writes)

    def dma(self, slot, fn, reads=(), writes=()):
        return self.add("sp", fn, reads, writes, slot=slot)

    def emit(self):
        nc = self.nc
        sy = self.sync
        for op in self.ops:
            if op.slot is not None:
                op.slot.count += 16
                op.tick = op.slot.count
                op.semkey = op.slot
            elif op.signal:
                sy.tick[op.eng] += 1
                op.tick = sy.tick[op.eng]
                op.semkey = op.eng
        base = {}
        for e in COMPUTE:
            base[e] = 0
        clock = {e: {} for e in ENGS}
        start_tick = dict(self._start_tick)
        start_slot = dict(self._start_slot)
        for e in ENGS:
            for k, v in start_tick.items():
                clock[e][k] = v
            for k, v in start_slot.items():
                clock[e][k] = v
        plan = {e: [] for e in ENGS}
        for op in self.ops:
            ck = clock[op.eng]
            need = {}
            for d in op.deps:
                if ck.get(d.semkey, 0) >= d.tick:
                    continue
                if need.get(d.semkey, 0) < d.tick:
                    need[d.semkey] = d.tick
            for d in op.deps:
                for k, v in d.known.items():
                    if ck.get(k, 0) < v:
                        ck[k] = v
            waits = list(need.items())
            for k, v in waits:
                if ck.get(k, 0) < v:
                    ck[k] = v
            if op.tick is not None:
                kn = dict(ck)
                if kn.get(op.semkey, 0) < op.tick:
                    kn[op.semkey] = op.tick
                op.known = kn
            plan[op.eng].append((op, waits))
        final_waits = [(s, s.count) for s in sy.slots[: self.nslot] if s.count > start_slot.get(s, 0)]
        esem = sy.esem

        def semof(k):
            return k.sem if isinstance(k, DmaSlot) else esem[k]

        def run(engname):
            def body(eng):
                for op, waits in plan[engname]:
                    for k, v in waits:
                        eng.wait_ge(semof(k), v)
                    ins = op.fn(eng)
                    if op.slot is not None:
                        ins.then_inc(op.slot.sem, 16)
                    elif op.tick is not None:
                        ins.then_inc(esem[engname], 1)
                if engname == "sp":
                    for s, v in final_waits:
                        eng.wait_ge(s.sem, v)
            return body

        with nc.Block() as block:
            block.tensor(run("pe"))
            block.scalar(run("act"))
            block.vector(run("dve"))
            block.gpsimd(run("pool"))
            block.sync(run("sp"))

    def begin(self):
        sy = self.sync
        self._start_tick = dict(sy.tick)
        self._start_slot = {s: s.count for s in sy.slots}
        return self


_UID = [0]


def uid():
    _UID[0] += 1
    return "_u%d" % _UID[0]


def fview(ap, dims):
    return bass.AP(tensor=ap.tensor, offset=ap.offset, ap=[list(ap.ap[0])] + [list(d) for d in dims])


def FV(t, p0, p1, off, dims):
    return fview(t[p0:p1, off:off + 1], dims)


class Rot:
    def __init__(self, items):
        self.items = items
        self.bufs = [Buf() for _ in items]
        self.i = 0

    def next(self):
        k = self.i % len(self.items)
        self.i += 1
        return self.items[k], self.bufs[k]


DBG = {}


def dump(P, name, t, bufs, dt=None):
    nc = P.nc
    shape = list(t.shape)
    d = nc.dram_tensor("dbg_" + name, shape, dt or t.dtype, kind="ExternalOutput").ap()
    P.dma(P.slot(), lambda e: e.dma_start(out=d[:, :], in_=t[:, :]), reads=bufs)


def phase_norm(sync, es_outer, xsrc, hnT, ident):
    nc = sync.nc
    P = Prog(sync).begin()
    with contextlib.ExitStack() as es:
        sfx = uid()

        def sb(name, shape, dt):
            return es.enter_context(nc.sbuf_tensor(name + sfx, shape, dt))
        xt = Rot([sb("n_xt%d" % i, [128, D], F32) for i in range(8)])
        junk = sb("n_junk", [128, D], BF16)
        hn0 = Rot([sb("n_hn%d" % i, [128, D], BF16) for i in range(2)])
        ss = sb("n_ss", [128, 32], F32)
        ln = sb("n_ln", [128, 32], F32)
        rs = sb("n_rs", [128, 32], F32)
        epsc = sb("n_eps", [128, 1], F32)
        ptr = Rot([es.enter_context(nc.psum_tensor("n_ptr%d" % i + sfx, [128, D], BF16)) for i in range(2)])
        bjunk = Buf()
        beps = Buf()
        bhn = Buf()
        slots = [P.slot() for _ in range(8)]
        P.dve(lambda e: e.memset(epsc[:], EPS), writes=[beps])
        NB = 4
        pendB = []
        for i0 in range(0, 32, NB):
            grp = []
            bssg = Buf()
            for i in range(i0, i0 + NB):
                x_t, bx = xt.next()
                sl = slots[i % len(slots)]
                P.dma(sl, lambda e, x_t=x_t, i=i: e.dma_start(out=x_t[:], in_=xsrc[128 * i:128 * (i + 1), :]), writes=[bx])
                P.act(lambda e, x_t=x_t, i=i: e.activation(out=junk[:], in_=x_t[:], func=AF.Square, accum_out=ss[:, i:i + 1]),
                      reads=[bx], writes=[bjunk, bssg])
                grp.append((i, x_t, bx))
            bln = Buf()
            P.act(lambda e, i0=i0: e.activation(out=ln[:, i0:i0 + NB], in_=ss[:, i0:i0 + NB], func=AF.Ln, bias=epsc[:], scale=1.0 / D),
                  reads=[bssg, beps], writes=[bln])
            brs = Buf()
            P.act(lambda e, i0=i0: e.activation(out=rs[:, i0:i0 + NB], in_=ln[:, i0:i0 + NB], func=AF.Exp, scale=-0.5),
                  reads=[bln], writes=[brs])
            for i, x_t, bx in grp:
                def partA(i=i, x_t=x_t, bx=bx, brs=brs):
                    h_t, bh = hn0.next()
                    P.dve(lambda e: e.tensor_scalar(out=h_t[:], in0=x_t[:], scalar1=rs[:, i:i + 1], scalar2=None, op0=ALU.mult),
                          reads=[bx, brs], writes=[bh])
                    p_t, bp = ptr.next()
                    for kc in range(8):
                        P.pe(lambda e, kc=kc: e.transpose(p_t[:, kc * 128:(kc + 1) * 128], h_t[:, kc * 128:(kc + 1) * 128], ident[:]),
                             reads=[bh], writes=[bp])

                    def partB():
                        P.dve(lambda e: e.tensor_copy(out=FV(hnT, 0, 128, 128 * i, [[S, 8], [1, 128]]),
                                                      in_=FV(p_t, 0, 128, 0, [[128, 8], [1, 128]])), reads=[bp], writes=[bhn])
                    return partB
                pendB.append(partA())
                if len(pendB) > 1:
                    pendB.pop(0)()
        while pendB:
            pendB.pop(0)()
        if DBG.get("hnT"):
            dump(P, "hnT", hnT, [bhn])
            dump(P, "rs", rs, [bhn])
        P.emit()


def phase_outproj(sync, xsrc, wo_dram, ysc, out):
    nc = sync.nc
    P = Prog(sync).begin()
    with contextlib.ExitStack() as es:
        sfx = uid()

        def sb(name, shape, dt):
            return es.enter_context(nc.sbuf_tensor(name + sfx, shape, dt))
        wst = Rot([sb("o_wst%d" % i, [128, D], F32) for i in range(2)])
        wo = sb("o_wo", [128, 8 * D], BF16)
        bwo = Buf()
        xt = Rot([sb("o_xt%d" % i, [128, D], F32) for i in range(3)])
        ot = Rot([sb("o_ot%d" % i, [128, D], F32) for i in range(2)])
        yt = Rot([sb("o_yt%d" % i, [128, 8 * 512], BF16) for i in range(2)])
        po = Rot([es.enter_context(nc.psum_tensor("o_po%d" % i + sfx, [128, 512], F32)) for i in range(4)])
        s_w = [P.slot() for _ in range(2)]
        s_x = [P.slot() for _ in range(3)]
        s_y = [P.slot() for _ in range(2)]
        s_o = [P.slot() for _ in range(2)]
        for kc in range(8):
            w_t, bw = wst.next()
            P.dma(s_w[kc % 2], lambda e, w_t=w_t, kc=kc: e.dma_start(out=w_t[:], in_=wo_dram[kc * 128:(kc + 1) * 128, :]), writes=[bw])
            if kc % 2 == 0:
                P.dve(lambda e, w_t=w_t, kc=kc: e.tensor_copy(out=wo[:, kc * D:(kc + 1) * D], in_=w_t[:]), reads=[bw], writes=[bwo])
            else:
                P.act(lambda e, w_t=w_t, kc=kc: e.activation(out=wo[:, kc * D:(kc + 1) * D], in_=w_t[:], func=AF.Copy), reads=[bw], writes=[bwo])
        xts = {}
        yts = {}

        def load_x(i):
            x_t, bx = xt.next()
            P.dma(s_x[i % 3], lambda e: e.dma_start(out=x_t[:], in_=xsrc[128 * i:128 * (i + 1), :]), writes=[bx])
            xts[i] = (x_t, bx)

        def load_y(tb):
            y_t, by = yt.next()
            P.dma(s_y[tb % 2], lambda e: e.dma_start(
                out=FV(y_t, 0, 128, 0, [[512, 8], [1, 512]]),
                in_=ysc[:, :, tb * 512:(tb + 1) * 512].rearrange("k p t -> p k t")), writes=[by])
            yts[tb] = (y_t, by)
        load_y(0)
        load_x(0)
        load_x(1)
        for i in range(32):
            if i + 2 < 32:
                load_x(i + 2)
            if i % 4 == 0 and i // 4 + 1 < 8:
                load_y(i // 4 + 1)
            y_t, by = yts[i // 4]
            x_t, bx = xts[i]
            o_t, bo = ot.next()
            for nb in range(2):
                p_t, bp = po.next()
                for kc in range(8):
                    P.pe(lambda e, p_t=p_t, y_t=y_t, kc=kc, nb=nb, i=i: e.matmul(
                        p_t[:], lhsT=y_t[:, kc * 512 + (i % 4) * 128: kc * 512 + (i % 4) * 128 + 128],
                        rhs=wo[:, kc * D + nb * 512: kc * D + nb * 512 + 512], start=(kc == 0), stop=(kc == 7)),
                        reads=[by, bwo], writes=[bp])
                P.dve(lambda e, p_t=p_t, x_t=x_t, o_t=o_t, nb=nb: e.tensor_tensor(
                    out=o_t[:, nb * 512:(nb + 1) * 512], in0=p_t[:], in1=x_t[:, nb * 512:(nb + 1) * 512], op=ALU.add),
                    reads=[bp, bx], writes=[bo])
            P.dma(s_o[i % 2], lambda e, o_t=o_t, i=i: e.dma_start(out=out[128 * i:128 * (i + 1), :], in_=o_t[:]), reads=[bo])
        P.emit()


def qk_geometry(d):
    L = S // d
    return L, L // 128


def merge_steps(a, b):
    out = []
    na, nb = len(a), len(b)
    if na == 0 or nb == 0:
        return list(a) + list(b)
    ia = ib = 0
    while ia < na or ib < nb:
        if ib >= nb or (ia < na and ia * nb <= ib * na):
            out.append(a[ia]); ia += 1
        else:
            out.append(b[ib]); ib += 1
    return out


def phase_attn(sync, layer, hnT, dr, ysc):
    nc = sync.nc
    P = Prog(sync).begin()
    isA = layer == "A"
    w_dram, wg_dram, bias_dram = dr["w"], dr["wg"], dr["bias"]
    units = [(hp, g) for hp in range(8) for g in ((0, 1, 2) if isA else (0,))]
    with contextlib.ExitStack() as es:
        sfx = uid()

        def sb(name, shape, dt):
            return es.enter_context(nc.sbuf_tensor(name + sfx, shape, dt))
        ngc = sb("a_ngc", [128, 8], F32)
        gq = sb("a_gq", [128, 3], F32)
        gk = sb("a_gk", [128, 3], F32)
        epsc = sb("a_eps", [128, 1], F32)
        blk = sb("a_blk", [128, 128], BF16)
        bconst = Buf()
        s_c = P.slot()
        P.dma(s_c, lambda e: e.dma_start(out=ngc[:], in_=dr["ng"][:, :]), writes=[bconst])
        P.dma(s_c, lambda e: e.dma_start(out=gq[:], in_=dr["gq"][:, :]), writes=[bconst])
        P.dma(s_c, lambda e: e.dma_start(out=gk[:], in_=dr["gk"][:, :]), writes=[bconst])
        P.dve(lambda e: e.memset(epsc[:], EPS), writes=[bconst])
        P.dve(lambda e: e.memset(blk[:], 0.0), reads=[bconst], writes=[bconst])
        P.dve(lambda e: e.memset(blk[0:64, 0:64], 1.0 / 64), reads=[bconst], writes=[bconst])
        P.dve(lambda e: e.memset(blk[64:128, 64:128], 1.0 / 64), reads=[bconst], writes=[bconst])
        P.dve(lambda e: e.tensor_scalar(out=gq[:], in0=gq[:], scalar1=0.125, scalar2=None, op0=ALU.mult),
              reads=[bconst], writes=[bconst])
        qTr = Rot([sb("a_qT%d" % i, [128, S], BF16) for i in range(2)])
        kTr = Rot([sb("a_kT%d" % i, [128, S], BF16) for i in range(2)])
        Vt = sb("a_V", [128, 32 * 192], BF16)
        bV = Buf()
        P.dve(lambda e: e.memset(FV(Vt, 0, 128, 64, [[192, 32], [1, 64]]), 1.0), writes=[bV])
        sgT = sb("a_sgT", [128, S], BF16)
        bsg = Buf()
        if isA:
            acc = [sb("a_acc%d" % h, [128, S], F32) for h in range(2)]
            bacc = [Buf(), Buf()]
        wst = Rot([sb("a_wst%d" % i, [128, 384], F32) for i in range(4)])
        s_w = [P.slot() for _ in range(4)]
        wb = Rot([sb("a_wb%d" % i, [128, 8 * 384], BF16) for i in range(2)])
        wgb = sb("a_wgb", [128, 8 * 128], BF16)
        bwg = Buf()
        ebw = 2 * 256 if isA else dr["ebw"]
        ebst = Rot([sb("a_ebst%d" % i, [128, 512 if isA else 768], F32) for i in range(2)])
        s_eb = [P.slot() for _ in range(2)]
        eb = Rot([sb("a_eb%d" % i, [128, ebw], BF16) for i in range(2)])
        sq = Rot([sb("a_sq%d" % i, [128, 512], BF16) for i in range(2)])
        lnb = Rot([sb("a_ln%d" % i, [128, 512], F32) for i in range(2)])
        rstd = Rot([sb("a_rstd%d" % i, [128, 512], F32) for i in range(2)])
        ew = 512 if isA else 768
        ex = Rot([sb("a_ex%d" % i, [128, ew], BF16) for i in range(4 if isA else 3)])
        pT = Rot([sb("a_pT%d" % i, [128, ew], BF16) for i in range(8 if isA else 7)])
        rec = Rot([sb("a_rec%d" % i, [128, 512], F32) for i in range(1 if isA else 2)])
        tmp = Rot([sb("a_tmp%d" % i, [128, 512], F32) for i in range(1)])
        if isA:
            yT = Rot([sb("a_yT%d" % i, [128, 1024], BF16) for i in range(2)])
            s_y = [P.slot() for _ in range(2)]
        else:
            yTB = [Rot([sb("b_yT%d_%d" % (h, i), [128, 1024], BF16) for i in range(2)]) for h in range(2)]
            s_yB = [[P.slot() for _ in range(2)] for h in range(2)]
        print("phase_attn", layer, "sbuf bytes remaining", nc.sbuf_bytes_remaining)
        banks = [es.enter_context(nc.psum_tensor("a_ps%d" % i + sfx, [128, 512], F32)) for i in range(8)]
        bbank = [Buf() for _ in range(8)]

        def bankrot(ids):
            r = Rot([banks[i] for i in ids])
            r.bufs = [bbank[i] for i in ids]
            return r
        pq = bankrot([0, 1, 7])
        pss = bankrot([2])
        if isA:
            ps_h = [bankrot([3]), bankrot([4])]
            po_h = [bankrot([5]), bankrot([6])]
            pv = bankrot([3, 4])
            pg = bankrot([5, 6])
        else:
            pq = bankrot([0, 1])
            psA = bankrot([3, 4])
            bRem = [Buf(), Buf()]
            po = bankrot([6, 7])
            pv = bankrot([3, 4])
            pg = bankrot([6, 7])

        def wload_steps(u):
            hp, g = u
            w_b, bw = wb.next()
            pend = []
            steps = []

            def conv(w_t, bs, kc):
                P.dve(lambda e: e.tensor_scalar(
                    out=w_b[:, kc * 384:(kc + 1) * 384], in0=w_t[:], scalar1=ngc[:, kc:kc + 1], scalar2=None, op0=ALU.mult),
                    reads=[bs, bconst], writes=[bw])
            for kc in range(8):
                def step(kc=kc):
                    w_t, bs = wst.next()
                    sl = s_w[(wst.i - 1) % 4]
                    P.dma(sl, lambda e: e.dma_start(out=w_t[:], in_=w_dram[hp, g, kc * 128:(kc + 1) * 128, :]), writes=[bs])
                    pend.append((w_t, bs, kc))
                    if len(pend) > 2:
                        conv(*pend.pop(0))
                steps.append(step)

            def flush():
                while pend:
                    conv(*pend.pop(0))
            steps.append(flush)
            return (w_b, bw), steps

        def gload_steps(hp):
            pend = []
            steps = []

            def conv(w_t, bs, kc):
                P.dve(lambda e: e.tensor_scalar(
                    out=wgb[:, kc * 128:(kc + 1) * 128], in0=w_t[:, 0:128], scalar1=ngc[:, kc:kc + 1], scalar2=None, op0=ALU.mult),
                    reads=[bs, bconst], writes=[bwg])
            for kc in range(8):
                def step(kc=kc):
                    w_t, bs = wst.next()
                    sl = s_w[(wst.i - 1) % 4]
                    P.dma(sl, lambda e: e.dma_start(out=w_t[:, 0:128], in_=wg_dram[hp, kc * 128:(kc + 1) * 128, :]), writes=[bs])
                    pend.append((w_t, bs, kc))
                    if len(pend) > 2:
                        conv(*pend.pop(0))
                steps.append(step)

            def flush():
                while pend:
                    conv(*pend.pop(0))
            steps.append(flush)
            return steps

        def eb_dma_A(u):
            hp, g = u
            st_, bs = ebst.next()
            sl = s_eb[(ebst.i - 1) % 2]
            P.dma(sl, lambda e: e.dma_start(out=st_[:, 0:512], in_=bias_dram[hp, g, :, :]), writes=[bs])
            return st_, bs

        def eb_conv_A(st_, bs):
            e_t, be = eb.next()
            P.act(lambda e: e.activation(out=e_t[:, 0:512], in_=st_[:, 0:512], func=AF.Exp), reads=[bs], writes=[be])
            return e_t, be

        def eb_steps_B(hp, h):
            e_t, be = eb.next()
            ebw1 = dr["ebw"]
            pieces = []
            off = 0
            while off < ebw1:
                n = min(768, ebw1 - off)
                pieces.append((off, n))
                off += n
            pend = []
            steps = []

            def conv(st_, bs, off, n):
                P.act(lambda e: e.activation(out=e_t[:, off: off + n], in_=st_[:, 0:n], func=AF.Exp), reads=[bs], writes=[be])
            for off, n in pieces:
                def step(off=off, n=n):
                    st_, bs = ebst.next()
                    sl = s_eb[(ebst.i - 1) % 2]
                    P.dma(sl, lambda e: e.dma_start(out=st_[:, 0:n], in_=bias_dram[2 * hp + h, :, off:off + n]), writes=[bs])
                    pend.append((st_, bs, off, n))
                    if len(pend) > 1:
                        conv(*pend.pop(0))
                steps.append(step)

            def flush():
                while pend:
                    conv(*pend.pop(0))
            steps.append(flush)
            return (e_t, be), steps

        def gate_steps():
            steps = []
            for tb in range(8):
                def step(tb=tb):
                    p_t, bp = pg.next()
                    for kc in range(8):
                        P.pe(lambda e, p_t=p_t, kc=kc: e.matmul(
                            p_t[:], lhsT=wgb[:, kc * 128:(kc + 1) * 128], rhs=hnT[:, kc * S + tb * 512: kc * S + tb * 512 + 512],
                            start=(kc == 0), stop=(kc == 7)), reads=[bwg], writes=[bp])
                    P.act(lambda e, p_t=p_t: e.activation(out=sgT[:, tb * 512:(tb + 1) * 512], in_=p_t[:], func=AF.Silu),
                          reads=[bp], writes=[bsg])
                steps.append(step)
            return steps

        def qk_steps(u, w_b, bw, q_t, bq, k_t, bk):
            hp, g = u
            d = DILS[g] if isA else 1
            L = S // d
            pend = []

            def tail(p_t, bp, s_t, bs, which, tb):
                ss_t, bss = pss.next()
                P.pe(lambda e: e.matmul(ss_t[:], lhsT=blk[:], rhs=s_t[:], start=True, stop=True),
                     reads=[bs, bconst], writes=[bss])
                l_t, bl = lnb.next()
                P.act(lambda e: e.activation(out=l_t[:], in_=ss_t[:], func=AF.Ln, bias=epsc[:], scale=1.0),
                      reads=[bss, bconst], writes=[bl])
                r_t, br = rstd.next()
                P.act(lambda e: e.activation(out=r_t[:], in_=l_t[:], func=AF.Exp, scale=-0.5), reads=[bl], writes=[br])
                if which == 0:
                    n = 512 // d
                    P.dve(lambda e: e.scalar_tensor_tensor(
                        out=FV(q_t, 0, 128, tb * n, [[L, d], [1, n]]),
                        in0=FV(p_t, 0, 128, 0, [[1, d], [d, n]]), scalar=gq[:, g:g + 1],
                        in1=FV(r_t, 0, 128, 0, [[1, d], [d, n]]), op0=ALU.mult, op1=ALU.mult),
                        reads=[bp, br, bconst], writes=[bq])
                else:
                    P.dve(lambda e: e.scalar_tensor_tensor(
                        out=k_t[:, tb * 512:(tb + 1) * 512], in0=p_t[:], scalar=gk[:, g:g + 1], in1=r_t[:],
                        op0=ALU.mult, op1=ALU.mult), reads=[bp, br, bconst], writes=[bk])

            steps = []
            for t in range(16):
                def step(t=t):
                    which, tb = divmod(t, 8)
                    p_t, bp = pq.next()
                    for kc in range(8):
                        P.pe(lambda e, kc=kc: e.matmul(
                            p_t[:], lhsT=w_b[:, kc * 384 + which * 128: kc * 384 + which * 128 + 128],
                            rhs=hnT[:, kc * S + tb * 512: kc * S + tb * 512 + 512], start=(kc == 0), stop=(kc == 7)),
                            reads=[bw], writes=[bp])
                    s_t, bs = sq.next()
                    P.act(lambda e: e.activation(out=s_t[:], in_=p_t[:], func=AF.Square), reads=[bp], writes=[bs])
                    if pend:
                        tail(*pend.pop())
                    pend.append((p_t, bp, s_t, bs, which, tb))
                steps.append(step)

            def flush():
                tail(*pend.pop())
            steps.append(flush)
            return steps

        def v_steps(u, w_b, bw):
            hp, g = u
            d = DILS[g] if isA else 1
            L, nC = qk_geometry(d)
            steps = []
            for c0 in range(0, 32, 4):
                def step(c0=c0):
                    p_t, bp = pv.next()
                    for cc in range(4):
                        c = c0 + cc
                        r, i = divmod(c, nC)
                        t0 = r + d * 128 * i
                        for kc in range(8):
                            P.pe(lambda e, kc=kc, cc=cc, t0=t0: e.matmul(
                                p_t[:, cc * 128:(cc + 1) * 128],
                                lhsT=FV(hnT, 0, 128, kc * S + t0, [[d, 128]]),
                                rhs=w_b[:, kc * 384 + 256: kc * 384 + 384], start=(kc == 0), stop=(kc == 7)),
                                reads=[bw], writes=[bp])
                    P.act(lambda e: e.activation(
                        out=FV(Vt, 0, 128, c0 * 192, [[192, 4], [128, 2], [1, 64]]),
                        in_=FV(p_t, 0, 128, 0, [[128, 4], [64, 2], [1, 64]]), func=AF.Copy), reads=[bp], writes=[bV])
                steps.append(step)
            return steps

        def attn_steps_A(u, q_t, bq, k_t, bk, e_t, be):
            hp, g = u
            d = DILS[g]
            first = g == 0
            L, nC = qk_geometry(d)
            pend = []
            steps = []

            def acc_out(h, srcf, sbuf, dstf):
                if first:
                    P.dve(lambda e: e.tensor_copy(out=dstf(), in_=srcf()), reads=[sbuf], writes=[bacc[h]])
                else:
                    P.dve(lambda e: e.tensor_tensor(out=dstf(), in0=srcf(), in1=dstf(), op=ALU.add), reads=[sbuf], writes=[bacc[h]])

            def make_tail(r, m, ptl, state):
                def tail():
                    for h in range(2):
                        vof = 0 if h == 0 else 64
                        acc_h = acc[h]
                        key = "po%d" % h

                        def pcol(i, lo):
                            p_t, bp = ptl[(h, i // 2)]
                            return p_t, bp, (i % 2) * 256 + lo

                        def mm(o_t, bo, col, n, chunk, pa, bpa, ca, start, stop, vof=vof):
                            P.pe(lambda e: e.matmul(
                                o_t[:, col:col + n], lhsT=Vt[:, chunk * 192 + vof: chunk * 192 + vof + 128],
                                rhs=pa[:, ca:ca + n], start=start, stop=stop), reads=[bV, bpa], writes=[bo])
                        if d == 16:
                            if r % 2 == 0:
                                state[key] = po_h[h].next()
                            o_t, bo = state[key]
                            base = (r % 2) * 256
                            c1 = r * nC
                            pa, bpa, ca = pcol(0, 64)
                            mm(o_t, bo, base, 64, c1, pa, bpa, ca, True, True)
                            pa, bpa, ca = pcol(0, 128)
                            mm(o_t, bo, base + 64, 128, c1, pa, bpa, ca, True, False)
                            pa, bpa, ca = pcol(1, 0)
                            mm(o_t, bo, base + 64, 128, c1 + 1, pa, bpa, ca, False, True)
                            pa, bpa, ca = pcol(1, 128)
                            mm(o_t, bo, base + 192, 64, c1 + 1, pa, bpa, ca, True, True)
                            if r % 2 == 1:
                                acc_out(h, lambda o_t=o_t: o_t[:, 0:512], bo,
                                        lambda acc_h=acc_h: FV(acc_h, 0, 128, r - 1, [[1, 2], [16, 256]]))
                            continue
                        if m == 0:
                            state[key] = po_h[h].next()
                            o_t, bo = state[key]
                            pa, bpa, ca = pcol(0, 64)
                            mm(o_t, bo, 64, 64, r * nC, pa, bpa, ca, True, True)
                        for jj in ([2 * m - 1] if m > 0 else []) + ([2 * m] if 2 * m <= nC - 2 else []):
                            if jj < 3:
                                slot = jj + 1
                            else:
                                slot = (jj - 3) % 4
                                if slot == 0:
                                    state[key] = po_h[h].next()
                            o_t, bo = state[key]
                            pa, bpa, ca = pcol(jj, 128)
                            pb_, bpb, cb = pcol(jj + 1, 0)
                            c1 = r * nC + jj
                            mm(o_t, bo, slot * 128, 128, c1, pa, bpa, ca, True, False)
                            mm(o_t, bo, slot * 128, 128, c1 + 1, pb_, bpb, cb, False, True)
                            if jj == 2:
                                acc_out(h, lambda o_t=o_t: o_t[:, 64:512], bo,
                                        lambda acc_h=acc_h: FV(acc_h, 0, 128, r, [[d, 448]]))
                            elif jj > 2 and slot == 3:
                                m0 = 64 + 128 * (jj - 3)
                                acc_out(h, lambda o_t=o_t: o_t[:, 0:512], bo,
                                        lambda acc_h=acc_h, m0=m0: FV(acc_h, 0, 128, r + d * m0, [[d, 512]]))
                        if m == nC // 2 - 1:
                            assert (nC - 2 - 3) % 4 == 3
                            o_t, bo = po_h[h].next()
                            pa, bpa, ca = pcol(nC - 1, 128)
                            mm(o_t, bo, 0, 64, r * nC + nC - 1, pa, bpa, ca, True, True)
                            acc_out(h, lambda o_t=o_t: o_t[:, 0:64], bo,
                                    lambda acc_h=acc_h: FV(acc_h, 0, 128, r + d * (L - 64), [[d, 64]]))
                return tail

            state = {}
            for r in range(d):
                ptl = {}
                for m in range(nC // 2):
                    def step(r=r, m=m, ptl=ptl, state=state):
                        stl = [ps_h[0].next(), ps_h[1].next()]
                        for cc in range(2):
                            i = 2 * m + cc
                            qlo = max(0, 128 * i - 64)
                            qhi = min(L, 128 * i + 192)
                            lo = qlo - (128 * i - 64)
                            n = qhi - qlo
                            for h in range(2):
                                s_t, bs = stl[h]
                                P.pe(lambda e, s_t=s_t, cc=cc, lo=lo, n=n, qlo=qlo, i=i, h=h: e.matmul(
                                    s_t[:, cc * 256 + lo: cc * 256 + lo + n],
                                    lhsT=FV(k_t, 64 * h, 64 * h + 64, r + d * 128 * i, [[d, 128]]),
                                    rhs=q_t[64 * h:64 * h + 64, r * L + qlo: r * L + qlo + n], start=True, stop=True),
                                    reads=[bk, bq], writes=[bs])
                        for h in range(2):
                            s_t, bs = stl[h]
                            x_t, bx = ex.next()
                            P.act(lambda e, x_t=x_t, s_t=s_t: e.activation(out=x_t[:, 0:512], in_=s_t[:], func=AF.Exp), reads=[bs], writes=[bx])
                            p_t, bp = pT.next()
                            P.dve(lambda e, p_t=p_t, x_t=x_t, h=h: e.tensor_tensor(
                                out=FV(p_t, 0, 128, 0, [[256, 2], [1, 256]]), in0=FV(x_t, 0, 128, 0, [[256, 2], [1, 256]]),
                                in1=FV(e_t, 0, 128, h * 256, [[0, 2], [1, 256]]), op=ALU.mult), reads=[bx, be], writes=[bp])
                            ptl[(h, m)] = (p_t, bp)
                        if pend:
                            pend.pop()()
                        pend.append(make_tail(r, m, ptl, state))
                    steps.append(step)

            def flush():
                pend.pop()()
            steps.append(flush)
            return steps

        def normalize_steps_A(hp):
            steps = []
            st = {}
            for tb in range(8):
                def step(tb=tb):
                    q4, hb = divmod(tb, 2)
                    if hb == 0:
                        st["y"] = yT.next()
                    y_t, by = st["y"]
                    cs = slice(tb * 512, (tb + 1) * 512)
                    r_t, br = rec.next()
                    t_t, bt = tmp.next()
                    P.act(lambda e: e.activation(out=r_t[0:64, :], in_=acc[0][64:128, cs], func=AF.Ln), reads=[bacc[0]], writes=[br])
                    P.act(lambda e: e.activation(out=r_t[64:128, :], in_=acc[1][0:64, cs], func=AF.Ln), reads=[bacc[1]], writes=[br])
                    P.act(lambda e: e.activation(out=r_t[:], in_=r_t[:], func=AF.Exp, scale=-1.0), reads=[br], writes=[br])
                    P.dve(lambda e: e.tensor_tensor(out=t_t[0:64, :], in0=acc[0][0:64, cs], in1=r_t[0:64, :], op=ALU.mult),
                          reads=[br], writes=[bt])
                    P.dve(lambda e: e.tensor_tensor(out=t_t[64:128, :], in0=acc[1][64:128, cs], in1=r_t[64:128, :], op=ALU.mult),
                          reads=[br], writes=[bt])
                    P.dve(lambda e: e.tensor_tensor(out=y_t[:, hb * 512:(hb + 1) * 512], in0=t_t[:], in1=sgT[:, cs], op=ALU.mult),
                          reads=[bt, bsg], writes=[by])
                    if hb == 1:
                        sl = s_y[(yT.i - 1) % 2]
                        P.dma(sl, lambda e: e.dma_start(out=ysc[hp, :, q4 * 1024:(q4 + 1) * 1024], in_=y_t[:]), reads=[by])
                steps.append(step)
            return steps

        def attn_steps_B(hp, h, q_t, bq, k_t, bk, e_t, be):
            tiles = dr["tiles"]
            hs = slice(64 * h, 64 * h + 64)
            vof = 0 if h == 0 else 64
            num = slice(0, 64) if h == 0 else slice(64, 128)
            den = slice(64, 128) if h == 0 else slice(0, 64)
            contribs = []
            for Q in range(32):
                full, part = [], []
                for R in range(32):
                    qlo, nr = tiles[R][1], tiles[R][2]
                    lo_r = max(qlo, 2 * Q)
                    hi_r = min(qlo + nr - 1, 2 * Q + 1)
                    if lo_r > hi_r:
                        continue
                    (full if hi_r - lo_r == 1 else part).append((R, lo_r, hi_r - lo_r + 1))
                assert full
                contribs.append(full + part)
            lastR = [max(R for R, _, _ in contribs[Q]) for Q in range(32)]
            ptl = {}
            st = dict(o=None, y=None)
            pend = []
            steps = []

            def do_block(Q):
                if Q % 4 == 0:
                    st["o"] = po.next()
                o_t, bo = st["o"]
                cl = contribs[Q]
                for n_, (R, row0, nrow) in enumerate(cl):
                    p_t, bp = ptl[R]
                    c0 = (row0 - tiles[R][1]) * 64
                    oc = (Q % 4) * 128 + (row0 - 2 * Q) * 64
                    nn = nrow * 64
                    P.pe(lambda e, p_t=p_t, c0=c0, oc=oc, nn=nn, R=R, first=(n_ == 0), last=(n_ == len(cl) - 1): e.matmul(
                        o_t[:, oc:oc + nn], lhsT=Vt[:, R * 192 + vof: R * 192 + vof + 128],
                        rhs=p_t[:, c0:c0 + nn], start=first, stop=last), reads=[bV, bp], writes=[bo])
                if Q % 4 == 3:
                    tb = Q // 4
                    cs = slice(tb * 512, (tb + 1) * 512)
                    if tb % 2 == 0:
                        st["y"] = yTB[h].next()
                    y_t, by = st["y"]
                    r_t, br = rec.next()
                    t_t, bt = tmp.next()
                    P.act(lambda e: e.activation(out=r_t[num, :], in_=o_t[den, :], func=AF.Ln), reads=[bo], writes=[br])
                    P.act(lambda e: e.activation(out=r_t[num, :], in_=r_t[num, :], func=AF.Exp, scale=-1.0), reads=[br], writes=[br])
                    P.dve(lambda e: e.tensor_tensor(out=t_t[num, :], in0=o_t[num, :], in1=r_t[num, :], op=ALU.mult),
                          reads=[bo, br], writes=[bt])
                    P.dve(lambda e: e.tensor_tensor(
                        out=y_t[num, (tb % 2) * 512:(tb % 2) * 512 + 512], in0=t_t[num, :], in1=sgT[num, cs], op=ALU.mult),
                        reads=[bt, bsg], writes=[by])
                    if tb % 2 == 1:
                        q4 = tb // 2
                        sl = s_yB[h][(yTB[h].i - 1) % 2]
                        P.dma(sl, lambda e: e.dma_start(
                            out=ysc[hp, num, q4 * 1024:(q4 + 1) * 1024], in_=y_t[num, :]), reads=[by])

            for R in range(32):
                def step(R=R):
                    toff, qlo, nr = tiles[R]
                    n = nr * 64
                    sA, bA = psA.next()
                    sB = banks[5]
                    bB = bbank[5]
                    n1 = min(n, 512)
                    P.pe(lambda e: e.matmul(
                        sA[:, 0:n1], lhsT=k_t[hs, R * 128:(R + 1) * 128], rhs=q_t[hs, qlo * 64: qlo * 64 + n1], start=True, stop=True),
                        reads=[bk, bq], writes=[bA])
                    x_t, bx = ex.next()
                    P.act(lambda e: e.activation(out=x_t[:, 0:n1], in_=sA[:, 0:n1], func=AF.Exp), reads=[bA], writes=[bx])
                    if n > 512:
                        n2 = n - 512
                        P.pe(lambda e: e.matmul(
                            sB[:, 0:n2], lhsT=k_t[hs, R * 128:(R + 1) * 128], rhs=q_t[hs, qlo * 64 + 512: qlo * 64 + 512 + n2], start=True, stop=True),
                            reads=[bk, bq], writes=[bB])
                        P.act(lambda e: e.activation(out=x_t[:, 512:512 + n2], in_=sB[:, 0:n2], func=AF.Exp), reads=[bB], writes=[bx])
                    p_t, bp = pT.next()
                    P.dve(lambda e: e.tensor_tensor(
                        out=p_t[:, 0:n], in0=x_t[:, 0:n], in1=e_t[:, toff: toff + n], op=ALU.mult),
                        reads=[bx, be], writes=[bp])
                    ptl[R] = (p_t, bp)
                    if pend:
                        pend.pop()()

                    def tail():
                        for Q in range(32):
                            if lastR[Q] == R:
                                do_block(Q)
                    pend.append(tail)
                steps.append(step)

            def flush():
                pend.pop()()
            steps.append(flush)
            return steps

        def run(steps):
            for st_ in steps:
                st_()

        dstop = DBG.get("stop")
        nU = len(units)
        wts = {}
        qk = {}
        ebd = {}
        ebB = {}
        wts[0], ws = wload_steps(units[0])
        run(ws)
        run(gload_steps(0))
        if isA:
            ebd[0] = eb_dma_A(units[0])
        if nU > 1:
            wts[1], ws1 = wload_steps(units[1])
        else:
            ws1 = []
        qk[0] = qTr.next() + kTr.next()
        run(merge_steps(qk_steps(units[0], *wts[0], *qk[0]), ws1))
        run(v_steps(units[0], *wts[0]))
        run(gate_steps())
        for ui, u in enumerate(units):
            hp, g = u
            nxt = units[ui + 1] if ui + 1 < nU else None
            wsteps = []
            if ui + 2 < nU:
                wts[ui + 2], wsteps = wload_steps(units[ui + 2])
            q_t, bq, k_t, bk = qk[ui]
            nsteps = []
            if nxt is not None:
                qk[ui + 1] = qTr.next() + kTr.next()
                nsteps = qk_steps(nxt, *wts[ui + 1], *qk[ui + 1])
            if isA:
                if nxt is not None:
                    ebd[ui + 1] = eb_dma_A(nxt)
                e_t, be = eb_conv_A(*ebd[ui])
                last = g == 2
                gsteps = gload_steps(hp + 1) if (g == 1 and hp + 1 < 8) else []
                run(merge_steps(merge_steps(attn_steps_A(u, q_t, bq, k_t, bk, e_t, be), nsteps), wsteps + gsteps))
                if last:
                    vs = v_steps(nxt, *wts[ui + 1]) if nxt is not None else []
                    run(merge_steps(normalize_steps_A(hp), vs))
                    if hp + 1 < 8:
                        run(gate_steps())
                elif nxt is not None:
                    run(v_steps(nxt, *wts[ui + 1]))
            else:
                gsteps = gload_steps(hp + 1) if hp + 1 < 8 else []
                if hp == 0:
                    ebB[(0, 0)], es0 = eb_steps_B(0, 0)
                    run(es0)
                ebB[(hp, 1)], es1 = eb_steps_B(hp, 1)
                a0 = attn_steps_B(hp, 0, q_t, bq, k_t, bk, *ebB[(hp, 0)])
                run(merge_steps(merge_steps(a0, nsteps[: len(nsteps) // 2]), wsteps + es1))
                es2 = []
                if hp + 1 < 8:
                    ebB[(hp + 1, 0)], es2 = eb_steps_B(hp + 1, 0)
                a1 = attn_steps_B(hp, 1, q_t, bq, k_t, bk, *ebB[(hp, 1)])
                run(merge_steps(merge_steps(a1, nsteps[len(nsteps) // 2:]), gsteps + es2))
                if nxt is not None:
                    run(v_steps(nxt, *wts[ui + 1]))
                if hp + 1 < 8:
                    run(gate_steps())
            if dstop == "hp0" and ((isA and g == 2) or not isA):
                break
        P.emit()


_T5_LUT = None


def _t5_bucket_np(rel):
    import math
    half, me = 16, 8
    ret = np.where(rel > 0, half, 0)
    n = np.abs(rel)
    nf = np.maximum(n, 1).astype(np.float32)
    large = me + (np.log(nf / np.float32(me)) / np.float32(math.log(1024 / me)) * np.float32(half - me)).astype(np.int32)
    large = np.minimum(large, half - 1)
    return ret + np.where(n < me, n, large)


def _bias_tiles_A(t5_bias):
    a = np.arange(128)[:, None]
    b = np.arange(256)[None, :]
    rel = a - b + 64
    valid = (b - a >= 0) & (b - a <= 128)
    out = np.empty((8, 3, 128, 2, 256), np.float32)
    for g, d in enumerate(DILS):
        idx = _t5_bucket_np(rel * d)
        for h in range(16):
            t = t5_bias[g * 16 + h][idx]
            out[h // 2, g, :, h % 2, :] = np.where(valid, t, np.float32(NEG))
    return out


def _geom_B():
    rows = 64
    r = np.arange(rows)
    rs = np.clip(r - 4, 0, rows - 8)
    c = np.arange(64)
    cs = np.clip(c - 8, 0, 64 - 16)
    tiles = []
    uniq = {}
    maps = []
    off = 0
    for R in range(32):
        krs = np.array([2 * R, 2 * R + 1])
        qrows = [q for q in range(rows) if (rs[q] <= krs[1]) and (rs[q] + 7 >= krs[0])]
        qlo, nr = qrows[0], len(qrows)
        assert qrows == list(range(qlo, qlo + nr))
        kr = np.repeat(krs, 64)[:, None]
        kc = np.tile(c, 2)[:, None]
        qr = np.repeat(np.arange(qlo, qlo + nr), 64)[None, :]
        qc = np.tile(c, nr)[None, :]
        valid = (kr >= rs[qr]) & (kr <= rs[qr] + 7) & (kc >= cs[qc]) & (kc < cs[qc] + 16)
        ridx = np.clip(kr - qr + 7, 0, 14)
        cidx = np.clip(kc - qc, -15, 15) + 15
        key = (nr, valid.tobytes(), ridx.tobytes())
        if key not in uniq:
            uniq[key] = off
            maps.append((off, ridx + 0 * cidx, cidx + 0 * ridx, valid))
            off += nr * 64
        tiles.append((uniq[key], qlo, nr))
    return tiles, maps, off


def _bias_tiles_B(rpb, maps, ebw):
    out = np.empty((16, 128, ebw), np.float32)
    for off, ridx, cidx, valid in maps:
        n = valid.shape[1]
        for h in range(16):
            out[h, :, off:off + n] = np.where(valid, rpb[h][ridx, cidx], np.float32(NEG))
    return out


def _unit_weights(w_in, ngroups):
    wu = np.empty((8, ngroups, D, 384), np.float32)
    for hp in range(8):
        for g in range(ngroups):
            for j in range(3):
                c0 = g * 3072 + j * 1024 + hp * 128
                wu[hp, g, :, j * 128:(j + 1) * 128] = w_in[:, c0:c0 + 128]
    gc = ngroups * 3072
    wg = np.ascontiguousarray(w_in[:, gc:gc + 1024].reshape(D, 8, 128).transpose(1, 0, 2))
    return wu, wg


_GEOM_B = None


def build_nc(layers="AB"):
    global _GEOM_B
    if _GEOM_B is None:
        _GEOM_B = _geom_B()
    tilesB, mapsB, ebwB = _GEOM_B
    nc = bass.Bass("TRN2", target_bir_lowering=False)

    def din(name, shape, dt=F32):
        return nc.dram_tensor(name, list(shape), dt, kind="ExternalInput").ap()
    x = din("x", [S, D])
    ident_d = din("ident_d", [128, 128])
    drA = dict(w=din("wA", [8, 3, D, 384]), wg=din("wgA", [8, D, 128]), bias=din("biasA", [8, 3, 128, 512]),
               ng=din("ngA", [128, 8]), gq=din("gqA", [128, 3]), gk=din("gkA", [128, 3]))
    woA = din("woA", [D, D])
    drB = dict(w=din("wB", [8, 1, D, 384]), wg=din("wgB", [8, D, 128]), bias=din("biasB", [16, 128, ebwB]),
               ng=din("ngB", [128, 8]), gq=din("gqB", [128, 3]), gk=din("gkB", [128, 3]), tiles=tilesB, ebw=ebwB)
    woB = din("woB", [D, D])
    out = nc.dram_tensor("out", [S, D], F32, kind="ExternalOutput").ap()
    ysc = nc.dram_tensor("ysc", [8, 128, S], BF16).ap()
    with contextlib.ExitStack() as es:
        sync = Sync(nc, es)
        hnT = es.enter_context(nc.sbuf_tensor("hnT", [128, 8 * S], BF16))
        ident = es.enter_context(nc.sbuf_tensor("ident", [128, 128], BF16))
        identf = es.enter_context(nc.sbuf_tensor("identf", [128, 128], F32))
        P0 = Prog(sync).begin()
        bi = Buf()
        P0.dma(P0.slot(), lambda e: e.dma_start(out=identf[:], in_=ident_d[:, :]), writes=[bi])
        P0.dve(lambda e: e.tensor_copy(out=ident[:], in_=identf[:]), reads=[bi], writes=[bi])
        P0.emit()
        src = x
        if "A" in layers:
            phase_norm(sync, es, src, hnT, ident)
            if DBG.get("stop") != "norm":
                phase_attn(sync, "A", hnT, drA, ysc)
            if not DBG.get("stop"):
                phase_outproj(sync, src, woA, ysc, out)
            src = out
        if "B" in layers:
            phase_norm(sync, es, src, hnT, ident)
            phase_attn(sync, "B", hnT, drB, ysc)
            phase_outproj(sync, src, woB, ysc, out)
    return nc


def host_inputs(norm_gain, a_w_in, a_w_out, a_q_gain, a_k_gain, t5_bias, b_w_in, b_w_out, b_q_gain, b_k_gain, b_rpb):
    global _GEOM_B
    if _GEOM_B is None:
        _GEOM_B = _geom_B()
    tilesB, mapsB, ebwB = _GEOM_B
    f = lambda a: np.ascontiguousarray(np.asarray(a, dtype=np.float32))
    wA, wgA = _unit_weights(f(a_w_in)[0], 3)
    wB, wgB = _unit_weights(f(b_w_in)[0], 1)
    ng = f(norm_gain)

    def gcol(gn):
        gn = f(gn).reshape(-1, 64)
        o = np.ones((128, 3), np.float32)
        for g in range(gn.shape[0]):
            o[:, g] = np.tile(gn[g], 2)
        return o
    shared = dict(
        ident_d=np.eye(128, dtype=np.float32),
        wA=wA, wgA=wgA, woA=f(a_w_out)[0],
        biasA=np.ascontiguousarray(_bias_tiles_A(f(t5_bias)).reshape(8, 3, 128, 512)),
        ngA=np.ascontiguousarray(ng[0].reshape(8, 128).T), gqA=gcol(a_q_gain[0]), gkA=gcol(a_k_gain[0]),
        wB=wB, wgB=wgB, woB=f(b_w_out)[0],
        biasB=_bias_tiles_B(f(b_rpb)[0], mapsB, ebwB),
        ngB=np.ascontiguousarray(ng[1].reshape(8, 128).T), gqB=gcol(b_q_gain), gkB=gcol(b_k_gain),
    )
    return shared


def kernel(x, norm_gain, a_w_in, a_w_out, a_q_gain, a_k_gain, t5_bias, b_w_in, b_w_out, b_q_gain, b_k_gain, b_rpb):
    x = np.ascontiguousarray(np.asarray(x, dtype=np.float32))
    shared = host_inputs(norm_gain, a_w_in, a_w_out, a_q_gain, a_k_gain, t5_bias, b_w_in, b_w_out, b_q_gain, b_k_gain, b_rpb)
    nc = build_nc("AB")
    in_maps = [dict(shared, x=x[c]) for c in range(NCORES)]
    res = run_bass_kernel_spmd(nc, in_maps, core_ids=list(range(NCORES)))
    return np.stack([np.asarray(r["out"], dtype=np.float32) for r in res.results], axis=0)
```

```python
import contextlib
import numpy as np
import concourse.bass as bass
import concourse.mybir as mybir
from concourse.bass_utils import run_bass_kernel_spmd

F32 = mybir.dt.float32
BF16 = mybir.dt.bfloat16
AF = mybir.ActivationFunctionType
ALU = mybir.AluOpType

S = 4096
D = 1024
NCORES = 8
DILS = (1, 4, 16)
EPS = 1e-6
NEG = -30000.0


class Buf:
    __slots__ = ("name", "lw", "rd")

    def __init__(self, name=""):
        self.name = name
        self.lw = None
        self.rd = []


class DmaSlot:
    __slots__ = ("sem", "count", "name")

    def __init__(self, name):
        self.name = name
        self.sem = None
        self.count = 0


class Op:
    __slots__ = ("eng", "fn", "deps", "slot", "signal", "tick", "semkey", "known", "idx")


COMPUTE = ("pe", "act", "dve", "pool")
ENGS = ("pe", "act", "dve", "pool", "sp")


class Sync:
    def __init__(self, nc, es, nslots=40):
        self.nc = nc
        self.esem = {e: es.enter_context(nc.semaphore("sem_" + e)) for e in COMPUTE}
        self.tick = {e: 0 for e in COMPUTE}
        self.slots = []
        for i in range(nslots):
            s = DmaSlot("dq%d" % i)
            s.sem = es.enter_context(nc.semaphore(s.name))
            self.slots.append(s)


class Prog:
    def __init__(self, sync):
        self.sync = sync
        self.nc = sync.nc
        self.ops = []
        self.nslot = 0

    def slot(self, name=""):
        s = self.sync.slots[self.nslot]
        self.nslot += 1
        return s

    def add(self, eng, fn, reads=(), writes=(), slot=None):
        op = Op()
        op.eng = eng
        op.fn = fn
        op.slot = slot
        op.signal = slot is not None
        op.tick = None
        op.idx = len(self.ops)
        deps = {}
        is_dma = slot is not None
        for b in reads:
            w = b.lw
            if w is not None:
                if is_dma or w.slot is not None or w.eng != eng or eng != "pe":
                    deps[w.idx] = w
        for b in writes:
            w = b.lw
            if w is not None and (is_dma or w.slot is not None or w.eng != eng):
                deps[w.idx] = w
            for r in b.rd:
                if is_dma or r.slot is not None or r.eng != eng:
                    deps[r.idx] = r
        for b in reads:
            b.rd.append(op)
        for b in writes:
            b.lw = op
            b.rd = []
        op.deps = list(deps.values())
        for d in op.deps:
            d.signal = True
        self.ops.append(op)
        return op

    def pe(self, fn, reads=(), writes=()):
        return self.add("pe", fn, reads, writes)

    def act(self, fn, reads=(), writes=()):
        return self.add("act", fn, reads, writes)

    def dve(self, fn, reads=(), writes=()):
        return self.add("dve", fn, reads, writes)

    def pool(self, fn, reads=(), writes=()):
        return self.add("pool", fn, reads, writes)

    def dma(self, slot, fn, reads=(), writes=()):
        return self.add("sp", fn, reads, writes, slot=slot)

    def emit(self):
        nc = self.nc
        sy = self.sync
        for op in self.ops:
            if op.slot is not None:
                op.slot.count += 16
                op.tick = op.slot.count
                op.semkey = op.slot
            elif op.signal:
                sy.tick[op.eng] += 1
                op.tick = sy.tick[op.eng]
                op.semkey = op.eng
        base = {}
        for e in COMPUTE:
            base[e] = 0
        clock = {e: {} for e in ENGS}
        start_tick = dict(self._start_tick)
        start_slot = dict(self._start_slot)
        for e in ENGS:
            for k, v in start_tick.items():
                clock[e][k] = v
            for k, v in start_slot.items():
                clock[e][k] = v
        plan = {e: [] for e in ENGS}
        for op in self.ops:
            ck = clock[op.eng]
            need = {}
            for d in op.deps:
                if ck.get(d.semkey, 0) >= d.tick:
                    continue
                if need.get(d.semkey, 0) < d.tick:
                    need[d.semkey] = d.tick
            for d in op.deps:
                for k, v in d.known.items():
                    if ck.get(k, 0) < v:
                        ck[k] = v
            waits = list(need.items())
            for k, v in waits:
                if ck.get(k, 0) < v:
                    ck[k] = v
            if op.tick is not None:
                kn = dict(ck)
                if kn.get(op.semkey, 0) < op.tick:
                    kn[op.semkey] = op.tick
                op.known = kn
            plan[op.eng].append((op, waits))
        final_waits = [(s, s.count) for s in sy.slots[: self.nslot] if s.count > start_slot.get(s, 0)]
        esem = sy.esem

        def semof(k):
            return k.sem if isinstance(k, DmaSlot) else esem[k]

        def run(engname):
            def body(eng):
                for op, waits in plan[engname]:
                    for k, v in waits:
                        eng.wait_ge(semof(k), v)
                    ins = op.fn(eng)
                    if op.slot is not None:
                        ins.then_inc(op.slot.sem, 16)
                    elif op.tick is not None:
                        ins.then_inc(esem[engname], 1)
                if engname == "sp":
                    for s, v in final_waits:
                        eng.wait_ge(s.sem, v)
            return body

        with nc.Block() as block:
            block.tensor(run("pe"))
            block.scalar(run("act"))
            block.vector(run("dve"))
            block.gpsimd(run("pool"))
            block.sync(run("sp"))

    def begin(self):
        sy = self.sync
        self._start_tick = dict(sy.tick)
        self._start_slot = {s: s.count for s in sy.slots}
        return self


_UID = [0]


def uid():
    _UID[0] += 1
    return "_u%d" % _UID[0]


def fview(ap, dims):
    return bass.AP(tensor=ap.tensor, offset=ap.offset, ap=[list(ap.ap[0])] + [list(d) for d in dims])


def FV(t, p0, p1, off, dims):
    return fview(t[p0:p1, off:off + 1], dims)


class Rot:
    def __init__(self, items):
        self.items = items
        self.bufs = [Buf() for _ in items]
        self.i = 0

    def next(self):
        k = self.i % len(self.items)
        self.i += 1
        return self.items[k], self.bufs[k]


DBG = {}


def dump(P, name, t, bufs, dt=None):
    nc = P.nc
    shape = list(t.shape)
    d = nc.dram_tensor("dbg_" + name, shape, dt or t.dtype, kind="ExternalOutput").ap()
    P.dma(P.slot(), lambda e: e.dma_start(out=d[:, :], in_=t[:, :]), reads=bufs)


def phase_norm(sync, es_outer, xsrc, hnT, ident):
    nc = sync.nc
    P = Prog(sync).begin()
    with contextlib.ExitStack() as es:
        sfx = uid()

        def sb(name, shape, dt):
            return es.enter_context(nc.sbuf_tensor(name + sfx, shape, dt))
        xt = Rot([sb("n_xt%d" % i, [128, D], F32) for i in range(8)])
        junk = sb("n_junk", [128, D], BF16)
        hn0 = Rot([sb("n_hn%d" % i, [128, D], BF16) for i in range(2)])
        ss = sb("n_ss", [128, 32], F32)
        ln = sb("n_ln", [128, 32], F32)
        rs = sb("n_rs", [128, 32], F32)
        epsc = sb("n_eps", [128, 1], F32)
        ptr = Rot([es.enter_context(nc.psum_tensor("n_ptr%d" % i + sfx, [128, D], BF16)) for i in range(2)])
        bjunk = Buf()
        beps = Buf()
        bhn = Buf()
        slots = [P.slot() for _ in range(8)]
        P.dve(lambda e: e.memset(epsc[:], EPS), writes=[beps])
        NB = 4
        pendB = []
        for i0 in range(0, 32, NB):
            grp = []
            bssg = Buf()
            for i in range(i0, i0 + NB):
                x_t, bx = xt.next()
                sl = slots[i % len(slots)]
                P.dma(sl, lambda e, x_t=x_t, i=i: e.dma_start(out=x_t[:], in_=xsrc[128 * i:128 * (i + 1), :]), writes=[bx])
                P.act(lambda e, x_t=x_t, i=i: e.activation(out=junk[:], in_=x_t[:], func=AF.Square, accum_out=ss[:, i:i + 1]),
                      reads=[bx], writes=[bjunk, bssg])
                grp.append((i, x_t, bx))
            bln = Buf()
            P.act(lambda e, i0=i0: e.activation(out=ln[:, i0:i0 + NB], in_=ss[:, i0:i0 + NB], func=AF.Ln, bias=epsc[:], scale=1.0 / D),
                  reads=[bssg, beps], writes=[bln])
            brs = Buf()
            P.act(lambda e, i0=i0: e.activation(out=rs[:, i0:i0 + NB], in_=ln[:, i0:i0 + NB], func=AF.Exp, scale=-0.5),
                  reads=[bln], writes=[brs])
            for i, x_t, bx in grp:
                def partA(i=i, x_t=x_t, bx=bx, brs=brs):
                    h_t, bh = hn0.next()
                    P.dve(lambda e: e.tensor_scalar(out=h_t[:], in0=x_t[:], scalar1=rs[:, i:i + 1], scalar2=None, op0=ALU.mult),
                          reads=[bx, brs], writes=[bh])
                    p_t, bp = ptr.next()
                    for kc in range(8):
                        P.pe(lambda e, kc=kc: e.transpose(p_t[:, kc * 128:(kc + 1) * 128], h_t[:, kc * 128:(kc + 1) * 128], ident[:]),
                             reads=[bh], writes=[bp])

                    def partB():
                        P.dve(lambda e: e.tensor_copy(out=FV(hnT, 0, 128, 128 * i, [[S, 8], [1, 128]]),
                                                      in_=FV(p_t, 0, 128, 0, [[128, 8], [1, 128]])), reads=[bp], writes=[bhn])
                    return partB
                pendB.append(partA())
                if len(pendB) > 1:
                    pendB.pop(0)()
        while pendB:
            pendB.pop(0)()
        if DBG.get("hnT"):
            dump(P, "hnT", hnT, [bhn])
            dump(P, "rs", rs, [bhn])
        P.emit()


def phase_outproj(sync, xsrc, wo_dram, ysc, out):
    nc = sync.nc
    P = Prog(sync).begin()
    with contextlib.ExitStack() as es:
        sfx = uid()

        def sb(name, shape, dt):
            return es.enter_context(nc.sbuf_tensor(name + sfx, shape, dt))
        wst = Rot([sb("o_wst%d" % i, [128, D], F32) for i in range(2)])
        wo = sb("o_wo", [128, 8 * D], BF16)
        bwo = Buf()
        xt = Rot([sb("o_xt%d" % i, [128, D], F32) for i in range(3)])
        ot = Rot([sb("o_ot%d" % i, [128, D], F32) for i in range(2)])
        yt = Rot([sb("o_yt%d" % i, [128, 8 * 512], BF16) for i in range(2)])
        po = Rot([es.enter_context(nc.psum_tensor("o_po%d" % i + sfx, [128, 512], F32)) for i in range(4)])
        s_w = [P.slot() for _ in range(2)]
        s_x = [P.slot() for _ in range(3)]
        s_y = [P.slot() for _ in range(2)]
        s_o = [P.slot() for _ in range(2)]
        for kc in range(8):
            w_t, bw = wst.next()
            P.dma(s_w[kc % 2], lambda e, w_t=w_t, kc=kc: e.dma_start(out=w_t[:], in_=wo_dram[kc * 128:(kc + 1) * 128, :]), writes=[bw])
            if kc % 2 == 0:
                P.dve(lambda e, w_t=w_t, kc=kc: e.tensor_copy(out=wo[:, kc * D:(kc + 1) * D], in_=w_t[:]), reads=[bw], writes=[bwo])
            else:
                P.act(lambda e, w_t=w_t, kc=kc: e.activation(out=wo[:, kc * D:(kc + 1) * D], in_=w_t[:], func=AF.Copy), reads=[bw], writes=[bwo])
        xts = {}
        yts = {}

        def load_x(i):
            x_t, bx = xt.next()
            P.dma(s_x[i % 3], lambda e: e.dma_start(out=x_t[:], in_=xsrc[128 * i:128 * (i + 1), :]), writes=[bx])
            xts[i] = (x_t, bx)

        def load_y(tb):
            y_t, by = yt.next()
            P.dma(s_y[tb % 2], lambda e: e.dma_start(
                out=FV(y_t, 0, 128, 0, [[512, 8], [1, 512]]),
                in_=ysc[:, :, tb * 512:(tb + 1) * 512].rearrange("k p t -> p k t")), writes=[by])
            yts[tb] = (y_t, by)
        load_y(0)
        load_x(0)
        load_x(1)
        for i in range(32):
            if i + 2 < 32:
                load_x(i + 2)
            if i % 4 == 0 and i // 4 + 1 < 8:
                load_y(i // 4 + 1)
            y_t, by = yts[i // 4]
            x_t, bx = xts[i]
            o_t, bo = ot.next()
            for nb in range(2):
                p_t, bp = po.next()
                for kc in range(8):
                    P.pe(lambda e, p_t=p_t, y_t=y_t, kc=kc, nb=nb, i=i: e.matmul(
                        p_t[:], lhsT=y_t[:, kc * 512 + (i % 4) * 128: kc * 512 + (i % 4) * 128 + 128],
                        rhs=wo[:, kc * D + nb * 512: kc * D + nb * 512 + 512], start=(kc == 0), stop=(kc == 7)),
                        reads=[by, bwo], writes=[bp])
                P.dve(lambda e, p_t=p_t, x_t=x_t, o_t=o_t, nb=nb: e.tensor_tensor(
                    out=o_t[:, nb * 512:(nb + 1) * 512], in0=p_t[:], in1=x_t[:, nb * 512:(nb + 1) * 512], op=ALU.add),
                    reads=[bp, bx], writes=[bo])
            P.dma(s_o[i % 2], lambda e, o_t=o_t, i=i: e.dma_start(out=out[128 * i:128 * (i + 1), :], in_=o_t[:]), reads=[bo])
        P.emit()


def qk_geometry(d):
    L = S // d
    return L, L // 128


def merge_steps(a, b):
    out = []
    na, nb = len(a), len(b)
    if na == 0 or nb == 0:
        return list(a) + list(b)
    ia = ib = 0
    while ia < na or ib < nb:
        if ib >= nb or (ia < na and ia * nb <= ib * na):
            out.append(a[ia]); ia += 1
        else:
            out.append(b[ib]); ib += 1
    return out


def phase_attn(sync, layer, hnT, dr, ysc):
    nc = sync.nc
    P = Prog(sync).begin()
    isA = layer == "A"
    w_dram, wg_dram, bias_dram = dr["w"], dr["wg"], dr["bias"]
    units = [(hp, g) for hp in range(8) for g in ((0, 1, 2) if isA else (0,))]
    with contextlib.ExitStack() as es:
        sfx = uid()

        def sb(name, shape, dt):
            return es.enter_context(nc.sbuf_tensor(name + sfx, shape, dt))
        ngc = sb("a_ngc", [128, 8], F32)
        gq = sb("a_gq", [128, 3], F32)
        gk = sb("a_gk", [128, 3], F32)
        epsc = sb("a_eps", [128, 1], F32)
        blk = sb("a_blk", [128, 128], BF16)
        bconst = Buf()
        s_c = P.slot()
        P.dma(s_c, lambda e: e.dma_start(out=ngc[:], in_=dr["ng"][:, :]), writes=[bconst])
        P.dma(s_c, lambda e: e.dma_start(out=gq[:], in_=dr["gq"][:, :]), writes=[bconst])
        P.dma(s_c, lambda e: e.dma_start(out=gk[:], in_=dr["gk"][:, :]), writes=[bconst])
        P.dve(lambda e: e.memset(epsc[:], EPS), writes=[bconst])
        P.dve(lambda e: e.memset(blk[:], 0.0), reads=[bconst], writes=[bconst])
        P.dve(lambda e: e.memset(blk[0:64, 0:64], 1.0 / 64), reads=[bconst], writes=[bconst])
        P.dve(lambda e: e.memset(blk[64:128, 64:128], 1.0 / 64), reads=[bconst], writes=[bconst])
        P.dve(lambda e: e.tensor_scalar(out=gq[:], in0=gq[:], scalar1=0.125, scalar2=None, op0=ALU.mult),
              reads=[bconst], writes=[bconst])
        qTr = Rot([sb("a_qT%d" % i, [128, S], BF16) for i in range(2)])
        kTr = Rot([sb("a_kT%d" % i, [128, S], BF16) for i in range(2)])
        Vr = Rot([sb("a_V%d" % i, [128, 32 * 192], BF16) for i in range(2)])
        for V_i, bV_i in zip(Vr.items, Vr.bufs):
            P.dve(lambda e, V_i=V_i: e.memset(FV(V_i, 0, 128, 64, [[192, 32], [1, 64]]), 1.0), writes=[bV_i])
        sgT = sb("a_sgT", [128, S], BF16)
        bsgt = [Buf() for _ in range(8)]
        if isA:
            acc = [sb("a_acc%d" % h, [128, S], F32) for h in range(2)]
            bacc = [Buf(), Buf()]
        wst = Rot([sb("a_wst%d" % i, [128, 384], F32) for i in range(4)])
        s_w = [P.slot() for _ in range(4)]
        wb = Rot([sb("a_wb%d" % i, [128, 8 * 384], BF16) for i in range(2)])
        wgb = sb("a_wgb", [128, 8 * 128], BF16)
        bwg = Buf()
        ebw = 2 * 256 if isA else dr["ebw"]
        ebst = Rot([sb("a_ebst%d" % i, [128, 512 if isA else 768], F32) for i in range(1 if isA else 2)])
        s_eb = [P.slot() for _ in range(2)]
        eb = Rot([sb("a_eb%d" % i, [128, ebw], BF16) for i in range(2)])
        sq = Rot([sb("a_sq%d" % i, [128, 512], BF16) for i in range(2)])
        rstd = Rot([sb("a_rstd%d" % i, [128, 512], F32) for i in range(2)])
        ew = 512 if isA else 768
        pT = Rot([sb("a_pT%d" % i, [128, ew], BF16) for i in range(8 if isA else 7)])
        rec = Rot([sb("a_rec%d" % i, [128, 512], F32) for i in range(1 if isA else 2)])
        if isA:
            yT = Rot([sb("a_yT%d" % i, [128, 1024], BF16) for i in range(2)])
            s_y = [P.slot() for _ in range(2)]
        else:
            yTB = [Rot([sb("b_yT%d_%d" % (h, i), [128, 1024], BF16) for i in range(2)]) for h in range(2)]
            s_yB = [[P.slot() for _ in range(2)] for h in range(2)]
        print("phase_attn", layer, "sbuf bytes remaining", nc.sbuf_bytes_remaining)
        banks = [es.enter_context(nc.psum_tensor("a_ps%d" % i + sfx, [128, 512], F32)) for i in range(8)]
        bbank = [Buf() for _ in range(8)]

        def bankrot(ids):
            r = Rot([banks[i] for i in ids])
            r.bufs = [bbank[i] for i in ids]
            return r
        pq = bankrot([0, 1, 7])
        pss = bankrot([2])
        if isA:
            ps_h = [bankrot([3]), bankrot([4])]
            po_h = [bankrot([5]), bankrot([6])]
            pv = bankrot([2])
            pg = bankrot([5, 6])
        else:
            pq = bankrot([0, 1])
            psA = bankrot([3, 4])
            bRem = [Buf(), Buf()]
            po = bankrot([6, 7])
            pv = bankrot([2])
            pg = bankrot([6, 7])

        def wload_steps(u):
            hp, g = u
            w_b, bw = wb.next()
            pend = []
            steps = []

            def conv(w_t, bs, kc):
                P.dve(lambda e: e.tensor_scalar(
                    out=w_b[:, kc * 384:(kc + 1) * 384], in0=w_t[:], scalar1=ngc[:, kc:kc + 1], scalar2=None, op0=ALU.mult),
                    reads=[bs, bconst], writes=[bw])
            for kc in range(8):
                def step(kc=kc):
                    w_t, bs = wst.next()
                    sl = s_w[(wst.i - 1) % 4]
                    P.dma(sl, lambda e: e.dma_start(out=w_t[:], in_=w_dram[hp, g, kc * 128:(kc + 1) * 128, :]), writes=[bs])
                    pend.append((w_t, bs, kc))
                    if len(pend) > 2:
                        conv(*pend.pop(0))
                steps.append(step)

            def flush():
                while pend:
                    conv(*pend.pop(0))
            steps.append(flush)
            return (w_b, bw), steps

        def gload_steps(hp):
            pend = []
            steps = []

            def conv(w_t, bs, kc):
                P.dve(lambda e: e.tensor_scalar(
                    out=wgb[:, kc * 128:(kc + 1) * 128], in0=w_t[:, 0:128], scalar1=ngc[:, kc:kc + 1], scalar2=None, op0=ALU.mult),
                    reads=[bs, bconst], writes=[bwg])
            for kc in range(8):
                def step(kc=kc):
                    w_t, bs = wst.next()
                    sl = s_w[(wst.i - 1) % 4]
                    P.dma(sl, lambda e: e.dma_start(out=w_t[:, 0:128], in_=wg_dram[hp, kc * 128:(kc + 1) * 128, :]), writes=[bs])
                    pend.append((w_t, bs, kc))
                    if len(pend) > 2:
                        conv(*pend.pop(0))
                steps.append(step)

            def flush():
                while pend:
                    conv(*pend.pop(0))
            steps.append(flush)
            return steps

        def eb_dma_A(u):
            hp, g = u
            st_, bs = ebst.next()
            sl = s_eb[0]
            P.dma(sl, lambda e: e.dma_start(out=st_[:, 0:512], in_=bias_dram[hp, g, :, :]), writes=[bs])
            return st_, bs

        def eb_conv_A(st_, bs):
            e_t, be = eb.next()
            P.act(lambda e: e.activation(out=e_t[:, 0:512], in_=st_[:, 0:512], func=AF.Exp), reads=[bs], writes=[be])
            return e_t, be

        def eb_steps_B(hp, h):
            e_t, be = eb.next()
            ebw1 = dr["ebw"]
            pieces = []
            off = 0
            while off < ebw1:
                n = min(768, ebw1 - off)
                pieces.append((off, n))
                off += n
            pend = []
            steps = []

            def conv(st_, bs, off, n):
                P.act(lambda e: e.activation(out=e_t[:, off: off + n], in_=st_[:, 0:n], func=AF.Exp), reads=[bs], writes=[be])
            for off, n in pieces:
                def step(off=off, n=n):
                    st_, bs = ebst.next()
                    sl = s_eb[(ebst.i - 1) % 2]
                    P.dma(sl, lambda e: e.dma_start(out=st_[:, 0:n], in_=bias_dram[2 * hp + h, :, off:off + n]), writes=[bs])
                    pend.append((st_, bs, off, n))
                    if len(pend) > 1:
                        conv(*pend.pop(0))
                steps.append(step)

            def flush():
                while pend:
                    conv(*pend.pop(0))
            steps.append(flush)
            return (e_t, be), steps

        def gate_steps():
            steps = []
            for tb in range(8):
                def step(tb=tb):
                    p_t, bp = pg.next()
                    for kc in range(8):
                        P.pe(lambda e, p_t=p_t, kc=kc: e.matmul(
                            p_t[:], lhsT=wgb[:, kc * 128:(kc + 1) * 128], rhs=hnT[:, kc * S + tb * 512: kc * S + tb * 512 + 512],
                            start=(kc == 0), stop=(kc == 7)), reads=[bwg], writes=[bp])
                    P.act(lambda e, p_t=p_t: e.activation(out=sgT[:, tb * 512:(tb + 1) * 512], in_=p_t[:], func=AF.Silu),
                          reads=[bp], writes=[bsgt[tb]])
                steps.append(step)
            return steps

        def qk_steps(u, w_b, bw, q_t, bq, k_t, bk):
            hp, g = u
            d = DILS[g] if isA else 1
            L = S // d
            pend = []

            def tail(p_t, bp, s_t, bs, which, tb):
                ss_t, bss = pss.next()
                P.pe(lambda e: e.matmul(ss_t[:], lhsT=blk[:], rhs=s_t[:], start=True, stop=True),
                     reads=[bs, bconst], writes=[bss])
                r_t, br = rstd.next()
                P.act(lambda e: e.activation(out=r_t[:], in_=ss_t[:], func=AF.Ln, bias=epsc[:], scale=1.0),
                      reads=[bss, bconst], writes=[br])
                P.act(lambda e: e.activation(out=r_t[:], in_=r_t[:], func=AF.Exp, scale=-0.5), reads=[br], writes=[br])
                if which == 0:
                    n = 512 // d
                    P.dve(lambda e: e.scalar_tensor_tensor(
                        out=FV(q_t, 0, 128, tb * n, [[L, d], [1, n]]),
                        in0=FV(p_t, 0, 128, 0, [[1, d], [d, n]]), scalar=gq[:, g:g + 1],
                        in1=FV(r_t, 0, 128, 0, [[1, d], [d, n]]), op0=ALU.mult, op1=ALU.mult),
                        reads=[bp, br, bconst], writes=[bq])
                else:
                    P.dve(lambda e: e.scalar_tensor_tensor(
                        out=k_t[:, tb * 512:(tb + 1) * 512], in0=p_t[:], scalar=gk[:, g:g + 1], in1=r_t[:],
                        op0=ALU.mult, op1=ALU.mult), reads=[bp, br, bconst], writes=[bk])

            steps = []
            for t in range(16):
                def step(t=t):
                    which, tb = divmod(t, 8)
                    p_t, bp = pq.next()
                    for kc in range(8):
                        P.pe(lambda e, kc=kc: e.matmul(
                            p_t[:], lhsT=w_b[:, kc * 384 + which * 128: kc * 384 + which * 128 + 128],
                            rhs=hnT[:, kc * S + tb * 512: kc * S + tb * 512 + 512], start=(kc == 0), stop=(kc == 7)),
                            reads=[bw], writes=[bp])
                    s_t, bs = sq.next()
                    P.act(lambda e: e.activation(out=s_t[:], in_=p_t[:], func=AF.Square), reads=[bp], writes=[bs])
                    if pend:
                        tail(*pend.pop())
                    pend.append((p_t, bp, s_t, bs, which, tb))
                steps.append(step)

            def flush():
                tail(*pend.pop())
            steps.append(flush)
            return steps

        def v_steps(u, w_b, bw, Vt, bV):
            hp, g = u
            d = DILS[g] if isA else 1
            L, nC = qk_geometry(d)
            steps = []
            for c0 in range(0, 32, 4):
                def step(c0=c0):
                    p_t, bp = pv.next()
                    for cc in range(4):
                        c = c0 + cc
                        r, i = divmod(c, nC)
                        t0 = r + d * 128 * i
                        for kc in range(8):
                            P.pe(lambda e, kc=kc, cc=cc, t0=t0: e.matmul(
                                p_t[:, cc * 128:(cc + 1) * 128],
                                lhsT=FV(hnT, 0, 128, kc * S + t0, [[d, 128]]),
                                rhs=w_b[:, kc * 384 + 256: kc * 384 + 384], start=(kc == 0), stop=(kc == 7)),
                                reads=[bw], writes=[bp])
                    P.act(lambda e: e.activation(
                        out=FV(Vt, 0, 128, c0 * 192, [[192, 4], [128, 2], [1, 64]]),
                        in_=FV(p_t, 0, 128, 0, [[128, 4], [64, 2], [1, 64]]), func=AF.Copy), reads=[bp], writes=[bV])
                steps.append(step)
            return steps

        def attn_steps_A(u, q_t, bq, k_t, bk, e_t, be, Vt, bV):
            hp, g = u
            d = DILS[g]
            first = g == 0
            L, nC = qk_geometry(d)
            pend = []
            steps = []

            def acc_out(h, srcf, sbuf, dstf):
                if first:
                    P.dve(lambda e: e.tensor_copy(out=dstf(), in_=srcf()), reads=[sbuf], writes=[bacc[h]])
                else:
                    P.dve(lambda e: e.tensor_tensor(out=dstf(), in0=srcf(), in1=dstf(), op=ALU.add), reads=[sbuf], writes=[bacc[h]])

            def make_tail(r, m, ptl, state):
                def tail():
                    for h in range(2):
                        vof = 0 if h == 0 else 64
                        acc_h = acc[h]
                        key = "po%d" % h

                        def pcol(i, lo):
                            p_t, bp = ptl[(h, i // 2)]
                            return p_t, bp, (i % 2) * 256 + lo

                        def mm(o_t, bo, col, n, chunk, pa, bpa, ca, start, stop, vof=vof):
                            P.pe(lambda e: e.matmul(
                                o_t[:, col:col + n], lhsT=Vt[:, chunk * 192 + vof: chunk * 192 + vof + 128],
                                rhs=pa[:, ca:ca + n], start=start, stop=stop), reads=[bV, bpa], writes=[bo])
                        if d == 16:
                            if r % 2 == 0:
                                state[key] = po_h[h].next()
                            o_t, bo = state[key]
                            base = (r % 2) * 256
                            c1 = r * nC
                            pa, bpa, ca = pcol(0, 64)
                            mm(o_t, bo, base, 64, c1, pa, bpa, ca, True, True)
                            pa, bpa, ca = pcol(0, 128)
                            mm(o_t, bo, base + 64, 128, c1, pa, bpa, ca, True, False)
                            pa, bpa, ca = pcol(1, 0)
                            mm(o_t, bo, base + 64, 128, c1 + 1, pa, bpa, ca, False, True)
                            pa, bpa, ca = pcol(1, 128)
                            mm(o_t, bo, base + 192, 64, c1 + 1, pa, bpa, ca, True, True)
                            if r % 2 == 1:
                                acc_out(h, lambda o_t=o_t: o_t[:, 0:512], bo,
                                        lambda acc_h=acc_h: FV(acc_h, 0, 128, r - 1, [[1, 2], [16, 256]]))
                            continue
                        if m == 0:
                            state[key] = po_h[h].next()
                            o_t, bo = state[key]
                            pa, bpa, ca = pcol(0, 64)
                            mm(o_t, bo, 64, 64, r * nC, pa, bpa, ca, True, True)
                        for jj in ([2 * m - 1] if m > 0 else []) + ([2 * m] if 2 * m <= nC - 2 else []):
                            if jj < 3:
                                slot = jj + 1
                            else:
                                slot = (jj - 3) % 4
                                if slot == 0:
                                    state[key] = po_h[h].next()
                            o_t, bo = state[key]
                            pa, bpa, ca = pcol(jj, 128)
                            pb_, bpb, cb = pcol(jj + 1, 0)
                            c1 = r * nC + jj
                            mm(o_t, bo, slot * 128, 128, c1, pa, bpa, ca, True, False)
                            mm(o_t, bo, slot * 128, 128, c1 + 1, pb_, bpb, cb, False, True)
                            if jj == 2:
                                acc_out(h, lambda o_t=o_t: o_t[:, 64:512], bo,
                                        lambda acc_h=acc_h: FV(acc_h, 0, 128, r, [[d, 448]]))
                            elif jj > 2 and slot == 3:
                                m0 = 64 + 128 * (jj - 3)
                                acc_out(h, lambda o_t=o_t: o_t[:, 0:512], bo,
                                        lambda acc_h=acc_h, m0=m0: FV(acc_h, 0, 128, r + d * m0, [[d, 512]]))
                        if m == nC // 2 - 1:
                            assert (nC - 2 - 3) % 4 == 3
                            o_t, bo = po_h[h].next()
                            pa, bpa, ca = pcol(nC - 1, 128)
                            mm(o_t, bo, 0, 64, r * nC + nC - 1, pa, bpa, ca, True, True)
                            acc_out(h, lambda o_t=o_t: o_t[:, 0:64], bo,
                                    lambda acc_h=acc_h: FV(acc_h, 0, 128, r + d * (L - 64), [[d, 64]]))
                return tail

            state = {}
            for r in range(d):
                ptl = {}
                for m in range(nC // 2):
                    def step(r=r, m=m, ptl=ptl, state=state):
                        stl = [ps_h[0].next(), ps_h[1].next()]
                        for cc in range(2):
                            i = 2 * m + cc
                            qlo = max(0, 128 * i - 64)
                            qhi = min(L, 128 * i + 192)
                            lo = qlo - (128 * i - 64)
                            n = qhi - qlo
                            for h in range(2):
                                s_t, bs = stl[h]
                                P.pe(lambda e, s_t=s_t, cc=cc, lo=lo, n=n, qlo=qlo, i=i, h=h: e.matmul(
                                    s_t[:, cc * 256 + lo: cc * 256 + lo + n],
                                    lhsT=FV(k_t, 64 * h, 64 * h + 64, r + d * 128 * i, [[d, 128]]),
                                    rhs=q_t[64 * h:64 * h + 64, r * L + qlo: r * L + qlo + n], start=True, stop=True),
                                    reads=[bk, bq], writes=[bs])
                        for h in range(2):
                            s_t, bs = stl[h]
                            p_t, bp = pT.next()
                            P.act(lambda e, p_t=p_t, s_t=s_t: e.activation(out=p_t[:, 0:512], in_=s_t[:], func=AF.Exp), reads=[bs], writes=[bp])
                            P.dve(lambda e, p_t=p_t, h=h: e.tensor_tensor(
                                out=FV(p_t, 0, 128, 0, [[256, 2], [1, 256]]), in0=FV(p_t, 0, 128, 0, [[256, 2], [1, 256]]),
                                in1=FV(e_t, 0, 128, h * 256, [[0, 2], [1, 256]]), op=ALU.mult), reads=[bp, be], writes=[bp])
                            ptl[(h, m)] = (p_t, bp)
                        if pend:
                            pend.pop()()
                        pend.append(make_tail(r, m, ptl, state))
                    steps.append(step)

            def flush():
                pend.pop()()
            steps.append(flush)
            return steps

        def normalize_steps_A(hp):
            steps = []
            st = {}
            for tb in range(8):
                def step(tb=tb):
                    q4, hb = divmod(tb, 2)
                    if hb == 0:
                        st["y"] = yT.next()
                    y_t, by = st["y"]
                    cs = slice(tb * 512, (tb + 1) * 512)
                    r_t, br = rec.next()
                    t_t, bt = r_t, br
                    P.act(lambda e: e.activation(out=r_t[0:64, :], in_=acc[0][64:128, cs], func=AF.Ln), reads=[bacc[0]], writes=[br])
                    P.act(lambda e: e.activation(out=r_t[64:128, :], in_=acc[1][0:64, cs], func=AF.Ln), reads=[bacc[1]], writes=[br])
                    P.act(lambda e: e.activation(out=r_t[:], in_=r_t[:], func=AF.Exp, scale=-1.0), reads=[br], writes=[br])
                    P.dve(lambda e: e.tensor_tensor(out=t_t[0:64, :], in0=acc[0][0:64, cs], in1=r_t[0:64, :], op=ALU.mult),
                          reads=[br], writes=[bt])
                    P.dve(lambda e: e.tensor_tensor(out=t_t[64:128, :], in0=acc[1][64:128, cs], in1=r_t[64:128, :], op=ALU.mult),
                          reads=[br], writes=[bt])
                    P.dve(lambda e: e.tensor_tensor(out=y_t[:, hb * 512:(hb + 1) * 512], in0=t_t[:], in1=sgT[:, cs], op=ALU.mult),
                          reads=[bt, bsgt[tb]], writes=[by])
                    if hb == 1:
                        sl = s_y[(yT.i - 1) % 2]
                        P.dma(sl, lambda e: e.dma_start(out=ysc[hp, :, q4 * 1024:(q4 + 1) * 1024], in_=y_t[:]), reads=[by])
                steps.append(step)
            return steps

        def attn_steps_B(hp, h, q_t, bq, k_t, bk, e_t, be, Vt, bV):
            tiles = dr["tiles"]
            hs = slice(64 * h, 64 * h + 64)
            vof = 0 if h == 0 else 64
            num = slice(0, 64) if h == 0 else slice(64, 128)
            den = slice(64, 128) if h == 0 else slice(0, 64)
            contribs = []
            for Q in range(32):
                full, part = [], []
                for R in range(32):
                    qlo, nr = tiles[R][1], tiles[R][2]
                    lo_r = max(qlo, 2 * Q)
                    hi_r = min(qlo + nr - 1, 2 * Q + 1)
                    if lo_r > hi_r:
                        continue
                    (full if hi_r - lo_r == 1 else part).append((R, lo_r, hi_r - lo_r + 1))
                assert full
                contribs.append(full + part)
            lastR = [max(R for R, _, _ in contribs[Q]) for Q in range(32)]
            ptl = {}
            st = dict(o=None, y=None)
            pend = []
            steps = []

            def do_block(Q):
                if Q % 4 == 0:
                    st["o"] = po.next()
                o_t, bo = st["o"]
                cl = contribs[Q]
                for n_, (R, row0, nrow) in enumerate(cl):
                    p_t, bp = ptl[R]
                    c0 = (row0 - tiles[R][1]) * 64
                    oc = (Q % 4) * 128 + (row0 - 2 * Q) * 64
                    nn = nrow * 64
                    P.pe(lambda e, p_t=p_t, c0=c0, oc=oc, nn=nn, R=R, first=(n_ == 0), last=(n_ == len(cl) - 1): e.matmul(
                        o_t[:, oc:oc + nn], lhsT=Vt[:, R * 192 + vof: R * 192 + vof + 128],
                        rhs=p_t[:, c0:c0 + nn], start=first, stop=last), reads=[bV, bp], writes=[bo])
                if Q % 4 == 3:
                    tb = Q // 4
                    cs = slice(tb * 512, (tb + 1) * 512)
                    if tb % 2 == 0:
                        st["y"] = yTB[h].next()
                    y_t, by = st["y"]
                    r_t, br = rec.next()
                    t_t, bt = r_t, br
                    P.act(lambda e: e.activation(out=r_t[num, :], in_=o_t[den, :], func=AF.Ln), reads=[bo], writes=[br])
                    P.act(lambda e: e.activation(out=r_t[num, :], in_=r_t[num, :], func=AF.Exp, scale=-1.0), reads=[br], writes=[br])
                    P.dve(lambda e: e.tensor_tensor(out=t_t[num, :], in0=o_t[num, :], in1=r_t[num, :], op=ALU.mult),
                          reads=[bo, br], writes=[bt])
                    P.dve(lambda e: e.tensor_tensor(
                        out=y_t[num, (tb % 2) * 512:(tb % 2) * 512 + 512], in0=t_t[num, :], in1=sgT[num, cs], op=ALU.mult),
                        reads=[bt, bsgt[tb]], writes=[by])
                    if tb % 2 == 1:
                        q4 = tb // 2
                        sl = s_yB[h][(yTB[h].i - 1) % 2]
                        P.dma(sl, lambda e: e.dma_start(
                            out=ysc[hp, num, q4 * 1024:(q4 + 1) * 1024], in_=y_t[num, :]), reads=[by])

            for R in range(32):
                def step(R=R):
                    toff, qlo, nr = tiles[R]
                    n = nr * 64
                    sA, bA = psA.next()
                    sB = banks[5]
                    bB = bbank[5]
                    n1 = min(n, 512)
                    P.pe(lambda e: e.matmul(
                        sA[:, 0:n1], lhsT=k_t[hs, R * 128:(R + 1) * 128], rhs=q_t[hs, qlo * 64: qlo * 64 + n1], start=True, stop=True),
                        reads=[bk, bq], writes=[bA])
                    p_t, bp = pT.next()
                    P.act(lambda e: e.activation(out=p_t[:, 0:n1], in_=sA[:, 0:n1], func=AF.Exp), reads=[bA], writes=[bp])
                    if n > 512:
                        n2 = n - 512
                        P.pe(lambda e: e.matmul(
                            sB[:, 0:n2], lhsT=k_t[hs, R * 128:(R + 1) * 128], rhs=q_t[hs, qlo * 64 + 512: qlo * 64 + 512 + n2], start=True, stop=True),
                            reads=[bk, bq], writes=[bB])
                        P.act(lambda e: e.activation(out=p_t[:, 512:512 + n2], in_=sB[:, 0:n2], func=AF.Exp), reads=[bB], writes=[bp])
                    P.dve(lambda e: e.tensor_tensor(
                        out=p_t[:, 0:n], in0=p_t[:, 0:n], in1=e_t[:, toff: toff + n], op=ALU.mult),
                        reads=[bp, be], writes=[bp])
                    ptl[R] = (p_t, bp)
                    if pend:
                        pend.pop()()

                    def tail():
                        for Q in range(32):
                            if lastR[Q] == R:
                                do_block(Q)
                    pend.append(tail)
                steps.append(step)

            def flush():
                pend.pop()()
            steps.append(flush)
            return steps

        def run(steps):
            for st_ in steps:
                st_()

        dstop = DBG.get("stop")
        nU = len(units)
        wts = {}
        qk = {}
        vv = {}
        ebd = {}
        ebB = {}
        wts[0], ws = wload_steps(units[0])
        run(ws)
        run(gload_steps(0))
        if isA:
            ebd[0] = eb_dma_A(units[0])
        if nU > 1:
            wts[1], ws1 = wload_steps(units[1])
        else:
            ws1 = []
        qk[0] = qTr.next() + kTr.next()
        vv[0] = Vr.next()
        run(merge_steps(qk_steps(units[0], *wts[0], *qk[0]) + v_steps(units[0], *wts[0], *vv[0]), ws1))
        run(gate_steps())
        for ui, u in enumerate(units):
            hp, g = u
            nxt = units[ui + 1] if ui + 1 < nU else None
            wsteps = []
            if ui + 2 < nU:
                wts[ui + 2], wsteps = wload_steps(units[ui + 2])
            q_t, bq, k_t, bk = qk[ui]
            V_t, bV = vv[ui]
            nsteps = []
            if nxt is not None:
                qk[ui + 1] = qTr.next() + kTr.next()
                vv[ui + 1] = Vr.next()
                nsteps = merge_steps(qk_steps(nxt, *wts[ui + 1], *qk[ui + 1]), v_steps(nxt, *wts[ui + 1], *vv[ui + 1]))
            if isA:
                e_t, be = eb_conv_A(*ebd[ui])
                if nxt is not None:
                    ebd[ui + 1] = eb_dma_A(nxt)
                last = g == 2
                gsteps = gload_steps(hp + 1) if (g == 1 and hp + 1 < 8) else []
                run(merge_steps(merge_steps(attn_steps_A(u, q_t, bq, k_t, bk, e_t, be, V_t, bV), nsteps), wsteps + gsteps))
                if last:
                    gs = ([lambda: None] * 2 + gate_steps()) if hp + 1 < 8 else []
                    ns = normalize_steps_A(hp)
                    seq = []
                    for k in range(max(len(ns), len(gs))):
                        if k < len(ns):
                            seq.append(ns[k])
                        if k < len(gs):
                            seq.append(gs[k])
                    run(seq)
            else:
                gsteps = gload_steps(hp + 1) if hp + 1 < 8 else []
                if hp == 0:
                    ebB[(0, 0)], es0 = eb_steps_B(0, 0)
                    run(es0)
                ebB[(hp, 1)], es1 = eb_steps_B(hp, 1)
                a0 = attn_steps_B(hp, 0, q_t, bq, k_t, bk, *ebB[(hp, 0)], V_t, bV)
                run(merge_steps(merge_steps(a0, nsteps[: len(nsteps) // 2]), wsteps + es1))
                es2 = []
                if hp + 1 < 8:
                    ebB[(hp + 1, 0)], es2 = eb_steps_B(hp + 1, 0)
                a1 = attn_steps_B(hp, 1, q_t, bq, k_t, bk, *ebB[(hp, 1)], V_t, bV)
                run(merge_steps(merge_steps(a1, nsteps[len(nsteps) // 2:]), gsteps + es2))
                if hp + 1 < 8:
                    run(gate_steps())
            if dstop == "hp0" and ((isA and g == 2) or not isA):
                break
        P.emit()


_T5_LUT = None


def _t5_bucket_np(rel):
    import math
    half, me = 16, 8
    ret = np.where(rel > 0, half, 0)
    n = np.abs(rel)
    nf = np.maximum(n, 1).astype(np.float32)
    large = me + (np.log(nf / np.float32(me)) / np.float32(math.log(1024 / me)) * np.float32(half - me)).astype(np.int32)
    large = np.minimum(large, half - 1)
    return ret + np.where(n < me, n, large)


def _bias_tiles_A(t5_bias):
    a = np.arange(128)[:, None]
    b = np.arange(256)[None, :]
    rel = a - b + 64
    valid = (b - a >= 0) & (b - a <= 128)
    out = np.empty((8, 3, 128, 2, 256), np.float32)
    for g, d in enumerate(DILS):
        idx = _t5_bucket_np(rel * d)
        for h in range(16):
            t = t5_bias[g * 16 + h][idx]
            out[h // 2, g, :, h % 2, :] = np.where(valid, t, np.float32(NEG))
    return out


def _geom_B():
    rows = 64
    r = np.arange(rows)
    rs = np.clip(r - 4, 0, rows - 8)
    c = np.arange(64)
    cs = np.clip(c - 8, 0, 64 - 16)
    tiles = []
    uniq = {}
    maps = []
    off = 0
    for R in range(32):
        krs = np.array([2 * R, 2 * R + 1])
        qrows = [q for q in range(rows) if (rs[q] <= krs[1]) and (rs[q] + 7 >= krs[0])]
        qlo, nr = qrows[0], len(qrows)
        assert qrows == list(range(qlo, qlo + nr))
        kr = np.repeat(krs, 64)[:, None]
        kc = np.tile(c, 2)[:, None]
        qr = np.repeat(np.arange(qlo, qlo + nr), 64)[None, :]
        qc = np.tile(c, nr)[None, :]
        valid = (kr >= rs[qr]) & (kr <= rs[qr] + 7) & (kc >= cs[qc]) & (kc < cs[qc] + 16)
        ridx = np.clip(kr - qr + 7, 0, 14)
        cidx = np.clip(kc - qc, -15, 15) + 15
        key = (nr, valid.tobytes(), ridx.tobytes())
        if key not in uniq:
            uniq[key] = off
            maps.append((off, ridx + 0 * cidx, cidx + 0 * ridx, valid))
            off += nr * 64
        tiles.append((uniq[key], qlo, nr))
    return tiles, maps, off


def _bias_tiles_B(rpb, maps, ebw):
    out = np.empty((16, 128, ebw), np.float32)
    for off, ridx, cidx, valid in maps:
        n = valid.shape[1]
        for h in range(16):
            out[h, :, off:off + n] = np.where(valid, rpb[h][ridx, cidx], np.float32(NEG))
    return out


def _unit_weights(w_in, ngroups):
    wu = np.empty((8, ngroups, D, 384), np.float32)
    for hp in range(8):
        for g in range(ngroups):
            for j in range(3):
                c0 = g * 3072 + j * 1024 + hp * 128
                wu[hp, g, :, j * 128:(j + 1) * 128] = w_in[:, c0:c0 + 128]
    gc = ngroups * 3072
    wg = np.ascontiguousarray(w_in[:, gc:gc + 1024].reshape(D, 8, 128).transpose(1, 0, 2))
    return wu, wg


_GEOM_B = None


def build_nc(layers="AB"):
    global _GEOM_B
    if _GEOM_B is None:
        _GEOM_B = _geom_B()
    tilesB, mapsB, ebwB = _GEOM_B
    nc = bass.Bass("TRN2", target_bir_lowering=False)

    def din(name, shape, dt=F32):
        return nc.dram_tensor(name, list(shape), dt, kind="ExternalInput").ap()
    x = din("x", [S, D])
    ident_d = din("ident_d", [128, 128])
    drA = dict(w=din("wA", [8, 3, D, 384]), wg=din("wgA", [8, D, 128]), bias=din("biasA", [8, 3, 128, 512]),
               ng=din("ngA", [128, 8]), gq=din("gqA", [128, 3]), gk=din("gkA", [128, 3]))
    woA = din("woA", [D, D])
    drB = dict(w=din("wB", [8, 1, D, 384]), wg=din("wgB", [8, D, 128]), bias=din("biasB", [16, 128, ebwB]),
               ng=din("ngB", [128, 8]), gq=din("gqB", [128, 3]), gk=din("gkB", [128, 3]), tiles=tilesB, ebw=ebwB)
    woB = din("woB", [D, D])
    out = nc.dram_tensor("out", [S, D], F32, kind="ExternalOutput").ap()
    ysc = nc.dram_tensor("ysc", [8, 128, S], BF16).ap()
    with contextlib.ExitStack() as es:
        sync = Sync(nc, es)
        hnT = es.enter_context(nc.sbuf_tensor("hnT", [128, 8 * S], BF16))
        ident = es.enter_context(nc.sbuf_tensor("ident", [128, 128], BF16))
        identf = es.enter_context(nc.sbuf_tensor("identf", [128, 128], F32))
        P0 = Prog(sync).begin()
        bi = Buf()
        P0.dma(P0.slot(), lambda e: e.dma_start(out=identf[:], in_=ident_d[:, :]), writes=[bi])
        P0.dve(lambda e: e.tensor_copy(out=ident[:], in_=identf[:]), reads=[bi], writes=[bi])
        P0.emit()
        src = x
        if "A" in layers:
            phase_norm(sync, es, src, hnT, ident)
            if DBG.get("stop") != "norm":
                phase_attn(sync, "A", hnT, drA, ysc)
            if not DBG.get("stop"):
                phase_outproj(sync, src, woA, ysc, out)
            src = out
        if "B" in layers:
            phase_norm(sync, es, src, hnT, ident)
            phase_attn(sync, "B", hnT, drB, ysc)
            phase_outproj(sync, src, woB, ysc, out)
    return nc


def host_inputs(norm_gain, a_w_in, a_w_out, a_q_gain, a_k_gain, t5_bias, b_w_in, b_w_out, b_q_gain, b_k_gain, b_rpb):
    global _GEOM_B
    if _GEOM_B is None:
        _GEOM_B = _geom_B()
    tilesB, mapsB, ebwB = _GEOM_B
    f = lambda a: np.ascontiguousarray(np.asarray(a, dtype=np.float32))
    wA, wgA = _unit_weights(f(a_w_in)[0], 3)
    wB, wgB = _unit_weights(f(b_w_in)[0], 1)
    ng = f(norm_gain)

    def gcol(gn):
        gn = f(gn).reshape(-1, 64)
        o = np.ones((128, 3), np.float32)
        for g in range(gn.shape[0]):
            o[:, g] = np.tile(gn[g], 2)
        return o
    shared = dict(
        ident_d=np.eye(128, dtype=np.float32),
        wA=wA, wgA=wgA, woA=f(a_w_out)[0],
        biasA=np.ascontiguousarray(_bias_tiles_A(f(t5_bias)).reshape(8, 3, 128, 512)),
        ngA=np.ascontiguousarray(ng[0].reshape(8, 128).T), gqA=gcol(a_q_gain[0]), gkA=gcol(a_k_gain[0]),
        wB=wB, wgB=wgB, woB=f(b_w_out)[0],
        biasB=_bias_tiles_B(f(b_rpb)[0], mapsB, ebwB),
        ngB=np.ascontiguousarray(ng[1].reshape(8, 128).T), gqB=gcol(b_q_gain), gkB=gcol(b_k_gain),
    )
    return shared


def kernel(x, norm_gain, a_w_in, a_w_out, a_q_gain, a_k_gain, t5_bias, b_w_in, b_w_out, b_q_gain, b_k_gain, b_rpb):
    x = np.ascontiguousarray(np.asarray(x, dtype=np.float32))
    shared = host_inputs(norm_gain, a_w_in, a_w_out, a_q_gain, a_k_gain, t5_bias, b_w_in, b_w_out, b_q_gain, b_k_gain, b_rpb)
    nc = build_nc("AB")
    in_maps = [dict(shared, x=x[c]) for c in range(NCORES)]
    res = run_bass_kernel_spmd(nc, in_maps, core_ids=list(range(NCORES)))
    return np.stack([np.asarray(r["out"], dtype=np.float32) for r in res.results], axis=0)
```

```python
import contextlib
import numpy as np
import concourse.bass as bass
import concourse.mybir as mybir
from concourse.bass_utils import run_bass_kernel_spmd

F32 = mybir.dt.float32
BF16 = mybir.dt.bfloat16
AF = mybir.ActivationFunctionType
ALU = mybir.AluOpType

S = 4096
D = 1024
NCORES = 8
DILS = (1, 4, 16)
EPS = 1e-6
NEG = -30000.0


class Buf:
    __slots__ = ("name", "lw", "rd")

    def __init__(self, name=""):
        self.name = name
        self.lw = None
        self.rd = []


class DmaSlot:
    __slots__ = ("sem", "count", "name")

    def __init__(self, name):
        self.name = name
        self.sem = None
        self.count = 0


class Op:
    __slots__ = ("eng", "fn", "deps", "slot", "signal", "tick", "semkey", "known", "idx")


STRICT = True
COMPUTE = ("pe", "act", "dve", "pool")
ENGS = ("pe", "act", "dve", "pool", "sp")


class Sync:
    def __init__(self, nc, es, nslots=40):
        self.nc = nc
        self.esem = {e: es.enter_context(nc.semaphore("sem_" + e)) for e in COMPUTE}
        self.tick = {e: 0 for e in COMPUTE}
        self.slots = []
        for i in range(nslots):
            s = DmaSlot("dq%d" % i)
            s.sem = es.enter_context(nc.semaphore(s.name))
            self.slots.append(s)


class Prog:
    def __init__(self, sync):
        self.sync = sync
        self.nc = sync.nc
        self.ops = []
        self.nslot = 0

    def slot(self, name=""):
        s = self.sync.slots[self.nslot]
        self.nslot += 1
        return s

    def add(self, eng, fn, reads=(), writes=(), slot=None):
        op = Op()
        op.eng = eng
        op.fn = fn
        op.slot = slot
        op.signal = slot is not None
        op.tick = None
        op.idx = len(self.ops)
        deps = {}
        is_dma = slot is not None
        for b in reads:
            w = b.lw
            if w is not None:
                if is_dma or w.slot is not None or w.eng != eng or eng != "pe":
                    deps[w.idx] = w
        for b in writes:
            w = b.lw
            if w is not None and (is_dma or w.slot is not None or w.eng != eng or (STRICT and eng != "pe")):
                deps[w.idx] = w
            for r in b.rd:
                if is_dma or r.slot is not None or r.eng != eng or (STRICT and eng != "pe"):
                    deps[r.idx] = r
        for b in reads:
            b.rd.append(op)
        for b in writes:
            b.lw = op
            b.rd = []
        op.deps = list(deps.values())
        for d in op.deps:
            d.signal = True
        self.ops.append(op)
        return op

    def pe(self, fn, reads=(), writes=()):
        return self.add("pe", fn, reads, writes)

    def act(self, fn, reads=(), writes=()):
        return self.add("act", fn, reads, writes)

    def dve(self, fn, reads=(), writes=()):
        return self.add("dve", fn, reads, writes)

    def pool(self, fn, reads=(), writes=()):
        return self.add("pool", fn, reads, writes)

    def dma(self, slot, fn, reads=(), writes=()):
        return self.add("sp", fn, reads, writes, slot=slot)

    def emit(self):
        nc = self.nc
        sy = self.sync
        for op in self.ops:
            if op.slot is not None:
                op.slot.count += 16
                op.tick = op.slot.count
                op.semkey = op.slot
            elif op.signal:
                sy.tick[op.eng] += 1
                op.tick = sy.tick[op.eng]
                op.semkey = op.eng
        base = {}
        for e in COMPUTE:
            base[e] = 0
        clock = {e: {} for e in ENGS}
        start_tick = dict(self._start_tick)
        start_slot = dict(self._start_slot)
        for e in ENGS:
            for k, v in start_tick.items():
                clock[e][k] = v
            for k, v in start_slot.items():
                clock[e][k] = v
        plan = {e: [] for e in ENGS}
        for op in self.ops:
            ck = clock[op.eng]
            need = {}
            for d in op.deps:
                if ck.get(d.semkey, 0) >= d.tick:
                    continue
                if need.get(d.semkey, 0) < d.tick:
                    need[d.semkey] = d.tick
            for d in op.deps:
                for k, v in d.known.items():
                    if ck.get(k, 0) < v:
                        ck[k] = v
            waits = list(need.items())
            for k, v in waits:
                if ck.get(k, 0) < v:
                    ck[k] = v
            if op.tick is not None:
                kn = dict(ck)
                if kn.get(op.semkey, 0) < op.tick:
                    kn[op.semkey] = op.tick
                op.known = kn
            plan[op.eng].append((op, waits))
        final_waits = [(s, s.count) for s in sy.slots[: self.nslot] if s.count > start_slot.get(s, 0)]
        esem = sy.esem

        def semof(k):
            return k.sem if isinstance(k, DmaSlot) else esem[k]

        def run(engname):
            def body(eng):
                for op, waits in plan[engname]:
                    for k, v in waits:
                        eng.wait_ge(semof(k), v)
                    ins = op.fn(eng)
                    if op.slot is not None:
                        ins.then_inc(op.slot.sem, 16)
                    elif op.tick is not None:
                        ins.then_inc(esem[engname], 1)
                if engname == "sp":
                    for s, v in final_waits:
                        eng.wait_ge(s.sem, v)
            return body

        with nc.Block() as block:
            block.tensor(run("pe"))
            block.scalar(run("act"))
            block.vector(run("dve"))
            block.gpsimd(run("pool"))
            block.sync(run("sp"))

    def begin(self):
        sy = self.sync
        self._start_tick = dict(sy.tick)
        self._start_slot = {s: s.count for s in sy.slots}
        return self


_UID = [0]


def uid():
    _UID[0] += 1
    return "_u%d" % _UID[0]


def fview(ap, dims):
    return bass.AP(tensor=ap.tensor, offset=ap.offset, ap=[list(ap.ap[0])] + [list(d) for d in dims])


def FV(t, p0, p1, off, dims):
    return fview(t[p0:p1, off:off + 1], dims)


class Rot:
    def __init__(self, items):
        self.items = items
        self.bufs = [Buf() for _ in items]
        self.i = 0

    def next(self):
        k = self.i % len(self.items)
        self.i += 1
        return self.items[k], self.bufs[k]


DBG = {}


def dump(P, name, t, bufs, dt=None):
    nc = P.nc
    shape = list(t.shape)
    d = nc.dram_tensor("dbg_" + name, shape, dt or t.dtype, kind="ExternalOutput").ap()
    P.dma(P.slot(), lambda e: e.dma_start(out=d[:, :], in_=t[:, :]), reads=bufs)


def phase_norm(sync, es_outer, xsrc, hnT, ident):
    nc = sync.nc
    P = Prog(sync).begin()
    with contextlib.ExitStack() as es:
        sfx = uid()

        def sb(name, shape, dt):
            return es.enter_context(nc.sbuf_tensor(name + sfx, shape, dt))
        xt = Rot([sb("n_xt%d" % i, [128, D], F32) for i in range(8)])
        junk = sb("n_junk", [128, D], BF16)
        hn0 = Rot([sb("n_hn%d" % i, [128, D], BF16) for i in range(2)])
        ss = sb("n_ss", [128, 32], F32)
        ln = sb("n_ln", [128, 32], F32)
        rs = sb("n_rs", [128, 32], F32)
        epsc = sb("n_eps", [128, 1], F32)
        ptr = Rot([es.enter_context(nc.psum_tensor("n_ptr%d" % i + sfx, [128, D], BF16)) for i in range(2)])
        bjunk = Buf()
        beps = Buf()
        bhn = Buf()
        slots = [P.slot() for _ in range(8)]
        P.dve(lambda e: e.memset(epsc[:], EPS), writes=[beps])
        NB = 4
        pendB = []
        for i0 in range(0, 32, NB):
            grp = []
            bssg = Buf()
            for i in range(i0, i0 + NB):
                x_t, bx = xt.next()
                sl = slots[i % len(slots)]
                P.dma(sl, lambda e, x_t=x_t, i=i: e.dma_start(out=x_t[:], in_=xsrc[128 * i:128 * (i + 1), :]), writes=[bx])
                P.act(lambda e, x_t=x_t, i=i: e.activation(out=junk[:], in_=x_t[:], func=AF.Square, accum_out=ss[:, i:i + 1]),
                      reads=[bx], writes=[bjunk, bssg])
                grp.append((i, x_t, bx))
            bln = Buf()
            P.act(lambda e, i0=i0: e.activation(out=ln[:, i0:i0 + NB], in_=ss[:, i0:i0 + NB], func=AF.Ln, bias=epsc[:], scale=1.0 / D),
                  reads=[bssg, beps], writes=[bln])
            brs = Buf()
            P.act(lambda e, i0=i0: e.activation(out=rs[:, i0:i0 + NB], in_=ln[:, i0:i0 + NB], func=AF.Exp, scale=-0.5),
                  reads=[bln], writes=[brs])
            for i, x_t, bx in grp:
                def partA(i=i, x_t=x_t, bx=bx, brs=brs):
                    h_t, bh = hn0.next()
                    P.dve(lambda e: e.tensor_scalar(out=h_t[:], in0=x_t[:], scalar1=rs[:, i:i + 1], scalar2=None, op0=ALU.mult),
                          reads=[bx, brs], writes=[bh])
                    p_t, bp = ptr.next()
                    for kc in range(8):
                        P.pe(lambda e, kc=kc: e.transpose(p_t[:, kc * 128:(kc + 1) * 128], h_t[:, kc * 128:(kc + 1) * 128], ident[:]),
                             reads=[bh], writes=[bp])

                    def partB():
                        P.dve(lambda e: e.tensor_copy(out=FV(hnT, 0, 128, 128 * i, [[S, 8], [1, 128]]),
                                                      in_=FV(p_t, 0, 128, 0, [[128, 8], [1, 128]])), reads=[bp], writes=[bhn])
                    return partB
                pendB.append(partA())
                if len(pendB) > 1:
                    pendB.pop(0)()
        while pendB:
            pendB.pop(0)()
        if DBG.get("hnT"):
            dump(P, "hnT", hnT, [bhn])
            dump(P, "rs", rs, [bhn])
        P.emit()


def phase_outproj(sync, xsrc, wo_dram, ysc, out):
    nc = sync.nc
    P = Prog(sync).begin()
    with contextlib.ExitStack() as es:
        sfx = uid()

        def sb(name, shape, dt):
            return es.enter_context(nc.sbuf_tensor(name + sfx, shape, dt))
        wst = Rot([sb("o_wst%d" % i, [128, D], F32) for i in range(2)])
        wo = sb("o_wo", [128, 8 * D], BF16)
        bwo = Buf()
        xt = Rot([sb("o_xt%d" % i, [128, D], F32) for i in range(3)])
        ot = Rot([sb("o_ot%d" % i, [128, D], F32) for i in range(2)])
        yt = Rot([sb("o_yt%d" % i, [128, 8 * 512], BF16) for i in range(2)])
        po = Rot([es.enter_context(nc.psum_tensor("o_po%d" % i + sfx, [128, 512], F32)) for i in range(4)])
        s_w = [P.slot() for _ in range(2)]
        s_x = [P.slot() for _ in range(3)]
        s_y = [P.slot() for _ in range(2)]
        s_o = [P.slot() for _ in range(2)]
        for kc in range(8):
            w_t, bw = wst.next()
            P.dma(s_w[kc % 2], lambda e, w_t=w_t, kc=kc: e.dma_start(out=w_t[:], in_=wo_dram[kc * 128:(kc + 1) * 128, :]), writes=[bw])
            if kc % 2 == 0:
                P.dve(lambda e, w_t=w_t, kc=kc: e.tensor_copy(out=wo[:, kc * D:(kc + 1) * D], in_=w_t[:]), reads=[bw], writes=[bwo])
            else:
                P.act(lambda e, w_t=w_t, kc=kc: e.activation(out=wo[:, kc * D:(kc + 1) * D], in_=w_t[:], func=AF.Copy), reads=[bw], writes=[bwo])
        xts = {}
        yts = {}

        def load_x(i):
            x_t, bx = xt.next()
            P.dma(s_x[i % 3], lambda e: e.dma_start(out=x_t[:], in_=xsrc[128 * i:128 * (i + 1), :]), writes=[bx])
            xts[i] = (x_t, bx)

        def load_y(tb):
            y_t, by = yt.next()
            P.dma(s_y[tb % 2], lambda e: e.dma_start(
                out=FV(y_t, 0, 128, 0, [[512, 8], [1, 512]]),
                in_=ysc[:, :, tb * 512:(tb + 1) * 512].rearrange("k p t -> p k t")), writes=[by])
            yts[tb] = (y_t, by)
        load_y(0)
        load_x(0)
        load_x(1)
        for i in range(32):
            if i + 2 < 32:
                load_x(i + 2)
            if i % 4 == 0 and i // 4 + 1 < 8:
                load_y(i // 4 + 1)
            y_t, by = yts[i // 4]
            x_t, bx = xts[i]
            o_t, bo = ot.next()
            for nb in range(2):
                p_t, bp = po.next()
                for kc in range(8):
                    P.pe(lambda e, p_t=p_t, y_t=y_t, kc=kc, nb=nb, i=i: e.matmul(
                        p_t[:], lhsT=y_t[:, kc * 512 + (i % 4) * 128: kc * 512 + (i % 4) * 128 + 128],
                        rhs=wo[:, kc * D + nb * 512: kc * D + nb * 512 + 512], start=(kc == 0), stop=(kc == 7)),
                        reads=[by, bwo], writes=[bp])
                P.dve(lambda e, p_t=p_t, x_t=x_t, o_t=o_t, nb=nb: e.tensor_tensor(
                    out=o_t[:, nb * 512:(nb + 1) * 512], in0=p_t[:], in1=x_t[:, nb * 512:(nb + 1) * 512], op=ALU.add),
                    reads=[bp, bx], writes=[bo])
            P.dma(s_o[i % 2], lambda e, o_t=o_t, i=i: e.dma_start(out=out[128 * i:128 * (i + 1), :], in_=o_t[:]), reads=[bo])
        P.emit()


def qk_geometry(d):
    L = S // d
    return L, L // 128


def merge_steps(a, b):
    out = []
    na, nb = len(a), len(b)
    if na == 0 or nb == 0:
        return list(a) + list(b)
    ia = ib = 0
    while ia < na or ib < nb:
        if ib >= nb or (ia < na and ia * nb <= ib * na):
            out.append(a[ia]); ia += 1
        else:
            out.append(b[ib]); ib += 1
    return out


def phase_attn(sync, layer, hnT, dr, ysc):
    nc = sync.nc
    P = Prog(sync).begin()
    isA = layer == "A"
    w_dram, wg_dram, bias_dram = dr["w"], dr["wg"], dr["bias"]
    units = [(hp, g) for hp in range(8) for g in ((0, 1, 2) if isA else (0,))]
    with contextlib.ExitStack() as es:
        sfx = uid()

        def sb(name, shape, dt):
            return es.enter_context(nc.sbuf_tensor(name + sfx, shape, dt))
        ngc = sb("a_ngc", [128, 8], F32)
        gq = sb("a_gq", [128, 3], F32)
        gk = sb("a_gk", [128, 3], F32)
        epsc = sb("a_eps", [128, 1], F32)
        blk = sb("a_blk", [128, 128], BF16)
        bconst = Buf()
        s_c = P.slot()
        P.dma(s_c, lambda e: e.dma_start(out=ngc[:], in_=dr["ng"][:, :]), writes=[bconst])
        P.dma(s_c, lambda e: e.dma_start(out=gq[:], in_=dr["gq"][:, :]), writes=[bconst])
        P.dma(s_c, lambda e: e.dma_start(out=gk[:], in_=dr["gk"][:, :]), writes=[bconst])
        P.dve(lambda e: e.memset(epsc[:], EPS), writes=[bconst])
        P.dve(lambda e: e.memset(blk[:], 0.0), reads=[bconst], writes=[bconst])
        P.dve(lambda e: e.memset(blk[0:64, 0:64], 1.0 / 64), reads=[bconst], writes=[bconst])
        P.dve(lambda e: e.memset(blk[64:128, 64:128], 1.0 / 64), reads=[bconst], writes=[bconst])
        P.dve(lambda e: e.tensor_scalar(out=gq[:], in0=gq[:], scalar1=0.125, scalar2=None, op0=ALU.mult),
              reads=[bconst], writes=[bconst])
        qTr = Rot([sb("a_qT%d" % i, [128, S], BF16) for i in range(2)])
        kTr = Rot([sb("a_kT%d" % i, [128, S], BF16) for i in range(2)])
        Vr = Rot([sb("a_V%d" % i, [128, 32 * 192], BF16) for i in range(2)])
        for V_i, bV_i in zip(Vr.items, Vr.bufs):
            P.dve(lambda e, V_i=V_i: e.memset(FV(V_i, 0, 128, 64, [[192, 32], [1, 64]]), 1.0), writes=[bV_i])
        sgT = sb("a_sgT", [128, S], BF16)
        bsgt = [Buf() for _ in range(8)]
        if isA:
            acc = [sb("a_acc%d" % h, [128, S], F32) for h in range(2)]
            bacc = [Buf(), Buf()]
        wst = Rot([sb("a_wst%d" % i, [128, 384], F32) for i in range(4)])
        s_w = [P.slot() for _ in range(4)]
        wb = Rot([sb("a_wb%d" % i, [128, 8 * 384], BF16) for i in range(2)])
        wgb = sb("a_wgb", [128, 8 * 128], BF16)
        bwg = Buf()
        ebw = 2 * 256 if isA else dr["ebw"]
        ebst = Rot([sb("a_ebst%d" % i, [128, 512 if isA else 768], F32) for i in range(1 if isA else 2)])
        s_eb = [P.slot() for _ in range(2)]
        eb = Rot([sb("a_eb%d" % i, [128, ebw], BF16) for i in range(2)])
        sq = Rot([sb("a_sq%d" % i, [128, 512], BF16) for i in range(2)])
        rstd = Rot([sb("a_rstd%d" % i, [128, 512], F32) for i in range(2)])
        ew = 512 if isA else 768
        pT = Rot([sb("a_pT%d" % i, [128, ew], BF16) for i in range(10 if isA else 9)])
        rec = Rot([sb("a_rec%d" % i, [128, 512], F32) for i in range(1 if isA else 2)])
        if isA:
            yT = Rot([sb("a_yT%d" % i, [128, 1024], BF16) for i in range(2)])
            s_y = [P.slot() for _ in range(2)]
        else:
            yTB = [Rot([sb("b_yT%d_%d" % (h, i), [128, 1024], BF16) for i in range(2)]) for h in range(2)]
            s_yB = [[P.slot() for _ in range(2)] for h in range(2)]
        print("phase_attn", layer, "sbuf bytes remaining", nc.sbuf_bytes_remaining)
        banks = [es.enter_context(nc.psum_tensor("a_ps%d" % i + sfx, [128, 512], F32)) for i in range(8)]
        bbank = [Buf() for _ in range(8)]

        def bankrot(ids):
            r = Rot([banks[i] for i in ids])
            r.bufs = [bbank[i] for i in ids]
            return r
        pq = bankrot([0, 1, 7])
        pss = bankrot([2])
        if isA:
            ps_h = [bankrot([3]), bankrot([4])]
            po_h = [bankrot([5]), bankrot([6])]
            pv = bankrot([2])
            pg = bankrot([5, 6])
        else:
            pq = bankrot([0, 1])
            psA = bankrot([3, 4])
            bRem = [Buf(), Buf()]
            po = bankrot([6, 7])
            pv = bankrot([2])
            pg = bankrot([6, 7])

        def wload_steps(u):
            hp, g = u
            w_b, bw = wb.next()
            pend = []
            steps = []

            def conv(w_t, bs, kc):
                P.dve(lambda e: e.tensor_scalar(
                    out=w_b[:, kc * 384:(kc + 1) * 384], in0=w_t[:], scalar1=ngc[:, kc:kc + 1], scalar2=None, op0=ALU.mult),
                    reads=[bs, bconst], writes=[bw])
            for kc in range(8):
                def step(kc=kc):
                    w_t, bs = wst.next()
                    sl = s_w[(wst.i - 1) % 4]
                    P.dma(sl, lambda e: e.dma_start(out=w_t[:], in_=w_dram[hp, g, kc * 128:(kc + 1) * 128, :]), writes=[bs])
                    pend.append((w_t, bs, kc))
                    if len(pend) > 2:
                        conv(*pend.pop(0))
                steps.append(step)

            def flush():
                while pend:
                    conv(*pend.pop(0))
            steps.append(flush)
            return (w_b, bw), steps

        def gload_steps(hp):
            pend = []
            steps = []

            def conv(w_t, bs, kc):
                P.dve(lambda e: e.tensor_scalar(
                    out=wgb[:, kc * 128:(kc + 1) * 128], in0=w_t[:, 0:128], scalar1=ngc[:, kc:kc + 1], scalar2=None, op0=ALU.mult),
                    reads=[bs, bconst], writes=[bwg])
            for kc in range(8):
                def step(kc=kc):
                    w_t, bs = wst.next()
                    sl = s_w[(wst.i - 1) % 4]
                    P.dma(sl, lambda e: e.dma_start(out=w_t[:, 0:128], in_=wg_dram[hp, kc * 128:(kc + 1) * 128, :]), writes=[bs])
                    pend.append((w_t, bs, kc))
                    if len(pend) > 2:
                        conv(*pend.pop(0))
                steps.append(step)

            def flush():
                while pend:
                    conv(*pend.pop(0))
            steps.append(flush)
            return steps

        def eb_dma_A(u):
            hp, g = u
            st_, bs = ebst.next()
            sl = s_eb[0]
            P.dma(sl, lambda e: e.dma_start(out=st_[:, 0:512], in_=bias_dram[hp, g, :, :]), writes=[bs])
            return st_, bs

        def eb_conv_A(st_, bs):
            e_t, be = eb.next()
            P.act(lambda e: e.activation(out=e_t[:, 0:512], in_=st_[:, 0:512], func=AF.Exp), reads=[bs], writes=[be])
            return e_t, be

        def eb_steps_B(hp, h):
            e_t, be = eb.next()
            ebw1 = dr["ebw"]
            pieces = []
            off = 0
            while off < ebw1:
                n = min(768, ebw1 - off)
                pieces.append((off, n))
                off += n
            pend = []
            steps = []

            def conv(st_, bs, off, n):
                P.act(lambda e: e.activation(out=e_t[:, off: off + n], in_=st_[:, 0:n], func=AF.Exp), reads=[bs], writes=[be])
            for off, n in pieces:
                def step(off=off, n=n):
                    st_, bs = ebst.next()
                    sl = s_eb[(ebst.i - 1) % 2]
                    P.dma(sl, lambda e: e.dma_start(out=st_[:, 0:n], in_=bias_dram[2 * hp + h, :, off:off + n]), writes=[bs])
                    pend.append((st_, bs, off, n))
                    if len(pend) > 1:
                        conv(*pend.pop(0))
                steps.append(step)

            def flush():
                while pend:
                    conv(*pend.pop(0))
            steps.append(flush)
            return (e_t, be), steps

        def gate_steps():
            steps = []
            for tb in range(8):
                def step(tb=tb):
                    p_t, bp = pg.next()
                    for kc in range(8):
                        P.pe(lambda e, p_t=p_t, kc=kc: e.matmul(
                            p_t[:], lhsT=wgb[:, kc * 128:(kc + 1) * 128], rhs=hnT[:, kc * S + tb * 512: kc * S + tb * 512 + 512],
                            start=(kc == 0), stop=(kc == 7)), reads=[bwg], writes=[bp])
                    P.act(lambda e, p_t=p_t: e.activation(out=sgT[:, tb * 512:(tb + 1) * 512], in_=p_t[:], func=AF.Silu),
                          reads=[bp], writes=[bsgt[tb]])
                steps.append(step)
            return steps

        def qk_steps(u, w_b, bw, q_t, bq, k_t, bk):
            hp, g = u
            d = DILS[g] if isA else 1
            L = S // d
            pend = []

            def tail(p_t, bp, s_t, bs, which, tb):
                ss_t, bss = pss.next()
                P.pe(lambda e: e.matmul(ss_t[:], lhsT=blk[:], rhs=s_t[:], start=True, stop=True),
                     reads=[bs, bconst], writes=[bss])
                r_t, br = rstd.next()
                P.act(lambda e: e.activation(out=r_t[:], in_=ss_t[:], func=AF.Ln, bias=epsc[:], scale=1.0),
                      reads=[bss, bconst], writes=[br])
                P.act(lambda e: e.activation(out=r_t[:], in_=r_t[:], func=AF.Exp, scale=-0.5), reads=[br], writes=[br])
                if which == 0:
                    n = 512 // d
                    P.dve(lambda e: e.scalar_tensor_tensor(
                        out=FV(q_t, 0, 128, tb * n, [[L, d], [1, n]]),
                        in0=FV(p_t, 0, 128, 0, [[1, d], [d, n]]), scalar=gq[:, g:g + 1],
                        in1=FV(r_t, 0, 128, 0, [[1, d], [d, n]]), op0=ALU.mult, op1=ALU.mult),
                        reads=[bp, br, bconst], writes=[bq])
                else:
                    P.dve(lambda e: e.scalar_tensor_tensor(
                        out=k_t[:, tb * 512:(tb + 1) * 512], in0=p_t[:], scalar=gk[:, g:g + 1], in1=r_t[:],
                        op0=ALU.mult, op1=ALU.mult), reads=[bp, br, bconst], writes=[bk])

            steps = []
            for t in range(16):
                def step(t=t):
                    which, tb = divmod(t, 8)
                    p_t, bp = pq.next()
                    for kc in range(8):
                        P.pe(lambda e, kc=kc: e.matmul(
                            p_t[:], lhsT=w_b[:, kc * 384 + which * 128: kc * 384 + which * 128 + 128],
                            rhs=hnT[:, kc * S + tb * 512: kc * S + tb * 512 + 512], start=(kc == 0), stop=(kc == 7)),
                            reads=[bw], writes=[bp])
                    s_t, bs = sq.next()
                    P.act(lambda e: e.activation(out=s_t[:], in_=p_t[:], func=AF.Square), reads=[bp], writes=[bs])
                    if pend:
                        tail(*pend.pop())
                    pend.append((p_t, bp, s_t, bs, which, tb))
                steps.append(step)

            def flush():
                tail(*pend.pop())
            steps.append(flush)
            return steps

        def v_steps(u, w_b, bw, Vt, bV):
            hp, g = u
            d = DILS[g] if isA else 1
            L, nC = qk_geometry(d)
            steps = []
            for c0 in range(0, 32, 4):
                def step(c0=c0):
                    p_t, bp = pv.next()
                    for cc in range(4):
                        c = c0 + cc
                        r, i = divmod(c, nC)
                        t0 = r + d * 128 * i
                        for kc in range(8):
                            P.pe(lambda e, kc=kc, cc=cc, t0=t0: e.matmul(
                                p_t[:, cc * 128:(cc + 1) * 128],
                                lhsT=FV(hnT, 0, 128, kc * S + t0, [[d, 128]]),
                                rhs=w_b[:, kc * 384 + 256: kc * 384 + 384], start=(kc == 0), stop=(kc == 7)),
                                reads=[bw], writes=[bp])
                    P.act(lambda e: e.activation(
                        out=FV(Vt, 0, 128, c0 * 192, [[192, 4], [128, 2], [1, 64]]),
                        in_=FV(p_t, 0, 128, 0, [[128, 4], [64, 2], [1, 64]]), func=AF.Copy), reads=[bp], writes=[bV])
                steps.append(step)
            return steps

        def attn_steps_A(u, q_t, bq, k_t, bk, e_t, be, Vt, bV):
            hp, g = u
            d = DILS[g]
            first = g == 0
            L, nC = qk_geometry(d)
            pend = []
            steps = []

            def acc_out(h, srcf, sbuf, dstf):
                if first:
                    P.dve(lambda e: e.tensor_copy(out=dstf(), in_=srcf()), reads=[sbuf], writes=[bacc[h]])
                else:
                    P.dve(lambda e: e.tensor_tensor(out=dstf(), in0=srcf(), in1=dstf(), op=ALU.add), reads=[sbuf, bacc[h]], writes=[bacc[h]])

            def make_tail(r, m, ptl, state):
                def tail():
                    for h in range(2):
                        vof = 0 if h == 0 else 64
                        acc_h = acc[h]
                        key = "po%d" % h

                        def pcol(i, lo):
                            p_t, bp = ptl[(h, i // 2)]
                            return p_t, bp, (i % 2) * 256 + lo

                        def mm(o_t, bo, col, n, chunk, pa, bpa, ca, start, stop, vof=vof):
                            P.pe(lambda e: e.matmul(
                                o_t[:, col:col + n], lhsT=Vt[:, chunk * 192 + vof: chunk * 192 + vof + 128],
                                rhs=pa[:, ca:ca + n], start=start, stop=stop), reads=[bV, bpa], writes=[bo])
                        if d == 16:
                            if r % 2 == 0:
                                state[key] = po_h[h].next()
                            o_t, bo = state[key]
                            base = (r % 2) * 256
                            c1 = r * nC
                            pa, bpa, ca = pcol(0, 64)
                            mm(o_t, bo, base, 64, c1, pa, bpa, ca, True, True)
                            pa, bpa, ca = pcol(0, 128)
                            mm(o_t, bo, base + 64, 128, c1, pa, bpa, ca, True, False)
                            pa, bpa, ca = pcol(1, 0)
                            mm(o_t, bo, base + 64, 128, c1 + 1, pa, bpa, ca, False, True)
                            pa, bpa, ca = pcol(1, 128)
                            mm(o_t, bo, base + 192, 64, c1 + 1, pa, bpa, ca, True, True)
                            if r % 2 == 1:
                                acc_out(h, lambda o_t=o_t: o_t[:, 0:512], bo,
                                        lambda acc_h=acc_h: FV(acc_h, 0, 128, r - 1, [[1, 2], [16, 256]]))
                            continue
                        if m == 0:
                            state[key] = po_h[h].next()
                            o_t, bo = state[key]
                            pa, bpa, ca = pcol(0, 64)
                            mm(o_t, bo, 64, 64, r * nC, pa, bpa, ca, True, True)
                        for jj in ([2 * m - 1] if m > 0 else []) + ([2 * m] if 2 * m <= nC - 2 else []):
                            if jj < 3:
                                slot = jj + 1
                            else:
                                slot = (jj - 3) % 4
                                if slot == 0:
                                    state[key] = po_h[h].next()
                            o_t, bo = state[key]
                            pa, bpa, ca = pcol(jj, 128)
                            pb_, bpb, cb = pcol(jj + 1, 0)
                            c1 = r * nC + jj
                            mm(o_t, bo, slot * 128, 128, c1, pa, bpa, ca, True, False)
                            mm(o_t, bo, slot * 128, 128, c1 + 1, pb_, bpb, cb, False, True)
                            if jj == 2:
                                acc_out(h, lambda o_t=o_t: o_t[:, 64:512], bo,
                                        lambda acc_h=acc_h: FV(acc_h, 0, 128, r, [[d, 448]]))
                            elif jj > 2 and slot == 3:
                                m0 = 64 + 128 * (jj - 3)
                                acc_out(h, lambda o_t=o_t: o_t[:, 0:512], bo,
                                        lambda acc_h=acc_h, m0=m0: FV(acc_h, 0, 128, r + d * m0, [[d, 512]]))
                        if m == nC // 2 - 1:
                            assert (nC - 2 - 3) % 4 == 3
                            o_t, bo = po_h[h].next()
                            pa, bpa, ca = pcol(nC - 1, 128)
                            mm(o_t, bo, 0, 64, r * nC + nC - 1, pa, bpa, ca, True, True)
                            acc_out(h, lambda o_t=o_t: o_t[:, 0:64], bo,
                                    lambda acc_h=acc_h: FV(acc_h, 0, 128, r + d * (L - 64), [[d, 64]]))
                return tail

            state = {}
            for r in range(d):
                ptl = {}
                for m in range(nC // 2):
                    def step(r=r, m=m, ptl=ptl, state=state):
                        stl = [ps_h[0].next(), ps_h[1].next()]
                        for cc in range(2):
                            i = 2 * m + cc
                            qlo = max(0, 128 * i - 64)
                            qhi = min(L, 128 * i + 192)
                            lo = qlo - (128 * i - 64)
                            n = qhi - qlo
                            for h in range(2):
                                s_t, bs = stl[h]
                                P.pe(lambda e, s_t=s_t, cc=cc, lo=lo, n=n, qlo=qlo, i=i, h=h: e.matmul(
                                    s_t[:, cc * 256 + lo: cc * 256 + lo + n],
                                    lhsT=FV(k_t, 64 * h, 64 * h + 64, r + d * 128 * i, [[d, 128]]),
                                    rhs=q_t[64 * h:64 * h + 64, r * L + qlo: r * L + qlo + n], start=True, stop=True),
                                    reads=[bk, bq], writes=[bs])
                        for h in range(2):
                            s_t, bs = stl[h]
                            p_t, bp = pT.next()
                            P.act(lambda e, p_t=p_t, s_t=s_t: e.activation(out=p_t[:, 0:512], in_=s_t[:], func=AF.Exp), reads=[bs], writes=[bp])
                            P.dve(lambda e, p_t=p_t, h=h: e.tensor_tensor(
                                out=FV(p_t, 0, 128, 0, [[256, 2], [1, 256]]), in0=FV(p_t, 0, 128, 0, [[256, 2], [1, 256]]),
                                in1=FV(e_t, 0, 128, h * 256, [[0, 2], [1, 256]]), op=ALU.mult), reads=[bp, be], writes=[bp])
                            ptl[(h, m)] = (p_t, bp)
                        if len(pend) > 1:
                            pend.pop(0)()
                        pend.append(make_tail(r, m, ptl, state))
                    steps.append(step)

            def flush():
                while pend:
                    pend.pop(0)()
            steps.append(flush)
            return steps

        def normalize_steps_A(hp):
            steps = []
            st = {}
            for tb in range(8):
                def step(tb=tb):
                    q4, hb = divmod(tb, 2)
                    if hb == 0:
                        st["y"] = yT.next()
                    y_t, by = st["y"]
                    cs = slice(tb * 512, (tb + 1) * 512)
                    r_t, br = rec.next()
                    t_t, bt = r_t, br
                    P.act(lambda e: e.activation(out=r_t[0:64, :], in_=acc[0][64:128, cs], func=AF.Ln), reads=[bacc[0]], writes=[br])
                    P.act(lambda e: e.activation(out=r_t[64:128, :], in_=acc[1][0:64, cs], func=AF.Ln), reads=[bacc[1]], writes=[br])
                    P.act(lambda e: e.activation(out=r_t[:], in_=r_t[:], func=AF.Exp, scale=-1.0), reads=[br], writes=[br])
                    P.dve(lambda e: e.tensor_tensor(out=t_t[0:64, :], in0=acc[0][0:64, cs], in1=r_t[0:64, :], op=ALU.mult),
                          reads=[br], writes=[bt])
                    P.dve(lambda e: e.tensor_tensor(out=t_t[64:128, :], in0=acc[1][64:128, cs], in1=r_t[64:128, :], op=ALU.mult),
                          reads=[br], writes=[bt])
                    P.dve(lambda e: e.tensor_tensor(out=y_t[:, hb * 512:(hb + 1) * 512], in0=t_t[:], in1=sgT[:, cs], op=ALU.mult),
                          reads=[bt, bsgt[tb]], writes=[by])
                    if hb == 1:
                        sl = s_y[(yT.i - 1) % 2]
                        P.dma(sl, lambda e: e.dma_start(out=ysc[hp, :, q4 * 1024:(q4 + 1) * 1024], in_=y_t[:]), reads=[by])
                steps.append(step)
            return steps

        def attn_steps_B(hp, h, q_t, bq, k_t, bk, e_t, be, Vt, bV):
            tiles = dr["tiles"]
            hs = slice(64 * h, 64 * h + 64)
            vof = 0 if h == 0 else 64
            num = slice(0, 64) if h == 0 else slice(64, 128)
            den = slice(64, 128) if h == 0 else slice(0, 64)
            contribs = []
            for Q in range(32):
                full, part = [], []
                for R in range(32):
                    qlo, nr = tiles[R][1], tiles[R][2]
                    lo_r = max(qlo, 2 * Q)
                    hi_r = min(qlo + nr - 1, 2 * Q + 1)
                    if lo_r > hi_r:
                        continue
                    (full if hi_r - lo_r == 1 else part).append((R, lo_r, hi_r - lo_r + 1))
                assert full
                contribs.append(full + part)
            lastR = [max(R for R, _, _ in contribs[Q]) for Q in range(32)]
            ptl = {}
            st = dict(o=None, y=None)
            pend = []
            steps = []

            def do_block(Q):
                if Q % 4 == 0:
                    st["o"] = po.next()
                o_t, bo = st["o"]
                cl = contribs[Q]
                for n_, (R, row0, nrow) in enumerate(cl):
                    p_t, bp = ptl[R]
                    c0 = (row0 - tiles[R][1]) * 64
                    oc = (Q % 4) * 128 + (row0 - 2 * Q) * 64
                    nn = nrow * 64
                    P.pe(lambda e, p_t=p_t, c0=c0, oc=oc, nn=nn, R=R, first=(n_ == 0), last=(n_ == len(cl) - 1): e.matmul(
                        o_t[:, oc:oc + nn], lhsT=Vt[:, R * 192 + vof: R * 192 + vof + 128],
                        rhs=p_t[:, c0:c0 + nn], start=first, stop=last), reads=[bV, bp], writes=[bo])
                if Q % 4 == 3:
                    tb = Q // 4
                    cs = slice(tb * 512, (tb + 1) * 512)
                    if tb % 2 == 0:
                        st["y"] = yTB[h].next()
                    y_t, by = st["y"]
                    r_t, br = rec.next()
                    t_t, bt = r_t, br
                    P.act(lambda e: e.activation(out=r_t[num, :], in_=o_t[den, :], func=AF.Ln), reads=[bo], writes=[br])
                    P.act(lambda e: e.activation(out=r_t[num, :], in_=r_t[num, :], func=AF.Exp, scale=-1.0), reads=[br], writes=[br])
                    P.dve(lambda e: e.tensor_tensor(out=t_t[num, :], in0=o_t[num, :], in1=r_t[num, :], op=ALU.mult),
                          reads=[bo, br], writes=[bt])
                    P.dve(lambda e: e.tensor_tensor(
                        out=y_t[num, (tb % 2) * 512:(tb % 2) * 512 + 512], in0=t_t[num, :], in1=sgT[num, cs], op=ALU.mult),
                        reads=[bt, bsgt[tb]], writes=[by])
                    if tb % 2 == 1:
                        q4 = tb // 2
                        sl = s_yB[h][(yTB[h].i - 1) % 2]
                        P.dma(sl, lambda e: e.dma_start(
                            out=ysc[hp, num, q4 * 1024:(q4 + 1) * 1024], in_=y_t[num, :]), reads=[by])

            for R in range(32):
                def step(R=R):
                    toff, qlo, nr = tiles[R]
                    n = nr * 64
                    sA, bA = psA.next()
                    sB = banks[5]
                    bB = bbank[5]
                    n1 = min(n, 512)
                    p_t, bp = pT.next()
                    if n > 512:
                        n2 = n - 512
                        P.pe(lambda e: e.matmul(
                            sB[:, 0:n2], lhsT=k_t[hs, R * 128:(R + 1) * 128], rhs=q_t[hs, qlo * 64 + 512: qlo * 64 + 512 + n2], start=True, stop=True),
                            reads=[bk, bq], writes=[bB])
                        P.act(lambda e: e.activation(out=p_t[:, 512:512 + n2], in_=sB[:, 0:n2], func=AF.Exp), reads=[bB], writes=[bp])
                    P.pe(lambda e: e.matmul(
                        sA[:, 0:n1], lhsT=k_t[hs, R * 128:(R + 1) * 128], rhs=q_t[hs, qlo * 64: qlo * 64 + n1], start=True, stop=True),
                        reads=[bk, bq], writes=[bA])
                    P.act(lambda e: e.activation(out=p_t[:, 0:n1], in_=sA[:, 0:n1], func=AF.Exp), reads=[bA], writes=[bp])
                    P.dve(lambda e: e.tensor_tensor(
                        out=p_t[:, 0:n], in0=p_t[:, 0:n], in1=e_t[:, toff: toff + n], op=ALU.mult),
                        reads=[bp, be], writes=[bp])
                    ptl[R] = (p_t, bp)
                    if len(pend) > 1:
                        pend.pop(0)()

                    def tail():
                        for Q in range(32):
                            if lastR[Q] == R:
                                do_block(Q)
                    pend.append(tail)
                steps.append(step)

            def flush():
                while pend:
                    pend.pop(0)()
            steps.append(flush)
            return steps

        def run(steps):
            for st_ in steps:
                st_()

        dstop = DBG.get("stop")
        nU = len(units)
        wts = {}
        qk = {}
        vv = {}
        ebd = {}
        ebB = {}
        wts[0], ws = wload_steps(units[0])
        run(ws)
        run(gload_steps(0))
        if isA:
            ebd[0] = eb_dma_A(units[0])
        if nU > 1:
            wts[1], ws1 = wload_steps(units[1])
        else:
            ws1 = []
        qk[0] = qTr.next() + kTr.next()
        vv[0] = Vr.next()
        run(merge_steps(qk_steps(units[0], *wts[0], *qk[0]) + v_steps(units[0], *wts[0], *vv[0]), ws1))
        run(gate_steps())
        for ui, u in enumerate(units):
            hp, g = u
            nxt = units[ui + 1] if ui + 1 < nU else None
            wsteps = []
            if ui + 2 < nU:
                wts[ui + 2], wsteps = wload_steps(units[ui + 2])
            q_t, bq, k_t, bk = qk[ui]
            V_t, bV = vv[ui]
            nsteps = []
            vsteps_late = []
            if nxt is not None:
                qk[ui + 1] = qTr.next() + kTr.next()
                vv[ui + 1] = Vr.next()
                qs_ = qk_steps(nxt, *wts[ui + 1], *qk[ui + 1])
                vs_ = v_steps(nxt, *wts[ui + 1], *vv[ui + 1])
                if isA and g == 2:
                    nsteps, vsteps_late = qs_, vs_
                else:
                    nsteps = merge_steps(qs_, vs_)
            if isA:
                e_t, be = eb_conv_A(*ebd[ui])
                if nxt is not None:
                    ebd[ui + 1] = eb_dma_A(nxt)
                last = g == 2
                gsteps = gload_steps(hp + 1) if (g == 1 and hp + 1 < 8) else []
                run(merge_steps(merge_steps(attn_steps_A(u, q_t, bq, k_t, bk, e_t, be, V_t, bV), nsteps), wsteps + gsteps))
                if last:
                    run(merge_steps(normalize_steps_A(hp), vsteps_late))
                    if hp + 1 < 8:
                        run(gate_steps())
            else:
                gsteps = gload_steps(hp + 1) if hp + 1 < 8 else []
                if hp == 0:
                    ebB[(0, 0)], es0 = eb_steps_B(0, 0)
                    run(es0)
                ebB[(hp, 1)], es1 = eb_steps_B(hp, 1)
                a0 = attn_steps_B(hp, 0, q_t, bq, k_t, bk, *ebB[(hp, 0)], V_t, bV)
                run(merge_steps(merge_steps(a0, nsteps[: len(nsteps) // 2]), wsteps + es1))
                es2 = []
                if hp + 1 < 8:
                    ebB[(hp + 1, 0)], es2 = eb_steps_B(hp + 1, 0)
                a1 = attn_steps_B(hp, 1, q_t, bq, k_t, bk, *ebB[(hp, 1)], V_t, bV)
                run(merge_steps(merge_steps(a1, nsteps[len(nsteps) // 2:]), gsteps + es2))
                if hp + 1 < 8:
                    run(gate_steps())
            if dstop == "hp0" and ((isA and g == 2) or not isA):
                break
        P.emit()


_T5_LUT = None


def _t5_bucket_np(rel):
    import math
    half, me = 16, 8
    ret = np.where(rel > 0, half, 0)
    n = np.abs(rel)
    nf = np.maximum(n, 1).astype(np.float32)
    large = me + (np.log(nf / np.float32(me)) / np.float32(math.log(1024 / me)) * np.float32(half - me)).astype(np.int32)
    large = np.minimum(large, half - 1)
    return ret + np.where(n < me, n, large)


def _bias_tiles_A(t5_bias):
    a = np.arange(128)[:, None]
    b = np.arange(256)[None, :]
    rel = a - b + 64
    valid = (b - a >= 0) & (b - a <= 128)
    out = np.empty((8, 3, 128, 2, 256), np.float32)
    for g, d in enumerate(DILS):
        idx = _t5_bucket_np(rel * d)
        for h in range(16):
            t = t5_bias[g * 16 + h][idx]
            out[h // 2, g, :, h % 2, :] = np.where(valid, t, np.float32(NEG))
    return out


def _geom_B():
    rows = 64
    r = np.arange(rows)
    rs = np.clip(r - 4, 0, rows - 8)
    c = np.arange(64)
    cs = np.clip(c - 8, 0, 64 - 16)
    tiles = []
    uniq = {}
    maps = []
    off = 0
    for R in range(32):
        krs = np.array([2 * R, 2 * R + 1])
        qrows = [q for q in range(rows) if (rs[q] <= krs[1]) and (rs[q] + 7 >= krs[0])]
        qlo, nr = qrows[0], len(qrows)
        assert qrows == list(range(qlo, qlo + nr))
        kr = np.repeat(krs, 64)[:, None]
        kc = np.tile(c, 2)[:, None]
        qr = np.repeat(np.arange(qlo, qlo + nr), 64)[None, :]
        qc = np.tile(c, nr)[None, :]
        valid = (kr >= rs[qr]) & (kr <= rs[qr] + 7) & (kc >= cs[qc]) & (kc < cs[qc] + 16)
        ridx = np.clip(kr - qr + 7, 0, 14)
        cidx = np.clip(kc - qc, -15, 15) + 15
        key = (nr, valid.tobytes(), ridx.tobytes())
        if key not in uniq:
            uniq[key] = off
            maps.append((off, ridx + 0 * cidx, cidx + 0 * ridx, valid))
            off += nr * 64
        tiles.append((uniq[key], qlo, nr))
    return tiles, maps, off


def _bias_tiles_B(rpb, maps, ebw):
    out = np.empty((16, 128, ebw), np.float32)
    for off, ridx, cidx, valid in maps:
        n = valid.shape[1]
        for h in range(16):
            out[h, :, off:off + n] = np.where(valid, rpb[h][ridx, cidx], np.float32(NEG))
    return out


def _unit_weights(w_in, ngroups):
    wu = np.empty((8, ngroups, D, 384), np.float32)
    for hp in range(8):
        for g in range(ngroups):
            for j in range(3):
                c0 = g * 3072 + j * 1024 + hp * 128
                wu[hp, g, :, j * 128:(j + 1) * 128] = w_in[:, c0:c0 + 128]
    gc = ngroups * 3072
    wg = np.ascontiguousarray(w_in[:, gc:gc + 1024].reshape(D, 8, 128).transpose(1, 0, 2))
    return wu, wg


_GEOM_B = None


def build_nc(layers="AB"):
    global _GEOM_B
    if _GEOM_B is None:
        _GEOM_B = _geom_B()
    tilesB, mapsB, ebwB = _GEOM_B
    nc = bass.Bass("TRN2", target_bir_lowering=False)

    def din(name, shape, dt=F32):
        return nc.dram_tensor(name, list(shape), dt, kind="ExternalInput").ap()
    x = din("x", [S, D])
    ident_d = din("ident_d", [128, 128])
    drA = dict(w=din("wA", [8, 3, D, 384]), wg=din("wgA", [8, D, 128]), bias=din("biasA", [8, 3, 128, 512]),
               ng=din("ngA", [128, 8]), gq=din("gqA", [128, 3]), gk=din("gkA", [128, 3]))
    woA = din("woA", [D, D])
    drB = dict(w=din("wB", [8, 1, D, 384]), wg=din("wgB", [8, D, 128]), bias=din("biasB", [16, 128, ebwB]),
               ng=din("ngB", [128, 8]), gq=din("gqB", [128, 3]), gk=din("gkB", [128, 3]), tiles=tilesB, ebw=ebwB)
    woB = din("woB", [D, D])
    out = nc.dram_tensor("out", [S, D], F32, kind="ExternalOutput").ap()
    ysc = nc.dram_tensor("ysc", [8, 128, S], BF16).ap()
    with contextlib.ExitStack() as es:
        sync = Sync(nc, es)
        hnT = es.enter_context(nc.sbuf_tensor("hnT", [128, 8 * S], BF16))
        ident = es.enter_context(nc.sbuf_tensor("ident", [128, 128], BF16))
        identf = es.enter_context(nc.sbuf_tensor("identf", [128, 128], F32))
        P0 = Prog(sync).begin()
        bi = Buf()
        P0.dma(P0.slot(), lambda e: e.dma_start(out=identf[:], in_=ident_d[:, :]), writes=[bi])
        P0.dve(lambda e: e.tensor_copy(out=ident[:], in_=identf[:]), reads=[bi], writes=[bi])
        P0.emit()
        src = x
        if "A" in layers:
            phase_norm(sync, es, src, hnT, ident)
            if DBG.get("stop") != "norm":
                phase_attn(sync, "A", hnT, drA, ysc)
            if not DBG.get("stop"):
                phase_outproj(sync, src, woA, ysc, out)
            src = out
        if "B" in layers:
            phase_norm(sync, es, src, hnT, ident)
            phase_attn(sync, "B", hnT, drB, ysc)
            phase_outproj(sync, src, woB, ysc, out)
    return nc


def host_inputs(norm_gain, a_w_in, a_w_out, a_q_gain, a_k_gain, t5_bias, b_w_in, b_w_out, b_q_gain, b_k_gain, b_rpb):
    global _GEOM_B
    if _GEOM_B is None:
        _GEOM_B = _geom_B()
    tilesB, mapsB, ebwB = _GEOM_B
    f = lambda a: np.ascontiguousarray(np.asarray(a, dtype=np.float32))
    wA, wgA = _unit_weights(f(a_w_in)[0], 3)
    wB, wgB = _unit_weights(f(b_w_in)[0], 1)
    ng = f(norm_gain)

    def gcol(gn):
        gn = f(gn).reshape(-1, 64)
        o = np.ones((128, 3), np.float32)
        for g in range(gn.shape[0]):
            o[:, g] = np.tile(gn[g], 2)
        return o
    shared = dict(
        ident_d=np.eye(128, dtype=np.float32),
        wA=wA, wgA=wgA, woA=f(a_w_out)[0],
        biasA=np.ascontiguousarray(_bias_tiles_A(f(t5_bias)).reshape(8, 3, 128, 512)),
        ngA=np.ascontiguousarray(ng[0].reshape(8, 128).T), gqA=gcol(a_q_gain[0]), gkA=gcol(a_k_gain[0]),
        wB=wB, wgB=wgB, woB=f(b_w_out)[0],
        biasB=_bias_tiles_B(f(b_rpb)[0], mapsB, ebwB),
        ngB=np.ascontiguousarray(ng[1].reshape(8, 128).T), gqB=gcol(b_q_gain), gkB=gcol(b_k_gain),
    )
    return shared


def kernel(x, norm_gain, a_w_in, a_w_out, a_q_gain, a_k_gain, t5_bias, b_w_in, b_w_out, b_q_gain, b_k_gain, b_rpb):
    x = np.ascontiguousarray(np.asarray(x, dtype=np.float32))
    shared = host_inputs(norm_gain, a_w_in, a_w_out, a_q_gain, a_k_gain, t5_bias, b_w_in, b_w_out, b_q_gain, b_k_gain, b_rpb)
    nc = build_nc("AB")
    in_maps = [dict(shared, x=x[c]) for c in range(NCORES)]
    res = run_bass_kernel_spmd(nc, in_maps, core_ids=list(range(NCORES)))
    return np.stack([np.asarray(r["out"], dtype=np.float32) for r in res.results], axis=0)
```

```python
import contextlib
import numpy as np
import concourse.bass as bass
import concourse.mybir as mybir
from concourse.bass_utils import run_bass_kernel_spmd

F32 = mybir.dt.float32
BF16 = mybir.dt.bfloat16
AF = mybir.ActivationFunctionType
ALU = mybir.AluOpType

S = 4096
D = 1024
NCORES = 8
DILS = (1, 4, 16)
EPS = 1e-6
NEG = -30000.0


class Buf:
    __slots__ = ("name", "lw", "rd")

    def __init__(self, name=""):
        self.name = name
        self.lw = None
        self.rd = []


class DmaSlot:
    __slots__ = ("sem", "count", "name")

    def __init__(self, name):
        self.name = name
        self.sem = None
        self.count = 0


class Op:
    __slots__ = ("eng", "fn", "deps", "slot", "signal", "tick", "semkey", "known", "idx")


STRICT = True
COMPUTE = ("pe", "act", "dve", "pool")
ENGS = ("pe", "act", "dve", "pool", "sp")


class Sync:
    def __init__(self, nc, es, nslots=40):
        self.nc = nc
        self.esem = {e: es.enter_context(nc.semaphore("sem_" + e)) for e in COMPUTE}
        self.tick = {e: 0 for e in COMPUTE}
        self.slots = []
        for i in range(nslots):
            s = DmaSlot("dq%d" % i)
            s.sem = es.enter_context(nc.semaphore(s.name))
            self.slots.append(s)


class Prog:
    def __init__(self, sync):
        self.sync = sync
        self.nc = sync.nc
        self.ops = []
        self.nslot = 0

    def slot(self, name=""):
        s = self.sync.slots[self.nslot]
        self.nslot += 1
        return s

    def add(self, eng, fn, reads=(), writes=(), slot=None):
        op = Op()
        op.eng = eng
        op.fn = fn
        op.slot = slot
        op.signal = slot is not None
        op.tick = None
        op.idx = len(self.ops)
        deps = {}
        is_dma = slot is not None
        for b in reads:
            w = b.lw
            if w is not None:
                if is_dma or w.slot is not None or w.eng != eng or eng != "pe":
                    deps[w.idx] = w
        for b in writes:
            w = b.lw
            if w is not None and (is_dma or w.slot is not None or w.eng != eng or (STRICT and eng != "pe")):
                deps[w.idx] = w
            for r in b.rd:
                if is_dma or r.slot is not None or r.eng != eng or (STRICT and eng != "pe"):
                    deps[r.idx] = r
        for b in reads:
            b.rd.append(op)
        for b in writes:
            b.lw = op
            b.rd = []
        op.deps = list(deps.values())
        for d in op.deps:
            d.signal = True
        self.ops.append(op)
        return op

    def pe(self, fn, reads=(), writes=()):
        return self.add("pe", fn, reads, writes)

    def act(self, fn, reads=(), writes=()):
        return self.add("act", fn, reads, writes)

    def dve(self, fn, reads=(), writes=()):
        return self.add("dve", fn, reads, writes)

    def pool(self, fn, reads=(), writes=()):
        return self.add("pool", fn, reads, writes)

    def dma(self, slot, fn, reads=(), writes=()):
        return self.add("sp", fn, reads, writes, slot=slot)

    def emit(self):
        nc = self.nc
        sy = self.sync
        for op in self.ops:
            if op.slot is not None:
                op.slot.count += 16
                op.tick = op.slot.count
                op.semkey = op.slot
            elif op.signal:
                sy.tick[op.eng] += 1
                op.tick = sy.tick[op.eng]
                op.semkey = op.eng
        base = {}
        for e in COMPUTE:
            base[e] = 0
        clock = {e: {} for e in ENGS}
        start_tick = dict(self._start_tick)
        start_slot = dict(self._start_slot)
        for e in ENGS:
            for k, v in start_tick.items():
                clock[e][k] = v
            for k, v in start_slot.items():
                clock[e][k] = v
        plan = {e: [] for e in ENGS}
        for op in self.ops:
            ck = clock[op.eng]
            need = {}
            for d in op.deps:
                if ck.get(d.semkey, 0) >= d.tick:
                    continue
                if need.get(d.semkey, 0) < d.tick:
                    need[d.semkey] = d.tick
            for d in op.deps:
                for k, v in d.known.items():
                    if ck.get(k, 0) < v:
                        ck[k] = v
            waits = list(need.items())
            for k, v in waits:
                if ck.get(k, 0) < v:
                    ck[k] = v
            if op.tick is not None:
                kn = dict(ck)
                if kn.get(op.semkey, 0) < op.tick:
                    kn[op.semkey] = op.tick
                op.known = kn
            plan[op.eng].append((op, waits))
        final_waits = [(s, s.count) for s in sy.slots[: self.nslot] if s.count > start_slot.get(s, 0)]
        esem = sy.esem

        def semof(k):
            return k.sem if isinstance(k, DmaSlot) else esem[k]

        def run(engname):
            def body(eng):
                for op, waits in plan[engname]:
                    for k, v in waits:
                        eng.wait_ge(semof(k), v)
                    ins = op.fn(eng)
                    if op.slot is not None:
                        ins.then_inc(op.slot.sem, 16)
                    elif op.tick is not None:
                        ins.then_inc(esem[engname], 1)
                if engname == "sp":
                    for s, v in final_waits:
                        eng.wait_ge(s.sem, v)
            return body

        with nc.Block() as block:
            block.tensor(run("pe"))
            block.scalar(run("act"))
            block.vector(run("dve"))
            block.gpsimd(run("pool"))
            block.sync(run("sp"))

    def begin(self):
        sy = self.sync
        self._start_tick = dict(sy.tick)
        self._start_slot = {s: s.count for s in sy.slots}
        return self


_UID = [0]


def uid():
    _UID[0] += 1
    return "_u%d" % _UID[0]


def fview(ap, dims):
    return bass.AP(tensor=ap.tensor, offset=ap.offset, ap=[list(ap.ap[0])] + [list(d) for d in dims])


def FV(t, p0, p1, off, dims):
    return fview(t[p0:p1, off:off + 1], dims)


class Rot:
    def __init__(self, items):
        self.items = items
        self.bufs = [Buf() for _ in items]
        self.i = 0

    def next(self):
        k = self.i % len(self.items)
        self.i += 1
        return self.items[k], self.bufs[k]


DBG = {}


def dump(P, name, t, bufs, dt=None):
    nc = P.nc
    shape = list(t.shape)
    d = nc.dram_tensor("dbg_" + name, shape, dt or t.dtype, kind="ExternalOutput").ap()
    P.dma(P.slot(), lambda e: e.dma_start(out=d[:, :], in_=t[:, :]), reads=bufs)


def phase_norm(sync, es_outer, xsrc, hnT, ident):
    nc = sync.nc
    P = Prog(sync).begin()
    with contextlib.ExitStack() as es:
        sfx = uid()

        def sb(name, shape, dt):
            return es.enter_context(nc.sbuf_tensor(name + sfx, shape, dt))
        xt = Rot([sb("n_xt%d" % i, [128, D], F32) for i in range(8)])
        junk = sb("n_junk", [128, D], BF16)
        hn0 = Rot([sb("n_hn%d" % i, [128, D], BF16) for i in range(2)])
        ss = sb("n_ss", [128, 32], F32)
        ln = sb("n_ln", [128, 32], F32)
        rs = sb("n_rs", [128, 32], F32)
        epsc = sb("n_eps", [128, 1], F32)
        ptr = Rot([es.enter_context(nc.psum_tensor("n_ptr%d" % i + sfx, [128, D], BF16)) for i in range(2)])
        bjunk = Buf()
        beps = Buf()
        bhn = Buf()
        slots = [P.slot() for _ in range(8)]
        P.dve(lambda e: e.memset(epsc[:], EPS), writes=[beps])
        NB = 4
        pendB = []
        for i0 in range(0, 32, NB):
            grp = []
            bssg = Buf()
            for i in range(i0, i0 + NB):
                x_t, bx = xt.next()
                sl = slots[i % len(slots)]
                P.dma(sl, lambda e, x_t=x_t, i=i: e.dma_start(out=x_t[:], in_=xsrc[128 * i:128 * (i + 1), :]), writes=[bx])
                P.act(lambda e, x_t=x_t, i=i: e.activation(out=junk[:], in_=x_t[:], func=AF.Square, accum_out=ss[:, i:i + 1]),
                      reads=[bx], writes=[bjunk, bssg])
                grp.append((i, x_t, bx))
            bln = Buf()
            P.act(lambda e, i0=i0: e.activation(out=ln[:, i0:i0 + NB], in_=ss[:, i0:i0 + NB], func=AF.Ln, bias=epsc[:], scale=1.0 / D),
                  reads=[bssg, beps], writes=[bln])
            brs = Buf()
            P.act(lambda e, i0=i0: e.activation(out=rs[:, i0:i0 + NB], in_=ln[:, i0:i0 + NB], func=AF.Exp, scale=-0.5),
                  reads=[bln], writes=[brs])
            for i, x_t, bx in grp:
                def partA(i=i, x_t=x_t, bx=bx, brs=brs):
                    h_t, bh = hn0.next()
                    P.dve(lambda e: e.tensor_scalar(out=h_t[:], in0=x_t[:], scalar1=rs[:, i:i + 1], scalar2=None, op0=ALU.mult),
                          reads=[bx, brs], writes=[bh])
                    p_t, bp = ptr.next()
                    for kc in range(8):
                        P.pe(lambda e, kc=kc: e.transpose(p_t[:, kc * 128:(kc + 1) * 128], h_t[:, kc * 128:(kc + 1) * 128], ident[:]),
                             reads=[bh], writes=[bp])

                    def partB():
                        P.dve(lambda e: e.tensor_copy(out=FV(hnT, 0, 128, 128 * i, [[S, 8], [1, 128]]),
                                                      in_=FV(p_t, 0, 128, 0, [[128, 8], [1, 128]])), reads=[bp], writes=[bhn])
                    return partB
                pendB.append(partA())
                if len(pendB) > 1:
                    pendB.pop(0)()
        while pendB:
            pendB.pop(0)()
        if DBG.get("hnT"):
            dump(P, "hnT", hnT, [bhn])
            dump(P, "rs", rs, [bhn])
        P.emit()


def phase_outproj(sync, xsrc, wo_dram, ysc, out):
    nc = sync.nc
    P = Prog(sync).begin()
    with contextlib.ExitStack() as es:
        sfx = uid()

        def sb(name, shape, dt):
            return es.enter_context(nc.sbuf_tensor(name + sfx, shape, dt))
        wst = Rot([sb("o_wst%d" % i, [128, D], F32) for i in range(2)])
        wo = sb("o_wo", [128, 8 * D], BF16)
        bwo = Buf()
        xt = Rot([sb("o_xt%d" % i, [128, D], F32) for i in range(3)])
        ot = Rot([sb("o_ot%d" % i, [128, D], F32) for i in range(2)])
        yt = Rot([sb("o_yt%d" % i, [128, 8 * 512], BF16) for i in range(2)])
        po = Rot([es.enter_context(nc.psum_tensor("o_po%d" % i + sfx, [128, 512], F32)) for i in range(4)])
        s_w = [P.slot() for _ in range(2)]
        s_x = [P.slot() for _ in range(3)]
        s_y = [P.slot() for _ in range(2)]
        s_o = [P.slot() for _ in range(2)]
        for kc in range(8):
            w_t, bw = wst.next()
            P.dma(s_w[kc % 2], lambda e, w_t=w_t, kc=kc: e.dma_start(out=w_t[:], in_=wo_dram[kc * 128:(kc + 1) * 128, :]), writes=[bw])
            if kc % 2 == 0:
                P.dve(lambda e, w_t=w_t, kc=kc: e.tensor_copy(out=wo[:, kc * D:(kc + 1) * D], in_=w_t[:]), reads=[bw], writes=[bwo])
            else:
                P.act(lambda e, w_t=w_t, kc=kc: e.activation(out=wo[:, kc * D:(kc + 1) * D], in_=w_t[:], func=AF.Copy), reads=[bw], writes=[bwo])
        xts = {}
        yts = {}

        def load_x(i):
            x_t, bx = xt.next()
            P.dma(s_x[i % 3], lambda e: e.dma_start(out=x_t[:], in_=xsrc[128 * i:128 * (i + 1), :]), writes=[bx])
            xts[i] = (x_t, bx)

        def load_y(tb):
            y_t, by = yt.next()
            P.dma(s_y[tb % 2], lambda e: e.dma_start(
                out=FV(y_t, 0, 128, 0, [[512, 8], [1, 512]]),
                in_=ysc[:, :, tb * 512:(tb + 1) * 512].rearrange("k p t -> p k t")), writes=[by])
            yts[tb] = (y_t, by)
        load_y(0)
        load_x(0)
        load_x(1)
        for i in range(32):
            if i + 2 < 32:
                load_x(i + 2)
            if i % 4 == 0 and i // 4 + 1 < 8:
                load_y(i // 4 + 1)
            y_t, by = yts[i // 4]
            x_t, bx = xts[i]
            o_t, bo = ot.next()
            for nb in range(2):
                p_t, bp = po.next()
                for kc in range(8):
                    P.pe(lambda e, p_t=p_t, y_t=y_t, kc=kc, nb=nb, i=i: e.matmul(
                        p_t[:], lhsT=y_t[:, kc * 512 + (i % 4) * 128: kc * 512 + (i % 4) * 128 + 128],
                        rhs=wo[:, kc * D + nb * 512: kc * D + nb * 512 + 512], start=(kc == 0), stop=(kc == 7)),
                        reads=[by, bwo], writes=[bp])
                P.dve(lambda e, p_t=p_t, x_t=x_t, o_t=o_t, nb=nb: e.tensor_tensor(
                    out=o_t[:, nb * 512:(nb + 1) * 512], in0=p_t[:], in1=x_t[:, nb * 512:(nb + 1) * 512], op=ALU.add),
                    reads=[bp, bx], writes=[bo])
            P.dma(s_o[i % 2], lambda e, o_t=o_t, i=i: e.dma_start(out=out[128 * i:128 * (i + 1), :], in_=o_t[:]), reads=[bo])
        P.emit()


def qk_geometry(d):
    L = S // d
    return L, L // 128


def merge_steps(a, b):
    out = []
    na, nb = len(a), len(b)
    if na == 0 or nb == 0:
        return list(a) + list(b)
    ia = ib = 0
    while ia < na or ib < nb:
        if ib >= nb or (ia < na and ia * nb <= ib * na):
            out.append(a[ia]); ia += 1
        else:
            out.append(b[ib]); ib += 1
    return out


def phase_attn(sync, layer, hnT, dr, ysc):
    nc = sync.nc
    P = Prog(sync).begin()
    isA = layer == "A"
    w_dram, wg_dram, bias_dram = dr["w"], dr["wg"], dr["bias"]
    units = [(hp, g) for hp in range(8) for g in ((0, 1, 2) if isA else (0,))]
    with contextlib.ExitStack() as es:
        sfx = uid()

        def sb(name, shape, dt):
            return es.enter_context(nc.sbuf_tensor(name + sfx, shape, dt))
        ngc = sb("a_ngc", [128, 8], F32)
        gq = sb("a_gq", [128, 3], F32)
        gk = sb("a_gk", [128, 3], F32)
        epsc = sb("a_eps", [128, 1], F32)
        blk = sb("a_blk", [128, 128], BF16)
        bconst = Buf()
        s_c = P.slot()
        P.dma(s_c, lambda e: e.dma_start(out=ngc[:], in_=dr["ng"][:, :]), writes=[bconst])
        P.dma(s_c, lambda e: e.dma_start(out=gq[:], in_=dr["gq"][:, :]), writes=[bconst])
        P.dma(s_c, lambda e: e.dma_start(out=gk[:], in_=dr["gk"][:, :]), writes=[bconst])
        P.dve(lambda e: e.memset(epsc[:], EPS), writes=[bconst])
        P.dve(lambda e: e.memset(blk[:], 0.0), reads=[bconst], writes=[bconst])
        P.dve(lambda e: e.memset(blk[0:64, 0:64], 1.0 / 64), reads=[bconst], writes=[bconst])
        P.dve(lambda e: e.memset(blk[64:128, 64:128], 1.0 / 64), reads=[bconst], writes=[bconst])
        P.dve(lambda e: e.tensor_scalar(out=gq[:], in0=gq[:], scalar1=0.125, scalar2=None, op0=ALU.mult),
              reads=[bconst], writes=[bconst])
        qTr = Rot([sb("a_qT%d" % i, [128, S], BF16) for i in range(2)])
        kTr = Rot([sb("a_kT%d" % i, [128, S], BF16) for i in range(2)])
        Vr = Rot([sb("a_V%d" % i, [128, 32 * 192], BF16) for i in range(2)])
        Vr.bufs = [[Buf() for _ in range(8)] for _ in range(2)]
        for V_i, bV_i in zip(Vr.items, Vr.bufs):
            P.dve(lambda e, V_i=V_i: e.memset(FV(V_i, 0, 128, 64, [[192, 32], [1, 64]]), 1.0), writes=bV_i)
        sgT = sb("a_sgT", [128, S], BF16)
        bsgt = [Buf() for _ in range(8)]
        if isA:
            acc = [sb("a_acc%d" % h, [128, S], F32) for h in range(2)]
            bacc = [Buf(), Buf()]
        wst = Rot([sb("a_wst%d" % i, [128, 384], F32) for i in range(4)])
        s_w = [P.slot() for _ in range(4)]
        wb = Rot([sb("a_wb%d" % i, [128, 8 * 384], BF16) for i in range(2)])
        bwk = [[Buf() for _ in range(8)] for _ in range(2)]
        wgb = sb("a_wgb", [128, 8 * 128], BF16)
        bwg = [Buf() for _ in range(8)]
        ebw = 2 * 256 if isA else dr["ebw"]
        ebst = Rot([sb("a_ebst%d" % i, [128, 512 if isA else 768], F32) for i in range(1 if isA else 2)])
        s_eb = [P.slot() for _ in range(2)]
        eb = Rot([sb("a_eb%d" % i, [128, ebw], BF16) for i in range(2)])
        be_slots = [[Buf() for _ in range((ebw + 767) // 768)] for _ in range(2)]
        sq = Rot([sb("a_sq%d" % i, [128, 512], BF16) for i in range(2)])
        rstd = Rot([sb("a_rstd%d" % i, [128, 512], F32) for i in range(2)])
        ew = 512 if isA else 768
        pT = Rot([sb("a_pT%d" % i, [128, ew], BF16) for i in range(10 if isA else 9)])
        rec = Rot([sb("a_rec%d" % i, [128, 512], F32) for i in range(1 if isA else 2)])
        if isA:
            yT = Rot([sb("a_yT%d" % i, [128, 1024], BF16) for i in range(2)])
            s_y = [P.slot() for _ in range(2)]
        else:
            yTB = [Rot([sb("b_yT%d_%d" % (h, i), [128, 1024], BF16) for i in range(2)]) for h in range(2)]
            s_yB = [[P.slot() for _ in range(2)] for h in range(2)]
        print("phase_attn", layer, "sbuf bytes remaining", nc.sbuf_bytes_remaining)
        banks = [es.enter_context(nc.psum_tensor("a_ps%d" % i + sfx, [128, 512], F32)) for i in range(8)]
        bbank = [Buf() for _ in range(8)]

        for i in range(8):
            P.dve(lambda e, i=i: e.memset(banks[i][:], 0.0), writes=[bbank[i]])

        def bankrot(ids):
            r = Rot([banks[i] for i in ids])
            r.bufs = [bbank[i] for i in ids]
            return r
        pq = bankrot([0, 1, 7])
        pss = bankrot([2])
        if isA:
            ps_h = [bankrot([3]), bankrot([4])]
            po_h = [bankrot([5]), bankrot([6])]
            pv = bankrot([2])
            pg = bankrot([5, 6])
        else:
            pq = bankrot([0, 1])
            psA = bankrot([3, 4])
            bRem = [Buf(), Buf()]
            po = bankrot([6, 7])
            pv = bankrot([2])
            pg = bankrot([6, 7])

        def wload_steps(u):
            hp, g = u
            w_b, bw0 = wb.next()
            k_ = (wb.i - 1) % len(wb.items)
            bw = bwk[k_]
            pend = []
            steps = []

            def conv(w_t, bs, kc):
                P.dve(lambda e: e.tensor_scalar(
                    out=w_b[:, kc * 384:(kc + 1) * 384], in0=w_t[:], scalar1=ngc[:, kc:kc + 1], scalar2=None, op0=ALU.mult),
                    reads=[bs, bconst], writes=[bw[kc]])
            for kc in range(8):
                def step(kc=kc):
                    w_t, bs = wst.next()
                    sl = s_w[(wst.i - 1) % 4]
                    P.dma(sl, lambda e: e.dma_start(out=w_t[:], in_=w_dram[hp, g, kc * 128:(kc + 1) * 128, :]), writes=[bs])
                    pend.append((w_t, bs, kc))
                    if len(pend) > 2:
                        conv(*pend.pop(0))
                steps.append(step)

            def flush():
                while pend:
                    conv(*pend.pop(0))
            steps.append(flush)
            return (w_b, bw), steps

        def gload_steps(hp):
            pend = []
            steps = []

            def conv(w_t, bs, kc):
                P.dve(lambda e: e.tensor_scalar(
                    out=wgb[:, kc * 128:(kc + 1) * 128], in0=w_t[:, 0:128], scalar1=ngc[:, kc:kc + 1], scalar2=None, op0=ALU.mult),
                    reads=[bs, bconst], writes=[bwg[kc]])
            for kc in range(8):
                def step(kc=kc):
                    w_t, bs = wst.next()
                    sl = s_w[(wst.i - 1) % 4]
                    P.dma(sl, lambda e: e.dma_start(out=w_t[:, 0:128], in_=wg_dram[hp, kc * 128:(kc + 1) * 128, :]), writes=[bs])
                    pend.append((w_t, bs, kc))
                    if len(pend) > 2:
                        conv(*pend.pop(0))
                steps.append(step)

            def flush():
                while pend:
                    conv(*pend.pop(0))
            steps.append(flush)
            return steps

        def eb_dma_A(u):
            hp, g = u
            st_, bs = ebst.next()
            sl = s_eb[0]
            P.dma(sl, lambda e: e.dma_start(out=st_[:, 0:512], in_=bias_dram[hp, g, :, :]), writes=[bs])
            return st_, bs

        def eb_conv_A(st_, bs):
            e_t, be = eb.next()
            P.act(lambda e: e.activation(out=e_t[:, 0:512], in_=st_[:, 0:512], func=AF.Exp), reads=[bs], writes=[be])
            return e_t, be

        def eb_steps_B(hp, h):
            e_t, be0 = eb.next()
            ebw1 = dr["ebw"]
            be = be_slots[(eb.i - 1) % len(eb.items)]
            pieces = []
            off = 0
            while off < ebw1:
                n = min(768, ebw1 - off)
                pieces.append((off, n))
                off += n
            pend = []
            steps = []

            def conv(st_, bs, off, n):
                P.act(lambda e: e.activation(out=e_t[:, off: off + n], in_=st_[:, 0:n], func=AF.Exp), reads=[bs], writes=[be[off // 768]])
            for off, n in pieces:
                def step(off=off, n=n):
                    st_, bs = ebst.next()
                    sl = s_eb[(ebst.i - 1) % 2]
                    P.dma(sl, lambda e: e.dma_start(out=st_[:, 0:n], in_=bias_dram[2 * hp + h, :, off:off + n]), writes=[bs])
                    pend.append((st_, bs, off, n))
                    if len(pend) > 1:
                        conv(*pend.pop(0))
                steps.append(step)

            def flush():
                while pend:
                    conv(*pend.pop(0))
            steps.append(flush)
            return (e_t, be), steps

        def gate_steps():
            steps = []
            for tb in range(8):
                def step(tb=tb):
                    p_t, bp = pg.next()
                    for kc in range(8):
                        P.pe(lambda e, p_t=p_t, kc=kc: e.matmul(
                            p_t[:], lhsT=wgb[:, kc * 128:(kc + 1) * 128], rhs=hnT[:, kc * S + tb * 512: kc * S + tb * 512 + 512],
                            start=(kc == 0), stop=(kc == 7)), reads=[bwg[kc]], writes=[bp])
                    P.act(lambda e, p_t=p_t: e.activation(out=sgT[:, tb * 512:(tb + 1) * 512], in_=p_t[:], func=AF.Silu),
                          reads=[bp], writes=[bsgt[tb]])
                steps.append(step)
            return steps

        def qk_steps(u, w_b, bw, q_t, bq, k_t, bk):
            hp, g = u
            d = DILS[g] if isA else 1
            L = S // d
            pend = []

            def tail(p_t, bp, s_t, bs, which, tb):
                ss_t, bss = pss.next()
                P.pe(lambda e: e.matmul(ss_t[:], lhsT=blk[:], rhs=s_t[:], start=True, stop=True),
                     reads=[bs, bconst], writes=[bss])
                r_t, br = rstd.next()
                P.act(lambda e: e.activation(out=r_t[:], in_=ss_t[:], func=AF.Ln, bias=epsc[:], scale=1.0),
                      reads=[bss, bconst], writes=[br])
                P.act(lambda e: e.activation(out=r_t[:], in_=r_t[:], func=AF.Exp, scale=-0.5), reads=[br], writes=[br])
                if which == 0:
                    n = 512 // d
                    P.dve(lambda e: e.scalar_tensor_tensor(
                        out=FV(q_t, 0, 128, tb * n, [[L, d], [1, n]]),
                        in0=FV(p_t, 0, 128, 0, [[1, d], [d, n]]), scalar=gq[:, g:g + 1],
                        in1=FV(r_t, 0, 128, 0, [[1, d], [d, n]]), op0=ALU.mult, op1=ALU.mult),
                        reads=[bp, br, bconst], writes=[bq])
                else:
                    P.dve(lambda e: e.scalar_tensor_tensor(
                        out=k_t[:, tb * 512:(tb + 1) * 512], in0=p_t[:], scalar=gk[:, g:g + 1], in1=r_t[:],
                        op0=ALU.mult, op1=ALU.mult), reads=[bp, br, bconst], writes=[bk])

            steps = []
            for t in range(16):
                def step(t=t):
                    which, tb = divmod(t, 8)
                    p_t, bp = pq.next()
                    for kc in range(8):
                        P.pe(lambda e, kc=kc: e.matmul(
                            p_t[:], lhsT=w_b[:, kc * 384 + which * 128: kc * 384 + which * 128 + 128],
                            rhs=hnT[:, kc * S + tb * 512: kc * S + tb * 512 + 512], start=(kc == 0), stop=(kc == 7)),
                            reads=[bw[kc]], writes=[bp])
                    s_t, bs = sq.next()
                    P.act(lambda e: e.activation(out=s_t[:], in_=p_t[:], func=AF.Square), reads=[bp], writes=[bs])
                    if pend:
                        tail(*pend.pop())
                    pend.append((p_t, bp, s_t, bs, which, tb))
                steps.append(step)

            def flush():
                tail(*pend.pop())
            steps.append(flush)
            return steps

        def v_steps(u, w_b, bw, Vt, bV):
            hp, g = u
            d = DILS[g] if isA else 1
            L, nC = qk_geometry(d)
            steps = []
            for c0 in range(0, 32, 4):
                def step(c0=c0):
                    p_t, bp = pv.next()
                    for cc in range(4):
                        c = c0 + cc
                        r, i = divmod(c, nC)
                        t0 = r + d * 128 * i
                        for kc in range(8):
                            P.pe(lambda e, kc=kc, cc=cc, t0=t0: e.matmul(
                                p_t[:, cc * 128:(cc + 1) * 128],
                                lhsT=FV(hnT, 0, 128, kc * S + t0, [[d, 128]]),
                                rhs=w_b[:, kc * 384 + 256: kc * 384 + 384], start=(kc == 0), stop=(kc == 7)),
                                reads=[bw[kc]], writes=[bp])
                    P.dve(lambda e: e.tensor_copy(
                        out=FV(Vt, 0, 128, c0 * 192, [[192, 4], [128, 2], [1, 64]]),
                        in_=FV(p_t, 0, 128, 0, [[128, 4], [64, 2], [1, 64]])), reads=[bp], writes=[bV[c0 // 4]])
                steps.append(step)
            return steps

        def attn_steps_A(u, q_t, bq, k_t, bk, e_t, be, Vt, bV):
            hp, g = u
            d = DILS[g]
            first = g == 0
            L, nC = qk_geometry(d)
            pend = []
            steps = []

            def acc_out(h, srcf, sbuf, dstf):
                if first:
                    P.dve(lambda e: e.tensor_copy(out=dstf(), in_=srcf()), reads=[sbuf], writes=[bacc[h]])
                else:
                    P.dve(lambda e: e.tensor_tensor(out=dstf(), in0=srcf(), in1=dstf(), op=ALU.add), reads=[sbuf, bacc[h]], writes=[bacc[h]])

            def make_tail(r, m, ptl, state):
                def tail():
                    for h in range(2):
                        vof = 0 if h == 0 else 64
                        acc_h = acc[h]
                        key = "po%d" % h

                        def pcol(i, lo):
                            p_t, bp = ptl[(h, i // 2)]
                            return p_t, bp, (i % 2) * 256 + lo

                        def mm(o_t, bo, col, n, chunk, pa, bpa, ca, start, stop, vof=vof):
                            P.pe(lambda e: e.matmul(
                                o_t[:, col:col + n], lhsT=Vt[:, chunk * 192 + vof: chunk * 192 + vof + 128],
                                rhs=pa[:, ca:ca + n], start=start, stop=stop), reads=[bV[chunk // 4], bpa], writes=[bo])
                        if d == 16:
                            if r % 2 == 0:
                                state[key] = po_h[h].next()
                            o_t, bo = state[key]
                            base = (r % 2) * 256
                            c1 = r * nC
                            pa, bpa, ca = pcol(0, 64)
                            mm(o_t, bo, base, 64, c1, pa, bpa, ca, True, True)
                            pa, bpa, ca = pcol(0, 128)
                            mm(o_t, bo, base + 64, 128, c1, pa, bpa, ca, True, False)
                            pa, bpa, ca = pcol(1, 0)
                            mm(o_t, bo, base + 64, 128, c1 + 1, pa, bpa, ca, False, True)
                            pa, bpa, ca = pcol(1, 128)
                            mm(o_t, bo, base + 192, 64, c1 + 1, pa, bpa, ca, True, True)
                            if r % 2 == 1:
                                acc_out(h, lambda o_t=o_t: o_t[:, 0:512], bo,
                                        lambda acc_h=acc_h: FV(acc_h, 0, 128, r - 1, [[1, 2], [16, 256]]))
                            continue
                        if m == 0:
                            state[key] = po_h[h].next()
                            o_t, bo = state[key]
                            pa, bpa, ca = pcol(0, 64)
                            mm(o_t, bo, 64, 64, r * nC, pa, bpa, ca, True, True)
                        for jj in ([2 * m - 1] if m > 0 else []) + ([2 * m] if 2 * m <= nC - 2 else []):
                            if jj < 3:
                                slot = jj + 1
                            else:
                                slot = (jj - 3) % 4
                                if slot == 0:
                                    state[key] = po_h[h].next()
                            o_t, bo = state[key]
                            pa, bpa, ca = pcol(jj, 128)
                            pb_, bpb, cb = pcol(jj + 1, 0)
                            c1 = r * nC + jj
                            mm(o_t, bo, slot * 128, 128, c1, pa, bpa, ca, True, False)
                            mm(o_t, bo, slot * 128, 128, c1 + 1, pb_, bpb, cb, False, True)
                            if jj == 2:
                                acc_out(h, lambda o_t=o_t: o_t[:, 64:512], bo,
                                        lambda acc_h=acc_h: FV(acc_h, 0, 128, r, [[d, 448]]))
                            elif jj > 2 and slot == 3:
                                m0 = 64 + 128 * (jj - 3)
                                acc_out(h, lambda o_t=o_t: o_t[:, 0:512], bo,
                                        lambda acc_h=acc_h, m0=m0: FV(acc_h, 0, 128, r + d * m0, [[d, 512]]))
                        if m == nC // 2 - 1:
                            assert (nC - 2 - 3) % 4 == 3
                            o_t, bo = po_h[h].next()
                            pa, bpa, ca = pcol(nC - 1, 128)
                            mm(o_t, bo, 0, 64, r * nC + nC - 1, pa, bpa, ca, True, True)
                            acc_out(h, lambda o_t=o_t: o_t[:, 0:64], bo,
                                    lambda acc_h=acc_h: FV(acc_h, 0, 128, r + d * (L - 64), [[d, 64]]))
                return tail

            state = {}
            for r in range(d):
                ptl = {}
                for m in range(nC // 2):
                    def step(r=r, m=m, ptl=ptl, state=state):
                        stl = [ps_h[0].next(), ps_h[1].next()]
                        for cc in range(2):
                            i = 2 * m + cc
                            qlo = max(0, 128 * i - 64)
                            qhi = min(L, 128 * i + 192)
                            lo = qlo - (128 * i - 64)
                            n = qhi - qlo
                            for h in range(2):
                                s_t, bs = stl[h]
                                P.pe(lambda e, s_t=s_t, cc=cc, lo=lo, n=n, qlo=qlo, i=i, h=h: e.matmul(
                                    s_t[:, cc * 256 + lo: cc * 256 + lo + n],
                                    lhsT=FV(k_t, 64 * h, 64 * h + 64, r + d * 128 * i, [[d, 128]]),
                                    rhs=q_t[64 * h:64 * h + 64, r * L + qlo: r * L + qlo + n], start=True, stop=True),
                                    reads=[bk, bq], writes=[bs])
                        for h in range(2):
                            s_t, bs = stl[h]
                            p_t, bp = pT.next()
                            P.act(lambda e, p_t=p_t, s_t=s_t: e.activation(out=p_t[:, 0:512], in_=s_t[:], func=AF.Exp), reads=[bs], writes=[bp])
                            P.dve(lambda e, p_t=p_t, h=h: e.tensor_tensor(
                                out=FV(p_t, 0, 128, 0, [[256, 2], [1, 256]]), in0=FV(p_t, 0, 128, 0, [[256, 2], [1, 256]]),
                                in1=FV(e_t, 0, 128, h * 256, [[0, 2], [1, 256]]), op=ALU.mult), reads=[bp, be], writes=[bp])
                            ptl[(h, m)] = (p_t, bp)
                        if len(pend) > 1:
                            pend.pop(0)()
                        pend.append(make_tail(r, m, ptl, state))
                    steps.append(step)

            def flush():
                while pend:
                    pend.pop(0)()
            steps.append(flush)
            return steps

        def normalize_steps_A(hp):
            steps = []
            st = {}
            for tb in range(8):
                def step(tb=tb):
                    q4, hb = divmod(tb, 2)
                    if hb == 0:
                        st["y"] = yT.next()
                    y_t, by = st["y"]
                    cs = slice(tb * 512, (tb + 1) * 512)
                    r_t, br = rec.next()
                    t_t, bt = r_t, br
                    P.act(lambda e: e.activation(out=r_t[0:64, :], in_=acc[0][64:128, cs], func=AF.Ln), reads=[bacc[0]], writes=[br])
                    P.act(lambda e: e.activation(out=r_t[64:128, :], in_=acc[1][0:64, cs], func=AF.Ln), reads=[bacc[1]], writes=[br])
                    P.act(lambda e: e.activation(out=r_t[:], in_=r_t[:], func=AF.Exp, scale=-1.0), reads=[br], writes=[br])
                    P.dve(lambda e: e.tensor_tensor(out=t_t[0:64, :], in0=acc[0][0:64, cs], in1=r_t[0:64, :], op=ALU.mult),
                          reads=[br], writes=[bt])
                    P.dve(lambda e: e.tensor_tensor(out=t_t[64:128, :], in0=acc[1][64:128, cs], in1=r_t[64:128, :], op=ALU.mult),
                          reads=[br], writes=[bt])
                    P.dve(lambda e: e.tensor_tensor(out=y_t[:, hb * 512:(hb + 1) * 512], in0=t_t[:], in1=sgT[:, cs], op=ALU.mult),
                          reads=[bt, bsgt[tb]], writes=[by])
                    if hb == 1:
                        sl = s_y[(yT.i - 1) % 2]
                        P.dma(sl, lambda e: e.dma_start(out=ysc[hp, :, q4 * 1024:(q4 + 1) * 1024], in_=y_t[:]), reads=[by])
                steps.append(step)
            return steps

        def attn_steps_B(hp, h, q_t, bq, k_t, bk, e_t, be, Vt, bV):
            tiles = dr["tiles"]
            hs = slice(64 * h, 64 * h + 64)
            vof = 0 if h == 0 else 64
            num = slice(0, 64) if h == 0 else slice(64, 128)
            den = slice(64, 128) if h == 0 else slice(0, 64)
            contribs = []
            for Q in range(32):
                full, part = [], []
                for R in range(32):
                    qlo, nr = tiles[R][1], tiles[R][2]
                    lo_r = max(qlo, 2 * Q)
                    hi_r = min(qlo + nr - 1, 2 * Q + 1)
                    if lo_r > hi_r:
                        continue
                    (full if hi_r - lo_r == 1 else part).append((R, lo_r, hi_r - lo_r + 1))
                assert full
                contribs.append(full + part)
            lastR = [max(R for R, _, _ in contribs[Q]) for Q in range(32)]
            ptl = {}
            st = dict(o=None, y=None)
            pend = []
            steps = []

            def do_block(Q):
                if Q % 4 == 0:
                    st["o"] = po.next()
                o_t, bo = st["o"]
                cl = contribs[Q]
                for n_, (R, row0, nrow) in enumerate(cl):
                    p_t, bp = ptl[R]
                    c0 = (row0 - tiles[R][1]) * 64
                    oc = (Q % 4) * 128 + (row0 - 2 * Q) * 64
                    nn = nrow * 64
                    P.pe(lambda e, p_t=p_t, c0=c0, oc=oc, nn=nn, R=R, first=(n_ == 0), last=(n_ == len(cl) - 1): e.matmul(
                        o_t[:, oc:oc + nn], lhsT=Vt[:, R * 192 + vof: R * 192 + vof + 128],
                        rhs=p_t[:, c0:c0 + nn], start=first, stop=last), reads=[bV[R // 4], bp], writes=[bo])
                if Q % 4 == 3:
                    tb = Q // 4
                    cs = slice(tb * 512, (tb + 1) * 512)
                    if tb % 2 == 0:
                        st["y"] = yTB[h].next()
                    y_t, by = st["y"]
                    r_t, br = rec.next()
                    t_t, bt = r_t, br
                    P.act(lambda e: e.activation(out=r_t[num, :], in_=o_t[den, :], func=AF.Ln), reads=[bo], writes=[br])
                    P.act(lambda e: e.activation(out=r_t[num, :], in_=r_t[num, :], func=AF.Exp, scale=-1.0), reads=[br], writes=[br])
                    P.dve(lambda e: e.tensor_tensor(out=t_t[num, :], in0=o_t[num, :], in1=r_t[num, :], op=ALU.mult),
                          reads=[bo, br], writes=[bt])
                    P.dve(lambda e: e.tensor_tensor(
                        out=y_t[num, (tb % 2) * 512:(tb % 2) * 512 + 512], in0=t_t[num, :], in1=sgT[num, cs], op=ALU.mult),
                        reads=[bt, bsgt[tb]], writes=[by])
                    if tb % 2 == 1:
                        q4 = tb // 2
                        sl = s_yB[h][(yTB[h].i - 1) % 2]
                        P.dma(sl, lambda e: e.dma_start(
                            out=ysc[hp, num, q4 * 1024:(q4 + 1) * 1024], in_=y_t[num, :]), reads=[by])

            for R in range(32):
                def step(R=R):
                    toff, qlo, nr = tiles[R]
                    n = nr * 64
                    sA, bA = psA.next()
                    sB = banks[5]
                    bB = bbank[5]
                    n1 = min(n, 512)
                    p_t, bp = pT.next()
                    if n > 512:
                        n2 = n - 512
                        P.pe(lambda e: e.matmul(
                            sB[:, 0:n2], lhsT=k_t[hs, R * 128:(R + 1) * 128], rhs=q_t[hs, qlo * 64 + 512: qlo * 64 + 512 + n2], start=True, stop=True),
                            reads=[bk, bq], writes=[bB])
                        P.act(lambda e: e.activation(out=p_t[:, 512:512 + n2], in_=sB[:, 0:n2], func=AF.Exp), reads=[bB], writes=[bp])
                    P.pe(lambda e: e.matmul(
                        sA[:, 0:n1], lhsT=k_t[hs, R * 128:(R + 1) * 128], rhs=q_t[hs, qlo * 64: qlo * 64 + n1], start=True, stop=True),
                        reads=[bk, bq], writes=[bA])
                    P.act(lambda e: e.activation(out=p_t[:, 0:n1], in_=sA[:, 0:n1], func=AF.Exp), reads=[bA], writes=[bp])
                    P.dve(lambda e: e.tensor_tensor(
                        out=p_t[:, 0:n], in0=p_t[:, 0:n], in1=e_t[:, toff: toff + n], op=ALU.mult),
                        reads=[bp] + be[toff // 768: (toff + n - 1) // 768 + 1], writes=[bp])
                    ptl[R] = (p_t, bp)
                    if len(pend) > 1:
                        pend.pop(0)()

                    def tail():
                        for Q in range(32):
                            if lastR[Q] == R:
                                do_block(Q)
                    pend.append(tail)
                steps.append(step)

            def flush():
                while pend:
                    pend.pop(0)()
            steps.append(flush)
            return steps

        def run(steps):
            for st_ in steps:
                st_()

        dstop = DBG.get("stop")
        nU = len(units)
        wts = {}
        qk = {}
        vv = {}
        ebd = {}
        ebB = {}
        wts[0], ws = wload_steps(units[0])
        run(ws)
        run(gload_steps(0))
        if isA:
            ebd[0] = eb_dma_A(units[0])
        if nU > 1:
            wts[1], ws1 = wload_steps(units[1])
        else:
            ws1 = []
        qk[0] = qTr.next() + kTr.next()
        vv[0] = Vr.next()
        run(merge_steps(qk_steps(units[0], *wts[0], *qk[0]) + v_steps(units[0], *wts[0], *vv[0]), ws1))
        run(gate_steps())
        for ui, u in enumerate(units):
            hp, g = u
            nxt = units[ui + 1] if ui + 1 < nU else None
            wsteps = []
            if ui + 2 < nU:
                wts[ui + 2], wsteps = wload_steps(units[ui + 2])
            q_t, bq, k_t, bk = qk[ui]
            V_t, bV = vv[ui]
            nsteps = []
            vsteps_late = []
            if nxt is not None:
                qk[ui + 1] = qTr.next() + kTr.next()
                vv[ui + 1] = Vr.next()
                qs_ = qk_steps(nxt, *wts[ui + 1], *qk[ui + 1])
                vs_ = v_steps(nxt, *wts[ui + 1], *vv[ui + 1])
                if isA and g == 2:
                    nsteps, vsteps_late = qs_, vs_
                else:
                    nsteps = merge_steps(qs_, vs_)
            if isA:
                e_t, be = eb_conv_A(*ebd[ui])
                if nxt is not None:
                    ebd[ui + 1] = eb_dma_A(nxt)
                last = g == 2
                gsteps = gload_steps(hp + 1) if (g == 1 and hp + 1 < 8) else []
                run(merge_steps(merge_steps(attn_steps_A(u, q_t, bq, k_t, bk, e_t, be, V_t, bV), nsteps), wsteps + gsteps))
                if last:
                    run(merge_steps(normalize_steps_A(hp), vsteps_late))
                    if hp + 1 < 8:
                        run(gate_steps())
            else:
                gsteps = gload_steps(hp + 1) if hp + 1 < 8 else []
                if hp == 0:
                    ebB[(0, 0)], es0 = eb_steps_B(0, 0)
                    run(es0)
                ebB[(hp, 1)], es1 = eb_steps_B(hp, 1)
                a0 = attn_steps_B(hp, 0, q_t, bq, k_t, bk, *ebB[(hp, 0)], V_t, bV)
                run(merge_steps(merge_steps(a0, nsteps[: len(nsteps) // 2]), wsteps + es1))
                es2 = []
                if hp + 1 < 8:
                    ebB[(hp + 1, 0)], es2 = eb_steps_B(hp + 1, 0)
                a1 = attn_steps_B(hp, 1, q_t, bq, k_t, bk, *ebB[(hp, 1)], V_t, bV)
                run(merge_steps(merge_steps(a1, nsteps[len(nsteps) // 2:]), gsteps + es2))
                if hp + 1 < 8:
                    run(gate_steps())
            if dstop == "hp0" and ((isA and g == 2) or not isA):
                break
        P.emit()


_T5_LUT = None


def _t5_bucket_np(rel):
    import math
    half, me = 16, 8
    ret = np.where(rel > 0, half, 0)
    n = np.abs(rel)
    nf = np.maximum(n, 1).astype(np.float32)
    large = me + (np.log(nf / np.float32(me)) / np.float32(math.log(1024 / me)) * np.float32(half - me)).astype(np.int32)
    large = np.minimum(large, half - 1)
    return ret + np.where(n < me, n, large)


def _bias_tiles_A(t5_bias):
    a = np.arange(128)[:, None]
    b = np.arange(256)[None, :]
    rel = a - b + 64
    valid = (b - a >= 0) & (b - a <= 128)
    out = np.empty((8, 3, 128, 2, 256), np.float32)
    for g, d in enumerate(DILS):
        idx = _t5_bucket_np(rel * d)
        for h in range(16):
            t = t5_bias[g * 16 + h][idx]
            out[h // 2, g, :, h % 2, :] = np.where(valid, t, np.float32(NEG))
    return out


def _geom_B():
    rows = 64
    r = np.arange(rows)
    rs = np.clip(r - 4, 0, rows - 8)
    c = np.arange(64)
    cs = np.clip(c - 8, 0, 64 - 16)
    tiles = []
    uniq = {}
    maps = []
    off = 0
    for R in range(32):
        krs = np.array([2 * R, 2 * R + 1])
        qrows = [q for q in range(rows) if (rs[q] <= krs[1]) and (rs[q] + 7 >= krs[0])]
        qlo, nr = qrows[0], len(qrows)
        assert qrows == list(range(qlo, qlo + nr))
        kr = np.repeat(krs, 64)[:, None]
        kc = np.tile(c, 2)[:, None]
        qr = np.repeat(np.arange(qlo, qlo + nr), 64)[None, :]
        qc = np.tile(c, nr)[None, :]
        valid = (kr >= rs[qr]) & (kr <= rs[qr] + 7) & (kc >= cs[qc]) & (kc < cs[qc] + 16)
        ridx = np.clip(kr - qr + 7, 0, 14)
        cidx = np.clip(kc - qc, -15, 15) + 15
        key = (nr, valid.tobytes(), ridx.tobytes())
        if key not in uniq:
            uniq[key] = off
            maps.append((off, ridx + 0 * cidx, cidx + 0 * ridx, valid))
            off += nr * 64
        tiles.append((uniq[key], qlo, nr))
    return tiles, maps, off


def _bias_tiles_B(rpb, maps, ebw):
    out = np.empty((16, 128, ebw), np.float32)
    for off, ridx, cidx, valid in maps:
        n = valid.shape[1]
        for h in range(16):
            out[h, :, off:off + n] = np.where(valid, rpb[h][ridx, cidx], np.float32(NEG))
    return out


def _unit_weights(w_in, ngroups):
    wu = np.empty((8, ngroups, D, 384), np.float32)
    for hp in range(8):
        for g in range(ngroups):
            for j in range(3):
                c0 = g * 3072 + j * 1024 + hp * 128
                wu[hp, g, :, j * 128:(j + 1) * 128] = w_in[:, c0:c0 + 128]
    gc = ngroups * 3072
    wg = np.ascontiguousarray(w_in[:, gc:gc + 1024].reshape(D, 8, 128).transpose(1, 0, 2))
    return wu, wg


_GEOM_B = None


def build_nc(layers="AB"):
    global _GEOM_B
    if _GEOM_B is None:
        _GEOM_B = _geom_B()
    tilesB, mapsB, ebwB = _GEOM_B
    nc = bass.Bass("TRN2", target_bir_lowering=False)

    def din(name, shape, dt=F32):
        return nc.dram_tensor(name, list(shape), dt, kind="ExternalInput").ap()
    x = din("x", [S, D])
    ident_d = din("ident_d", [128, 128])
    drA = dict(w=din("wA", [8, 3, D, 384]), wg=din("wgA", [8, D, 128]), bias=din("biasA", [8, 3, 128, 512]),
               ng=din("ngA", [128, 8]), gq=din("gqA", [128, 3]), gk=din("gkA", [128, 3]))
    woA = din("woA", [D, D])
    drB = dict(w=din("wB", [8, 1, D, 384]), wg=din("wgB", [8, D, 128]), bias=din("biasB", [16, 128, ebwB]),
               ng=din("ngB", [128, 8]), gq=din("gqB", [128, 3]), gk=din("gkB", [128, 3]), tiles=tilesB, ebw=ebwB)
    woB = din("woB", [D, D])
    out = nc.dram_tensor("out", [S, D], F32, kind="ExternalOutput").ap()
    ysc = nc.dram_tensor("ysc", [8, 128, S], BF16).ap()
    with contextlib.ExitStack() as es:
        sync = Sync(nc, es)
        hnT = es.enter_context(nc.sbuf_tensor("hnT", [128, 8 * S], BF16))
        ident = es.enter_context(nc.sbuf_tensor("ident", [128, 128], BF16))
        identf = es.enter_context(nc.sbuf_tensor("identf", [128, 128], F32))
        P0 = Prog(sync).begin()
        bi = Buf()
        P0.dma(P0.slot(), lambda e: e.dma_start(out=identf[:], in_=ident_d[:, :]), writes=[bi])
        P0.dve(lambda e: e.tensor_copy(out=ident[:], in_=identf[:]), reads=[bi], writes=[bi])
        P0.emit()
        src = x
        if "A" in layers:
            phase_norm(sync, es, src, hnT, ident)
            if DBG.get("stop") != "norm":
                phase_attn(sync, "A", hnT, drA, ysc)
            if not DBG.get("stop"):
                phase_outproj(sync, src, woA, ysc, out)
            src = out
        if "B" in layers:
            phase_norm(sync, es, src, hnT, ident)
            phase_attn(sync, "B", hnT, drB, ysc)
            phase_outproj(sync, src, woB, ysc, out)
    return nc


def host_inputs(norm_gain, a_w_in, a_w_out, a_q_gain, a_k_gain, t5_bias, b_w_in, b_w_out, b_q_gain, b_k_gain, b_rpb):
    global _GEOM_B
    if _GEOM_B is None:
        _GEOM_B = _geom_B()
    tilesB, mapsB, ebwB = _GEOM_B
    f = lambda a: np.ascontiguousarray(np.asarray(a, dtype=np.float32))
    wA, wgA = _unit_weights(f(a_w_in)[0], 3)
    wB, wgB = _unit_weights(f(b_w_in)[0], 1)
    ng = f(norm_gain)

    def gcol(gn):
        gn = f(gn).reshape(-1, 64)
        o = np.ones((128, 3), np.float32)
        for g in range(gn.shape[0]):
            o[:, g] = np.tile(gn[g], 2)
        return o
    shared = dict(
        ident_d=np.eye(128, dtype=np.float32),
        wA=wA, wgA=wgA, woA=f(a_w_out)[0],
        biasA=np.ascontiguousarray(_bias_tiles_A(f(t5_bias)).reshape(8, 3, 128, 512)),
        ngA=np.ascontiguousarray(ng[0].reshape(8, 128).T), gqA=gcol(a_q_gain[0]), gkA=gcol(a_k_gain[0]),
        wB=wB, wgB=wgB, woB=f(b_w_out)[0],
        biasB=_bias_tiles_B(f(b_rpb)[0], mapsB, ebwB),
        ngB=np.ascontiguousarray(ng[1].reshape(8, 128).T), gqB=gcol(b_q_gain), gkB=gcol(b_k_gain),
    )
    return shared


def kernel(x, norm_gain, a_w_in, a_w_out, a_q_gain, a_k_gain, t5_bias, b_w_in, b_w_out, b_q_gain, b_k_gain, b_rpb):
    x = np.ascontiguousarray(np.asarray(x, dtype=np.float32))
    shared = host_inputs(norm_gain, a_w_in, a_w_out, a_q_gain, a_k_gain, t5_bias, b_w_in, b_w_out, b_q_gain, b_k_gain, b_rpb)
    nc = build_nc("AB")
    in_maps = [dict(shared, x=x[c]) for c in range(NCORES)]
    res = run_bass_kernel_spmd(nc, in_maps, core_ids=list(range(NCORES)))
    return np.stack([np.asarray(r["out"], dtype=np.float32) for r in res.results], axis=0)
```
